# Optimizing a Trainium2 kernel written in Bass

```python
import math
import jax
import jax.numpy as jnp
from jax import lax
import numpy as np

D_MODEL = 1024
BATCH = 2
SEQ = 8192
DEPTH = 2

N_ATTN_LAYERS = (DEPTH + 1) // 2
N_SSD_LAYERS = DEPTH // 2

HEAD_DIM = 64
DIFF_HEADS = 4
DIFF_V = 2 * HEAD_DIM
SB_HEADS = 8
SB_DIM = HEAD_DIM
DIFF_QK_W = DIFF_HEADS * 2 * HEAD_DIM
SB_W = SB_HEADS * SB_DIM
MIX_WIDTH = DIFF_HEADS * DIFF_V + SB_W
ATTN_IN_WIDTH = 3 * DIFF_QK_W + 3 * SB_W
Q_BLOCK = 128
ROPE_THETA = 10000.0
NORM_EPS = 1e-6
SUBLN_EPS = 1e-5

SSD_EXPAND = 2
D_INNER = SSD_EXPAND * D_MODEL
SSD_HEAD_DIM = 64
SSD_HEADS = D_INNER // SSD_HEAD_DIM
SSD_GROUPS = 8
SSD_STATE = 128
SSD_CONV = 4
SSD_CHUNK = 128
SSD_GN = SSD_GROUPS * SSD_STATE
CONV_DIM = D_INNER + 2 * SSD_GN
SSD_IN_WIDTH = 2 * D_INNER + 2 * SSD_GN + SSD_HEADS

D_FF = (((8 * D_MODEL + 2) // 3 + 255) // 256) * 256

kernel_name = "hybrid_diffattn_stickbreak_ssd_swiglu"


def rmsnorm(x, w, eps=NORM_EPS):
    xf = x.astype(jnp.float32)
    y = xf * lax.rsqrt(jnp.mean(xf * xf, axis=-1, keepdims=True) + eps)
    return (y * w.astype(jnp.float32)).astype(x.dtype)


def rope(t, pos):
    half = HEAD_DIM // 2
    inv_freq = ROPE_THETA ** (-jnp.arange(half, dtype=jnp.float32) / half)
    ang = pos.astype(jnp.float32)[:, None] * inv_freq[None, :]
    shape = (1, pos.shape[0]) + (1,) * (t.ndim - 3) + (half,)
    cos = jnp.cos(ang).reshape(shape)
    sin = jnp.sin(ang).reshape(shape)
    tf = t.astype(jnp.float32)
    t1, t2 = tf[..., :half], tf[..., half:]
    out = jnp.concatenate([t1 * cos - t2 * sin, t2 * cos + t1 * sin], axis=-1)
    return out.astype(t.dtype)


def swiglu(h, w_gate, w_up, w_down):
    return (jax.nn.silu(h @ w_gate) * (h @ w_up)) @ w_down


def attn_mixer(h, w_in, lq1, lk1, lq2, lk2, subln_w, w_out, lambda_init):
    f32 = jnp.float32
    b, s, _ = h.shape
    nb = s // Q_BLOCK
    pos = jnp.arange(s, dtype=jnp.int32)
    proj = h @ w_in
    dq, dk, dv, sq, sk, sv = jnp.split(
        proj,
        [DIFF_QK_W, 2 * DIFF_QK_W, 3 * DIFF_QK_W, 3 * DIFF_QK_W + SB_W, 3 * DIFF_QK_W + 2 * SB_W],
        axis=-1)
    dq = rope(dq.reshape(b, s, DIFF_HEADS, 2, HEAD_DIM), pos).transpose(0, 2, 3, 1, 4)
    dk = rope(dk.reshape(b, s, DIFF_HEADS, 2, HEAD_DIM), pos).transpose(0, 2, 3, 1, 4)
    dv = dv.reshape(b, s, DIFF_HEADS, DIFF_V).transpose(0, 2, 1, 3)
    sq = sq.reshape(b, s, SB_HEADS, SB_DIM).transpose(0, 2, 1, 3)
    sk = sk.reshape(b, s, SB_HEADS, SB_DIM).transpose(0, 2, 1, 3)
    sv = sv.reshape(b, s, SB_HEADS, SB_DIM).transpose(0, 2, 1, 3)
    scale = HEAD_DIM ** -0.5
    lam = (jnp.exp(jnp.sum(lq1.astype(f32) * lk1.astype(f32)))
           - jnp.exp(jnp.sum(lq2.astype(f32) * lk2.astype(f32))) + lambda_init)

    dq_blocks = jnp.moveaxis(dq.reshape(b, DIFF_HEADS, 2, nb, Q_BLOCK, HEAD_DIM), 3, 0)
    sq_blocks = jnp.moveaxis(sq.reshape(b, SB_HEADS, nb, Q_BLOCK, SB_DIM), 2, 0)
    starts = jnp.arange(nb, dtype=jnp.int32) * Q_BLOCK

    def block(args):
        start, dq_b, sq_b = args
        qpos = start + jnp.arange(Q_BLOCK, dtype=jnp.int32)
        causal = pos[None, :] <= qpos[:, None]
        strict = pos[None, :] < qpos[:, None]
        sc = jnp.einsum('bhiqd,bhikd->bhiqk', dq_b, dk).astype(f32) * scale
        p = jax.nn.softmax(jnp.where(causal, sc, -jnp.inf), axis=-1)
        p_diff = p[:, :, 0] - lam * p[:, :, 1]
        o_diff = jnp.einsum('bhqk,bhkv->bhqv', p_diff.astype(dv.dtype), dv)
        z = jnp.einsum('bhqd,bhkd->bhqk', sq_b, sk).astype(f32) * scale
        log_keep = jnp.where(strict, jax.nn.log_sigmoid(-z), 0.0)
        tail = lax.cumsum(log_keep, axis=3, reverse=True) - log_keep
        a_sb = jnp.where(strict, jnp.exp(jax.nn.log_sigmoid(z) + tail), 0.0)
        o_sb = jnp.einsum('bhqk,bhkd->bhqd', a_sb.astype(sv.dtype), sv)
        return o_diff, o_sb

    o_diff, o_sb = lax.map(block, (starts, dq_blocks, sq_blocks))
    o_diff = o_diff.transpose(1, 0, 3, 2, 4).reshape(b, s, DIFF_HEADS, DIFF_V)
    o_diff = rmsnorm(o_diff, subln_w, SUBLN_EPS) * (1.0 - lambda_init)
    o_sb = o_sb.transpose(1, 0, 3, 2, 4).reshape(b, s, SB_W)
    mixed = jnp.concatenate([o_diff.reshape(b, s, DIFF_HEADS * DIFF_V), o_sb], axis=-1)
    return mixed @ w_out


def causal_depthwise_conv(u, w, bias):
    out = lax.conv_general_dilated(
        u, w[:, None, :].astype(u.dtype), window_strides=(1,), padding=[(SSD_CONV - 1, 0)],
        dimension_numbers=('NWC', 'WIO', 'NWC'), feature_group_count=u.shape[-1])
    return out + bias


def ssd_scan(x, dt, a, bmat, cmat):
    f32 = jnp.float32
    b, s, nh, p = x.shape
    nc = s // SSD_CHUNK
    r = nh // SSD_GROUPS
    L = SSD_CHUNK
    xdt = (x.astype(f32) * dt[..., None]).reshape(b, nc, L, SSD_GROUPS, r, p)
    acum = jnp.cumsum((dt * a).reshape(b, nc, L, SSD_GROUPS, r), axis=2)
    acum = acum.transpose(0, 1, 3, 4, 2)
    bm = bmat.astype(f32).reshape(b, nc, L, SSD_GROUPS, SSD_STATE)
    cm = cmat.astype(f32).reshape(b, nc, L, SSD_GROUPS, SSD_STATE)
    idx = jnp.arange(L)
    lower = idx[:, None] >= idx[None, :]
    seg = acum[..., :, None] - acum[..., None, :]
    decay = jnp.exp(jnp.where(lower, seg, -jnp.inf))
    cb = jnp.einsum('bclgn,bcmgn->bcglm', cm, bm)
    y_diag = jnp.einsum('bcgrlm,bcmgrp->bclgrp', cb[:, :, :, None] * decay, xdt)
    state_decay = jnp.exp(acum[..., -1:] - acum)
    states = jnp.einsum('bclgn,bcgrl,bclgrp->bcgrpn', bm, state_decay, xdt)
    chunk_decay = jnp.exp(acum[..., -1])

    def step(carry, inp):
        st, dec = inp
        return carry * dec[..., None, None] + st, carry

    init = jnp.zeros((b, SSD_GROUPS, r, p, SSD_STATE), f32)
    _, prev = lax.scan(step, init, (jnp.moveaxis(states, 1, 0), jnp.moveaxis(chunk_decay, 1, 0)))
    prev = jnp.moveaxis(prev, 0, 1)
    y_off = jnp.einsum('bclgn,bcgrpn,bcgrl->bclgrp', cm, prev, jnp.exp(acum))
    return (y_diag + y_off).reshape(b, s, nh, p)


def ssd_mixer(h, w_in, conv_w, conv_b, dt_bias, a_log, d_skip, gnorm_w, w_out):
    f32 = jnp.float32
    b, s, _ = h.shape
    proj = h @ w_in
    z, xbc, dt = jnp.split(proj, [D_INNER, D_INNER + CONV_DIM], axis=-1)
    xbc = jax.nn.silu(causal_depthwise_conv(xbc, conv_w, conv_b))
    xs, bmat, cmat = jnp.split(xbc, [D_INNER, D_INNER + SSD_GN], axis=-1)
    xs = xs.reshape(b, s, SSD_HEADS, SSD_HEAD_DIM)
    dt = jax.nn.softplus(dt.astype(f32) + dt_bias.astype(f32))
    a = -jnp.exp(a_log.astype(f32))
    y = ssd_scan(xs, dt, a,
                 bmat.reshape(b, s, SSD_GROUPS, SSD_STATE),
                 cmat.reshape(b, s, SSD_GROUPS, SSD_STATE))
    y = y + d_skip.astype(f32)[:, None] * xs.astype(f32)
    gated = y.reshape(b, s, D_INNER) * jax.nn.silu(z.astype(f32))
    gated = gated.reshape(b, s, SSD_GROUPS, D_INNER // SSD_GROUPS)
    gated = gated * lax.rsqrt(jnp.mean(gated * gated, axis=-1, keepdims=True) + SUBLN_EPS)
    y = (gated.reshape(b, s, D_INNER) * gnorm_w.astype(f32)).astype(h.dtype)
    return y @ w_out


def setup_inputs(seed: int = 0) -> dict:
    key = jax.random.key(seed)
    ks = iter(jax.random.split(key, 32))

    def nrm(shape, scale):
        return jax.random.normal(next(ks), shape, jnp.float32) * scale

    def gain(shape):
        return 1.0 + nrm(shape, 0.02)

    na, ns = N_ATTN_LAYERS, N_SSD_LAYERS
    x = nrm((BATCH, SEQ, D_MODEL), 1.0)
    attn_norm = gain((na, D_MODEL))
    attn_w_in = nrm((na, D_MODEL, ATTN_IN_WIDTH), D_MODEL ** -0.5)
    diff_lq1 = nrm((na, HEAD_DIM), 0.1)
    diff_lk1 = nrm((na, HEAD_DIM), 0.1)
    diff_lq2 = nrm((na, HEAD_DIM), 0.1)
    diff_lk2 = nrm((na, HEAD_DIM), 0.1)
    diff_subln = gain((na, DIFF_V))
    attn_w_out = nrm((na, MIX_WIDTH, D_MODEL), MIX_WIDTH ** -0.5)
    ssd_norm = gain((ns, D_MODEL))
    ssd_w_in = nrm((ns, D_MODEL, SSD_IN_WIDTH), D_MODEL ** -0.5)
    ssd_conv_w = nrm((ns, SSD_CONV, CONV_DIM), SSD_CONV ** -0.5)
    ssd_conv_b = nrm((ns, CONV_DIM), 0.02)
    u = jax.random.uniform(next(ks), (ns, SSD_HEADS), jnp.float32)
    dt0 = jnp.exp(u * (math.log(0.1) - math.log(0.001)) + math.log(0.001))
    ssd_dt_bias = dt0 + jnp.log(-jnp.expm1(-dt0))
    ssd_a_log = jnp.log(jax.random.uniform(next(ks), (ns, SSD_HEADS), jnp.float32, 1.0, 16.0))
    ssd_d = gain((ns, SSD_HEADS))
    ssd_gnorm = gain((ns, D_INNER))
    ssd_w_out = nrm((ns, D_INNER, D_MODEL), D_INNER ** -0.5)
    ffn_norm = gain((DEPTH, D_MODEL))
    ffn_w_gate = nrm((DEPTH, D_MODEL, D_FF), D_MODEL ** -0.5)
    ffn_w_up = nrm((DEPTH, D_MODEL, D_FF), D_MODEL ** -0.5)
    ffn_w_down = nrm((DEPTH, D_FF, D_MODEL), D_FF ** -0.5)
    final_norm = gain((D_MODEL,))
    return {
        "x": x,
        "attn_norm": attn_norm, "attn_w_in": attn_w_in,
        "diff_lq1": diff_lq1, "diff_lk1": diff_lk1, "diff_lq2": diff_lq2, "diff_lk2": diff_lk2,
        "diff_subln": diff_subln, "attn_w_out": attn_w_out,
        "ssd_norm": ssd_norm, "ssd_w_in": ssd_w_in, "ssd_conv_w": ssd_conv_w, "ssd_conv_b": ssd_conv_b,
        "ssd_dt_bias": ssd_dt_bias, "ssd_a_log": ssd_a_log, "ssd_d": ssd_d, "ssd_gnorm": ssd_gnorm,
        "ssd_w_out": ssd_w_out,
        "ffn_norm": ffn_norm, "ffn_w_gate": ffn_w_gate, "ffn_w_up": ffn_w_up, "ffn_w_down": ffn_w_down,
        "final_norm": final_norm,
    }


def reference(x, attn_norm, attn_w_in, diff_lq1, diff_lk1, diff_lq2, diff_lk2, diff_subln, attn_w_out,
              ssd_norm, ssd_w_in, ssd_conv_w, ssd_conv_b, ssd_dt_bias, ssd_a_log, ssd_d, ssd_gnorm,
              ssd_w_out, ffn_norm, ffn_w_gate, ffn_w_up, ffn_w_down, final_norm):
    h = x
    for layer in range(DEPTH):
        i = layer // 2
        if layer % 2 == 0:
            lambda_init = 0.8 - 0.6 * math.exp(-0.3 * layer)
            h = h + attn_mixer(rmsnorm(h, attn_norm[i]), attn_w_in[i],
                               diff_lq1[i], diff_lk1[i], diff_lq2[i], diff_lk2[i],
                               diff_subln[i], attn_w_out[i], lambda_init)
        else:
            h = h + ssd_mixer(rmsnorm(h, ssd_norm[i]), ssd_w_in[i], ssd_conv_w[i], ssd_conv_b[i],
                              ssd_dt_bias[i], ssd_a_log[i], ssd_d[i], ssd_gnorm[i], ssd_w_out[i])
        h = h + swiglu(rmsnorm(h, ffn_norm[layer]), ffn_w_gate[layer], ffn_w_up[layer], ffn_w_down[layer])
    return rmsnorm(h, final_norm)
```

```python
import contextlib
import math
import os

import numpy as np
import concourse.bass as bass
import concourse.mybir as mybir
from concourse.bass_utils import run_bass_kernel_spmd

F32 = mybir.dt.float32
BF16 = mybir.dt.bfloat16
AF = mybir.ActivationFunctionType
ALU = mybir.AluOpType

D = 1024
SEQ = 8192
NT = 2048
NEG = -30000.0
GROUPS = [[0, 1, 2, 3], [4, 5, 6, 7]]
D_FF = 2816
FH = D_FF // 2


class Buf:
    __slots__ = ("name", "last_w", "rd_c", "rd_d")
    ALL = []

    def __init__(self, name=""):
        self.name = name
        self.last_w = None
        self.rd_c = {}
        self.rd_d = []
        Buf.ALL.append(self)

    @staticmethod
    def reset_all():
        for b in Buf.ALL:
            b.last_w = None
            b.rd_c = {}
            b.rd_d = []


class _Rec:
    def __init__(self):
        self.call = None

    def __getattr__(self, name):
        def f(*a, **k):
            self.call = (name, a, k)
            return None
        return f


def _freeze(fn):
    r = _Rec()
    fn(r)
    name, a, k = r.call
    return lambda e: getattr(e, name)(*a, **k)


class SemState:
    def __init__(self, nc):
        self.nc = nc
        self.st = contextlib.ExitStack()
        self.sems = {}
        self.base = {e: 0 for e in Sched.ENGS}
        self.semval = {}
        self.bar_count = 0

    def sem(self, key):
        if key not in self.sems:
            name = key if isinstance(key, str) else "_".join(str(k) for k in key)
            self.sems[key] = self.st.enter_context(self.nc.semaphore("s_" + name))
        return self.sems[key]

    def close(self):
        self.st.close()


class Sched:
    ENGS = ("sp", "act", "dve", "pool", "pe")
    STATE = None

    def __init__(self, nc, n_dma_sems=6):
        self.nc = nc
        self.state = Sched.STATE
        self.ops = {e: [] for e in self.ENGS}
        self.nops = {e: 0 for e in self.ENGS}
        self.known = {e: {} for e in self.ENGS}
        self.needed = {e: set() for e in self.ENGS}
        self.n_dma_sems = n_dma_sems
        self.dma_ring = {e: 0 for e in self.ENGS}
        self.semval = self.state.semval
        self.dma_keys = []

    def _wait(self, eng, ev):
        if ev is None:
            return
        if ev[0] == "c":
            _, src, idx = ev
            if src == "pe" and eng == "pe":
                return
            if self.known[eng].get(src, 0) >= idx:
                return
            self.known[eng][src] = idx
            self.needed[src].add(idx)
            self.ops[eng].append(("wait", ev))
        else:
            _, key, val = ev
            if self.known[eng].get(key, 0) >= val:
                return
            self.known[eng][key] = val
            self.ops[eng].append(("wait", ev))

    def _deps(self, eng, reads, writes):
        for b in reads:
            self._wait(eng, b.last_w)
        for b in writes:
            self._wait(eng, b.last_w)
            for src, idx in list(b.rd_c.items()):
                self._wait(eng, ("c", src, idx))
            for ev in b.rd_d:
                self._wait(eng, ev)

    def _commit(self, ev, reads, writes):
        for b in reads:
            if ev[0] == "c":
                if b.rd_c.get(ev[1], 0) < ev[2]:
                    b.rd_c[ev[1]] = ev[2]
            else:
                b.rd_d.append(ev)
        for b in writes:
            b.last_w = ev
            b.rd_c = {}
            b.rd_d = []

    def op(self, eng, fn, reads=(), writes=()):
        self._deps(eng, reads, writes)
        self.nops[eng] += 1
        idx = self.nops[eng]
        ev = ("c", eng, idx)
        self.ops[eng].append(("op", idx, _freeze(fn)))
        self._commit(ev, reads, writes)
        return ev

    def dma(self, eng, out, in_, reads=(), writes=(), inc=16, fn=None, ring=None):
        if ring is None:
            i = self.dma_ring[eng]
            self.dma_ring[eng] = i + 1
            key = ("dsem", eng, i % self.n_dma_sems)
        else:
            key = ring
        if key not in self.semval:
            self.semval[key] = 0
        if key not in self.dma_keys:
            self.dma_keys.append(key)
        prev = self.semval[key]
        if prev > 0:
            self._wait(eng, ("d", key, prev))
        self._deps(eng, reads, writes)
        self.semval[key] = prev + inc
        ev = ("d", key, prev + inc)
        if fn is None:
            fn = lambda e, out=out, in_=in_: e.dma_start(out=out, in_=in_)
        self.ops[eng].append(("dma", key, fn, inc))
        self._commit(ev, reads, writes)
        return ev

    def drain(self):
        for key in self.dma_keys:
            self._wait(key[1], ("d", key, self.semval[key]))

    def emit(self):
        nc = self.nc
        stt = self.state
        self.drain()
        sems = {}
        for e in self.ENGS:
            sems[e] = stt.sem("eng_" + e)
        for k in self.dma_keys:
            sems[k] = stt.sem(k)
        bar = stt.sem("bar")
        rank = {}
        for e in self.ENGS:
            if self.nops[e] > 0:
                self.needed[e].add(self.nops[e])
            rank[e] = {idx: stt.base[e] + r + 1 for r, idx in enumerate(sorted(self.needed[e]))}
        bar_target = stt.bar_count + 5
        with nc.Block() as block:
            def run(ename):
                def body(eng):
                    for ent in self.ops[ename]:
                        if ent[0] == "wait":
                            ev = ent[1]
                            if ev[0] == "c":
                                eng.wait_ge(sems[ev[1]], rank[ev[1]][ev[2]])
                            else:
                                eng.wait_ge(sems[ev[1]], ev[2])
                        elif ent[0] == "op":
                            ins = ent[2](eng)
                            if ent[1] in rank[ename]:
                                ins.then_inc(sems[ename], 1)
                        else:
                            ins = ent[2](eng)
                            ins.then_inc(sems[ent[1]], ent[3])
                    if self.nops[ename] > 0:
                        eng.wait_ge(sems[ename], rank[ename][self.nops[ename]])
                    eng.sem_inc(bar, 1)
                    eng.wait_ge(bar, bar_target)
                return body

            block.sync(run("sp"))
            block.scalar(run("act"))
            block.vector(run("dve"))
            block.gpsimd(run("pool"))
            block.tensor(run("pe"))
        if os.environ.get("KVERB"):
            print("emit: ops", {e: self.nops[e] for e in self.ENGS}, "entries", {e: len(self.ops[e]) for e in self.ENGS}, flush=True)
        for e in self.ENGS:
            stt.base[e] += len(self.needed[e])
        stt.bar_count = bar_target
        Buf.reset_all()


class Ctx:
    def __init__(self, nc):
        self.nc = nc
        self.S = Sched(nc)
        self.st = contextlib.ExitStack()
        self.n = 0

    CNT = [0]

    def sb(self, shape, dt=F32, name=None):
        Ctx.CNT[0] += 1
        return self.st.enter_context(self.nc.sbuf_tensor(name or ("t%d" % Ctx.CNT[0]), list(shape), dt))

    def ps(self, shape, dt=F32, name=None):
        Ctx.CNT[0] += 1
        return self.st.enter_context(self.nc.psum_tensor(name or ("p%d" % Ctx.CNT[0]), list(shape), dt))

    def close(self):
        self.S.emit()
        self.st.close()


COLL_N = [0]


def collective(S, kind, src, dst, reads, writes):
    def fn(e):
        return e.collective_compute(kind, ALU.bypass, replica_groups=GROUPS, ins=[src], outs=[dst])
    COLL_N[0] += 1
    return S.dma("pool", None, None, reads=reads, writes=writes, inc=1, fn=fn, ring=("csem", "pool", COLL_N[0] % 4))


def rmsnorm_fm(C, xT, bx, wn, bwn, xn, bxn, ones_f, bones, pss, eps=1e-6, ntok=NT, sqb=None, rstd=None, brstd=None, slices=None, out_f32=False):
    S = C.S
    if slices is None:
        slices = [slice(tt * 512, (tt + 1) * 512) for tt in range(ntok // 512)]
    for tt, sl in enumerate(slices):
        wd_ = sl.stop - sl.start
        ps, bps = pss[tt % len(pss)]
        for k in range(8):
            sq, bsq = sqb[k % len(sqb)]
            S.op("act", lambda e, sq=sq, k=k, sl=sl: e.activation(out=sq[:, 0:wd_], in_=xT[:, k, sl], func=AF.Square),
                 reads=[bx], writes=[bsq])
            S.op("pe", lambda e, ps=ps, sq=sq, k=k: e.matmul(ps[:, 0:wd_], lhsT=ones_f[:], rhs=sq[:, 0:wd_], start=(k == 0), stop=(k == 7)),
                 reads=[bsq, bones], writes=[bps])
        S.op("act", lambda e, ps=ps, sl=sl: e.activation(out=rstd[:, sl], in_=ps[:, 0:wd_], func=AF.Ln, scale=1.0 / D, bias=eps),
             reads=[bps], writes=[brstd])
        S.op("act", lambda e, sl=sl: e.activation(out=rstd[:, sl], in_=rstd[:, sl], func=AF.Exp, scale=-0.5),
             reads=[brstd], writes=[brstd])
        for k in range(8):
            S.op("dve", lambda e, k=k, sl=sl: e.scalar_tensor_tensor(out=xn[:, k, sl], in0=xT[:, k, sl], scalar=wn[:, k:k + 1],
                                                                     in1=rstd[:, sl], op0=ALU.mult, op1=ALU.mult),
                 reads=[bx, bwn, brstd], writes=[bxn])


def ffn_fm(C, T, layer, hT, bh, ones_f, bones, banks):
    S = C.S
    xn = C.sb([128, 8, NT], BF16); bxn = Buf()
    rstd = C.sb([128, NT], F32); brstd = Buf()
    wn = C.sb([128, 8], F32); bwn = Buf()
    S.dma("sp", wn[:], T["ffn_norm"][layer], writes=[bwn])
    sqb = [(C.sb([128, 512], F32), Buf()) for _ in range(2)]
    rmsnorm_fm(C, hT, bh, wn, bwn, xn, bxn, ones_f, bones, banks[0:2], sqb=sqb, rstd=rstd, brstd=brstd)
    wg = C.sb([128, 8, FH], BF16); bwg = Buf()
    wu = C.sb([128, 8, FH], BF16); bwu = Buf()
    wd = C.sb([128, 11, D], BF16); bwd = Buf()
    act = [(C.sb([128, 11, 512], BF16), Buf()) for _ in range(2)]
    sg = [(C.sb([128, 512], F32), Buf()) for _ in range(2)]
    for half in range(2):
        c0 = half * FH
        for k in range(8):
            S.dma("pool", wg[:, k, :], T["ffn_wg"][layer][k * 128:(k + 1) * 128, c0:c0 + FH], writes=[bwg])
            S.dma("pool", wu[:, k, :], T["ffn_wu"][layer][k * 128:(k + 1) * 128, c0:c0 + FH], writes=[bwu])
        for fc in range(11):
            S.dma("pool", wd[:, fc, :], T["ffn_wd"][layer][c0 + fc * 128:c0 + (fc + 1) * 128, :], writes=[bwd])
        for tt in range(NT // 512):
            sl = slice(tt * 512, (tt + 1) * 512)
            a, ba = act[tt % 2]
            for fc in range(11):
                pg, bpg = banks[(2 * fc) % 4]
                pu, bpu = banks[(2 * fc + 1) % 4]
                for k in range(8):
                    S.op("pe", lambda e, pg=pg, k=k, fc=fc, sl=sl: e.matmul(pg[:], lhsT=wg[:, k, fc * 128:(fc + 1) * 128], rhs=xn[:, k, sl],
                                                                            start=(k == 0), stop=(k == 7)),
                         reads=[bwg, bxn], writes=[bpg])
                for k in range(8):
                    S.op("pe", lambda e, pu=pu, k=k, fc=fc, sl=sl: e.matmul(pu[:], lhsT=wu[:, k, fc * 128:(fc + 1) * 128], rhs=xn[:, k, sl],
                                                                            start=(k == 0), stop=(k == 7)),
                         reads=[bwu, bxn], writes=[bpu])
                s_, bs_ = sg[fc % 2]
                S.op("act", lambda e, s_=s_, pg=pg: e.activation(out=s_[:], in_=pg[:], func=AF.Silu), reads=[bpg], writes=[bs_])
                S.op("dve", lambda e, a=a, fc=fc, s_=s_, pu=pu: e.tensor_tensor(out=a[:, fc, :], in0=pu[:], in1=s_[:], op=ALU.mult),
                     reads=[bpu, bs_], writes=[ba])
            for oc in range(8):
                po, bpo = banks[4 + oc % 4]
                for fc in range(11):
                    S.op("pe", lambda e, po=po, fc=fc, oc=oc, a=a: e.matmul(po[:], lhsT=wd[:, fc, oc * 128:(oc + 1) * 128], rhs=a[:, fc, :],
                                                                            start=(fc == 0), stop=(fc == 10)),
                         reads=[bwd, ba], writes=[bpo])
                S.op("dve", lambda e, po=po, oc=oc, sl=sl: e.tensor_tensor(out=hT[:, oc, sl], in0=po[:], in1=hT[:, oc, sl], op=ALU.add),
                     reads=[bpo, bh], writes=[bh])


def load_consts(C, T):
    S = C.S
    k = {}
    idf = C.sb([128, 128], F32); bidf = Buf()
    S.dma("sp", idf[:], T["ident"], writes=[bidf])
    k["identb"] = C.sb([128, 128], BF16); k["bident"] = Buf()
    S.op("dve", lambda e: e.tensor_copy(out=k["identb"][:], in_=idf[:]), reads=[bidf], writes=[k["bident"]])
    k["ones_f"] = C.sb([128, 128], F32); k["bones"] = Buf()
    S.op("pool", lambda e: e.memset(k["ones_f"][:], 1.0), writes=[k["bones"]])
    k["idf"] = idf; k["bidf"] = bidf
    return k


def phase_attn_inproj(nc, T):
    C = Ctx(nc); S = C.S
    K = load_consts(C, T)
    banks = [(C.ps([128, 512], F32), Buf()) for _ in range(8)]
    xT = C.sb([128, 8, NT], F32); bx = Buf()
    xsrc = T["xT0"].rearrange("(k p) n -> p k n", p=128)
    for tt in range(4):
        S.dma("sp", xT[:, :, tt * 512:(tt + 1) * 512], xsrc[:, :, tt * 512:(tt + 1) * 512], writes=[bx])
    cs = C.sb([128, 2, NT], F32); bcs = Buf()
    S.dma("act", cs[:], T["cs_tab"], writes=[bcs])
    wn = C.sb([128, 8], F32); bwn = Buf()
    S.dma("act", wn[:], T["attn_norm"], writes=[bwn])
    xn = C.sb([128, 8, NT], BF16); bxn = Buf()
    rstd = C.sb([128, NT], F32); brstd = Buf()
    sqb = [(C.sb([128, 512], F32), Buf()) for _ in range(2)]
    rmsnorm_fm(C, xT, bx, wn, bwn, xn, bxn, K["ones_f"], K["bones"], banks[0:2], sqb=sqb, rstd=rstd, brstd=brstd)

    kcut = int(os.environ.get("KCUT", "9"))
    if kcut <= 1:
        C.close()
        return
    wb = [(C.sb([128, 8, 512], BF16), Buf()) for _ in range(2)]
    wsw = [(C.sb([128, 8, 512], BF16), Buf()) for _ in range(2)]
    ob = [(C.sb([128, NT], BF16), Buf()) for _ in range(2)]
    t1 = [(C.sb([128, 512], F32), Buf()) for _ in range(2)]
    t2 = [(C.sb([128, 512], F32), Buf()) for _ in range(2)]
    vst = [(C.sb([128, 4, 129], BF16), Buf()) for _ in range(2)]
    vss = [(C.sb([128, 512], BF16), Buf()) for _ in range(2)]
    for v, bv in vst:
        S.op("pool", lambda e, v=v: e.memset(v[:], 1.0), writes=[bv])
    wsrc = T["w_in0"].rearrange("(k p) c -> p k c", p=128)
    swsrc = T["w_sw0"].rearrange("(k p) c -> p k c", p=128)
    dst_fm = {0: ("QT", 0), 1: ("KTd", 0), 3: ("QT", 512), 4: ("KTs", 0)}
    nchunk = 0
    for g in range(6):
        w, bw = wb[g % 2]
        for k in range(8):
            S.dma("pool", w[:, k, :], wsrc[:, k, g * 512:(g + 1) * 512], writes=[bw])
        if g < 2:
            w2, bw2 = wsw[g % 2]
            for k in range(8):
                S.dma("pool", w2[:, k, :], swsrc[:, k, g * 512:(g + 1) * 512], writes=[bw2])
        if g in dst_fm:
            dname, roff = dst_fm[g]
            for cc in range(4):
                o, bo = ob[nchunk % 2]; nchunk += 1
                for tt in range(4):
                    sl = slice(tt * 512, (tt + 1) * 512)
                    p1, bp1 = banks[2 + (2 * tt) % 4]
                    for k in range(8):
                        S.op("pe", lambda e, p1=p1, w=w, k=k, cc=cc, sl=sl: e.matmul(p1[:], lhsT=w[:, k, cc * 128:(cc + 1) * 128], rhs=xn[:, k, sl],
                                                                                     start=(k == 0), stop=(k == 7)),
                             reads=[bw, bxn], writes=[bp1])
                    if g < 2:
                        p2, bp2 = banks[2 + (2 * tt + 1) % 4]
                        for k in range(8):
                            S.op("pe", lambda e, p2=p2, w2=w2, k=k, cc=cc, sl=sl: e.matmul(p2[:], lhsT=w2[:, k, cc * 128:(cc + 1) * 128], rhs=xn[:, k, sl],
                                                                                           start=(k == 0), stop=(k == 7)),
                                 reads=[bw2, bxn], writes=[bp2])
                        a1, ba1 = t1[tt % 2]
                        a2, ba2 = t2[tt % 2]
                        S.op("dve", lambda e, a1=a1, p1=p1, sl=sl: e.tensor_tensor(out=a1[:], in0=p1[:], in1=cs[:, 0, sl], op=ALU.mult),
                             reads=[bp1, bcs], writes=[ba1])
                        S.op("dve", lambda e, a2=a2, p2=p2, sl=sl: e.tensor_tensor(out=a2[:], in0=p2[:], in1=cs[:, 1, sl], op=ALU.mult),
                             reads=[bp2, bcs], writes=[ba2])
                        S.op("pool", lambda e, o=o, a1=a1, a2=a2, sl=sl: e.tensor_tensor(out=o[:, sl], in0=a1[:], in1=a2[:], op=ALU.add),
                             reads=[ba1, ba2], writes=[bo])
                    else:
                        sc = 0.125 if g == 3 else 1.0
                        S.op("act", lambda e, o=o, p1=p1, sl=sl, sc=sc: e.activation(out=o[:, sl], in_=p1[:], func=AF.Copy, scale=sc),
                             reads=[bp1], writes=[bo])
                if dname == "QT":
                    r0 = roff + cc * 128
                    S.dma("sp", T["QT"][r0:r0 + 128, :], o[:], reads=[bo], writes=[T["b_QT"]])
                else:
                    ch, r0 = cc // 2, (cc % 2) * 128
                    S.dma("sp", T[dname][ch][r0:r0 + 128, :], o[:], reads=[bo], writes=[T["b_" + dname][ch]])
                    if cc % 2 == 1:
                        collective(S, "AllGather", T[dname][ch], T[dname + "_all"][ch], reads=[T["b_" + dname][ch]], writes=[T["b_" + dname + "_all"][ch]])
        else:
            for tb in range(16):
                p1, bp1 = banks[2 + tb % 4]
                for k in range(8):
                    S.op("pe", lambda e, p1=p1, w=w, k=k, tb=tb: e.matmul(p1[:], lhsT=xn[:, k, tb * 128:(tb + 1) * 128], rhs=w[:, k, :],
                                                                          start=(k == 0), stop=(k == 7)),
                         reads=[bw, bxn], writes=[bp1])
                if g == 2:
                    v, bv = vst[tb % 2]
                    S.op("act", lambda e, v=v, p1=p1: e.activation(out=v[:, :, 0:128], in_=p1[:].rearrange("p (h c) -> p h c", c=128), func=AF.Copy),
                         reads=[bp1], writes=[bv])
                    ch, r0 = tb // 4, (tb % 4) * 128
                    S.dma("sp", T["Vd"][ch][r0:r0 + 128, :], v[:].rearrange("p h c -> p (h c)"), reads=[bv], writes=[T["b_Vd"][ch]])
                    if tb % 4 == 3:
                        collective(S, "AllGather", T["Vd"][ch], T["Vd_all"][ch], reads=[T["b_Vd"][ch]], writes=[T["b_Vd_all"][ch]])
                else:
                    v, bv = vss[tb % 2]
                    S.op("act", lambda e, v=v, p1=p1: e.activation(out=v[:], in_=p1[:], func=AF.Copy), reads=[bp1], writes=[bv])
                    ch, r0 = tb // 8, (tb % 8) * 128
                    S.dma("sp", T["Vs"][ch][r0:r0 + 128, :], v[:], reads=[bv], writes=[T["b_Vs"][ch]])
                    if tb % 8 == 7:
                        collective(S, "AllGather", T["Vs"][ch], T["Vs_all"][ch], reads=[T["b_Vs"][ch]], writes=[T["b_Vs_all"][ch]])
    C.close()


def tile_iters(J):
    out = []
    for kb in range(16 * J + 16):
        i0 = max(0, kb // 4 - 4 * J)
        z = kb - 16 * J if kb >= 16 * J else None
        out.append((kb, kb % 4, kb // 4, i0, z))
    return out


def phase_attn_core(nc, T):
    C = Ctx(nc); S = C.S
    K = load_consts(C, T)
    identb, bident = K["identb"], K["bident"]
    ones_f, bones = K["ones_f"], K["bones"]
    psall = C.ps([128, 8, 512], F32)
    banks = [(psall[:, i, :], Buf()) for i in range(8)]
    nuf = C.sb([128, 128], F32); bnuf = Buf()
    S.dma("sp", nuf[:], T["nu"], writes=[bnuf])
    NU = C.sb([128, 128], BF16); bNU = Buf()
    S.op("dve", lambda e: e.tensor_copy(out=NU[:], in_=nuf[:]), reads=[bnuf], writes=[bNU])
    onesb = C.sb([128, 128], BF16); bonesb = Buf()
    S.op("pool", lambda e: e.memset(onesb[:], 1.0), writes=[bonesb])
    nmI = C.sb([128, 16, 512], BF16); bnmI = Buf()
    nmS = C.sb([128, 16, 512], BF16); bnmS = Buf()
    for z in range(0, 16, 4):
        S.dma("pool", nmI[:, z:z + 4, :], T["nm_incl"][:, z:z + 4, :], writes=[bnmI])
        S.dma("pool", nmS[:, z:z + 4, :], T["nm_strict"][:, z:z + 4, :], writes=[bnmS])
    lv = C.sb([128, 4, 64], F32); blv = Buf()
    for i, nm in enumerate(("lq1", "lk1", "lq2", "lk2")):
        S.dma("sp", lv[:, i:i + 1, :], T[nm].partition_broadcast(128), writes=[blv])
    lt = C.sb([128, 2, 64], F32); blt = Buf()
    S.op("dve", lambda e: e.tensor_tensor(out=lt[:, 0, :], in0=lv[:, 0, :], in1=lv[:, 1, :], op=ALU.mult), reads=[blv], writes=[blt])
    S.op("dve", lambda e: e.tensor_tensor(out=lt[:, 1, :], in0=lv[:, 2, :], in1=lv[:, 3, :], op=ALU.mult), reads=[blv], writes=[blt])
    ls = C.sb([128, 4], F32); bls = Buf()
    S.op("dve", lambda e: e.reduce_sum(out=ls[:, 0:1], in_=lt[:, 0, :], axis=mybir.AxisListType.X), reads=[blt], writes=[bls])
    S.op("dve", lambda e: e.reduce_sum(out=ls[:, 1:2], in_=lt[:, 1, :], axis=mybir.AxisListType.X), reads=[blt], writes=[bls])
    S.op("act", lambda e: e.activation(out=ls[:, 0:2], in_=ls[:, 0:2], func=AF.Exp), reads=[bls], writes=[bls])
    S.op("dve", lambda e: e.tensor_tensor(out=ls[:, 2:3], in0=ls[:, 1:2], in1=ls[:, 0:1], op=ALU.subtract), reads=[bls], writes=[bls])
    S.op("dve", lambda e: e.tensor_scalar(out=ls[:, 3:4], in0=ls[:, 2:3], scalar1=-0.2, scalar2=None, op0=ALU.add), reads=[bls], writes=[bls])
    negl = ls[:, 3:4]
    sl_ = C.sb([128, 1], F32); bsl = Buf()
    S.dma("sp", sl_[:], T["subln"].rearrange("o v -> v o"), writes=[bsl])
    S.op("dve", lambda e: e.tensor_scalar(out=sl_[:], in0=sl_[:], scalar1=0.8, scalar2=None, op0=ALU.mult), reads=[bsl], writes=[bsl])

    kbuf = [(C.sb([128, 4, NT], BF16), Buf()) for _ in range(2)]
    vbuf = [(C.sb([128, 64 * 129], BF16), Buf()) for _ in range(2)]
    qbuf = [(C.sb([128, NT], BF16), Buf()) for _ in range(2)]
    ostg = [(C.sb([128, 512], BF16), Buf()) for _ in range(2)]
    nload = 0
    nout = 0

    Eb = [(C.sb([128, 2, 512], BF16), Buf()) for _ in range(2)]
    ep = [(C.sb([128, 512], F32), Buf()) for _ in range(5)]
    ktd_all = [a.rearrange("(r x) n -> x r n", r=4) for a in T["KTd_all"]]
    vd_all = [a.rearrange("(r j t) (h c) -> t r j h c", r=4, j=4, h=4) for a in T["Vd_all"]]
    for H in range(4):
        kt, bkt = kbuf[nload % 2]; vv, bvv = vbuf[nload % 2]; qt, bqt = qbuf[nload % 2]; nload += 1
        for r in range(4):
            S.dma("sp", kt[:, r, :], ktd_all[H // 2][(H % 2) * 128:(H % 2 + 1) * 128, r, :], reads=[T["b_KTd_all"][H // 2]], writes=[bkt])
            for vc in range(4):
                b0 = r * 16 + 4 * vc
                S.dma("sp", vv[:, b0 * 129:(b0 + 4) * 129].rearrange("p (j c) -> p j c", c=129), vd_all[vc][:, r, :, H, :],
                      reads=[T["b_Vd_all"][vc]], writes=[bvv])
        S.dma("sp", qt[:], T["QT"][H * 128:(H + 1) * 128, :], reads=[T["b_QT"]], writes=[bqt])
        v4 = vv[:].rearrange("p (b c) -> p b c", c=129)
        for J in range(4):
            its = tile_iters(J)
            q0 = J * 512
            for bk_i in (4, 5, 6, 7):
                S.op("dve", lambda e, bk_i=bk_i: e.memset(banks[bk_i][0], 0.0), writes=[banks[bk_i][1]])

            def stage1(it, slot):
                kb, r, j, i0, z = it
                w0 = i0 * 128
                for m in range(2):
                    ps, bps = banks[2 * slot + m]
                    S.op("pe", lambda e, ps=ps, m=m, r=r, j=j, w0=w0: e.matmul(
                        ps[:, w0:512], lhsT=kt[m * 64:(m + 1) * 64, r, j * 128:(j + 1) * 128], rhs=qt[m * 64:(m + 1) * 64, q0 + w0:q0 + 512],
                        start=True, stop=(z is None)), reads=[bkt, bqt], writes=[bps])
                    if z is not None:
                        S.op("pe", lambda e, ps=ps, z=z, w0=w0: e.matmul(ps[:, w0:512], lhsT=identb[:], rhs=nmI[:, z, w0:512], start=False, stop=True),
                             reads=[bident, bnmI], writes=[bps])

            def stage2(it, slot):
                kb, r, j, i0, z = it
                w0 = i0 * 128
                E, bE = Eb[slot]
                S.op("act", lambda e, E=E, slot=slot, w0=w0: e.activation(out=E[:, :, w0:512], in_=psall[:, 2 * slot:2 * slot + 2, w0:512], func=AF.Exp, scale=0.125),
                     reads=[banks[2 * slot][1], banks[2 * slot + 1][1]], writes=[bE])
                for m in range(2):
                    S.op("pe", lambda e, E=E, m=m, r=r, j=j, w0=w0: e.matmul(banks[4 + m][0][:, w0:512], lhsT=v4[:, r * 16 + j, 0:128], rhs=E[:, m, w0:512],
                                                                             start=False, stop=False, skip_group_check=True),
                         reads=[bE, bvv], writes=[banks[4 + m][1]])
                    S.op("pe", lambda e, E=E, m=m, w0=w0: e.matmul(banks[6 + m][0][:, w0:512], lhsT=onesb[:], rhs=E[:, m, w0:512],
                                                                   start=False, stop=False, skip_group_check=True),
                         reads=[bE, bonesb], writes=[banks[6 + m][1]])

            for n in range(len(its) + 1):
                if n < len(its):
                    stage1(its[n], n % 2)
                if n >= 1:
                    stage2(its[n - 1], (n - 1) % 2)
            (r1, br1), (t1, bt1), (t2, bt2), (od, bod), (sq, bsq) = ep
            S.op("dve", lambda e: e.reciprocal(out=r1[:], in_=banks[6][0]), reads=[banks[6][1]], writes=[br1])
            S.op("dve", lambda e: e.tensor_tensor(out=t1[:], in0=banks[4][0], in1=r1[:], op=ALU.mult), reads=[banks[4][1], br1], writes=[bt1])
            S.op("dve", lambda e: e.reciprocal(out=r1[:], in_=banks[7][0]), reads=[banks[7][1], bt1], writes=[br1])
            S.op("dve", lambda e: e.tensor_tensor(out=t2[:], in0=banks[5][0], in1=r1[:], op=ALU.mult), reads=[banks[5][1], br1], writes=[bt2])
            S.op("dve", lambda e: e.scalar_tensor_tensor(out=od[:], in0=t2[:], scalar=negl, in1=t1[:], op0=ALU.mult, op1=ALU.add),
                 reads=[bt1, bt2, bls], writes=[bod])
            S.op("act", lambda e: e.activation(out=sq[:], in_=od[:], func=AF.Square), reads=[bod], writes=[bsq])
            pss, bpss = banks[0]
            S.op("pe", lambda e: e.matmul(pss, lhsT=ones_f[:], rhs=sq[:], start=True, stop=True), reads=[bones, bsq], writes=[bpss])
            S.op("act", lambda e: e.activation(out=t1[:], in_=pss, func=AF.Ln, scale=1.0 / 128, bias=1e-5), reads=[bpss, bod], writes=[bt1])
            S.op("act", lambda e: e.activation(out=t1[:], in_=t1[:], func=AF.Exp, scale=-0.5), reads=[bt1], writes=[bt1])
            o_, bo_ = ostg[nout % 2]; nout += 1
            S.op("dve", lambda e, o_=o_: e.scalar_tensor_tensor(out=o_[:], in0=od[:], scalar=sl_[:, 0:1], in1=t1[:], op0=ALU.mult, op1=ALU.mult),
                 reads=[bod, bsl, bt1], writes=[bo_])
            S.dma("sp", T["mixT"][H * 128:(H + 1) * 128, q0:q0 + 512], o_[:], reads=[bo_], writes=[T["b_mixT"]])

    eb = [(C.sb([128, 2, 512], F32), Buf()) for _ in range(2)]
    spb = [(C.sb([128, 2, 512], F32), Buf()) for _ in range(2)]
    hib = [(C.sb([128, 2, 512], BF16), Buf()) for _ in range(2)]
    lob = [(C.sb([128, 2, 512], BF16), Buf()) for _ in range(2)]
    Ab = [(C.sb([128, 2, 512], BF16), Buf()) for _ in range(2)]
    fb = [(C.sb([128, 512], F32), Buf()) for _ in range(2)]
    OT = C.sb([128, 512], F32); bOT = Buf()
    kts_all = [a.rearrange("(r x) n -> x r n", r=4) for a in T["KTs_all"]]
    vs_all = [a.rearrange("(r j t) (pr c) -> t r j pr c", r=4, j=8, pr=4) for a in T["Vs_all"]]
    for pr in range(4):
        kt, bkt = kbuf[nload % 2]; vv, bvv = vbuf[nload % 2]; qt, bqt = qbuf[nload % 2]; nload += 1
        for r in range(4):
            S.dma("sp", kt[:, r, :], kts_all[pr // 2][(pr % 2) * 128:(pr % 2 + 1) * 128, r, :], reads=[T["b_KTs_all"][pr // 2]], writes=[bkt])
            for vc in range(2):
                b0 = r * 16 + 8 * vc
                S.dma("sp", vv[:, b0 * 128:(b0 + 8) * 128].rearrange("p (j c) -> p j c", c=128), vs_all[vc][:, r, :, pr, :],
                      reads=[T["b_Vs_all"][vc]], writes=[bvv])
        S.dma("sp", qt[:], T["QT"][512 + pr * 128:512 + (pr + 1) * 128, :], reads=[T["b_QT"]], writes=[bqt])
        v4 = vv[:, 0:64 * 128].rearrange("p (b c) -> p b c", c=128)
        for J in range(4):
            its = tile_iters(J)
            q0 = J * 512
            S.op("pool", lambda e: e.memset(OT[:], 0.0), writes=[bOT])

            def stage1(it, slot):
                kb, r, j, i0, z = it
                w0 = i0 * 128
                ee, bee = eb[slot]; sp_, bsp = spb[slot]; hi, bhi = hib[slot]; lo, blo = lob[slot]
                for hh in range(2):
                    p0 = hh * 64
                    ps, bps = banks[2 * slot + hh]
                    S.op("pe", lambda e, ps=ps, r=r, j=j, w0=w0, p0=p0: e.matmul(
                        ps[:, w0:512], lhsT=kt[p0:p0 + 64, r, j * 128:(j + 1) * 128], rhs=qt[p0:p0 + 64, q0 + w0:q0 + 512],
                        start=True, stop=(z is None)), reads=[bkt, bqt], writes=[bps])
                    if z is not None:
                        S.op("pe", lambda e, ps=ps, z=z, w0=w0: e.matmul(ps[:, w0:512], lhsT=identb[:], rhs=nmS[:, z, w0:512], start=False, stop=True),
                             reads=[bident, bnmS], writes=[bps])
                zz = psall[:, 2 * slot:2 * slot + 2, w0:512]
                bz = [banks[2 * slot][1], banks[2 * slot + 1][1]]
                S.op("act", lambda e, ee=ee, zz=zz, w0=w0: e.activation(out=ee[:, :, w0:512], in_=zz, func=AF.Exp), reads=bz, writes=[bee])
                S.op("act", lambda e, ee=ee, sp_=sp_, w0=w0: e.activation(out=sp_[:, :, w0:512], in_=ee[:, :, w0:512], func=AF.Ln, bias=1.0),
                     reads=[bee], writes=[bsp])
                S.op("dve", lambda e, hi=hi, sp_=sp_, w0=w0: e.tensor_copy(out=hi[:, :, w0:512], in_=sp_[:, :, w0:512]), reads=[bsp], writes=[bhi])
                S.op("dve", lambda e, lo=lo, hi=hi, sp_=sp_, w0=w0: e.tensor_tensor(out=lo[:, :, w0:512], in0=sp_[:, :, w0:512], in1=hi[:, :, w0:512], op=ALU.subtract),
                     reads=[bsp, bhi], writes=[blo])

            def stage2(it, slot):
                kb, r, j, i0, z = it
                w0 = i0 * 128
                hi, bhi = hib[slot]; lo, blo = lob[slot]; A, bA = Ab[slot]; f, bf_ = fb[slot]
                bz = [banks[2 * slot][1], banks[2 * slot + 1][1]]
                for hh in range(2):
                    ps, bps = banks[2 * slot + hh]
                    S.op("pe", lambda e, ps=ps, hi=hi, hh=hh, w0=w0: e.matmul(ps[:, w0:512], lhsT=NU[:], rhs=hi[:, hh, w0:512], start=False, stop=False, skip_group_check=True),
                         reads=[bNU, bhi], writes=[bps])
                    S.op("pe", lambda e, ps=ps, lo=lo, hh=hh, w0=w0: e.matmul(ps[:, w0:512], lhsT=NU[:], rhs=lo[:, hh, w0:512], start=False, stop=True, skip_group_check=True),
                         reads=[bNU, blo], writes=[bps])
                zz = psall[:, 2 * slot:2 * slot + 2, w0:512]
                S.op("act", lambda e, A=A, zz=zz, w0=w0: e.activation(out=A[:, :, w0:512], in_=zz, func=AF.Exp), reads=bz, writes=[bA])
                pP, bpP = banks[4 + slot]
                pC, bpC = banks[6 + slot]
                for hh in range(2):
                    p0 = hh * 64
                    S.op("pe", lambda e, pP=pP, A=A, hh=hh, p0=p0, r=r, j=j, w0=w0: e.matmul(pP[p0:p0 + 64, w0:512], lhsT=v4[:, r * 16 + j, p0:p0 + 64], rhs=A[:, hh, w0:512],
                                                                                            start=True, stop=True),
                         reads=[bA, bvv], writes=[bpP])
                    S.op("pe", lambda e, pC=pC, hi=hi, hh=hh, p0=p0, w0=w0: e.matmul(pC[p0:p0 + 64, w0:512], lhsT=onesb[:, 0:64], rhs=hi[:, hh, w0:512], start=True, stop=False),
                         reads=[bhi, bonesb], writes=[bpC])
                    S.op("pe", lambda e, pC=pC, lo=lo, hh=hh, p0=p0, w0=w0: e.matmul(pC[p0:p0 + 64, w0:512], lhsT=onesb[:, 0:64], rhs=lo[:, hh, w0:512], start=False, stop=True),
                         reads=[blo, bonesb], writes=[bpC])
                S.op("act", lambda e, f=f, pC=pC, w0=w0: e.activation(out=f[:, w0:512], in_=pC[:, w0:512], func=AF.Exp, scale=-1.0), reads=[bpC], writes=[bf_])
                S.op("dve", lambda e, f=f, w0=w0: e.tensor_tensor(out=OT[:, w0:512], in0=OT[:, w0:512], in1=f[:, w0:512], op=ALU.mult), reads=[bOT, bf_], writes=[bOT])
                S.op("dve", lambda e, pP=pP, w0=w0: e.tensor_tensor(out=OT[:, w0:512], in0=pP[:, w0:512], in1=OT[:, w0:512], op=ALU.add), reads=[bOT, bpP], writes=[bOT])

            for n in range(len(its) + 1):
                if n < len(its):
                    stage1(its[n], n % 2)
                if n >= 1:
                    stage2(its[n - 1], (n - 1) % 2)
            o_, bo_ = ostg[nout % 2]; nout += 1
            S.op("act", lambda e, o_=o_: e.activation(out=o_[:], in_=OT[:], func=AF.Copy), reads=[bOT], writes=[bo_])
            S.dma("sp", T["mixT"][(4 + pr) * 128:(5 + pr) * 128, q0:q0 + 512], o_[:], reads=[bo_], writes=[T["b_mixT"]])
    C.close()


def phase_attn_out_ffn(nc, T, stage):
    C = Ctx(nc); S = C.S
    K = load_consts(C, T)
    banks = [(C.ps([128, 512], F32), Buf()) for _ in range(8)]
    hT = C.sb([128, 8, NT], F32); bh = Buf()
    xsrc = T["xT0"].rearrange("(k p) n -> p k n", p=128)
    for tt in range(4):
        S.dma("sp", hT[:, :, tt * 512:(tt + 1) * 512], xsrc[:, :, tt * 512:(tt + 1) * 512], writes=[bh])
    with contextlib.ExitStack() as st2:
        mT = st2.enter_context(nc.sbuf_tensor("mTf", [128, 8, NT], BF16)); bmT = Buf()
        wo = st2.enter_context(nc.sbuf_tensor("wo", [128, 8, D], BF16)); bwo = Buf()
        msrc = T["mixT"].rearrange("(c p) n -> p c n", p=128)
        for c in range(8):
            S.dma("act", mT[:, c, :], msrc[:, c, :], reads=[T["b_mixT"]], writes=[bmT])
            S.dma("pool", wo[:, c, :], T["w_out0"][c * 128:(c + 1) * 128, :], writes=[bwo])
        for tt in range(4):
            sl = slice(tt * 512, (tt + 1) * 512)
            for oc in range(8):
                po, bpo = banks[oc % 4]
                for c in range(8):
                    S.op("pe", lambda e, po=po, c=c, oc=oc, sl=sl: e.matmul(po[:], lhsT=wo[:, c, oc * 128:(oc + 1) * 128], rhs=mT[:, c, sl],
                                                                            start=(c == 0), stop=(c == 7)),
                         reads=[bwo, bmT], writes=[bpo])
                S.op("dve", lambda e, po=po, oc=oc, sl=sl: e.tensor_tensor(out=hT[:, oc, sl], in0=po[:], in1=hT[:, oc, sl], op=ALU.add),
                     reads=[bpo, bh], writes=[bh])
        if stage == "mix":
            dst = T["dbg"].rearrange("(k p) n -> p k n", p=128)
            for c in range(8):
                S.op("dve", lambda e, c=c: e.tensor_copy(out=hT[:, c, :], in_=mT[:, c, :]), reads=[bmT, bh], writes=[bh])
            for tt in range(4):
                S.dma("sp", dst[:, :, tt * 512:(tt + 1) * 512], hT[:, :, tt * 512:(tt + 1) * 512], reads=[bh])
            C.S.emit()
            st2.close(); C.st.close()
            return
        if stage == "attn":
            dst = T["dbg"].rearrange("(k p) n -> p k n", p=128)
            for tt in range(4):
                S.dma("sp", dst[:, :, tt * 512:(tt + 1) * 512], hT[:, :, tt * 512:(tt + 1) * 512], reads=[bh])
            C.S.emit()
            st2.close(); C.st.close()
            return
        C.S.emit()
    C.S = Sched(nc); S = C.S
    ffn_fm(C, T, 0, hT, bh, K["ones_f"], K["bones"], banks)
    if stage == "ffn0":
        dst = T["dbg"].rearrange("(k p) n -> p k n", p=128)
        for tt in range(4):
            S.dma("sp", dst[:, :, tt * 512:(tt + 1) * 512], hT[:, :, tt * 512:(tt + 1) * 512], reads=[bh])
        C.close()
        return
    zc = C.sb([128, 4], F32); bzc = Buf()
    S.op("pool", lambda e: e.memset(zc[:], 0.0), writes=[bzc])
    for ch in range(16):
        k, hf = ch // 2, ch % 2
        S.dma("sp", T["HX"][ch][:, 0:3], zc[0:64, 0:3], reads=[bzc], writes=[T["b_HX"][ch]])
        S.dma("sp", T["HX"][ch][:, 3:3 + NT], hT[hf * 64:(hf + 1) * 64, k, :], reads=[bh], writes=[T["b_HX"][ch]])
        collective(S, "AllGather", T["HX"][ch], T["HXA"][ch], reads=[T["b_HX"][ch]], writes=[T["b_HXA"][ch]])
    C.close()


def load_h1_contig(C, T, x1, bx1, halo):
    S = C.S
    off = 3 if halo else 0
    cache = {}

    def rank_of(e):
        if "c" not in cache:
            cache["c"] = e.partition_id() % 4
        return cache["c"]
    for ch in range(16):
        k, hf = ch // 2, ch % 2
        p0 = hf * 64
        for r in range(4):
            dstv = x1[p0:p0 + 64, k, off:off + NT].rearrange("p (m r t) -> p m r t", m=4, r=4)[:, :, r, :]

            def fn(e, dstv=dstv, ch=ch, r=r):
                c = rank_of(e)
                src = T["HXA"][ch][r * 64:(r + 1) * 64, bass.ds(c * 512 + 3, 512)]
                return e.dma_start(out=dstv, in_=src.rearrange("p (m t) -> p m t", m=4))
            S.dma("sp", None, None, reads=[T["b_HXA"][ch]], writes=[bx1], fn=fn)
        if halo:
            def fn2(e, ch=ch, p0=p0, k=k):
                c = rank_of(e)
                src = T["HXA"][ch][3 * 64:4 * 64, bass.ds(c * 512, 3)]
                return e.dma_start(out=x1[p0:p0 + 64, k, 0:3], in_=src)
            S.dma("sp", None, None, reads=[T["b_HXA"][ch]], writes=[bx1], fn=fn2)


def phase_ssd_inproj(nc, T):
    C = Ctx(nc); S = C.S
    K = load_consts(C, T)
    identb, bident = K["identb"], K["bident"]
    banks = [(C.ps([128, 512], F32), Buf()) for _ in range(8)]
    x1 = C.sb([128, 8, 3 + NT], F32); bx1 = Buf()
    load_h1_contig(C, T, x1, bx1, True)
    wn = C.sb([128, 8], F32); bwn = Buf()
    S.dma("sp", wn[:], T["ssd_norm"], writes=[bwn])
    xn = C.sb([128, 8, 3 + NT], BF16); bxn = Buf()
    rstd = C.sb([128, 3 + NT], F32); brstd = Buf()
    sqb = [(C.sb([128, 512], F32), Buf()) for _ in range(2)]
    slices = [slice(0, 3)] + [slice(3 + tt * 512, 3 + (tt + 1) * 512) for tt in range(4)]
    rmsnorm_fm(C, x1, bx1, wn, bwn, xn, bxn, K["ones_f"], K["bones"], banks[0:2], sqb=sqb, rstd=rstd, brstd=brstd, slices=slices)

    cw = C.sb([128, 32, 4], F32); bcw = Buf()
    cb = C.sb([128, 32], F32); bcb = Buf()
    S.dma("sp", cw[:], T["conv_w"], writes=[bcw])
    S.dma("sp", cb[:], T["conv_b"], writes=[bcb])
    wb = [(C.sb([128, 8, 512], BF16), Buf()) for _ in range(2)]
    wsrc = T["ssd_w_in"].rearrange("(k p) c -> p k c", p=128)
    ub = [(C.sb([128, 3 + NT], F32), Buf()) for _ in range(2)]
    accb = [(C.sb([128, NT], F32), Buf()) for _ in range(2)]
    xcb = [(C.sb([128, NT], BF16), Buf()) for _ in range(2)]
    ctmp = C.sb([128, NT], F32); bctmp = Buf()
    tst = [(C.sb([128, 8, 128], BF16), Buf()) for _ in range(2)]
    pTs = [(banks[6][0][:].bitcast(BF16), banks[6][1]), (banks[7][0][:].bitcast(BF16), banks[7][1])]
    nst = 0
    for g in range(8):
        w, bw = wb[g % 2]
        for k in range(8):
            S.dma("pool", w[:, k, :], wsrc[:, k, 2048 + g * 512:2048 + (g + 1) * 512], writes=[bw])
        for c4 in range(4):
            cc = g * 4 + c4
            u, bu = ub[cc % 2]; acc, bacc = accb[cc % 2]; xc, bxc = xcb[cc % 2]
            veng = "dve"
            for ti, sl in enumerate(slices):
                wd_ = sl.stop - sl.start
                p1, bp1 = banks[2 + ti % 4]
                for k in range(8):
                    S.op("pe", lambda e, p1=p1, w=w, k=k, c4=c4, sl=sl, wd_=wd_: e.matmul(p1[:, 0:wd_], lhsT=w[:, k, c4 * 128:(c4 + 1) * 128], rhs=xn[:, k, sl],
                                                                                         start=(k == 0), stop=(k == 7)),
                         reads=[bw, bxn], writes=[bp1])
                S.op("act", lambda e, u=u, p1=p1, sl=sl, wd_=wd_: e.activation(out=u[:, sl], in_=p1[:, 0:wd_], func=AF.Copy), reads=[bp1], writes=[bu])
            S.op(veng, lambda e, acc=acc, u=u, cc=cc: e.tensor_scalar(out=acc[:], in0=u[:, 0:NT], scalar1=cw[:, cc, 0:1], scalar2=None, op0=ALU.mult),
                 reads=[bu, bcw], writes=[bacc])
            for tap in range(1, 4):
                if veng == "dve":
                    S.op(veng, lambda e, acc=acc, u=u, cc=cc, tap=tap: e.scalar_tensor_tensor(out=acc[:], in0=u[:, tap:tap + NT], scalar=cw[:, cc, tap:tap + 1],
                                                                                             in1=acc[:], op0=ALU.mult, op1=ALU.add),
                         reads=[bu, bcw, bacc], writes=[bacc])
                else:
                    S.op(veng, lambda e, u=u, cc=cc, tap=tap: e.tensor_scalar(out=ctmp[:], in0=u[:, tap:tap + NT], scalar1=cw[:, cc, tap:tap + 1], scalar2=None, op0=ALU.mult),
                         reads=[bu, bcw], writes=[bctmp])
                    S.op(veng, lambda e, acc=acc: e.tensor_tensor(out=acc[:], in0=acc[:], in1=ctmp[:], op=ALU.add), reads=[bacc, bctmp], writes=[bacc])
            S.op("act", lambda e, xc=xc, acc=acc, cc=cc: e.activation(out=xc[:], in_=acc[:], func=AF.Silu, bias=cb[:, cc:cc + 1]),
                 reads=[bacc, bcb], writes=[bxc])
            if cc >= 16:
                nm = "BT" if cc < 24 else "CT"
                gi = cc - 16 if cc < 24 else cc - 24
                S.dma("sp", T[nm][gi * 128:(gi + 1) * 128, :], xc[:], reads=[bxc], writes=[T["b_" + nm]])
            if cc < 24:
                dname = "XS" if cc < 16 else "BTOK"
                col0 = cc * 128 if cc < 16 else (cc - 16) * 128
                for half in range(2):
                    pT, bpT = pTs[nst % 2]
                    st_, bst = tst[nst % 2]; nst += 1
                    for tb8 in range(8):
                        tb = half * 8 + tb8
                        S.op("pe", lambda e, pT=pT, xc=xc, tb=tb, tb8=tb8: e.transpose(pT[:, tb8 * 128:(tb8 + 1) * 128], xc[:, tb * 128:(tb + 1) * 128], identb[:]),
                             reads=[bxc, bident], writes=[bpT])
                    S.op("dve" if nst % 2 == 0 else "act", (lambda e, st_=st_, pT=pT: e.tensor_copy(out=st_[:], in_=pT.rearrange("p (b c) -> p b c", c=128)))
                         if nst % 2 == 0 else (lambda e, st_=st_, pT=pT: e.activation(out=st_[:], in_=pT.rearrange("p (b c) -> p b c", c=128), func=AF.Copy)),
                         reads=[bpT], writes=[bst])
                    dstv = T[dname][half * 1024:(half + 1) * 1024, col0:col0 + 128].rearrange("(b t) c -> t b c", t=128)
                    S.dma("sp", dstv, st_[:], reads=[bst], writes=[T["b_" + dname]])
    zst = [(C.sb([128, 512], F32), Buf()) for _ in range(2)]
    nz = 0
    for g in range(4):
        w, bw = wb[g % 2]
        for k in range(8):
            S.dma("pool", w[:, k, :], wsrc[:, k, g * 512:(g + 1) * 512], writes=[bw])
        for tb in range(16):
            p1, bp1 = banks[2 + tb % 4]
            for k in range(8):
                S.op("pe", lambda e, p1=p1, w=w, k=k, tb=tb: e.matmul(p1[:], lhsT=xn[:, k, 3 + tb * 128:3 + (tb + 1) * 128], rhs=w[:, k, :],
                                                                      start=(k == 0), stop=(k == 7)),
                     reads=[bw, bxn], writes=[bp1])
            z_, bz = zst[nz % 2]; nz += 1
            S.op("act", lambda e, z_=z_, p1=p1: e.activation(out=z_[:], in_=p1[:], func=AF.Silu), reads=[bp1], writes=[bz])
            S.dma("sp", T["ZS"][tb * 128:(tb + 1) * 128, g * 512:(g + 1) * 512], z_[:], reads=[bz], writes=[T["b_ZS"]])
    wdt = C.sb([128, 8, 32], BF16); bwdt = Buf()
    for k in range(8):
        S.dma("pool", wdt[:, k, :], wsrc[:, k, 6144:6176], writes=[bwdt])
    hv = C.sb([128, 3, 32], F32); bhv = Buf()
    S.dma("sp", hv[:, 0:1, :], T["dt_bias"].partition_broadcast(128), writes=[bhv])
    S.dma("sp", hv[:, 1:2, :], T["a_log"].partition_broadcast(128), writes=[bhv])
    S.op("act", lambda e: e.activation(out=hv[:, 2, :], in_=hv[:, 1, :], func=AF.Exp), reads=[bhv], writes=[bhv])
    dst_ = [(C.sb([128, 64], F32), Buf()) for _ in range(2)]
    for tb in range(16):
        p1, bp1 = banks[2 + tb % 4]
        for k in range(8):
            S.op("pe", lambda e, p1=p1, k=k, tb=tb: e.matmul(p1[:, 0:32], lhsT=xn[:, k, 3 + tb * 128:3 + (tb + 1) * 128], rhs=wdt[:, k, :],
                                                             start=(k == 0), stop=(k == 7)),
                 reads=[bwdt, bxn], writes=[bp1])
        d_, bd = dst_[tb % 2]
        S.op("dve", lambda e, d_=d_, p1=p1: e.tensor_tensor(out=d_[:, 0:32], in0=p1[:, 0:32], in1=hv[:, 0, :], op=ALU.add), reads=[bp1, bhv], writes=[bd])
        S.op("act", lambda e, d_=d_: e.activation(out=d_[:, 0:32], in_=d_[:, 0:32], func=AF.Exp), reads=[bd], writes=[bd])
        S.op("act", lambda e, d_=d_: e.activation(out=d_[:, 0:32], in_=d_[:, 0:32], func=AF.Ln, bias=1.0), reads=[bd], writes=[bd])
        S.op("dve", lambda e, d_=d_: e.scalar_tensor_tensor(out=d_[:, 32:64], in0=d_[:, 0:32], scalar=-1.0, in1=hv[:, 2, :], op0=ALU.mult, op1=ALU.mult),
             reads=[bd, bhv], writes=[bd])
        S.dma("sp", T["DTD"][tb * 128:(tb + 1) * 128, :], d_[:], reads=[bd], writes=[T["b_DTD"]])
    C.close()


def ssd_consts(C, T):
    S = C.S
    k = {}
    for nm in ("tri_incl", "tri_gt", "ntri_incl"):
        k[nm] = C.sb([128, 128], F32); k["b_" + nm] = Buf()
        S.dma("sp", k[nm][:], T[nm], writes=[k["b_" + nm]])
    return k


def phase_ssd_states(nc, T):
    C = Ctx(nc); S = C.S
    K = load_consts(C, T)
    K2 = ssd_consts(C, T)
    banks = [(C.ps([128, 512], F32), Buf()) for _ in range(8)]
    Sloc = C.sb([128, 32, 64], F32); bS = Buf()
    S.op("pool", lambda e: e.memset(Sloc[:], 0.0), writes=[bS])
    ldsum = C.sb([128, 32], F32); bld = Buf()
    S.op("pool", lambda e: e.memset(ldsum[:], 0.0), writes=[bld])
    xsb = [(C.sb([128, 32, 64], BF16), Buf()) for _ in range(2)]
    btb = [(C.sb([128, 1024], BF16), Buf()) for _ in range(2)]
    dtb = [(C.sb([128, 64], F32), Buf()) for _ in range(2)]
    smb = [(C.sb([128, 3, 32], F32), Buf()) for _ in range(2)]
    xwb = [(C.sb([128, 32, 64], BF16), Buf()) for _ in range(2)]
    stb = [(C.sb([128, 2048], F32), Buf()) for _ in range(2)]
    for c in range(16):
        xs, bxs = xsb[c % 2]; bt, bbt = btb[c % 2]; dt_, bdt = dtb[c % 2]; sm, bsm = smb[c % 2]; xw, bxw = xwb[c % 2]; st_, bst = stb[c % 2]
        rows = slice(c * 128, (c + 1) * 128)
        S.dma("sp", xs[:].rearrange("p h c -> p (h c)"), T["XS"][rows, :], reads=[T["b_XS"]], writes=[bxs])
        S.dma("act", bt[:], T["BTOK"][rows, :], reads=[T["b_BTOK"]], writes=[bbt])
        S.dma("sp", dt_[:], T["DTD"][rows, :], reads=[T["b_DTD"]], writes=[bdt])
        pa, bpa = banks[c % 2]
        S.op("pe", lambda e, pa=pa, dt_=dt_: e.matmul(pa[:, 0:32], lhsT=K2["tri_gt"][:], rhs=dt_[:, 32:64], start=True, stop=True),
             reads=[K2["b_tri_gt"], bdt], writes=[bpa])
        S.op("pe", lambda e, pa=pa, dt_=dt_: e.matmul(pa[:, 32:64], lhsT=K["ones_f"][:], rhs=dt_[:, 32:64], start=True, stop=True),
             reads=[K["bones"], bdt], writes=[bpa])
        S.op("act", lambda e, sm=sm, pa=pa: e.activation(out=sm[:, 0:2, :], in_=pa[:, 0:64].rearrange("p (a h) -> p a h", a=2), func=AF.Exp),
             reads=[bpa], writes=[bsm])
        S.op("dve", lambda e, sm=sm, dt_=dt_: e.tensor_tensor(out=sm[:, 2, :], in0=sm[:, 0, :], in1=dt_[:, 0:32], op=ALU.mult), reads=[bsm, bdt], writes=[bsm])
        S.op("dve", lambda e, xw=xw, xs=xs, sm=sm: e.tensor_tensor(out=xw[:], in0=xs[:], in1=sm[:, 2, :].unsqueeze(2).to_broadcast([128, 32, 64]), op=ALU.mult),
             reads=[bxs, bsm], writes=[bxw])
        xwf = xw[:].rearrange("p h c -> p (h c)")
        for g in range(8):
            ps_, bps = banks[2 + g // 2]
            S.op("pe", lambda e, ps_=ps_, bt=bt, g=g, xwf=xwf: e.matmul(ps_[:, (g % 2) * 256:(g % 2 + 1) * 256], lhsT=bt[:, g * 128:(g + 1) * 128],
                                                                        rhs=xwf[:, g * 256:(g + 1) * 256], start=True, stop=True),
                 reads=[bbt, bxw], writes=[bps])
        for bq in range(4):
            ps_, bps = banks[2 + bq]
            S.op("act" if bq % 2 == 0 else "dve",
                 (lambda e, st_=st_, ps_=ps_, bq=bq: e.activation(out=st_[:, bq * 512:(bq + 1) * 512], in_=ps_[:], func=AF.Copy)) if bq % 2 == 0 else
                 (lambda e, st_=st_, ps_=ps_, bq=bq: e.tensor_copy(out=st_[:, bq * 512:(bq + 1) * 512], in_=ps_[:])),
                 reads=[bps], writes=[bst])
        S.dma("sp", T["ST"][c], st_[:], reads=[bst], writes=[T["b_ST"]])
        S.op("pool", lambda e, sm=sm: e.tensor_tensor(out=Sloc[:], in0=Sloc[:], in1=sm[:, 1, :].unsqueeze(2).to_broadcast([128, 32, 64]), op=ALU.mult),
             reads=[bS, bsm], writes=[bS])
        S.op("pool", lambda e, st_=st_: e.tensor_tensor(out=Sloc[:].rearrange("p h c -> p (h c)"), in0=Sloc[:].rearrange("p h c -> p (h c)"), in1=st_[:], op=ALU.add),
             reads=[bS, bst], writes=[bS])
        S.op("dve", lambda e, pa=pa: e.tensor_tensor(out=ldsum[:], in0=pa[:, 32:64], in1=ldsum[:], op=ALU.add), reads=[bpa, bld], writes=[bld])
    S.dma("sp", T["SXa"], Sloc[:].rearrange("p h c -> p (h c)"), reads=[bS], writes=[T["b_SXa"]])
    S.dma("sp", T["SXb"], ldsum[:], reads=[bld], writes=[T["b_SXb"]])
    collective(S, "AllGather", T["SXa"], T["SXa_all"], reads=[T["b_SXa"]], writes=[T["b_SXa_all"]])
    collective(S, "AllGather", T["SXb"], T["SXb_all"], reads=[T["b_SXb"]], writes=[T["b_SXb_all"]])
    C.close()


def phase_ssd_scan(nc, T):
    C = Ctx(nc); S = C.S
    K = load_consts(C, T)
    K2 = ssd_consts(C, T)
    identb, bident = K["identb"], K["bident"]
    identf, bidentf = K["idf"], K["bidf"]
    ones_f, bones = K["ones_f"], K["bones"]
    banks = [(C.ps([128, 512], F32), Buf()) for _ in range(8)]
    prev = C.sb([128, 32, 64], F32); bprev = Buf()
    prevb = C.sb([128, 2048], BF16); bprevb = Buf()
    msk = C.sb([128, 20], F32); bmsk = Buf()
    S.dma("sp", msk[:], T["selmask"], writes=[bmsk])
    ld = C.sb([128, 4, 32], F32); bldg = Buf()
    S.dma("sp", ld[:], T["SXb_all"].rearrange("(r p) h -> p r h", p=128), reads=[T["b_SXb_all"]], writes=[bldg])
    coef = C.sb([128, 4, 32], F32); bcoef = Buf()
    for r in range(4):
        S.op("dve", lambda e, r=r: e.tensor_scalar(out=coef[:, r, :], in0=ld[:, 0, :], scalar1=msk[:, 4 + 4 * r:5 + 4 * r], scalar2=None, op0=ALU.mult),
             reads=[bldg, bmsk], writes=[bcoef])
        for r2 in range(1, 4):
            S.op("dve", lambda e, r=r, r2=r2: e.scalar_tensor_tensor(out=coef[:, r, :], in0=ld[:, r2, :], scalar=msk[:, 4 + 4 * r + r2:5 + 4 * r + r2],
                                                                     in1=coef[:, r, :], op0=ALU.mult, op1=ALU.add),
                 reads=[bldg, bmsk, bcoef], writes=[bcoef])
        S.op("act", lambda e, r=r: e.activation(out=coef[:, r, :], in_=coef[:, r, :], func=AF.Exp), reads=[bcoef], writes=[bcoef])
        S.op("dve", lambda e, r=r: e.tensor_scalar(out=coef[:, r, :], in0=coef[:, r, :], scalar1=msk[:, r:r + 1], scalar2=None, op0=ALU.mult),
             reads=[bcoef, bmsk], writes=[bcoef])
    S.op("pool", lambda e: e.memset(prev[:], 0.0), writes=[bprev])
    sg_ = [(C.sb([128, 32, 64], F32), Buf()) for _ in range(2)]
    for r in range(4):
        t_, bt_ = sg_[r % 2]
        S.dma("sp", t_[:].rearrange("p h c -> p (h c)"), T["SXa_all"][r * 128:(r + 1) * 128, :], reads=[T["b_SXa_all"]], writes=[bt_])
        S.op("dve", lambda e, t_=t_, r=r: e.tensor_tensor(out=t_[:], in0=t_[:], in1=coef[:, r, :].unsqueeze(2).to_broadcast([128, 32, 64]), op=ALU.mult),
             reads=[bt_, bcoef], writes=[bt_])
        S.op("dve", lambda e, t_=t_: e.tensor_tensor(out=prev[:], in0=prev[:], in1=t_[:], op=ALU.add), reads=[bt_, bprev], writes=[bprev])
    prevf = prev[:].rearrange("p h c -> p (h c)")
    S.op("act", lambda e: e.activation(out=prevb[:], in_=prevf, func=AF.Copy), reads=[bprev], writes=[bprevb])
    hv = C.sb([128, 32], F32); bhv = Buf()
    S.dma("sp", hv[:].unsqueeze(1), T["ssd_d"].partition_broadcast(128), writes=[bhv])
    gw = C.sb([128, 16], F32); bgw = Buf()
    S.dma("sp", gw[:], T["gnorm"], writes=[bgw])
    negut = C.sb([128, 4, 128], F32); bneg = Buf()
    negutb = C.sb([128, 4, 128], BF16); bnegb = Buf()
    for r in range(4):
        S.dma("sp", negut[:, r, :], T["neg_ut"], writes=[bneg])
    S.op("dve", lambda e: e.tensor_copy(out=negutb[:], in_=negut[:]), reads=[bneg], writes=[bnegb])
    xsb = [(C.sb([128, 32, 64], BF16), Buf()) for _ in range(2)]
    dtb = [(C.sb([128, 64], F32), Buf()) for _ in range(2)]
    btb = [(C.sb([128, 8, 128], BF16), Buf()) for _ in range(2)]
    ctb = [(C.sb([128, 8, 128], BF16), Buf()) for _ in range(2)]
    zsb = [(C.sb([128, 2048], F32), Buf()) for _ in range(2)]
    stb = [(C.sb([128, 2048], F32), Buf()) for _ in range(2)]
    Yh = C.sb([128, 32, 128], BF16); Yl = C.sb([128, 32, 128], BF16); bY = Buf()
    Zh = C.sb([128, 32, 128], BF16); Zl = C.sb([128, 32, 128], BF16); bZ = Buf()
    dth = C.sb([128, 2, 32], BF16); bdth = Buf()
    trib = C.sb([128, 128], BF16); ntrib = C.sb([128, 128], BF16); onesb = C.sb([128, 128], BF16); btb_ = Buf()
    S.op("dve", lambda e: e.tensor_copy(out=trib[:], in_=K2["tri_incl"][:]), reads=[K2["b_tri_incl"]], writes=[btb_])
    S.op("dve", lambda e: e.tensor_copy(out=ntrib[:], in_=K2["ntri_incl"][:]), reads=[K2["b_ntri_incl"]], writes=[btb_])
    S.op("dve", lambda e: e.memset(onesb[:], 1.0), writes=[btb_])
    smb = [(C.sb([128, 3, 32], F32), Buf()) for _ in range(2)]
    xdt = C.sb([128, 32, 64], BF16); bxdt = Buf()
    Dmb = [(C.sb([128, 4, 128], F32), Buf()) for _ in range(2)]
    MTb = [(C.sb([128, 4, 128], BF16), Buf()) for _ in range(2)]
    tyb = [(C.sb([128, 4, 64], F32), Buf()) for _ in range(2)]
    yc = C.sb([128, 8, 256], F32); byc = Buf()
    gy = C.sb([128, 8, 256], F32); bgy = Buf()
    ynb = C.sb([128, 2048], BF16); byn = Buf()
    nrm = C.sb([128, 3, 8], F32); bnrm = Buf()
    junk = C.sb([128, 256], F32); bjunk = Buf()
    yTs = [(C.sb([128, 16, 128], BF16), Buf()) for _ in range(2)]
    bt_src = T["BT"].rearrange("(g n) t -> n g t", n=128)
    ct_src = T["CT"].rearrange("(g n) t -> n g t", n=128)
    yt_dst = T["YT"].rearrange("(cc p) t -> p cc t", p=128)
    pT0 = banks[6][0][:].bitcast(BF16); pT1 = banks[7][0][:].bitcast(BF16)
    half_bufs = [[Buf(), Buf()] for _ in range(3)]
    for c in range(16):
        xs, bxs = xsb[c % 2]; dt_, bdt = dtb[c % 2]; bt, bbt = btb[c % 2]; ct, bct = ctb[c % 2]
        zs, bzs = zsb[c % 2]; st_, bst = stb[c % 2]; sm, bsm = smb[c % 2]
        rows = slice(c * 128, (c + 1) * 128)
        S.dma("sp", xs[:].rearrange("p h c -> p (h c)"), T["XS"][rows, :], reads=[T["b_XS"]], writes=[bxs])
        S.dma("sp", dt_[:], T["DTD"][rows, :], reads=[T["b_DTD"]], writes=[bdt])
        S.dma("act", bt[:], bt_src[:, :, rows], reads=[T["b_BT"]], writes=[bbt])
        S.dma("act", ct[:], ct_src[:, :, rows], reads=[T["b_CT"]], writes=[bct])
        S.dma("sp", zs[:], T["ZS"][rows, :], reads=[T["b_ZS"]], writes=[bzs])
        S.dma("sp", st_[:], T["ST"][c], reads=[T["b_ST"]], writes=[bst])
        pa, bpa = banks[0]
        S.op("pe", lambda e, dt_=dt_: e.matmul(pa[:, 0:32], lhsT=K2["tri_incl"][:], rhs=dt_[:, 32:64], start=True, stop=True),
             reads=[K2["b_tri_incl"], bdt], writes=[bpa])
        S.op("pe", lambda e, dt_=dt_: e.matmul(pa[:, 32:64], lhsT=ones_f[:], rhs=dt_[:, 32:64], start=True, stop=True),
             reads=[bones, bdt], writes=[bpa])
        S.op("act", lambda e, sm=sm: e.activation(out=sm[:, 0:2, :], in_=pa[:, 0:64].rearrange("p (a h) -> p a h", a=2), func=AF.Exp),
             reads=[bpa], writes=[bsm])
        S.op("dve", lambda e, dt_=dt_: e.tensor_copy(out=dth[:, 0, :], in_=dt_[:, 32:64]), reads=[bdt], writes=[bdth])
        S.op("dve", lambda e, dt_=dt_: e.tensor_tensor(out=dth[:, 1, :], in0=dt_[:, 32:64], in1=dth[:, 0, :], op=ALU.subtract), reads=[bdt, bdth], writes=[bdth])
        for x_, Yx, Zx in ((0, Yh, Zh), (1, Yl, Zl)):
            S.op("dve", lambda e, x_=x_, Yx=Yx: e.tensor_tensor(out=Yx[:], in0=trib[:].unsqueeze(1).to_broadcast([128, 32, 128]),
                                                                in1=dth[:, x_, :].unsqueeze(2).to_broadcast([128, 32, 128]), op=ALU.mult),
                 reads=[btb_, bdth], writes=[bY])
            S.op("pool", lambda e, x_=x_, Zx=Zx: e.tensor_copy(out=Zx[:], in_=dth[:, x_, :].unsqueeze(2).to_broadcast([128, 32, 128])), reads=[bdth], writes=[bZ])
        S.op("dve", lambda e, xs=xs, dt_=dt_: e.tensor_tensor(out=xdt[:], in0=xs[:], in1=dt_[:, 0:32].unsqueeze(2).to_broadcast([128, 32, 64]), op=ALU.mult),
             reads=[bxs, bdt], writes=[bxdt])
        for g in range(8):
            pseg, bpseg = banks[1 + g % 2]
            Dm, bDm = Dmb[g % 2]; MT, bMT = MTb[g % 2]; ty, bty = tyb[g % 2]
            hs = slice(4 * g, 4 * g + 4)
            S.op("pe", lambda e, pseg=pseg, hs=hs: e.matmul(pseg[:], lhsT=onesb[:], rhs=Yh[:, hs, :].rearrange("p h l -> p (h l)"), start=True, stop=False),
                 reads=[btb_, bY], writes=[bpseg])
            S.op("pe", lambda e, pseg=pseg, hs=hs: e.matmul(pseg[:], lhsT=onesb[:], rhs=Yl[:, hs, :].rearrange("p h l -> p (h l)"), start=False, stop=False),
                 reads=[btb_, bY], writes=[bpseg])
            S.op("pe", lambda e, pseg=pseg, hs=hs: e.matmul(pseg[:], lhsT=ntrib[:], rhs=Zh[:, hs, :].rearrange("p h l -> p (h l)"), start=False, stop=False),
                 reads=[btb_, bZ], writes=[bpseg])
            S.op("pe", lambda e, pseg=pseg, hs=hs: e.matmul(pseg[:], lhsT=ntrib[:], rhs=Zl[:, hs, :].rearrange("p h l -> p (h l)"), start=False, stop=False),
                 reads=[btb_, bZ], writes=[bpseg])
            S.op("pe", lambda e, pseg=pseg: e.matmul(pseg[:], lhsT=identb[:], rhs=negutb[:].rearrange("p h l -> p (h l)"), start=False, stop=True),
                 reads=[bident, bnegb], writes=[bpseg])
            S.op("act", lambda e, Dm=Dm, pseg=pseg: e.activation(out=Dm[:].rearrange("p h l -> p (h l)"), in_=pseg[:], func=AF.Exp), reads=[bpseg], writes=[bDm])
            pg = banks[3][0]; bpg = half_bufs[0][g % 2]
            gsl = slice((g % 2) * 128, (g % 2 + 1) * 128)
            S.op("pe", lambda e, bt=bt, ct=ct, g=g, gsl=gsl: e.matmul(pg[:, gsl], lhsT=bt[:, g, :], rhs=ct[:, g, :], start=True, stop=True),
                 reads=[bbt, bct], writes=[bpg])
            S.op("dve", lambda e, MT=MT, Dm=Dm, gsl=gsl: e.tensor_tensor(out=MT[:], in0=Dm[:], in1=pg[:, gsl].unsqueeze(1).to_broadcast([128, 4, 128]), op=ALU.mult),
                 reads=[bDm, bpg], writes=[bMT])
            pyd = banks[4][0]; bpyd = half_bufs[1][g % 2]
            pyo = banks[5][0]; bpyo = half_bufs[2][g % 2]
            ysl = slice((g % 2) * 256, (g % 2 + 1) * 256)
            for r in range(4):
                S.op("pe", lambda e, MT=MT, r=r, g=g, ysl=ysl: e.matmul(pyd[:, ysl.start + r * 64:ysl.start + (r + 1) * 64], lhsT=MT[:, r, :], rhs=xdt[:, 4 * g + r, :],
                                                                        start=True, stop=True),
                     reads=[bMT, bxdt], writes=[bpyd])
            S.op("pe", lambda e, ct=ct, g=g, ysl=ysl: e.matmul(pyo[:, ysl], lhsT=ct[:, g, :], rhs=prevb[:, g * 256:(g + 1) * 256], start=True, stop=True),
                 reads=[bct, bprevb], writes=[bpyo])
            S.op("dve", lambda e, ty=ty, sm=sm, hs=hs, ysl=ysl: e.tensor_tensor(out=ty[:], in0=pyo[:, ysl].rearrange("p (r c) -> p r c", r=4),
                                                                                in1=sm[:, 0, hs].unsqueeze(2).to_broadcast([128, 4, 64]), op=ALU.mult),
                 reads=[bpyo, bsm], writes=[bty])
            S.op("dve", lambda e, ty=ty, g=g, ysl=ysl: e.tensor_tensor(out=yc[:, g, :], in0=pyd[:, ysl], in1=ty[:].rearrange("p r c -> p (r c)"), op=ALU.add),
                 reads=[bpyd, bty], writes=[byc])
        ycv = yc[:].rearrange("p g (r c) -> p (g r) c", r=4)
        S.op("pool", lambda e, xs=xs: e.tensor_tensor(out=gy[:].rearrange("p g (r c) -> p (g r) c", r=4), in0=xs[:], in1=hv[:].unsqueeze(2).to_broadcast([128, 32, 64]), op=ALU.mult),
             reads=[bxs, bhv], writes=[bgy])
        S.op("pool", lambda e: e.tensor_tensor(out=yc[:], in0=yc[:], in1=gy[:], op=ALU.add), reads=[byc, bgy], writes=[byc])
        S.op("dve", lambda e, zs=zs: e.tensor_tensor(out=gy[:], in0=yc[:], in1=zs[:].rearrange("p (g c) -> p g c", g=8), op=ALU.mult), reads=[byc, bzs, bgy], writes=[bgy])
        for g in range(8):
            S.op("act", lambda e, g=g: e.activation(out=junk[:], in_=gy[:, g, :], func=AF.Square, accum_out=nrm[:, 0, g:g + 1]), reads=[bgy], writes=[bjunk, bnrm])
        S.op("act", lambda e: e.activation(out=nrm[:, 1, :], in_=nrm[:, 0, :], func=AF.Ln, scale=1.0 / 256, bias=1e-5), reads=[bnrm], writes=[bnrm])
        S.op("act", lambda e: e.activation(out=nrm[:, 2, :], in_=nrm[:, 1, :], func=AF.Exp, scale=-0.5), reads=[bnrm], writes=[bnrm])
        S.op("dve", lambda e: e.tensor_tensor(out=ynb[:].rearrange("p (g c) -> p g c", g=8), in0=gy[:], in1=nrm[:, 2, :].unsqueeze(2).to_broadcast([128, 8, 256]), op=ALU.mult),
             reads=[bgy, bnrm], writes=[byn])
        yT, byT = yTs[c % 2]
        for cc in range(16):
            pT = pT0 if cc < 8 else pT1
            S.op("pe", lambda e, pT=pT, cc=cc: e.transpose(pT[:, (cc % 8) * 128:(cc % 8 + 1) * 128], ynb[:, cc * 128:(cc + 1) * 128], identb[:]),
                 reads=[byn, bident], writes=[banks[6][1] if cc < 8 else banks[7][1]])
        S.op("dve", lambda e, yT=yT: e.tensor_tensor(out=yT[:, 0:8, :], in0=pT0.rearrange("p (b c) -> p b c", c=128), in1=gw[:, 0:8].unsqueeze(2).to_broadcast([128, 8, 128]), op=ALU.mult),
             reads=[banks[6][1], bgw], writes=[byT])
        S.op("dve", lambda e, yT=yT: e.tensor_tensor(out=yT[:, 8:16, :], in0=pT1.rearrange("p (b c) -> p b c", c=128), in1=gw[:, 8:16].unsqueeze(2).to_broadcast([128, 8, 128]), op=ALU.mult),
             reads=[banks[7][1], bgw], writes=[byT])
        S.dma("sp", yt_dst[:, :, rows], yT[:], reads=[byT], writes=[T["b_YT"]])
        S.op("pool", lambda e, sm=sm: e.tensor_tensor(out=prev[:], in0=prev[:], in1=sm[:, 1, :].unsqueeze(2).to_broadcast([128, 32, 64]), op=ALU.mult),
             reads=[bprev, bsm], writes=[bprev])
        S.op("pool", lambda e, st_=st_: e.tensor_tensor(out=prevf, in0=prevf, in1=st_[:], op=ALU.add), reads=[bprev, bst], writes=[bprev])
        S.op("act", lambda e: e.activation(out=prevb[:], in_=prevf, func=AF.Copy), reads=[bprev, bprevb], writes=[bprevb])
    C.close()


def phase_ssd_out_ffn(nc, T, stage):
    C = Ctx(nc); S = C.S
    K = load_consts(C, T)
    banks = [(C.ps([128, 512], F32), Buf()) for _ in range(8)]
    hT = C.sb([128, 8, NT], F32); bh = Buf()
    load_h1_contig(C, T, hT, bh, False)
    dst = T["dbg"].rearrange("(k p) n -> p k n", p=128)
    with contextlib.ExitStack() as st2:
        yT = st2.enter_context(nc.sbuf_tensor("yTf", [128, 16, NT], BF16)); byT = Buf()
        wo = st2.enter_context(nc.sbuf_tensor("wo1", [128, 16, D], BF16)); bwo = Buf()
        ysrc = T["YT"].rearrange("(cc p) t -> p cc t", p=128)
        for cc in range(16):
            S.dma("act", yT[:, cc, :], ysrc[:, cc, :], reads=[T["b_YT"]], writes=[byT])
            S.dma("pool", wo[:, cc, :], T["ssd_w_out"][cc * 128:(cc + 1) * 128, :], writes=[bwo])
        for tt in range(4):
            sl = slice(tt * 512, (tt + 1) * 512)
            for oc in range(8):
                po, bpo = banks[oc % 4]
                for cc in range(16):
                    S.op("pe", lambda e, po=po, cc=cc, oc=oc, sl=sl: e.matmul(po[:], lhsT=wo[:, cc, oc * 128:(oc + 1) * 128], rhs=yT[:, cc, sl],
                                                                              start=(cc == 0), stop=(cc == 15)),
                         reads=[bwo, byT], writes=[bpo])
                S.op("dve", lambda e, po=po, oc=oc, sl=sl: e.tensor_tensor(out=hT[:, oc, sl], in0=po[:], in1=hT[:, oc, sl], op=ALU.add),
                     reads=[bpo, bh], writes=[bh])
        if stage == "ssd":
            for tt in range(4):
                S.dma("sp", dst[:, :, tt * 512:(tt + 1) * 512], hT[:, :, tt * 512:(tt + 1) * 512], reads=[bh])
            C.S.emit()
            st2.close(); C.st.close()
            return
        C.S.emit()
    C.S = Sched(nc); S = C.S
    Cf = Ctx(nc); Cf.S = C.S
    ffn_fm(Cf, T, 1, hT, bh, K["ones_f"], K["bones"], banks)
    C.S.emit()
    Cf.st.close()
    C.S = Sched(nc); S = C.S
    if stage != "ffn1":
        fw = C.sb([128, 8], F32); bfw = Buf()
        S.dma("sp", fw[:], T["final_norm"], writes=[bfw])
        rstd = C.sb([128, NT], F32); brstd = Buf()
        sq = [(C.sb([128, 512], F32), Buf()) for _ in range(2)]
        for tt in range(4):
            sl = slice(tt * 512, (tt + 1) * 512)
            ps, bps = banks[tt % 2]
            for k in range(8):
                q_, bq_ = sq[k % 2]
                S.op("act", lambda e, q_=q_, k=k, sl=sl: e.activation(out=q_[:], in_=hT[:, k, sl], func=AF.Square), reads=[bh], writes=[bq_])
                S.op("pe", lambda e, ps=ps, q_=q_, k=k: e.matmul(ps[:], lhsT=K["ones_f"][:], rhs=q_[:], start=(k == 0), stop=(k == 7)),
                     reads=[bq_, K["bones"]], writes=[bps])
            S.op("act", lambda e, ps=ps, sl=sl: e.activation(out=rstd[:, sl], in_=ps[:], func=AF.Ln, scale=1.0 / D, bias=1e-6), reads=[bps], writes=[brstd])
            S.op("act", lambda e, sl=sl: e.activation(out=rstd[:, sl], in_=rstd[:, sl], func=AF.Exp, scale=-0.5), reads=[brstd], writes=[brstd])
            for k in range(8):
                S.op("dve", lambda e, k=k, sl=sl: e.scalar_tensor_tensor(out=hT[:, k, sl], in0=hT[:, k, sl], scalar=fw[:, k:k + 1], in1=rstd[:, sl],
                                                                         op0=ALU.mult, op1=ALU.mult),
                     reads=[bh, bfw, brstd], writes=[bh])
    for tt in range(4):
        S.dma("sp", dst[:, :, tt * 512:(tt + 1) * 512], hT[:, :, tt * 512:(tt + 1) * 512], reads=[bh])
    C.close()


def build_program(stage):
    Buf.ALL = []
    nc = bass.Bass("TRN2", target_bir_lowering=False)
    Sched.STATE = SemState(nc)
    T = {}

    def inp(name, shape, dt=F32):
        T[name] = nc.dram_tensor(name, list(shape), dt, kind="ExternalInput").ap()

    def scr(name, shape, dt):
        T[name] = nc.dram_tensor(name, list(shape), dt).ap()
        T["b_" + name] = Buf(name)

    inp("xT0", [D, NT]); inp("cs_tab", [128, 2, NT]); inp("attn_norm", [128, 8])
    inp("w_in0", [D, 3072]); inp("w_sw0", [D, 1024]); inp("w_out0", [D, D])
    inp("nm_incl", [128, 16, 512]); inp("nm_strict", [128, 16, 512])
    inp("ident", [128, 128]); inp("nu", [128, 128])
    for nm in ("lq1", "lk1", "lq2", "lk2"):
        inp(nm, [1, 64])
    inp("subln", [1, 128])
    inp("ffn_norm", [2, 128, 8]); inp("ffn_wg", [2, D, D_FF]); inp("ffn_wu", [2, D, D_FF]); inp("ffn_wd", [2, D_FF, D])
    scr("QT", [1536, NT], BF16)
    def scr_list(name, n, shape, dt):
        T[name] = [nc.dram_tensor("%s_%d" % (name, i), list(shape), dt).ap() for i in range(n)]
        T["b_" + name] = [Buf(name) for _ in range(n)]

    scr_list("KTd", 4, [256, NT], BF16); scr_list("KTd_all", 4, [1024, NT], BF16)
    scr_list("KTs", 2, [256, NT], BF16); scr_list("KTs_all", 2, [1024, NT], BF16)
    scr_list("Vd", 4, [512, 516], BF16); scr_list("Vd_all", 4, [2048, 516], BF16)
    scr_list("Vs", 2, [1024, 512], BF16); scr_list("Vs_all", 2, [4096, 512], BF16)
    scr("mixT", [D, NT], BF16)
    scr_list("HX", 16, [64, 3 + NT], F32); scr_list("HXA", 16, [256, 3 + NT], F32)
    inp("ssd_norm", [128, 8]); inp("ssd_w_in", [D, 6176]); inp("conv_w", [128, 32, 4]); inp("conv_b", [128, 32])
    inp("dt_bias", [1, 32]); inp("a_log", [1, 32]); inp("ssd_d", [1, 32]); inp("gnorm", [128, 16]); inp("ssd_w_out", [2048, D])
    inp("final_norm", [128, 8]); inp("selmask", [128, 20])
    inp("tri_incl", [128, 128]); inp("tri_gt", [128, 128]); inp("ntri_incl", [128, 128]); inp("neg_ut", [128, 128])
    scr("XS", [NT, 2048], BF16); scr("BTOK", [NT, 1024], BF16); scr("BT", [1024, NT], BF16); scr("CT", [1024, NT], BF16)
    scr("ZS", [NT, 2048], F32); scr("DTD", [NT, 64], F32)
    T["ST"] = [nc.dram_tensor("ST_%d" % i, [128, 2048], F32).ap() for i in range(16)]; T["b_ST"] = Buf("ST")
    scr("SXa", [128, 2048], F32); scr("SXa_all", [512, 2048], F32); scr("SXb", [128, 32], F32); scr("SXb_all", [512, 32], F32)
    scr("YT", [2048, NT], BF16)
    T["dbg"] = nc.dram_tensor("dbg", [D, NT], F32, kind="ExternalOutput").ap()

    nph = int(os.environ.get("KPH", "99"))
    phase_attn_inproj(nc, T)
    if nph >= 2:
        phase_attn_core(nc, T)
    if nph >= 3:
        phase_attn_out_ffn(nc, T, stage)
    if stage not in ("mix", "attn", "ffn0"):
        if nph >= 4:
            phase_ssd_inproj(nc, T)
        if nph >= 5:
            phase_ssd_states(nc, T)
        if nph >= 6:
            phase_ssd_scan(nc, T)
        if nph >= 7:
            phase_ssd_out_ffn(nc, T, stage)
    Sched.STATE.close()
    return nc


def rope_tables(q):
    half = 32
    inv_freq = (10000.0 ** (-np.arange(half, dtype=np.float32) / half)).astype(np.float32)
    n = np.arange(NT)
    pos = ((4 * (n // 128) + q) * 128 + (n % 128)).astype(np.float32)
    ang = pos[None, :] * inv_freq[:, None]
    cos = np.cos(ang).astype(np.float32)
    sin = np.sin(ang).astype(np.float32)
    tab = np.zeros((128, 2, NT), np.float32)
    for p in range(128):
        d = p % 64
        tab[p, 0] = cos[d % 32]
        tab[p, 1] = -sin[d % 32] if d < 32 else sin[d % 32]
    return tab


def neg_masks(q):
    jj = np.arange(128)[:, None]
    tt = np.arange(128)[None, :]
    out = []
    for strict in (False, True):
        keep_diag = (jj < tt) if strict else (jj <= tt)
        m = np.zeros((128, 16, 512), np.float32)
        for kbz in range(16):
            for i in range(4):
                z = kbz - 4 * i
                blk = m[:, kbz, i * 128:(i + 1) * 128]
                if z < 0:
                    continue
                if z > 3 or z > q:
                    blk[:] = NEG
                elif z == q:
                    blk[:] = np.where(keep_diag, 0.0, NEG)
        out.append(m)
    return out


def make_in_maps(inputs):
    f = lambda a: np.ascontiguousarray(np.asarray(a, dtype=np.float32))
    x = f(inputs["x"])
    w_in = f(inputs["attn_w_in"][0])
    qk = w_in[:, :1024].reshape(D, 16, 2, 32)
    w_sw = np.ascontiguousarray(qk[:, :, ::-1, :].reshape(D, 1024))
    ar = np.arange(128)
    common = {
        "w_in0": w_in, "w_sw0": w_sw, "w_out0": f(inputs["attn_w_out"][0]),
        "attn_norm": f(inputs["attn_norm"][0].reshape(8, 128).T),
        "ident": np.eye(128, dtype=np.float32),
        "nu": -(np.arange(128)[:, None] >= np.arange(128)[None, :]).astype(np.float32),
        "lq1": f(inputs["diff_lq1"]), "lk1": f(inputs["diff_lk1"]), "lq2": f(inputs["diff_lq2"]), "lk2": f(inputs["diff_lk2"]),
        "subln": f(inputs["diff_subln"]),
        "ffn_norm": f(np.stack([inputs["ffn_norm"][l].reshape(8, 128).T for l in range(2)])),
        "ffn_wg": f(inputs["ffn_w_gate"]), "ffn_wu": f(inputs["ffn_w_up"]), "ffn_wd": f(inputs["ffn_w_down"]),
        "ssd_norm": f(inputs["ssd_norm"][0].reshape(8, 128).T), "ssd_w_in": f(inputs["ssd_w_in"][0]),
        "conv_w": f(inputs["ssd_conv_w"][0].reshape(4, 32, 128).transpose(2, 1, 0)),
        "conv_b": f(inputs["ssd_conv_b"][0].reshape(32, 128).T),
        "dt_bias": f(inputs["ssd_dt_bias"]), "a_log": f(inputs["ssd_a_log"]), "ssd_d": f(inputs["ssd_d"]),
        "gnorm": f(inputs["ssd_gnorm"][0].reshape(16, 128).T), "ssd_w_out": f(inputs["ssd_w_out"][0]),
        "final_norm": f(inputs["final_norm"].reshape(8, 128).T),
        "tri_incl": (ar[:, None] <= ar[None, :]).astype(np.float32),
        "tri_gt": (ar[:, None] > ar[None, :]).astype(np.float32),
        "ntri_incl": -(ar[:, None] <= ar[None, :]).astype(np.float32),
        "neg_ut": np.where(ar[None, :] < ar[:, None], NEG, 0.0).astype(np.float32),
    }
    maps = []
    for c in range(8):
        b, q = c // 4, c % 4
        n = np.arange(NT)
        pos = (4 * (n // 128) + q) * 128 + (n % 128)
        m = dict(common)
        m["xT0"] = np.ascontiguousarray(x[b, pos, :].T)
        m["cs_tab"] = rope_tables(q)
        mi, ms = neg_masks(q)
        m["nm_incl"], m["nm_strict"] = mi, ms
        sel = np.zeros((128, 20), np.float32)
        for r in range(4):
            sel[:, r] = 1.0 if r < q else 0.0
            for r2 in range(4):
                sel[:, 4 + 4 * r + r2] = 1.0 if (r < r2 < q) else 0.0
        m["selmask"] = sel
        maps.append(m)
    return maps


def kernel(**inputs):
    stage = os.environ.get("KSTAGE", "final")
    nc = build_program(stage)
    maps = make_in_maps(inputs)
    res = run_bass_kernel_spmd(nc, maps, core_ids=list(range(8)))
    out = np.zeros((2, SEQ, D), np.float32)
    for c in range(8):
        b, q = c // 4, c % 4
        o = res.results[c]["dbg"]
        if stage in ("mix", "attn", "ffn0"):
            n = np.arange(NT)
            pos = (4 * (n // 128) + q) * 128 + (n % 128)
            out[b, pos, :] = o.T
        else:
            out[b, q * NT:(q + 1) * NT, :] = o.T
    return out
```

```python
import contextlib
import math
import os

import numpy as np
import concourse.bass as bass
import concourse.mybir as mybir
from concourse.bass_utils import run_bass_kernel_spmd

F32 = mybir.dt.float32
BF16 = mybir.dt.bfloat16
AF = mybir.ActivationFunctionType
ALU = mybir.AluOpType

D = 1024
SEQ = 8192
NT = 2048
NEG = -30000.0
GROUPS = [[0, 1, 2, 3], [4, 5, 6, 7]]
D_FF = 2816
FH = D_FF // 2


class Buf:
    __slots__ = ("name", "last_w", "rd_c", "rd_d")
    ALL = []

    def __init__(self, name=""):
        self.name = name
        self.last_w = None
        self.rd_c = {}
        self.rd_d = []
        Buf.ALL.append(self)

    @staticmethod
    def reset_all():
        for b in Buf.ALL:
            b.last_w = None
            b.rd_c = {}
            b.rd_d = []


class _Rec:
    def __init__(self):
        self.call = None

    def __getattr__(self, name):
        def f(*a, **k):
            self.call = (name, a, k)
            return None
        return f


def _freeze(fn):
    r = _Rec()
    fn(r)
    name, a, k = r.call
    return lambda e: getattr(e, name)(*a, **k)


class SemState:
    def __init__(self, nc):
        self.nc = nc
        self.st = contextlib.ExitStack()
        self.sems = {}
        self.base = {e: 0 for e in Sched.ENGS}
        self.semval = {}
        self.bar_count = 0

    def sem(self, key):
        if key not in self.sems:
            name = key if isinstance(key, str) else "_".join(str(k) for k in key)
            self.sems[key] = self.st.enter_context(self.nc.semaphore("s_" + name))
        return self.sems[key]

    def close(self):
        self.st.close()


class Sched:
    ENGS = ("sp", "act", "dve", "pool", "pe")
    STATE = None

    def __init__(self, nc, n_dma_sems=6):
        self.nc = nc
        self.state = Sched.STATE
        self.ops = {e: [] for e in self.ENGS}
        self.nops = {e: 0 for e in self.ENGS}
        self.known = {e: {} for e in self.ENGS}
        self.needed = {e: set() for e in self.ENGS}
        self.n_dma_sems = n_dma_sems
        self.dma_ring = {e: 0 for e in self.ENGS}
        self.semval = self.state.semval
        self.dma_keys = []

    def _wait(self, eng, ev):
        if ev is None:
            return
        if ev[0] == "c":
            _, src, idx = ev
            if src == "pe" and eng == "pe":
                return
            if self.known[eng].get(src, 0) >= idx:
                return
            self.known[eng][src] = idx
            self.needed[src].add(idx)
            self.ops[eng].append(("wait", ev))
        else:
            _, key, val = ev
            if self.known[eng].get(key, 0) >= val:
                return
            self.known[eng][key] = val
            self.ops[eng].append(("wait", ev))

    def _deps(self, eng, reads, writes):
        for b in reads:
            self._wait(eng, b.last_w)
        for b in writes:
            self._wait(eng, b.last_w)
            for src, idx in list(b.rd_c.items()):
                self._wait(eng, ("c", src, idx))
            for ev in b.rd_d:
                self._wait(eng, ev)

    def _commit(self, ev, reads, writes):
        for b in reads:
            if ev[0] == "c":
                if b.rd_c.get(ev[1], 0) < ev[2]:
                    b.rd_c[ev[1]] = ev[2]
            else:
                b.rd_d.append(ev)
        for b in writes:
            b.last_w = ev
            b.rd_c = {}
            b.rd_d = []

    def op(self, eng, fn, reads=(), writes=()):
        self._deps(eng, reads, writes)
        self.nops[eng] += 1
        idx = self.nops[eng]
        ev = ("c", eng, idx)
        self.ops[eng].append(("op", idx, _freeze(fn)))
        self._commit(ev, reads, writes)
        return ev

    def dma(self, eng, out, in_, reads=(), writes=(), inc=16, fn=None, ring=None):
        if ring is None:
            i = self.dma_ring[eng]
            self.dma_ring[eng] = i + 1
            key = ("dsem", eng, i % self.n_dma_sems)
        else:
            key = ring
        if key not in self.semval:
            self.semval[key] = 0
        if key not in self.dma_keys:
            self.dma_keys.append(key)
        prev = self.semval[key]
        if prev > 0:
            self._wait(eng, ("d", key, prev))
        self._deps(eng, reads, writes)
        self.semval[key] = prev + inc
        ev = ("d", key, prev + inc)
        if fn is None:
            fn = lambda e, out=out, in_=in_: e.dma_start(out=out, in_=in_)
        self.ops[eng].append(("dma", key, fn, inc))
        self._commit(ev, reads, writes)
        return ev

    def drain(self):
        for key in self.dma_keys:
            self._wait(key[1], ("d", key, self.semval[key]))

    def emit(self):
        nc = self.nc
        stt = self.state
        self.drain()
        sems = {}
        for e in self.ENGS:
            sems[e] = stt.sem("eng_" + e)
        for k in self.dma_keys:
            sems[k] = stt.sem(k)
        bar = stt.sem("bar")
        rank = {}
        for e in self.ENGS:
            if self.nops[e] > 0:
                self.needed[e].add(self.nops[e])
            rank[e] = {idx: stt.base[e] + r + 1 for r, idx in enumerate(sorted(self.needed[e]))}
        bar_target = stt.bar_count + 5
        with nc.Block() as block:
            def run(ename):
                def body(eng):
                    for ent in self.ops[ename]:
                        if ent[0] == "wait":
                            ev = ent[1]
                            if ev[0] == "c":
                                eng.wait_ge(sems[ev[1]], rank[ev[1]][ev[2]])
                            else:
                                eng.wait_ge(sems[ev[1]], ev[2])
                        elif ent[0] == "op":
                            ins = ent[2](eng)
                            if ent[1] in rank[ename]:
                                ins.then_inc(sems[ename], 1)
                        else:
                            ins = ent[2](eng)
                            ins.then_inc(sems[ent[1]], ent[3])
                    if self.nops[ename] > 0:
                        eng.wait_ge(sems[ename], rank[ename][self.nops[ename]])
                    eng.sem_inc(bar, 1)
                    eng.wait_ge(bar, bar_target)
                return body

            block.sync(run("sp"))
            block.scalar(run("act"))
            block.vector(run("dve"))
            block.gpsimd(run("pool"))
            block.tensor(run("pe"))
        if os.environ.get("KVERB"):
            print("emit: ops", {e: self.nops[e] for e in self.ENGS}, "entries", {e: len(self.ops[e]) for e in self.ENGS}, flush=True)
        for e in self.ENGS:
            stt.base[e] += len(self.needed[e])
        stt.bar_count = bar_target
        Buf.reset_all()


class Ctx:
    def __init__(self, nc):
        self.nc = nc
        self.S = Sched(nc)
        self.st = contextlib.ExitStack()
        self.n = 0

    CNT = [0]

    def sb(self, shape, dt=F32, name=None):
        Ctx.CNT[0] += 1
        return self.st.enter_context(self.nc.sbuf_tensor(name or ("t%d" % Ctx.CNT[0]), list(shape), dt))

    def ps(self, shape, dt=F32, name=None):
        Ctx.CNT[0] += 1
        return self.st.enter_context(self.nc.psum_tensor(name or ("p%d" % Ctx.CNT[0]), list(shape), dt))

    def close(self):
        self.S.emit()
        self.st.close()


COLL_N = [0]


def collective(S, kind, src, dst, reads, writes):
    def fn(e):
        return e.collective_compute(kind, ALU.bypass, replica_groups=GROUPS, ins=[src], outs=[dst])
    COLL_N[0] += 1
    return S.dma("pool", None, None, reads=reads, writes=writes, inc=1, fn=fn, ring=("csem", "pool", COLL_N[0] % 4))


def rmsnorm_fm(C, xT, bx, wn, bwn, xn, bxn, ones_f, bones, pss, eps=1e-6, ntok=NT, sqb=None, rstd=None, brstd=None, slices=None, out_f32=False):
    S = C.S
    if slices is None:
        slices = [slice(tt * 512, (tt + 1) * 512) for tt in range(ntok // 512)]
    for tt, sl in enumerate(slices):
        wd_ = sl.stop - sl.start
        ps, bps = pss[tt % len(pss)]
        for k in range(8):
            sq, bsq = sqb[k % len(sqb)]
            S.op("act", lambda e, sq=sq, k=k, sl=sl: e.activation(out=sq[:, 0:wd_], in_=xT[:, k, sl], func=AF.Square),
                 reads=[bx], writes=[bsq])
            S.op("pe", lambda e, ps=ps, sq=sq, k=k: e.matmul(ps[:, 0:wd_], lhsT=ones_f[:], rhs=sq[:, 0:wd_], start=(k == 0), stop=(k == 7)),
                 reads=[bsq, bones], writes=[bps])
        S.op("act", lambda e, ps=ps, sl=sl: e.activation(out=rstd[:, sl], in_=ps[:, 0:wd_], func=AF.Ln, scale=1.0 / D, bias=eps),
             reads=[bps], writes=[brstd])
        S.op("act", lambda e, sl=sl: e.activation(out=rstd[:, sl], in_=rstd[:, sl], func=AF.Exp, scale=-0.5),
             reads=[brstd], writes=[brstd])
        for k in range(8):
            S.op("dve", lambda e, k=k, sl=sl: e.scalar_tensor_tensor(out=xn[:, k, sl], in0=xT[:, k, sl], scalar=wn[:, k:k + 1],
                                                                     in1=rstd[:, sl], op0=ALU.mult, op1=ALU.mult),
                 reads=[bx, bwn, brstd], writes=[bxn])


def ffn_fm(C, T, layer, hT, bh, ones_f, bones, banks):
    S = C.S
    xn = C.sb([128, 8, NT], BF16); bxn = Buf()
    rstd = C.sb([128, NT], F32); brstd = Buf()
    wn = C.sb([128, 8], F32); bwn = Buf()
    S.dma("sp", wn[:], T["ffn_norm"][layer], writes=[bwn])
    sqb = [(C.sb([128, 512], F32), Buf()) for _ in range(2)]
    rmsnorm_fm(C, hT, bh, wn, bwn, xn, bxn, ones_f, bones, banks[0:2], sqb=sqb, rstd=rstd, brstd=brstd)
    wg = C.sb([128, 8, FH], BF16); bwg = Buf()
    wu = C.sb([128, 8, FH], BF16); bwu = Buf()
    wd = C.sb([128, 11, D], BF16); bwd = Buf()
    act = [(C.sb([128, 11, 512], BF16), Buf()) for _ in range(2)]
    sg = [(C.sb([128, 512], F32), Buf()) for _ in range(2)]
    for half in range(2):
        c0 = half * FH
        for k in range(8):
            S.dma("pool", wg[:, k, :], T["ffn_wg"][layer][k * 128:(k + 1) * 128, c0:c0 + FH], writes=[bwg])
            S.dma("pool", wu[:, k, :], T["ffn_wu"][layer][k * 128:(k + 1) * 128, c0:c0 + FH], writes=[bwu])
        for fc in range(11):
            S.dma("pool", wd[:, fc, :], T["ffn_wd"][layer][c0 + fc * 128:c0 + (fc + 1) * 128, :], writes=[bwd])
        for tt in range(NT // 512):
            sl = slice(tt * 512, (tt + 1) * 512)
            a, ba = act[tt % 2]
            for fc in range(11):
                pg, bpg = banks[(2 * fc) % 4]
                pu, bpu = banks[(2 * fc + 1) % 4]
                for k in range(8):
                    S.op("pe", lambda e, pg=pg, k=k, fc=fc, sl=sl: e.matmul(pg[:], lhsT=wg[:, k, fc * 128:(fc + 1) * 128], rhs=xn[:, k, sl],
                                                                            start=(k == 0), stop=(k == 7)),
                         reads=[bwg, bxn], writes=[bpg])
                for k in range(8):
                    S.op("pe", lambda e, pu=pu, k=k, fc=fc, sl=sl: e.matmul(pu[:], lhsT=wu[:, k, fc * 128:(fc + 1) * 128], rhs=xn[:, k, sl],
                                                                            start=(k == 0), stop=(k == 7)),
                         reads=[bwu, bxn], writes=[bpu])
                s_, bs_ = sg[fc % 2]
                S.op("act", lambda e, s_=s_, pg=pg: e.activation(out=s_[:], in_=pg[:], func=AF.Silu), reads=[bpg], writes=[bs_])
                S.op("dve", lambda e, a=a, fc=fc, s_=s_, pu=pu: e.tensor_tensor(out=a[:, fc, :], in0=pu[:], in1=s_[:], op=ALU.mult),
                     reads=[bpu, bs_], writes=[ba])
            for oc in range(8):
                po, bpo = banks[4 + oc % 4]
                for fc in range(11):
                    S.op("pe", lambda e, po=po, fc=fc, oc=oc, a=a: e.matmul(po[:], lhsT=wd[:, fc, oc * 128:(oc + 1) * 128], rhs=a[:, fc, :],
                                                                            start=(fc == 0), stop=(fc == 10)),
                         reads=[bwd, ba], writes=[bpo])
                S.op("dve", lambda e, po=po, oc=oc, sl=sl: e.tensor_tensor(out=hT[:, oc, sl], in0=po[:], in1=hT[:, oc, sl], op=ALU.add),
                     reads=[bpo, bh], writes=[bh])


def load_consts(C, T):
    S = C.S
    k = {}
    idf = C.sb([128, 128], F32); bidf = Buf()
    S.dma("sp", idf[:], T["ident"], writes=[bidf])
    k["identb"] = C.sb([128, 128], BF16); k["bident"] = Buf()
    S.op("dve", lambda e: e.tensor_copy(out=k["identb"][:], in_=idf[:]), reads=[bidf], writes=[k["bident"]])
    k["ones_f"] = C.sb([128, 128], F32); k["bones"] = Buf()
    S.op("pool", lambda e: e.memset(k["ones_f"][:], 1.0), writes=[k["bones"]])
    k["idf"] = idf; k["bidf"] = bidf
    return k


def phase_attn_inproj(nc, T):
    C = Ctx(nc); S = C.S
    K = load_consts(C, T)
    banks = [(C.ps([128, 512], F32), Buf()) for _ in range(8)]
    xT = C.sb([128, 8, NT], F32); bx = Buf()
    xsrc = T["xT0"].rearrange("(k p) n -> p k n", p=128)
    for tt in range(4):
        S.dma("sp", xT[:, :, tt * 512:(tt + 1) * 512], xsrc[:, :, tt * 512:(tt + 1) * 512], writes=[bx])
    cs = C.sb([128, 2, NT], F32); bcs = Buf()
    S.dma("act", cs[:], T["cs_tab"], writes=[bcs])
    wn = C.sb([128, 8], F32); bwn = Buf()
    S.dma("act", wn[:], T["attn_norm"], writes=[bwn])
    xn = C.sb([128, 8, NT], BF16); bxn = Buf()
    rstd = C.sb([128, NT], F32); brstd = Buf()
    sqb = [(C.sb([128, 512], F32), Buf()) for _ in range(2)]
    rmsnorm_fm(C, xT, bx, wn, bwn, xn, bxn, K["ones_f"], K["bones"], banks[0:2], sqb=sqb, rstd=rstd, brstd=brstd)

    kcut = int(os.environ.get("KCUT", "9"))
    if kcut <= 1:
        C.close()
        return
    wb = [(C.sb([128, 8, 512], BF16), Buf()) for _ in range(2)]
    wsw = [(C.sb([128, 8, 512], BF16), Buf()) for _ in range(2)]
    ob = [(C.sb([128, NT], BF16), Buf()) for _ in range(2)]
    t1 = [(C.sb([128, 512], F32), Buf()) for _ in range(2)]
    t2 = [(C.sb([128, 512], F32), Buf()) for _ in range(2)]
    vst = [(C.sb([128, 4, 129], BF16), Buf()) for _ in range(2)]
    vss = [(C.sb([128, 512], BF16), Buf()) for _ in range(2)]
    for v, bv in vst:
        S.op("pool", lambda e, v=v: e.memset(v[:], 1.0), writes=[bv])
    wsrc = T["w_in0"].rearrange("(k p) c -> p k c", p=128)
    swsrc = T["w_sw0"].rearrange("(k p) c -> p k c", p=128)
    dst_fm = {0: ("QT", 0), 1: ("KTd", 0), 3: ("QT", 512), 4: ("KTs", 0)}
    nchunk = 0
    for g in range(6):
        w, bw = wb[g % 2]
        for k in range(8):
            S.dma("pool", w[:, k, :], wsrc[:, k, g * 512:(g + 1) * 512], writes=[bw])
        if g < 2:
            w2, bw2 = wsw[g % 2]
            for k in range(8):
                S.dma("pool", w2[:, k, :], swsrc[:, k, g * 512:(g + 1) * 512], writes=[bw2])
        if g in dst_fm:
            dname, roff = dst_fm[g]
            for cc in range(4):
                o, bo = ob[nchunk % 2]; nchunk += 1
                for tt in range(4):
                    sl = slice(tt * 512, (tt + 1) * 512)
                    p1, bp1 = banks[2 + (2 * tt) % 4]
                    for k in range(8):
                        S.op("pe", lambda e, p1=p1, w=w, k=k, cc=cc, sl=sl: e.matmul(p1[:], lhsT=w[:, k, cc * 128:(cc + 1) * 128], rhs=xn[:, k, sl],
                                                                                     start=(k == 0), stop=(k == 7)),
                             reads=[bw, bxn], writes=[bp1])
                    if g < 2:
                        p2, bp2 = banks[2 + (2 * tt + 1) % 4]
                        for k in range(8):
                            S.op("pe", lambda e, p2=p2, w2=w2, k=k, cc=cc, sl=sl: e.matmul(p2[:], lhsT=w2[:, k, cc * 128:(cc + 1) * 128], rhs=xn[:, k, sl],
                                                                                           start=(k == 0), stop=(k == 7)),
                                 reads=[bw2, bxn], writes=[bp2])
                        a1, ba1 = t1[tt % 2]
                        a2, ba2 = t2[tt % 2]
                        S.op("dve", lambda e, a1=a1, p1=p1, sl=sl: e.tensor_tensor(out=a1[:], in0=p1[:], in1=cs[:, 0, sl], op=ALU.mult),
                             reads=[bp1, bcs], writes=[ba1])
                        S.op("dve", lambda e, a2=a2, p2=p2, sl=sl: e.tensor_tensor(out=a2[:], in0=p2[:], in1=cs[:, 1, sl], op=ALU.mult),
                             reads=[bp2, bcs], writes=[ba2])
                        S.op("pool", lambda e, o=o, a1=a1, a2=a2, sl=sl: e.tensor_tensor(out=o[:, sl], in0=a1[:], in1=a2[:], op=ALU.add),
                             reads=[ba1, ba2], writes=[bo])
                    else:
                        sc = 0.125 if g == 3 else 1.0
                        S.op("act", lambda e, o=o, p1=p1, sl=sl, sc=sc: e.activation(out=o[:, sl], in_=p1[:], func=AF.Copy, scale=sc),
                             reads=[bp1], writes=[bo])
                if dname == "QT":
                    r0 = roff + cc * 128
                    S.dma("sp", T["QT"][r0:r0 + 128, :], o[:], reads=[bo], writes=[T["b_QT"]])
                else:
                    ch, r0 = cc // 2, (cc % 2) * 128
                    S.dma("sp", T[dname][ch][r0:r0 + 128, :], o[:], reads=[bo], writes=[T["b_" + dname][ch]])
                    if cc % 2 == 1:
                        collective(S, "AllGather", T[dname][ch], T[dname + "_all"][ch], reads=[T["b_" + dname][ch]], writes=[T["b_" + dname + "_all"][ch]])
        else:
            for tb in range(16):
                p1, bp1 = banks[2 + tb % 4]
                for k in range(8):
                    S.op("pe", lambda e, p1=p1, w=w, k=k, tb=tb: e.matmul(p1[:], lhsT=xn[:, k, tb * 128:(tb + 1) * 128], rhs=w[:, k, :],
                                                                          start=(k == 0), stop=(k == 7)),
                         reads=[bw, bxn], writes=[bp1])
                if g == 2:
                    v, bv = vst[tb % 2]
                    S.op("act", lambda e, v=v, p1=p1: e.activation(out=v[:, :, 0:128], in_=p1[:].rearrange("p (h c) -> p h c", c=128), func=AF.Copy),
                         reads=[bp1], writes=[bv])
                    ch, r0 = tb // 4, (tb % 4) * 128
                    S.dma("sp", T["Vd"][ch][r0:r0 + 128, :], v[:].rearrange("p h c -> p (h c)"), reads=[bv], writes=[T["b_Vd"][ch]])
                    if tb % 4 == 3:
                        collective(S, "AllGather", T["Vd"][ch], T["Vd_all"][ch], reads=[T["b_Vd"][ch]], writes=[T["b_Vd_all"][ch]])
                else:
                    v, bv = vss[tb % 2]
                    S.op("act", lambda e, v=v, p1=p1: e.activation(out=v[:], in_=p1[:], func=AF.Copy), reads=[bp1], writes=[bv])
                    ch, r0 = tb // 8, (tb % 8) * 128
                    S.dma("sp", T["Vs"][ch][r0:r0 + 128, :], v[:], reads=[bv], writes=[T["b_Vs"][ch]])
                    if tb % 8 == 7:
                        collective(S, "AllGather", T["Vs"][ch], T["Vs_all"][ch], reads=[T["b_Vs"][ch]], writes=[T["b_Vs_all"][ch]])
    C.close()


def tile_iters(J):
    out = []
    for kb in range(16 * J + 16):
        i0 = max(0, kb // 4 - 4 * J)
        z = kb - 16 * J if kb >= 16 * J else None
        out.append((kb, kb % 4, kb // 4, i0, z))
    return out


def phase_attn_core(nc, T):
    C = Ctx(nc); S = C.S
    K = load_consts(C, T)
    identb, bident = K["identb"], K["bident"]
    ones_f, bones = K["ones_f"], K["bones"]
    psall = C.ps([128, 8, 512], F32)
    banks = [(psall[:, i, :], Buf()) for i in range(8)]
    nuf = C.sb([128, 128], F32); bnuf = Buf()
    S.dma("sp", nuf[:], T["nu"], writes=[bnuf])
    NU = C.sb([128, 128], BF16); bNU = Buf()
    S.op("dve", lambda e: e.tensor_copy(out=NU[:], in_=nuf[:]), reads=[bnuf], writes=[bNU])
    onesb = C.sb([128, 128], BF16); bonesb = Buf()
    S.op("pool", lambda e: e.memset(onesb[:], 1.0), writes=[bonesb])
    nmI = C.sb([128, 16, 512], BF16); bnmI = Buf()
    nmS = C.sb([128, 16, 512], BF16); bnmS = Buf()
    for z in range(0, 16, 4):
        S.dma("pool", nmI[:, z:z + 4, :], T["nm_incl"][:, z:z + 4, :], writes=[bnmI])
        S.dma("pool", nmS[:, z:z + 4, :], T["nm_strict"][:, z:z + 4, :], writes=[bnmS])
    lv = C.sb([128, 4, 64], F32); blv = Buf()
    for i, nm in enumerate(("lq1", "lk1", "lq2", "lk2")):
        S.dma("sp", lv[:, i:i + 1, :], T[nm].partition_broadcast(128), writes=[blv])
    lt = C.sb([128, 2, 64], F32); blt = Buf()
    S.op("dve", lambda e: e.tensor_tensor(out=lt[:, 0, :], in0=lv[:, 0, :], in1=lv[:, 1, :], op=ALU.mult), reads=[blv], writes=[blt])
    S.op("dve", lambda e: e.tensor_tensor(out=lt[:, 1, :], in0=lv[:, 2, :], in1=lv[:, 3, :], op=ALU.mult), reads=[blv], writes=[blt])
    ls = C.sb([128, 4], F32); bls = Buf()
    S.op("dve", lambda e: e.reduce_sum(out=ls[:, 0:1], in_=lt[:, 0, :], axis=mybir.AxisListType.X), reads=[blt], writes=[bls])
    S.op("dve", lambda e: e.reduce_sum(out=ls[:, 1:2], in_=lt[:, 1, :], axis=mybir.AxisListType.X), reads=[blt], writes=[bls])
    S.op("act", lambda e: e.activation(out=ls[:, 0:2], in_=ls[:, 0:2], func=AF.Exp), reads=[bls], writes=[bls])
    S.op("dve", lambda e: e.tensor_tensor(out=ls[:, 2:3], in0=ls[:, 1:2], in1=ls[:, 0:1], op=ALU.subtract), reads=[bls], writes=[bls])
    S.op("dve", lambda e: e.tensor_scalar(out=ls[:, 3:4], in0=ls[:, 2:3], scalar1=-0.2, scalar2=None, op0=ALU.add), reads=[bls], writes=[bls])
    negl = ls[:, 3:4]
    sl_ = C.sb([128, 1], F32); bsl = Buf()
    S.dma("sp", sl_[:], T["subln"].rearrange("o v -> v o"), writes=[bsl])
    S.op("dve", lambda e: e.tensor_scalar(out=sl_[:], in0=sl_[:], scalar1=0.8, scalar2=None, op0=ALU.mult), reads=[bsl], writes=[bsl])

    kbuf = [(C.sb([128, 4, NT], BF16), Buf()) for _ in range(2)]
    vbuf = [(C.sb([128, 64 * 129], BF16), Buf()) for _ in range(2)]
    qbuf = [(C.sb([128, NT], BF16), Buf()) for _ in range(2)]
    ostg = [(C.sb([128, 512], BF16), Buf()) for _ in range(2)]
    nload = 0
    nout = 0

    Eb = [(C.sb([128, 2, 512], BF16), Buf()) for _ in range(2)]
    ep = [(C.sb([128, 512], F32), Buf()) for _ in range(5)]
    ktd_all = [a.rearrange("(r x) n -> x r n", r=4) for a in T["KTd_all"]]
    vd_all = [a.rearrange("(r j t) (h c) -> t r j h c", r=4, j=4, h=4) for a in T["Vd_all"]]
    for H in range(4):
        kt, bkt = kbuf[nload % 2]; vv, bvv = vbuf[nload % 2]; qt, bqt = qbuf[nload % 2]; nload += 1
        for r in range(4):
            S.dma("sp", kt[:, r, :], ktd_all[H // 2][(H % 2) * 128:(H % 2 + 1) * 128, r, :], reads=[T["b_KTd_all"][H // 2]], writes=[bkt])
            for vc in range(4):
                b0 = r * 16 + 4 * vc
                S.dma("sp", vv[:, b0 * 129:(b0 + 4) * 129].rearrange("p (j c) -> p j c", c=129), vd_all[vc][:, r, :, H, :],
                      reads=[T["b_Vd_all"][vc]], writes=[bvv])
        S.dma("sp", qt[:], T["QT"][H * 128:(H + 1) * 128, :], reads=[T["b_QT"]], writes=[bqt])
        v4 = vv[:].rearrange("p (b c) -> p b c", c=129)
        for J in range(4):
            its = tile_iters(J)
            q0 = J * 512
            for bk_i in (4, 5, 6, 7):
                S.op("dve", lambda e, bk_i=bk_i: e.memset(banks[bk_i][0], 0.0), writes=[banks[bk_i][1]])

            def stage1(it, slot):
                kb, r, j, i0, z = it
                w0 = i0 * 128
                for m in range(2):
                    ps, bps = banks[2 * slot + m]
                    S.op("pe", lambda e, ps=ps, m=m, r=r, j=j, w0=w0: e.matmul(
                        ps[:, w0:512], lhsT=kt[m * 64:(m + 1) * 64, r, j * 128:(j + 1) * 128], rhs=qt[m * 64:(m + 1) * 64, q0 + w0:q0 + 512],
                        start=True, stop=(z is None)), reads=[bkt, bqt], writes=[bps])
                    if z is not None:
                        S.op("pe", lambda e, ps=ps, z=z, w0=w0: e.matmul(ps[:, w0:512], lhsT=identb[:], rhs=nmI[:, z, w0:512], start=False, stop=True),
                             reads=[bident, bnmI], writes=[bps])

            def stage2(it, slot):
                kb, r, j, i0, z = it
                w0 = i0 * 128
                E, bE = Eb[slot]
                S.op("act", lambda e, E=E, slot=slot, w0=w0: e.activation(out=E[:, :, w0:512], in_=psall[:, 2 * slot:2 * slot + 2, w0:512], func=AF.Exp, scale=0.125),
                     reads=[banks[2 * slot][1], banks[2 * slot + 1][1]], writes=[bE])
                for m in range(2):
                    S.op("pe", lambda e, E=E, m=m, r=r, j=j, w0=w0: e.matmul(banks[4 + m][0][:, w0:512], lhsT=v4[:, r * 16 + j, 0:128], rhs=E[:, m, w0:512],
                                                                             start=False, stop=False, skip_group_check=True),
                         reads=[bE, bvv], writes=[banks[4 + m][1]])
                    S.op("pe", lambda e, E=E, m=m, w0=w0: e.matmul(banks[6 + m][0][:, w0:512], lhsT=onesb[:], rhs=E[:, m, w0:512],
                                                                   start=False, stop=False, skip_group_check=True),
                         reads=[bE, bonesb], writes=[banks[6 + m][1]])

            for n in range(len(its) + 1):
                if n < len(its):
                    stage1(its[n], n % 2)
                if n >= 1:
                    stage2(its[n - 1], (n - 1) % 2)
            (r1, br1), (t1, bt1), (t2, bt2), (od, bod), (sq, bsq) = ep
            S.op("dve", lambda e: e.reciprocal(out=r1[:], in_=banks[6][0]), reads=[banks[6][1]], writes=[br1])
            S.op("dve", lambda e: e.tensor_tensor(out=t1[:], in0=banks[4][0], in1=r1[:], op=ALU.mult), reads=[banks[4][1], br1], writes=[bt1])
            S.op("dve", lambda e: e.reciprocal(out=r1[:], in_=banks[7][0]), reads=[banks[7][1], bt1], writes=[br1])
            S.op("dve", lambda e: e.tensor_tensor(out=t2[:], in0=banks[5][0], in1=r1[:], op=ALU.mult), reads=[banks[5][1], br1], writes=[bt2])
            S.op("dve", lambda e: e.scalar_tensor_tensor(out=od[:], in0=t2[:], scalar=negl, in1=t1[:], op0=ALU.mult, op1=ALU.add),
                 reads=[bt1, bt2, bls], writes=[bod])
            S.op("act", lambda e: e.activation(out=sq[:], in_=od[:], func=AF.Square), reads=[bod], writes=[bsq])
            pss, bpss = banks[0]
            S.op("pe", lambda e: e.matmul(pss, lhsT=ones_f[:], rhs=sq[:], start=True, stop=True), reads=[bones, bsq], writes=[bpss])
            S.op("act", lambda e: e.activation(out=t1[:], in_=pss, func=AF.Ln, scale=1.0 / 128, bias=1e-5), reads=[bpss, bod], writes=[bt1])
            S.op("act", lambda e: e.activation(out=t1[:], in_=t1[:], func=AF.Exp, scale=-0.5), reads=[bt1], writes=[bt1])
            o_, bo_ = ostg[nout % 2]; nout += 1
            S.op("dve", lambda e, o_=o_: e.scalar_tensor_tensor(out=o_[:], in0=od[:], scalar=sl_[:, 0:1], in1=t1[:], op0=ALU.mult, op1=ALU.mult),
                 reads=[bod, bsl, bt1], writes=[bo_])
            S.dma("sp", T["mixT"][H * 128:(H + 1) * 128, q0:q0 + 512], o_[:], reads=[bo_], writes=[T["b_mixT"]])

    eb = [(C.sb([128, 2, 512], F32), Buf()) for _ in range(2)]
    spb = [(C.sb([128, 2, 512], F32), Buf()) for _ in range(2)]
    hib = [(C.sb([128, 2, 512], BF16), Buf()) for _ in range(2)]
    lob = [(C.sb([128, 2, 512], BF16), Buf()) for _ in range(2)]
    Ab = [(C.sb([128, 2, 512], BF16), Buf()) for _ in range(2)]
    fb = [(C.sb([128, 512], F32), Buf()) for _ in range(2)]
    OT = C.sb([128, 512], F32); bOT = Buf()
    kts_all = [a.rearrange("(r x) n -> x r n", r=4) for a in T["KTs_all"]]
    vs_all = [a.rearrange("(r j t) (pr c) -> t r j pr c", r=4, j=8, pr=4) for a in T["Vs_all"]]
    for pr in range(4):
        kt, bkt = kbuf[nload % 2]; vv, bvv = vbuf[nload % 2]; qt, bqt = qbuf[nload % 2]; nload += 1
        for r in range(4):
            S.dma("sp", kt[:, r, :], kts_all[pr // 2][(pr % 2) * 128:(pr % 2 + 1) * 128, r, :], reads=[T["b_KTs_all"][pr // 2]], writes=[bkt])
            for vc in range(2):
                b0 = r * 16 + 8 * vc
                S.dma("sp", vv[:, b0 * 128:(b0 + 8) * 128].rearrange("p (j c) -> p j c", c=128), vs_all[vc][:, r, :, pr, :],
                      reads=[T["b_Vs_all"][vc]], writes=[bvv])
        S.dma("sp", qt[:], T["QT"][512 + pr * 128:512 + (pr + 1) * 128, :], reads=[T["b_QT"]], writes=[bqt])
        v4 = vv[:, 0:64 * 128].rearrange("p (b c) -> p b c", c=128)
        for J in range(4):
            its = tile_iters(J)
            q0 = J * 512
            S.op("pool", lambda e: e.memset(OT[:], 0.0), writes=[bOT])

            def stage1(it, slot):
                kb, r, j, i0, z = it
                w0 = i0 * 128
                ee, bee = eb[slot]; sp_, bsp = spb[slot]; hi, bhi = hib[slot]; lo, blo = lob[slot]
                for hh in range(2):
                    p0 = hh * 64
                    ps, bps = banks[2 * slot + hh]
                    S.op("pe", lambda e, ps=ps, r=r, j=j, w0=w0, p0=p0: e.matmul(
                        ps[:, w0:512], lhsT=kt[p0:p0 + 64, r, j * 128:(j + 1) * 128], rhs=qt[p0:p0 + 64, q0 + w0:q0 + 512],
                        start=True, stop=(z is None)), reads=[bkt, bqt], writes=[bps])
                    if z is not None:
                        S.op("pe", lambda e, ps=ps, z=z, w0=w0: e.matmul(ps[:, w0:512], lhsT=identb[:], rhs=nmS[:, z, w0:512], start=False, stop=True),
                             reads=[bident, bnmS], writes=[bps])
                zz = psall[:, 2 * slot:2 * slot + 2, w0:512]
                bz = [banks[2 * slot][1], banks[2 * slot + 1][1]]
                S.op("act", lambda e, ee=ee, zz=zz, w0=w0: e.activation(out=ee[:, :, w0:512], in_=zz, func=AF.Exp), reads=bz, writes=[bee])
                S.op("act", lambda e, ee=ee, sp_=sp_, w0=w0: e.activation(out=sp_[:, :, w0:512], in_=ee[:, :, w0:512], func=AF.Ln, bias=1.0),
                     reads=[bee], writes=[bsp])
                S.op("dve", lambda e, hi=hi, sp_=sp_, w0=w0: e.tensor_copy(out=hi[:, :, w0:512], in_=sp_[:, :, w0:512]), reads=[bsp], writes=[bhi])
                S.op("dve", lambda e, lo=lo, hi=hi, sp_=sp_, w0=w0: e.tensor_tensor(out=lo[:, :, w0:512], in0=sp_[:, :, w0:512], in1=hi[:, :, w0:512], op=ALU.subtract),
                     reads=[bsp, bhi], writes=[blo])

            def stage2(it, slot):
                kb, r, j, i0, z = it
                w0 = i0 * 128
                hi, bhi = hib[slot]; lo, blo = lob[slot]; A, bA = Ab[slot]; f, bf_ = fb[slot]
                bz = [banks[2 * slot][1], banks[2 * slot + 1][1]]
                for hh in range(2):
                    ps, bps = banks[2 * slot + hh]
                    S.op("pe", lambda e, ps=ps, hi=hi, hh=hh, w0=w0: e.matmul(ps[:, w0:512], lhsT=NU[:], rhs=hi[:, hh, w0:512], start=False, stop=False, skip_group_check=True),
                         reads=[bNU, bhi], writes=[bps])
                    S.op("pe", lambda e, ps=ps, lo=lo, hh=hh, w0=w0: e.matmul(ps[:, w0:512], lhsT=NU[:], rhs=lo[:, hh, w0:512], start=False, stop=True, skip_group_check=True),
                         reads=[bNU, blo], writes=[bps])
                zz = psall[:, 2 * slot:2 * slot + 2, w0:512]
                S.op("act", lambda e, A=A, zz=zz, w0=w0: e.activation(out=A[:, :, w0:512], in_=zz, func=AF.Exp), reads=bz, writes=[bA])
                pP, bpP = banks[4 + slot]
                pC, bpC = banks[6 + slot]
                for hh in range(2):
                    p0 = hh * 64
                    S.op("pe", lambda e, pP=pP, A=A, hh=hh, p0=p0, r=r, j=j, w0=w0: e.matmul(pP[p0:p0 + 64, w0:512], lhsT=v4[:, r * 16 + j, p0:p0 + 64], rhs=A[:, hh, w0:512],
                                                                                            start=True, stop=True),
                         reads=[bA, bvv], writes=[bpP])
                    S.op("pe", lambda e, pC=pC, hi=hi, hh=hh, p0=p0, w0=w0: e.matmul(pC[p0:p0 + 64, w0:512], lhsT=onesb[:, 0:64], rhs=hi[:, hh, w0:512], start=True, stop=False),
                         reads=[bhi, bonesb], writes=[bpC])
                    S.op("pe", lambda e, pC=pC, lo=lo, hh=hh, p0=p0, w0=w0: e.matmul(pC[p0:p0 + 64, w0:512], lhsT=onesb[:, 0:64], rhs=lo[:, hh, w0:512], start=False, stop=True),
                         reads=[blo, bonesb], writes=[bpC])
                S.op("act", lambda e, f=f, pC=pC, w0=w0: e.activation(out=f[:, w0:512], in_=pC[:, w0:512], func=AF.Exp, scale=-1.0), reads=[bpC], writes=[bf_])
                S.op("dve", lambda e, f=f, w0=w0: e.tensor_tensor(out=OT[:, w0:512], in0=OT[:, w0:512], in1=f[:, w0:512], op=ALU.mult), reads=[bOT, bf_], writes=[bOT])
                S.op("dve", lambda e, pP=pP, w0=w0: e.tensor_tensor(out=OT[:, w0:512], in0=pP[:, w0:512], in1=OT[:, w0:512], op=ALU.add), reads=[bOT, bpP], writes=[bOT])

            for n in range(len(its) + 1):
                if n < len(its):
                    stage1(its[n], n % 2)
                if n >= 1:
                    stage2(its[n - 1], (n - 1) % 2)
            o_, bo_ = ostg[nout % 2]; nout += 1
            S.op("act", lambda e, o_=o_: e.activation(out=o_[:], in_=OT[:], func=AF.Copy), reads=[bOT], writes=[bo_])
            S.dma("sp", T["mixT"][(4 + pr) * 128:(5 + pr) * 128, q0:q0 + 512], o_[:], reads=[bo_], writes=[T["b_mixT"]])
    C.close()


def phase_attn_out_ffn(nc, T, stage):
    C = Ctx(nc); S = C.S
    K = load_consts(C, T)
    banks = [(C.ps([128, 512], F32), Buf()) for _ in range(8)]
    hT = C.sb([128, 8, NT], F32); bh = Buf()
    xsrc = T["xT0"].rearrange("(k p) n -> p k n", p=128)
    for tt in range(4):
        S.dma("sp", hT[:, :, tt * 512:(tt + 1) * 512], xsrc[:, :, tt * 512:(tt + 1) * 512], writes=[bh])
    with contextlib.ExitStack() as st2:
        mT = st2.enter_context(nc.sbuf_tensor("mTf", [128, 8, NT], BF16)); bmT = Buf()
        wo = st2.enter_context(nc.sbuf_tensor("wo", [128, 8, D], BF16)); bwo = Buf()
        msrc = T["mixT"].rearrange("(c p) n -> p c n", p=128)
        for c in range(8):
            S.dma("act", mT[:, c, :], msrc[:, c, :], reads=[T["b_mixT"]], writes=[bmT])
            S.dma("pool", wo[:, c, :], T["w_out0"][c * 128:(c + 1) * 128, :], writes=[bwo])
        for tt in range(4):
            sl = slice(tt * 512, (tt + 1) * 512)
            for oc in range(8):
                po, bpo = banks[oc % 4]
                for c in range(8):
                    S.op("pe", lambda e, po=po, c=c, oc=oc, sl=sl: e.matmul(po[:], lhsT=wo[:, c, oc * 128:(oc + 1) * 128], rhs=mT[:, c, sl],
                                                                            start=(c == 0), stop=(c == 7)),
                         reads=[bwo, bmT], writes=[bpo])
                S.op("dve", lambda e, po=po, oc=oc, sl=sl: e.tensor_tensor(out=hT[:, oc, sl], in0=po[:], in1=hT[:, oc, sl], op=ALU.add),
                     reads=[bpo, bh], writes=[bh])
        if stage == "mix":
            dst = T["dbg"].rearrange("(k p) n -> p k n", p=128)
            for c in range(8):
                S.op("dve", lambda e, c=c: e.tensor_copy(out=hT[:, c, :], in_=mT[:, c, :]), reads=[bmT, bh], writes=[bh])
            for tt in range(4):
                S.dma("sp", dst[:, :, tt * 512:(tt + 1) * 512], hT[:, :, tt * 512:(tt + 1) * 512], reads=[bh])
            C.S.emit()
            st2.close(); C.st.close()
            return
        if stage == "attn":
            dst = T["dbg"].rearrange("(k p) n -> p k n", p=128)
            for tt in range(4):
                S.dma("sp", dst[:, :, tt * 512:(tt + 1) * 512], hT[:, :, tt * 512:(tt + 1) * 512], reads=[bh])
            C.S.emit()
            st2.close(); C.st.close()
            return
        C.S.emit()
    C.S = Sched(nc); S = C.S
    ffn_fm(C, T, 0, hT, bh, K["ones_f"], K["bones"], banks)
    if stage == "ffn0":
        dst = T["dbg"].rearrange("(k p) n -> p k n", p=128)
        for tt in range(4):
            S.dma("sp", dst[:, :, tt * 512:(tt + 1) * 512], hT[:, :, tt * 512:(tt + 1) * 512], reads=[bh])
        C.close()
        return
    zc = C.sb([128, 4], F32); bzc = Buf()
    S.op("pool", lambda e: e.memset(zc[:], 0.0), writes=[bzc])
    for ch in range(16):
        k, hf = ch // 2, ch % 2
        S.dma("sp", T["HX"][ch][:, 0:3], zc[0:64, 0:3], reads=[bzc], writes=[T["b_HX"][ch]])
        S.dma("sp", T["HX"][ch][:, 3:3 + NT], hT[hf * 64:(hf + 1) * 64, k, :], reads=[bh], writes=[T["b_HX"][ch]])
        collective(S, "AllGather", T["HX"][ch], T["HXA"][ch], reads=[T["b_HX"][ch]], writes=[T["b_HXA"][ch]])
    C.close()


def load_h1_contig(C, T, x1, bx1, halo):
    S = C.S
    off = 3 if halo else 0
    cache = {}

    def rank_of(e):
        if "c" not in cache:
            cache["c"] = e.partition_id() % 4
        return cache["c"]
    for ch in range(16):
        k, hf = ch // 2, ch % 2
        p0 = hf * 64
        for r in range(4):
            dstv = x1[p0:p0 + 64, k, off:off + NT].rearrange("p (m r t) -> p m r t", m=4, r=4)[:, :, r, :]

            def fn(e, dstv=dstv, ch=ch, r=r):
                c = rank_of(e)
                src = T["HXA"][ch][r * 64:(r + 1) * 64, bass.ds(c * 512 + 3, 512)]
                return e.dma_start(out=dstv, in_=src.rearrange("p (m t) -> p m t", m=4))
            S.dma("sp", None, None, reads=[T["b_HXA"][ch]], writes=[bx1], fn=fn)
        if halo:
            def fn2(e, ch=ch, p0=p0, k=k):
                c = rank_of(e)
                src = T["HXA"][ch][3 * 64:4 * 64, bass.ds(c * 512, 3)]
                return e.dma_start(out=x1[p0:p0 + 64, k, 0:3], in_=src)
            S.dma("sp", None, None, reads=[T["b_HXA"][ch]], writes=[bx1], fn=fn2)


def phase_ssd_inproj(nc, T):
    C = Ctx(nc); S = C.S
    K = load_consts(C, T)
    identb, bident = K["identb"], K["bident"]
    banks = [(C.ps([128, 512], F32), Buf()) for _ in range(8)]
    x1 = C.sb([128, 8, 3 + NT], F32); bx1 = Buf()
    load_h1_contig(C, T, x1, bx1, True)
    wn = C.sb([128, 8], F32); bwn = Buf()
    S.dma("sp", wn[:], T["ssd_norm"], writes=[bwn])
    xn = C.sb([128, 8, 3 + NT], BF16); bxn = Buf()
    rstd = C.sb([128, 3 + NT], F32); brstd = Buf()
    sqb = [(C.sb([128, 512], F32), Buf()) for _ in range(2)]
    slices = [slice(0, 3)] + [slice(3 + tt * 512, 3 + (tt + 1) * 512) for tt in range(4)]
    rmsnorm_fm(C, x1, bx1, wn, bwn, xn, bxn, K["ones_f"], K["bones"], banks[0:2], sqb=sqb, rstd=rstd, brstd=brstd, slices=slices)

    cw = C.sb([128, 32, 4], F32); bcw = Buf()
    cb = C.sb([128, 32], F32); bcb = Buf()
    S.dma("sp", cw[:], T["conv_w"], writes=[bcw])
    S.dma("sp", cb[:], T["conv_b"], writes=[bcb])
    wb = [(C.sb([128, 8, 512], BF16), Buf()) for _ in range(2)]
    wsrc = T["ssd_w_in"].rearrange("(k p) c -> p k c", p=128)
    ub = [(C.sb([128, 3 + NT], F32), Buf()) for _ in range(2)]
    accb = [(C.sb([128, NT], F32), Buf()) for _ in range(2)]
    xcb = [(C.sb([128, NT], BF16), Buf()) for _ in range(2)]
    ctmp = C.sb([128, NT], F32); bctmp = Buf()
    tst = [(C.sb([128, 8, 128], BF16), Buf()) for _ in range(2)]
    pTs = [(banks[6][0][:].bitcast(BF16), banks[6][1]), (banks[7][0][:].bitcast(BF16), banks[7][1])]
    nst = 0
    for g in range(8):
        w, bw = wb[g % 2]
        for k in range(8):
            S.dma("pool", w[:, k, :], wsrc[:, k, 2048 + g * 512:2048 + (g + 1) * 512], writes=[bw])
        for c4 in range(4):
            cc = g * 4 + c4
            u, bu = ub[cc % 2]; acc, bacc = accb[cc % 2]; xc, bxc = xcb[cc % 2]
            veng = "dve"
            for ti, sl in enumerate(slices):
                wd_ = sl.stop - sl.start
                p1, bp1 = banks[2 + ti % 4]
                for k in range(8):
                    S.op("pe", lambda e, p1=p1, w=w, k=k, c4=c4, sl=sl, wd_=wd_: e.matmul(p1[:, 0:wd_], lhsT=w[:, k, c4 * 128:(c4 + 1) * 128], rhs=xn[:, k, sl],
                                                                                         start=(k == 0), stop=(k == 7)),
                         reads=[bw, bxn], writes=[bp1])
                S.op("act", lambda e, u=u, p1=p1, sl=sl, wd_=wd_: e.activation(out=u[:, sl], in_=p1[:, 0:wd_], func=AF.Copy), reads=[bp1], writes=[bu])
            S.op(veng, lambda e, acc=acc, u=u, cc=cc: e.tensor_scalar(out=acc[:], in0=u[:, 0:NT], scalar1=cw[:, cc, 0:1], scalar2=None, op0=ALU.mult),
                 reads=[bu, bcw], writes=[bacc])
            for tap in range(1, 4):
                if veng == "dve":
                    S.op(veng, lambda e, acc=acc, u=u, cc=cc, tap=tap: e.scalar_tensor_tensor(out=acc[:], in0=u[:, tap:tap + NT], scalar=cw[:, cc, tap:tap + 1],
                                                                                             in1=acc[:], op0=ALU.mult, op1=ALU.add),
                         reads=[bu, bcw, bacc], writes=[bacc])
                else:
                    S.op(veng, lambda e, u=u, cc=cc, tap=tap: e.tensor_scalar(out=ctmp[:], in0=u[:, tap:tap + NT], scalar1=cw[:, cc, tap:tap + 1], scalar2=None, op0=ALU.mult),
                         reads=[bu, bcw], writes=[bctmp])
                    S.op(veng, lambda e, acc=acc: e.tensor_tensor(out=acc[:], in0=acc[:], in1=ctmp[:], op=ALU.add), reads=[bacc, bctmp], writes=[bacc])
            S.op("act", lambda e, xc=xc, acc=acc, cc=cc: e.activation(out=xc[:], in_=acc[:], func=AF.Silu, bias=cb[:, cc:cc + 1]),
                 reads=[bacc, bcb], writes=[bxc])
            if cc >= 16:
                nm = "BT" if cc < 24 else "CT"
                gi = cc - 16 if cc < 24 else cc - 24
                S.dma("sp", T[nm][gi * 128:(gi + 1) * 128, :], xc[:], reads=[bxc], writes=[T["b_" + nm]])
            if cc < 24:
                dname = "XS" if cc < 16 else "BTOK"
                col0 = cc * 128 if cc < 16 else (cc - 16) * 128
                for half in range(2):
                    pT, bpT = pTs[nst % 2]
                    st_, bst = tst[nst % 2]; nst += 1
                    for tb8 in range(8):
                        tb = half * 8 + tb8
                        S.op("pe", lambda e, pT=pT, xc=xc, tb=tb, tb8=tb8: e.transpose(pT[:, tb8 * 128:(tb8 + 1) * 128], xc[:, tb * 128:(tb + 1) * 128], identb[:]),
                             reads=[bxc, bident], writes=[bpT])
                    S.op("dve" if nst % 2 == 0 else "act", (lambda e, st_=st_, pT=pT: e.tensor_copy(out=st_[:], in_=pT.rearrange("p (b c) -> p b c", c=128)))
                         if nst % 2 == 0 else (lambda e, st_=st_, pT=pT: e.activation(out=st_[:], in_=pT.rearrange("p (b c) -> p b c", c=128), func=AF.Copy)),
                         reads=[bpT], writes=[bst])
                    dstv = T[dname][half * 1024:(half + 1) * 1024, col0:col0 + 128].rearrange("(b t) c -> t b c", t=128)
                    S.dma("sp", dstv, st_[:], reads=[bst], writes=[T["b_" + dname]])
    zst = [(C.sb([128, 512], F32), Buf()) for _ in range(2)]
    nz = 0
    for g in range(4):
        w, bw = wb[g % 2]
        for k in range(8):
            S.dma("pool", w[:, k, :], wsrc[:, k, g * 512:(g + 1) * 512], writes=[bw])
        for tb in range(16):
            p1, bp1 = banks[2 + tb % 4]
            for k in range(8):
                S.op("pe", lambda e, p1=p1, w=w, k=k, tb=tb: e.matmul(p1[:], lhsT=xn[:, k, 3 + tb * 128:3 + (tb + 1) * 128], rhs=w[:, k, :],
                                                                      start=(k == 0), stop=(k == 7)),
                     reads=[bw, bxn], writes=[bp1])
            z_, bz = zst[nz % 2]; nz += 1
            S.op("act", lambda e, z_=z_, p1=p1: e.activation(out=z_[:], in_=p1[:], func=AF.Silu), reads=[bp1], writes=[bz])
            S.dma("sp", T["ZS"][tb * 128:(tb + 1) * 128, g * 512:(g + 1) * 512], z_[:], reads=[bz], writes=[T["b_ZS"]])
    wdt = C.sb([128, 8, 32], BF16); bwdt = Buf()
    for k in range(8):
        S.dma("pool", wdt[:, k, :], wsrc[:, k, 6144:6176], writes=[bwdt])
    hv = C.sb([128, 3, 32], F32); bhv = Buf()
    S.dma("sp", hv[:, 0:1, :], T["dt_bias"].partition_broadcast(128), writes=[bhv])
    S.dma("sp", hv[:, 1:2, :], T["a_log"].partition_broadcast(128), writes=[bhv])
    S.op("act", lambda e: e.activation(out=hv[:, 2, :], in_=hv[:, 1, :], func=AF.Exp), reads=[bhv], writes=[bhv])
    dst_ = [(C.sb([128, 64], F32), Buf()) for _ in range(2)]
    for tb in range(16):
        p1, bp1 = banks[2 + tb % 4]
        for k in range(8):
            S.op("pe", lambda e, p1=p1, k=k, tb=tb: e.matmul(p1[:, 0:32], lhsT=xn[:, k, 3 + tb * 128:3 + (tb + 1) * 128], rhs=wdt[:, k, :],
                                                             start=(k == 0), stop=(k == 7)),
                 reads=[bwdt, bxn], writes=[bp1])
        d_, bd = dst_[tb % 2]
        S.op("dve", lambda e, d_=d_, p1=p1: e.tensor_tensor(out=d_[:, 0:32], in0=p1[:, 0:32], in1=hv[:, 0, :], op=ALU.add), reads=[bp1, bhv], writes=[bd])
        S.op("act", lambda e, d_=d_: e.activation(out=d_[:, 0:32], in_=d_[:, 0:32], func=AF.Exp), reads=[bd], writes=[bd])
        S.op("act", lambda e, d_=d_: e.activation(out=d_[:, 0:32], in_=d_[:, 0:32], func=AF.Ln, bias=1.0), reads=[bd], writes=[bd])
        S.op("dve", lambda e, d_=d_: e.scalar_tensor_tensor(out=d_[:, 32:64], in0=d_[:, 0:32], scalar=-1.0, in1=hv[:, 2, :], op0=ALU.mult, op1=ALU.mult),
             reads=[bd, bhv], writes=[bd])
        S.dma("sp", T["DTD"][tb * 128:(tb + 1) * 128, :], d_[:], reads=[bd], writes=[T["b_DTD"]])
    C.close()


def ssd_consts(C, T):
    S = C.S
    k = {}
    for nm in ("tri_incl", "tri_gt", "ntri_incl"):
        k[nm] = C.sb([128, 128], F32); k["b_" + nm] = Buf()
        S.dma("sp", k[nm][:], T[nm], writes=[k["b_" + nm]])
    return k


def phase_ssd_states(nc, T):
    C = Ctx(nc); S = C.S
    K = load_consts(C, T)
    K2 = ssd_consts(C, T)
    banks = [(C.ps([128, 512], F32), Buf()) for _ in range(8)]
    Sloc = C.sb([128, 32, 64], F32); bS = Buf()
    S.op("pool", lambda e: e.memset(Sloc[:], 0.0), writes=[bS])
    ldsum = C.sb([128, 32], F32); bld = Buf()
    S.op("pool", lambda e: e.memset(ldsum[:], 0.0), writes=[bld])
    xsb = [(C.sb([128, 32, 64], BF16), Buf()) for _ in range(2)]
    btb = [(C.sb([128, 1024], BF16), Buf()) for _ in range(2)]
    dtb = [(C.sb([128, 64], F32), Buf()) for _ in range(2)]
    smb = [(C.sb([128, 3, 32], F32), Buf()) for _ in range(2)]
    xwb = [(C.sb([128, 32, 64], BF16), Buf()) for _ in range(2)]
    stb = [(C.sb([128, 2048], F32), Buf()) for _ in range(2)]
    for c in range(16):
        xs, bxs = xsb[c % 2]; bt, bbt = btb[c % 2]; dt_, bdt = dtb[c % 2]; sm, bsm = smb[c % 2]; xw, bxw = xwb[c % 2]; st_, bst = stb[c % 2]
        rows = slice(c * 128, (c + 1) * 128)
        S.dma("sp", xs[:].rearrange("p h c -> p (h c)"), T["XS"][rows, :], reads=[T["b_XS"]], writes=[bxs])
        S.dma("act", bt[:], T["BTOK"][rows, :], reads=[T["b_BTOK"]], writes=[bbt])
        S.dma("sp", dt_[:], T["DTD"][rows, :], reads=[T["b_DTD"]], writes=[bdt])
        pa, bpa = banks[c % 2]
        S.op("pe", lambda e, pa=pa, dt_=dt_: e.matmul(pa[:, 0:32], lhsT=K2["tri_gt"][:], rhs=dt_[:, 32:64], start=True, stop=True),
             reads=[K2["b_tri_gt"], bdt], writes=[bpa])
        S.op("pe", lambda e, pa=pa, dt_=dt_: e.matmul(pa[:, 32:64], lhsT=K["ones_f"][:], rhs=dt_[:, 32:64], start=True, stop=True),
             reads=[K["bones"], bdt], writes=[bpa])
        S.op("act", lambda e, sm=sm, pa=pa: e.activation(out=sm[:, 0:2, :], in_=pa[:, 0:64].rearrange("p (a h) -> p a h", a=2), func=AF.Exp),
             reads=[bpa], writes=[bsm])
        S.op("dve", lambda e, sm=sm, dt_=dt_: e.tensor_tensor(out=sm[:, 2, :], in0=sm[:, 0, :], in1=dt_[:, 0:32], op=ALU.mult), reads=[bsm, bdt], writes=[bsm])
        S.op("dve", lambda e, xw=xw, xs=xs, sm=sm: e.tensor_tensor(out=xw[:], in0=xs[:], in1=sm[:, 2, :].unsqueeze(2).to_broadcast([128, 32, 64]), op=ALU.mult),
             reads=[bxs, bsm], writes=[bxw])
        xwf = xw[:].rearrange("p h c -> p (h c)")
        for g in range(8):
            ps_, bps = banks[2 + g // 2]
            S.op("pe", lambda e, ps_=ps_, bt=bt, g=g, xwf=xwf: e.matmul(ps_[:, (g % 2) * 256:(g % 2 + 1) * 256], lhsT=bt[:, g * 128:(g + 1) * 128],
                                                                        rhs=xwf[:, g * 256:(g + 1) * 256], start=True, stop=True),
                 reads=[bbt, bxw], writes=[bps])
        for bq in range(4):
            ps_, bps = banks[2 + bq]
            S.op("act" if bq % 2 == 0 else "dve",
                 (lambda e, st_=st_, ps_=ps_, bq=bq: e.activation(out=st_[:, bq * 512:(bq + 1) * 512], in_=ps_[:], func=AF.Copy)) if bq % 2 == 0 else
                 (lambda e, st_=st_, ps_=ps_, bq=bq: e.tensor_copy(out=st_[:, bq * 512:(bq + 1) * 512], in_=ps_[:])),
                 reads=[bps], writes=[bst])
        S.dma("sp", T["ST"][c], st_[:], reads=[bst], writes=[T["b_ST"]])
        S.op("pool", lambda e, sm=sm: e.tensor_tensor(out=Sloc[:], in0=Sloc[:], in1=sm[:, 1, :].unsqueeze(2).to_broadcast([128, 32, 64]), op=ALU.mult),
             reads=[bS, bsm], writes=[bS])
        S.op("pool", lambda e, st_=st_: e.tensor_tensor(out=Sloc[:].rearrange("p h c -> p (h c)"), in0=Sloc[:].rearrange("p h c -> p (h c)"), in1=st_[:], op=ALU.add),
             reads=[bS, bst], writes=[bS])
        S.op("dve", lambda e, pa=pa: e.tensor_tensor(out=ldsum[:], in0=pa[:, 32:64], in1=ldsum[:], op=ALU.add), reads=[bpa, bld], writes=[bld])
    S.dma("sp", T["SXa"], Sloc[:].rearrange("p h c -> p (h c)"), reads=[bS], writes=[T["b_SXa"]])
    S.dma("sp", T["SXb"], ldsum[:], reads=[bld], writes=[T["b_SXb"]])
    collective(S, "AllGather", T["SXa"], T["SXa_all"], reads=[T["b_SXa"]], writes=[T["b_SXa_all"]])
    collective(S, "AllGather", T["SXb"], T["SXb_all"], reads=[T["b_SXb"]], writes=[T["b_SXb_all"]])
    C.close()


def phase_ssd_scan(nc, T):
    C = Ctx(nc); S = C.S
    K = load_consts(C, T)
    K2 = ssd_consts(C, T)
    identb, bident = K["identb"], K["bident"]
    identf, bidentf = K["idf"], K["bidf"]
    ones_f, bones = K["ones_f"], K["bones"]
    banks = [(C.ps([128, 512], F32), Buf()) for _ in range(8)]
    prev = C.sb([128, 32, 64], F32); bprev = Buf()
    prevb = C.sb([128, 2048], BF16); bprevb = Buf()
    msk = C.sb([128, 20], F32); bmsk = Buf()
    S.dma("sp", msk[:], T["selmask"], writes=[bmsk])
    ld = C.sb([128, 4, 32], F32); bldg = Buf()
    S.dma("sp", ld[:], T["SXb_all"].rearrange("(r p) h -> p r h", p=128), reads=[T["b_SXb_all"]], writes=[bldg])
    coef = C.sb([128, 4, 32], F32); bcoef = Buf()
    for r in range(4):
        S.op("dve", lambda e, r=r: e.tensor_scalar(out=coef[:, r, :], in0=ld[:, 0, :], scalar1=msk[:, 4 + 4 * r:5 + 4 * r], scalar2=None, op0=ALU.mult),
             reads=[bldg, bmsk], writes=[bcoef])
        for r2 in range(1, 4):
            S.op("dve", lambda e, r=r, r2=r2: e.scalar_tensor_tensor(out=coef[:, r, :], in0=ld[:, r2, :], scalar=msk[:, 4 + 4 * r + r2:5 + 4 * r + r2],
                                                                     in1=coef[:, r, :], op0=ALU.mult, op1=ALU.add),
                 reads=[bldg, bmsk, bcoef], writes=[bcoef])
        S.op("act", lambda e, r=r: e.activation(out=coef[:, r, :], in_=coef[:, r, :], func=AF.Exp), reads=[bcoef], writes=[bcoef])
        S.op("dve", lambda e, r=r: e.tensor_scalar(out=coef[:, r, :], in0=coef[:, r, :], scalar1=msk[:, r:r + 1], scalar2=None, op0=ALU.mult),
             reads=[bcoef, bmsk], writes=[bcoef])
    S.op("pool", lambda e: e.memset(prev[:], 0.0), writes=[bprev])
    sg_ = [(C.sb([128, 32, 64], F32), Buf()) for _ in range(2)]
    for r in range(4):
        t_, bt_ = sg_[r % 2]
        S.dma("sp", t_[:].rearrange("p h c -> p (h c)"), T["SXa_all"][r * 128:(r + 1) * 128, :], reads=[T["b_SXa_all"]], writes=[bt_])
        S.op("dve", lambda e, t_=t_, r=r: e.tensor_tensor(out=t_[:], in0=t_[:], in1=coef[:, r, :].unsqueeze(2).to_broadcast([128, 32, 64]), op=ALU.mult),
             reads=[bt_, bcoef], writes=[bt_])
        S.op("dve", lambda e, t_=t_: e.tensor_tensor(out=prev[:], in0=prev[:], in1=t_[:], op=ALU.add), reads=[bt_, bprev], writes=[bprev])
    prevf = prev[:].rearrange("p h c -> p (h c)")
    S.op("act", lambda e: e.activation(out=prevb[:], in_=prevf, func=AF.Copy), reads=[bprev], writes=[bprevb])
    hv = C.sb([128, 32], F32); bhv = Buf()
    S.dma("sp", hv[:].unsqueeze(1), T["ssd_d"].partition_broadcast(128), writes=[bhv])
    gw = C.sb([128, 16], F32); bgw = Buf()
    S.dma("sp", gw[:], T["gnorm"], writes=[bgw])
    negut = C.sb([128, 4, 128], F32); bneg = Buf()
    negutb = C.sb([128, 4, 128], BF16); bnegb = Buf()
    for r in range(4):
        S.dma("sp", negut[:, r, :], T["neg_ut"], writes=[bneg])
    S.op("dve", lambda e: e.tensor_copy(out=negutb[:], in_=negut[:]), reads=[bneg], writes=[bnegb])
    xsb = [(C.sb([128, 32, 64], BF16), Buf()) for _ in range(2)]
    dtb = [(C.sb([128, 64], F32), Buf()) for _ in range(2)]
    btb = [(C.sb([128, 8, 128], BF16), Buf()) for _ in range(2)]
    ctb = [(C.sb([128, 8, 128], BF16), Buf()) for _ in range(2)]
    zsb = [(C.sb([128, 2048], F32), Buf()) for _ in range(2)]
    stb = [(C.sb([128, 2048], F32), Buf()) for _ in range(2)]
    dth = C.sb([128, 2, 32], BF16); bdth = Buf()
    trib = C.sb([128, 128], BF16); ntrib = C.sb([128, 128], BF16); onesb = C.sb([128, 128], BF16); btb_ = Buf()
    S.op("dve", lambda e: e.tensor_copy(out=trib[:], in_=K2["tri_incl"][:]), reads=[K2["b_tri_incl"]], writes=[btb_])
    S.op("dve", lambda e: e.tensor_copy(out=ntrib[:], in_=K2["ntri_incl"][:]), reads=[K2["b_ntri_incl"]], writes=[btb_])
    S.op("dve", lambda e: e.memset(onesb[:], 1.0), writes=[btb_])
    smb = [(C.sb([128, 3, 32], F32), Buf()) for _ in range(2)]
    xdt = C.sb([128, 32, 64], BF16); bxdt = Buf()
    Dmb = [(C.sb([128, 4, 128], F32), Buf()) for _ in range(2)]
    MTb = [(C.sb([128, 4, 128], BF16), Buf()) for _ in range(2)]
    tyb = [(C.sb([128, 4, 64], F32), Buf()) for _ in range(2)]
    yc = C.sb([128, 8, 256], F32); byc = Buf()
    gy = C.sb([128, 8, 256], F32); bgy = Buf()
    ynb = C.sb([128, 2048], BF16); byn = Buf()
    nrm = C.sb([128, 3, 8], F32); bnrm = Buf()
    junk = C.sb([128, 256], F32); bjunk = Buf()
    yTs = [(C.sb([128, 16, 128], BF16), Buf()) for _ in range(2)]
    bt_src = T["BT"].rearrange("(g n) t -> n g t", n=128)
    ct_src = T["CT"].rearrange("(g n) t -> n g t", n=128)
    yt_dst = T["YT"].rearrange("(cc p) t -> p cc t", p=128)
    pT0 = banks[6][0][:].bitcast(BF16); pT1 = banks[7][0][:].bitcast(BF16)
    half_bufs = [[Buf(), Buf()] for _ in range(3)]
    for c in range(16):
        xs, bxs = xsb[c % 2]; dt_, bdt = dtb[c % 2]; bt, bbt = btb[c % 2]; ct, bct = ctb[c % 2]
        zs, bzs = zsb[c % 2]; st_, bst = stb[c % 2]; sm, bsm = smb[c % 2]
        rows = slice(c * 128, (c + 1) * 128)
        S.dma("sp", xs[:].rearrange("p h c -> p (h c)"), T["XS"][rows, :], reads=[T["b_XS"]], writes=[bxs])
        S.dma("sp", dt_[:], T["DTD"][rows, :], reads=[T["b_DTD"]], writes=[bdt])
        S.dma("act", bt[:], bt_src[:, :, rows], reads=[T["b_BT"]], writes=[bbt])
        S.dma("act", ct[:], ct_src[:, :, rows], reads=[T["b_CT"]], writes=[bct])
        S.dma("sp", zs[:], T["ZS"][rows, :], reads=[T["b_ZS"]], writes=[bzs])
        S.dma("sp", st_[:], T["ST"][c], reads=[T["b_ST"]], writes=[bst])
        pa, bpa = banks[0]
        S.op("pe", lambda e, dt_=dt_: e.matmul(pa[:, 0:32], lhsT=K2["tri_incl"][:], rhs=dt_[:, 32:64], start=True, stop=True),
             reads=[K2["b_tri_incl"], bdt], writes=[bpa])
        S.op("pe", lambda e, dt_=dt_: e.matmul(pa[:, 32:64], lhsT=ones_f[:], rhs=dt_[:, 32:64], start=True, stop=True),
             reads=[bones, bdt], writes=[bpa])
        S.op("act", lambda e, sm=sm: e.activation(out=sm[:, 0:2, :], in_=pa[:, 0:64].rearrange("p (a h) -> p a h", a=2), func=AF.Exp),
             reads=[bpa], writes=[bsm])
        S.op("dve", lambda e, dt_=dt_: e.tensor_copy(out=dth[:, 0, :], in_=dt_[:, 32:64]), reads=[bdt], writes=[bdth])
        S.op("dve", lambda e, dt_=dt_: e.tensor_tensor(out=dth[:, 1, :], in0=dt_[:, 32:64], in1=dth[:, 0, :], op=ALU.subtract), reads=[bdt, bdth], writes=[bdth])
        S.op("dve", lambda e, xs=xs, dt_=dt_: e.tensor_tensor(out=xdt[:], in0=xs[:], in1=dt_[:, 0:32].unsqueeze(2).to_broadcast([128, 32, 64]), op=ALU.mult),
             reads=[bxs, bdt], writes=[bxdt])
        for g in range(8):
            pseg, bpseg = banks[1 + g % 2]
            Dm, bDm = Dmb[g % 2]; MT, bMT = MTb[g % 2]; ty, bty = tyb[g % 2]
            hs = slice(4 * g, 4 * g + 4)
            psv = pseg[:].rearrange("p (h l) -> p h l", h=4)
            for x_ in range(2):
                S.op("pe", lambda e, psv=psv, x_=x_, hs=hs: e.matmul(psv, lhsT=ntrib[:], rhs=dth[:, x_, hs].unsqueeze(2).to_broadcast([128, 4, 128]),
                                                                     start=(x_ == 0), stop=False),
                     reads=[btb_, bdth], writes=[bpseg])
            for r in range(4):
                for x_ in range(2):
                    S.op("pe", lambda e, pseg=pseg, x_=x_, r=r, g=g: e.matmul(pseg[:, r * 128:(r + 1) * 128], lhsT=dth[:, x_, 4 * g + r:4 * g + r + 1].to_broadcast([128, 128]),
                                                                              rhs=trib[:], start=False, stop=False),
                         reads=[btb_, bdth], writes=[bpseg])
            S.op("pe", lambda e, pseg=pseg: e.matmul(pseg[:], lhsT=identb[:], rhs=negutb[:].rearrange("p h l -> p (h l)"), start=False, stop=True),
                 reads=[bident, bnegb], writes=[bpseg])
            S.op("act", lambda e, Dm=Dm, pseg=pseg: e.activation(out=Dm[:].rearrange("p h l -> p (h l)"), in_=pseg[:], func=AF.Exp), reads=[bpseg], writes=[bDm])
            pg = banks[3][0]; bpg = half_bufs[0][g % 2]
            gsl = slice((g % 2) * 128, (g % 2 + 1) * 128)
            S.op("pe", lambda e, bt=bt, ct=ct, g=g, gsl=gsl: e.matmul(pg[:, gsl], lhsT=bt[:, g, :], rhs=ct[:, g, :], start=True, stop=True),
                 reads=[bbt, bct], writes=[bpg])
            S.op("dve", lambda e, MT=MT, Dm=Dm, gsl=gsl: e.tensor_tensor(out=MT[:], in0=Dm[:], in1=pg[:, gsl].unsqueeze(1).to_broadcast([128, 4, 128]), op=ALU.mult),
                 reads=[bDm, bpg], writes=[bMT])
            pyd = banks[4][0]; bpyd = half_bufs[1][g % 2]
            pyo = banks[5][0]; bpyo = half_bufs[2][g % 2]
            ysl = slice((g % 2) * 256, (g % 2 + 1) * 256)
            for r in range(4):
                S.op("pe", lambda e, MT=MT, r=r, g=g, ysl=ysl: e.matmul(pyd[:, ysl.start + r * 64:ysl.start + (r + 1) * 64], lhsT=MT[:, r, :], rhs=xdt[:, 4 * g + r, :],
                                                                        start=True, stop=True),
                     reads=[bMT, bxdt], writes=[bpyd])
            S.op("pe", lambda e, ct=ct, g=g, ysl=ysl: e.matmul(pyo[:, ysl], lhsT=ct[:, g, :], rhs=prevb[:, g * 256:(g + 1) * 256], start=True, stop=True),
                 reads=[bct, bprevb], writes=[bpyo])
            S.op("dve", lambda e, ty=ty, sm=sm, hs=hs, ysl=ysl: e.tensor_tensor(out=ty[:], in0=pyo[:, ysl].rearrange("p (r c) -> p r c", r=4),
                                                                                in1=sm[:, 0, hs].unsqueeze(2).to_broadcast([128, 4, 64]), op=ALU.mult),
                 reads=[bpyo, bsm], writes=[bty])
            S.op("dve", lambda e, ty=ty, g=g, ysl=ysl: e.tensor_tensor(out=yc[:, g, :], in0=pyd[:, ysl], in1=ty[:].rearrange("p r c -> p (r c)"), op=ALU.add),
                 reads=[bpyd, bty], writes=[byc])
        ycv = yc[:].rearrange("p g (r c) -> p (g r) c", r=4)
        S.op("pool", lambda e, xs=xs: e.tensor_tensor(out=gy[:].rearrange("p g (r c) -> p (g r) c", r=4), in0=xs[:], in1=hv[:].unsqueeze(2).to_broadcast([128, 32, 64]), op=ALU.mult),
             reads=[bxs, bhv], writes=[bgy])
        S.op("pool", lambda e: e.tensor_tensor(out=yc[:], in0=yc[:], in1=gy[:], op=ALU.add), reads=[byc, bgy], writes=[byc])
        S.op("dve", lambda e, zs=zs: e.tensor_tensor(out=gy[:], in0=yc[:], in1=zs[:].rearrange("p (g c) -> p g c", g=8), op=ALU.mult), reads=[byc, bzs, bgy], writes=[bgy])
        for g in range(8):
            S.op("act", lambda e, g=g: e.activation(out=junk[:], in_=gy[:, g, :], func=AF.Square, accum_out=nrm[:, 0, g:g + 1]), reads=[bgy], writes=[bjunk, bnrm])
        S.op("act", lambda e: e.activation(out=nrm[:, 1, :], in_=nrm[:, 0, :], func=AF.Ln, scale=1.0 / 256, bias=1e-5), reads=[bnrm], writes=[bnrm])
        S.op("act", lambda e: e.activation(out=nrm[:, 2, :], in_=nrm[:, 1, :], func=AF.Exp, scale=-0.5), reads=[bnrm], writes=[bnrm])
        S.op("dve", lambda e: e.tensor_tensor(out=ynb[:].rearrange("p (g c) -> p g c", g=8), in0=gy[:], in1=nrm[:, 2, :].unsqueeze(2).to_broadcast([128, 8, 256]), op=ALU.mult),
             reads=[bgy, bnrm], writes=[byn])
        yT, byT = yTs[c % 2]
        for cc in range(16):
            pT = pT0 if cc < 8 else pT1
            S.op("pe", lambda e, pT=pT, cc=cc: e.transpose(pT[:, (cc % 8) * 128:(cc % 8 + 1) * 128], ynb[:, cc * 128:(cc + 1) * 128], identb[:]),
                 reads=[byn, bident], writes=[banks[6][1] if cc < 8 else banks[7][1]])
        S.op("dve", lambda e, yT=yT: e.tensor_tensor(out=yT[:, 0:8, :], in0=pT0.rearrange("p (b c) -> p b c", c=128), in1=gw[:, 0:8].unsqueeze(2).to_broadcast([128, 8, 128]), op=ALU.mult),
             reads=[banks[6][1], bgw], writes=[byT])
        S.op("dve", lambda e, yT=yT: e.tensor_tensor(out=yT[:, 8:16, :], in0=pT1.rearrange("p (b c) -> p b c", c=128), in1=gw[:, 8:16].unsqueeze(2).to_broadcast([128, 8, 128]), op=ALU.mult),
             reads=[banks[7][1], bgw], writes=[byT])
        S.dma("sp", yt_dst[:, :, rows], yT[:], reads=[byT], writes=[T["b_YT"]])
        S.op("pool", lambda e, sm=sm: e.tensor_tensor(out=prev[:], in0=prev[:], in1=sm[:, 1, :].unsqueeze(2).to_broadcast([128, 32, 64]), op=ALU.mult),
             reads=[bprev, bsm], writes=[bprev])
        S.op("pool", lambda e, st_=st_: e.tensor_tensor(out=prevf, in0=prevf, in1=st_[:], op=ALU.add), reads=[bprev, bst], writes=[bprev])
        S.op("act", lambda e: e.activation(out=prevb[:], in_=prevf, func=AF.Copy), reads=[bprev, bprevb], writes=[bprevb])
    C.close()


def phase_ssd_out_ffn(nc, T, stage):
    C = Ctx(nc); S = C.S
    K = load_consts(C, T)
    banks = [(C.ps([128, 512], F32), Buf()) for _ in range(8)]
    hT = C.sb([128, 8, NT], F32); bh = Buf()
    load_h1_contig(C, T, hT, bh, False)
    dst = T["dbg"].rearrange("(k p) n -> p k n", p=128)
    with contextlib.ExitStack() as st2:
        yT = st2.enter_context(nc.sbuf_tensor("yTf", [128, 16, NT], BF16)); byT = Buf()
        wo = st2.enter_context(nc.sbuf_tensor("wo1", [128, 16, D], BF16)); bwo = Buf()
        ysrc = T["YT"].rearrange("(cc p) t -> p cc t", p=128)
        for cc in range(16):
            S.dma("act", yT[:, cc, :], ysrc[:, cc, :], reads=[T["b_YT"]], writes=[byT])
            S.dma("pool", wo[:, cc, :], T["ssd_w_out"][cc * 128:(cc + 1) * 128, :], writes=[bwo])
        for tt in range(4):
            sl = slice(tt * 512, (tt + 1) * 512)
            for oc in range(8):
                po, bpo = banks[oc % 4]
                for cc in range(16):
                    S.op("pe", lambda e, po=po, cc=cc, oc=oc, sl=sl: e.matmul(po[:], lhsT=wo[:, cc, oc * 128:(oc + 1) * 128], rhs=yT[:, cc, sl],
                                                                              start=(cc == 0), stop=(cc == 15)),
                         reads=[bwo, byT], writes=[bpo])
                S.op("dve", lambda e, po=po, oc=oc, sl=sl: e.tensor_tensor(out=hT[:, oc, sl], in0=po[:], in1=hT[:, oc, sl], op=ALU.add),
                     reads=[bpo, bh], writes=[bh])
        if stage == "ssd":
            for tt in range(4):
                S.dma("sp", dst[:, :, tt * 512:(tt + 1) * 512], hT[:, :, tt * 512:(tt + 1) * 512], reads=[bh])
            C.S.emit()
            st2.close(); C.st.close()
            return
        C.S.emit()
    C.S = Sched(nc); S = C.S
    Cf = Ctx(nc); Cf.S = C.S
    ffn_fm(Cf, T, 1, hT, bh, K["ones_f"], K["bones"], banks)
    C.S.emit()
    Cf.st.close()
    C.S = Sched(nc); S = C.S
    if stage != "ffn1":
        fw = C.sb([128, 8], F32); bfw = Buf()
        S.dma("sp", fw[:], T["final_norm"], writes=[bfw])
        rstd = C.sb([128, NT], F32); brstd = Buf()
        sq = [(C.sb([128, 512], F32), Buf()) for _ in range(2)]
        for tt in range(4):
            sl = slice(tt * 512, (tt + 1) * 512)
            ps, bps = banks[tt % 2]
            for k in range(8):
                q_, bq_ = sq[k % 2]
                S.op("act", lambda e, q_=q_, k=k, sl=sl: e.activation(out=q_[:], in_=hT[:, k, sl], func=AF.Square), reads=[bh], writes=[bq_])
                S.op("pe", lambda e, ps=ps, q_=q_, k=k: e.matmul(ps[:], lhsT=K["ones_f"][:], rhs=q_[:], start=(k == 0), stop=(k == 7)),
                     reads=[bq_, K["bones"]], writes=[bps])
            S.op("act", lambda e, ps=ps, sl=sl: e.activation(out=rstd[:, sl], in_=ps[:], func=AF.Ln, scale=1.0 / D, bias=1e-6), reads=[bps], writes=[brstd])
            S.op("act", lambda e, sl=sl: e.activation(out=rstd[:, sl], in_=rstd[:, sl], func=AF.Exp, scale=-0.5), reads=[brstd], writes=[brstd])
            for k in range(8):
                S.op("dve", lambda e, k=k, sl=sl: e.scalar_tensor_tensor(out=hT[:, k, sl], in0=hT[:, k, sl], scalar=fw[:, k:k + 1], in1=rstd[:, sl],
                                                                         op0=ALU.mult, op1=ALU.mult),
                     reads=[bh, bfw, brstd], writes=[bh])
    for tt in range(4):
        S.dma("sp", dst[:, :, tt * 512:(tt + 1) * 512], hT[:, :, tt * 512:(tt + 1) * 512], reads=[bh])
    C.close()


def build_program(stage):
    Buf.ALL = []
    nc = bass.Bass("TRN2", target_bir_lowering=False)
    Sched.STATE = SemState(nc)
    T = {}

    def inp(name, shape, dt=F32):
        T[name] = nc.dram_tensor(name, list(shape), dt, kind="ExternalInput").ap()

    def scr(name, shape, dt):
        T[name] = nc.dram_tensor(name, list(shape), dt).ap()
        T["b_" + name] = Buf(name)

    inp("xT0", [D, NT]); inp("cs_tab", [128, 2, NT]); inp("attn_norm", [128, 8])
    inp("w_in0", [D, 3072]); inp("w_sw0", [D, 1024]); inp("w_out0", [D, D])
    inp("nm_incl", [128, 16, 512]); inp("nm_strict", [128, 16, 512])
    inp("ident", [128, 128]); inp("nu", [128, 128])
    for nm in ("lq1", "lk1", "lq2", "lk2"):
        inp(nm, [1, 64])
    inp("subln", [1, 128])
    inp("ffn_norm", [2, 128, 8]); inp("ffn_wg", [2, D, D_FF]); inp("ffn_wu", [2, D, D_FF]); inp("ffn_wd", [2, D_FF, D])
    scr("QT", [1536, NT], BF16)
    def scr_list(name, n, shape, dt):
        T[name] = [nc.dram_tensor("%s_%d" % (name, i), list(shape), dt).ap() for i in range(n)]
        T["b_" + name] = [Buf(name) for _ in range(n)]

    scr_list("KTd", 4, [256, NT], BF16); scr_list("KTd_all", 4, [1024, NT], BF16)
    scr_list("KTs", 2, [256, NT], BF16); scr_list("KTs_all", 2, [1024, NT], BF16)
    scr_list("Vd", 4, [512, 516], BF16); scr_list("Vd_all", 4, [2048, 516], BF16)
    scr_list("Vs", 2, [1024, 512], BF16); scr_list("Vs_all", 2, [4096, 512], BF16)
    scr("mixT", [D, NT], BF16)
    scr_list("HX", 16, [64, 3 + NT], F32); scr_list("HXA", 16, [256, 3 + NT], F32)
    inp("ssd_norm", [128, 8]); inp("ssd_w_in", [D, 6176]); inp("conv_w", [128, 32, 4]); inp("conv_b", [128, 32])
    inp("dt_bias", [1, 32]); inp("a_log", [1, 32]); inp("ssd_d", [1, 32]); inp("gnorm", [128, 16]); inp("ssd_w_out", [2048, D])
    inp("final_norm", [128, 8]); inp("selmask", [128, 20])
    inp("tri_incl", [128, 128]); inp("tri_gt", [128, 128]); inp("ntri_incl", [128, 128]); inp("neg_ut", [128, 128])
    scr("XS", [NT, 2048], BF16); scr("BTOK", [NT, 1024], BF16); scr("BT", [1024, NT], BF16); scr("CT", [1024, NT], BF16)
    scr("ZS", [NT, 2048], F32); scr("DTD", [NT, 64], F32)
    T["ST"] = [nc.dram_tensor("ST_%d" % i, [128, 2048], F32).ap() for i in range(16)]; T["b_ST"] = Buf("ST")
    scr("SXa", [128, 2048], F32); scr("SXa_all", [512, 2048], F32); scr("SXb", [128, 32], F32); scr("SXb_all", [512, 32], F32)
    scr("YT", [2048, NT], BF16)
    T["dbg"] = nc.dram_tensor("dbg", [D, NT], F32, kind="ExternalOutput").ap()

    nph = int(os.environ.get("KPH", "99"))
    phase_attn_inproj(nc, T)
    if nph >= 2:
        phase_attn_core(nc, T)
    if nph >= 3:
        phase_attn_out_ffn(nc, T, stage)
    if stage not in ("mix", "attn", "ffn0"):
        if nph >= 4:
            phase_ssd_inproj(nc, T)
        if nph >= 5:
            phase_ssd_states(nc, T)
        if nph >= 6:
            phase_ssd_scan(nc, T)
        if nph >= 7:
            phase_ssd_out_ffn(nc, T, stage)
    Sched.STATE.close()
    return nc


def rope_tables(q):
    half = 32
    inv_freq = (10000.0 ** (-np.arange(half, dtype=np.float32) / half)).astype(np.float32)
    n = np.arange(NT)
    pos = ((4 * (n // 128) + q) * 128 + (n % 128)).astype(np.float32)
    ang = pos[None, :] * inv_freq[:, None]
    cos = np.cos(ang).astype(np.float32)
    sin = np.sin(ang).astype(np.float32)
    tab = np.zeros((128, 2, NT), np.float32)
    for p in range(128):
        d = p % 64
        tab[p, 0] = cos[d % 32]
        tab[p, 1] = -sin[d % 32] if d < 32 else sin[d % 32]
    return tab


def neg_masks(q):
    jj = np.arange(128)[:, None]
    tt = np.arange(128)[None, :]
    out = []
    for strict in (False, True):
        keep_diag = (jj < tt) if strict else (jj <= tt)
        m = np.zeros((128, 16, 512), np.float32)
        for kbz in range(16):
            for i in range(4):
                z = kbz - 4 * i
                blk = m[:, kbz, i * 128:(i + 1) * 128]
                if z < 0:
                    continue
                if z > 3 or z > q:
                    blk[:] = NEG
                elif z == q:
                    blk[:] = np.where(keep_diag, 0.0, NEG)
        out.append(m)
    return out


def make_in_maps(inputs):
    f = lambda a: np.ascontiguousarray(np.asarray(a, dtype=np.float32))
    x = f(inputs["x"])
    w_in = f(inputs["attn_w_in"][0])
    qk = w_in[:, :1024].reshape(D, 16, 2, 32)
    w_sw = np.ascontiguousarray(qk[:, :, ::-1, :].reshape(D, 1024))
    ar = np.arange(128)
    common = {
        "w_in0": w_in, "w_sw0": w_sw, "w_out0": f(inputs["attn_w_out"][0]),
        "attn_norm": f(inputs["attn_norm"][0].reshape(8, 128).T),
        "ident": np.eye(128, dtype=np.float32),
        "nu": -(np.arange(128)[:, None] >= np.arange(128)[None, :]).astype(np.float32),
        "lq1": f(inputs["diff_lq1"]), "lk1": f(inputs["diff_lk1"]), "lq2": f(inputs["diff_lq2"]), "lk2": f(inputs["diff_lk2"]),
        "subln": f(inputs["diff_subln"]),
        "ffn_norm": f(np.stack([inputs["ffn_norm"][l].reshape(8, 128).T for l in range(2)])),
        "ffn_wg": f(inputs["ffn_w_gate"]), "ffn_wu": f(inputs["ffn_w_up"]), "ffn_wd": f(inputs["ffn_w_down"]),
        "ssd_norm": f(inputs["ssd_norm"][0].reshape(8, 128).T), "ssd_w_in": f(inputs["ssd_w_in"][0]),
        "conv_w": f(inputs["ssd_conv_w"][0].reshape(4, 32, 128).transpose(2, 1, 0)),
        "conv_b": f(inputs["ssd_conv_b"][0].reshape(32, 128).T),
        "dt_bias": f(inputs["ssd_dt_bias"]), "a_log": f(inputs["ssd_a_log"]), "ssd_d": f(inputs["ssd_d"]),
        "gnorm": f(inputs["ssd_gnorm"][0].reshape(16, 128).T), "ssd_w_out": f(inputs["ssd_w_out"][0]),
        "final_norm": f(inputs["final_norm"].reshape(8, 128).T),
        "tri_incl": (ar[:, None] <= ar[None, :]).astype(np.float32),
        "tri_gt": (ar[:, None] > ar[None, :]).astype(np.float32),
        "ntri_incl": -(ar[:, None] <= ar[None, :]).astype(np.float32),
        "neg_ut": np.where(ar[None, :] < ar[:, None], NEG, 0.0).astype(np.float32),
    }
    maps = []
    for c in range(8):
        b, q = c // 4, c % 4
        n = np.arange(NT)
        pos = (4 * (n // 128) + q) * 128 + (n % 128)
        m = dict(common)
        m["xT0"] = np.ascontiguousarray(x[b, pos, :].T)
        m["cs_tab"] = rope_tables(q)
        mi, ms = neg_masks(q)
        m["nm_incl"], m["nm_strict"] = mi, ms
        sel = np.zeros((128, 20), np.float32)
        for r in range(4):
            sel[:, r] = 1.0 if r < q else 0.0
            for r2 in range(4):
                sel[:, 4 + 4 * r + r2] = 1.0 if (r < r2 < q) else 0.0
        m["selmask"] = sel
        maps.append(m)
    return maps


def kernel(**inputs):
    stage = os.environ.get("KSTAGE", "final")
    nc = build_program(stage)
    maps = make_in_maps(inputs)
    res = run_bass_kernel_spmd(nc, maps, core_ids=list(range(8)))
    out = np.zeros((2, SEQ, D), np.float32)
    for c in range(8):
        b, q = c // 4, c % 4
        o = res.results[c]["dbg"]
        if stage in ("mix", "attn", "ffn0"):
            n = np.arange(NT)
            pos = (4 * (n // 128) + q) * 128 + (n % 128)
            out[b, pos, :] = o.T
        else:
            out[b, q * NT:(q + 1) * NT, :] = o.T
    return out
```

```python
import contextlib
import math
import os

import numpy as np
import concourse.bass as bass
import concourse.mybir as mybir
from concourse.bass_utils import run_bass_kernel_spmd

F32 = mybir.dt.float32
BF16 = mybir.dt.bfloat16
AF = mybir.ActivationFunctionType
ALU = mybir.AluOpType

D = 1024
SEQ = 8192
NT = 2048
NEG = -30000.0
GROUPS = [[0, 1, 2, 3], [4, 5, 6, 7]]
D_FF = 2816
FH = D_FF // 2


class Buf:
    __slots__ = ("name", "last_w", "rd_c", "rd_d")
    ALL = []

    def __init__(self, name=""):
        self.name = name
        self.last_w = None
        self.rd_c = {}
        self.rd_d = []
        Buf.ALL.append(self)

    @staticmethod
    def reset_all():
        for b in Buf.ALL:
            b.last_w = None
            b.rd_c = {}
            b.rd_d = []


class _Rec:
    def __init__(self):
        self.call = None

    def __getattr__(self, name):
        def f(*a, **k):
            self.call = (name, a, k)
            return None
        return f


def _freeze(fn):
    r = _Rec()
    fn(r)
    name, a, k = r.call
    return lambda e: getattr(e, name)(*a, **k)


class SemState:
    def __init__(self, nc):
        self.nc = nc
        self.st = contextlib.ExitStack()
        self.sems = {}
        self.base = {e: 0 for e in Sched.ENGS}
        self.semval = {}
        self.bar_count = 0

    def sem(self, key):
        if key not in self.sems:
            name = key if isinstance(key, str) else "_".join(str(k) for k in key)
            self.sems[key] = self.st.enter_context(self.nc.semaphore("s_" + name))
        return self.sems[key]

    def close(self):
        self.st.close()


class Sched:
    ENGS = ("sp", "act", "dve", "pool", "pe")
    STATE = None

    def __init__(self, nc, n_dma_sems=6):
        self.nc = nc
        self.state = Sched.STATE
        self.ops = {e: [] for e in self.ENGS}
        self.nops = {e: 0 for e in self.ENGS}
        self.known = {e: {} for e in self.ENGS}
        self.needed = {e: set() for e in self.ENGS}
        self.n_dma_sems = n_dma_sems
        self.dma_ring = {e: 0 for e in self.ENGS}
        self.semval = self.state.semval
        self.dma_keys = []

    def _wait(self, eng, ev):
        if ev is None:
            return
        if ev[0] == "c":
            _, src, idx = ev
            if src == "pe" and eng == "pe":
                return
            if self.known[eng].get(src, 0) >= idx:
                return
            self.known[eng][src] = idx
            self.needed[src].add(idx)
            self.ops[eng].append(("wait", ev))
        else:
            _, key, val = ev
            if self.known[eng].get(key, 0) >= val:
                return
            self.known[eng][key] = val
            self.ops[eng].append(("wait", ev))

    def _deps(self, eng, reads, writes):
        for b in reads:
            self._wait(eng, b.last_w)
        for b in writes:
            self._wait(eng, b.last_w)
            for src, idx in list(b.rd_c.items()):
                self._wait(eng, ("c", src, idx))
            for ev in b.rd_d:
                self._wait(eng, ev)

    def _commit(self, ev, reads, writes):
        for b in reads:
            if ev[0] == "c":
                if b.rd_c.get(ev[1], 0) < ev[2]:
                    b.rd_c[ev[1]] = ev[2]
            else:
                b.rd_d.append(ev)
        for b in writes:
            b.last_w = ev
            b.rd_c = {}
            b.rd_d = []

    def op(self, eng, fn, reads=(), writes=()):
        self._deps(eng, reads, writes)
        self.nops[eng] += 1
        idx = self.nops[eng]
        ev = ("c", eng, idx)
        self.ops[eng].append(("op", idx, _freeze(fn)))
        self._commit(ev, reads, writes)
        return ev

    def dma(self, eng, out, in_, reads=(), writes=(), inc=16, fn=None, ring=None):
        if ring is None:
            i = self.dma_ring[eng]
            self.dma_ring[eng] = i + 1
            key = ("dsem", eng, i % self.n_dma_sems)
        else:
            key = ring
        if key not in self.semval:
            self.semval[key] = 0
        if key not in self.dma_keys:
            self.dma_keys.append(key)
        prev = self.semval[key]
        if prev > 0:
            self._wait(eng, ("d", key, prev))
        self._deps(eng, reads, writes)
        self.semval[key] = prev + inc
        ev = ("d", key, prev + inc)
        if fn is None:
            fn = lambda e, out=out, in_=in_: e.dma_start(out=out, in_=in_)
        self.ops[eng].append(("dma", key, fn, inc))
        self._commit(ev, reads, writes)
        return ev

    def drain(self):
        for key in self.dma_keys:
            self._wait(key[1], ("d", key, self.semval[key]))

    def emit(self):
        nc = self.nc
        stt = self.state
        self.drain()
        sems = {}
        for e in self.ENGS:
            sems[e] = stt.sem("eng_" + e)
        for k in self.dma_keys:
            sems[k] = stt.sem(k)
        bar = stt.sem("bar")
        rank = {}
        for e in self.ENGS:
            if self.nops[e] > 0:
                self.needed[e].add(self.nops[e])
            rank[e] = {idx: stt.base[e] + r + 1 for r, idx in enumerate(sorted(self.needed[e]))}
        bar_target = stt.bar_count + 5
        with nc.Block() as block:
            def run(ename):
                def body(eng):
                    for ent in self.ops[ename]:
                        if ent[0] == "wait":
                            ev = ent[1]
                            if ev[0] == "c":
                                eng.wait_ge(sems[ev[1]], rank[ev[1]][ev[2]])
                            else:
                                eng.wait_ge(sems[ev[1]], ev[2])
                        elif ent[0] == "op":
                            ins = ent[2](eng)
                            if ent[1] in rank[ename]:
                                ins.then_inc(sems[ename], 1)
                        else:
                            ins = ent[2](eng)
                            ins.then_inc(sems[ent[1]], ent[3])
                    if self.nops[ename] > 0:
                        eng.wait_ge(sems[ename], rank[ename][self.nops[ename]])
                    eng.sem_inc(bar, 1)
                    eng.wait_ge(bar, bar_target)
                return body

            block.sync(run("sp"))
            block.scalar(run("act"))
            block.vector(run("dve"))
            block.gpsimd(run("pool"))
            block.tensor(run("pe"))
        if os.environ.get("KVERB"):
            print("emit: ops", {e: self.nops[e] for e in self.ENGS}, "entries", {e: len(self.ops[e]) for e in self.ENGS}, flush=True)
        for e in self.ENGS:
            stt.base[e] += len(self.needed[e])
        stt.bar_count = bar_target
        Buf.reset_all()


class Ctx:
    def __init__(self, nc):
        self.nc = nc
        self.S = Sched(nc)
        self.st = contextlib.ExitStack()
        self.n = 0

    CNT = [0]

    def sb(self, shape, dt=F32, name=None):
        Ctx.CNT[0] += 1
        return self.st.enter_context(self.nc.sbuf_tensor(name or ("t%d" % Ctx.CNT[0]), list(shape), dt))

    def ps(self, shape, dt=F32, name=None):
        Ctx.CNT[0] += 1
        return self.st.enter_context(self.nc.psum_tensor(name or ("p%d" % Ctx.CNT[0]), list(shape), dt))

    def close(self):
        self.S.emit()
        self.st.close()


COLL_N = [0]


def collective(S, kind, src, dst, reads, writes):
    def fn(e):
        return e.collective_compute(kind, ALU.bypass, replica_groups=GROUPS, ins=[src], outs=[dst])
    COLL_N[0] += 1
    return S.dma("pool", None, None, reads=reads, writes=writes, inc=1, fn=fn, ring=("csem", "pool", COLL_N[0] % 4))


def rmsnorm_fm(C, xT, bx, wn, bwn, xn, bxn, ones_f, bones, pss, eps=1e-6, ntok=NT, sqb=None, rstd=None, brstd=None, slices=None, out_f32=False):
    S = C.S
    if slices is None:
        slices = [slice(tt * 512, (tt + 1) * 512) for tt in range(ntok // 512)]
    for tt, sl in enumerate(slices):
        wd_ = sl.stop - sl.start
        ps, bps = pss[tt % len(pss)]
        for k in range(8):
            sq, bsq = sqb[k % len(sqb)]
            S.op("act", lambda e, sq=sq, k=k, sl=sl: e.activation(out=sq[:, 0:wd_], in_=xT[:, k, sl], func=AF.Square),
                 reads=[bx], writes=[bsq])
            S.op("pe", lambda e, ps=ps, sq=sq, k=k: e.matmul(ps[:, 0:wd_], lhsT=ones_f[:], rhs=sq[:, 0:wd_], start=(k == 0), stop=(k == 7)),
                 reads=[bsq, bones], writes=[bps])
        S.op("act", lambda e, ps=ps, sl=sl: e.activation(out=rstd[:, sl], in_=ps[:, 0:wd_], func=AF.Ln, scale=1.0 / D, bias=eps),
             reads=[bps], writes=[brstd])
        S.op("act", lambda e, sl=sl: e.activation(out=rstd[:, sl], in_=rstd[:, sl], func=AF.Exp, scale=-0.5),
             reads=[brstd], writes=[brstd])
        for k in range(8):
            S.op("dve", lambda e, k=k, sl=sl: e.scalar_tensor_tensor(out=xn[:, k, sl], in0=xT[:, k, sl], scalar=wn[:, k:k + 1],
                                                                     in1=rstd[:, sl], op0=ALU.mult, op1=ALU.mult),
                 reads=[bx, bwn, brstd], writes=[bxn])


def ffn_fm(C, T, layer, hT, bh, ones_f, bones, banks):
    S = C.S
    xn = C.sb([128, 8, NT], BF16); bxn = Buf()
    rstd = C.sb([128, NT], F32); brstd = Buf()
    wn = C.sb([128, 8], F32); bwn = Buf()
    S.dma("sp", wn[:], T["ffn_norm"][layer], writes=[bwn])
    sqb = [(C.sb([128, 512], F32), Buf()) for _ in range(2)]
    rmsnorm_fm(C, hT, bh, wn, bwn, xn, bxn, ones_f, bones, banks[0:2], sqb=sqb, rstd=rstd, brstd=brstd)
    wg = C.sb([128, 8, FH], BF16); bwg = Buf()
    wu = C.sb([128, 8, FH], BF16); bwu = Buf()
    wd = C.sb([128, 11, D], BF16); bwd = Buf()
    act = [(C.sb([128, 11, 512], BF16), Buf()) for _ in range(2)]
    sg = [(C.sb([128, 512], F32), Buf()) for _ in range(2)]
    for half in range(2):
        c0 = half * FH
        for k in range(8):
            S.dma("pool", wg[:, k, :], T["ffn_wg"][layer][k * 128:(k + 1) * 128, c0:c0 + FH], writes=[bwg])
            S.dma("pool", wu[:, k, :], T["ffn_wu"][layer][k * 128:(k + 1) * 128, c0:c0 + FH], writes=[bwu])
        for fc in range(11):
            S.dma("pool", wd[:, fc, :], T["ffn_wd"][layer][c0 + fc * 128:c0 + (fc + 1) * 128, :], writes=[bwd])
        for tt in range(NT // 512):
            sl = slice(tt * 512, (tt + 1) * 512)
            a, ba = act[tt % 2]
            for fc in range(11):
                pg, bpg = banks[(2 * fc) % 4]
                pu, bpu = banks[(2 * fc + 1) % 4]
                for k in range(8):
                    S.op("pe", lambda e, pg=pg, k=k, fc=fc, sl=sl: e.matmul(pg[:], lhsT=wg[:, k, fc * 128:(fc + 1) * 128], rhs=xn[:, k, sl],
                                                                            start=(k == 0), stop=(k == 7)),
                         reads=[bwg, bxn], writes=[bpg])
                for k in range(8):
                    S.op("pe", lambda e, pu=pu, k=k, fc=fc, sl=sl: e.matmul(pu[:], lhsT=wu[:, k, fc * 128:(fc + 1) * 128], rhs=xn[:, k, sl],
                                                                            start=(k == 0), stop=(k == 7)),
                         reads=[bwu, bxn], writes=[bpu])
                s_, bs_ = sg[fc % 2]
                S.op("act", lambda e, s_=s_, pg=pg: e.activation(out=s_[:], in_=pg[:], func=AF.Silu), reads=[bpg], writes=[bs_])
                S.op("dve", lambda e, a=a, fc=fc, s_=s_, pu=pu: e.tensor_tensor(out=a[:, fc, :], in0=pu[:], in1=s_[:], op=ALU.mult),
                     reads=[bpu, bs_], writes=[ba])
            for oc in range(8):
                po, bpo = banks[4 + oc % 4]
                for fc in range(11):
                    S.op("pe", lambda e, po=po, fc=fc, oc=oc, a=a: e.matmul(po[:], lhsT=wd[:, fc, oc * 128:(oc + 1) * 128], rhs=a[:, fc, :],
                                                                            start=(fc == 0), stop=(fc == 10)),
                         reads=[bwd, ba], writes=[bpo])
                S.op("dve", lambda e, po=po, oc=oc, sl=sl: e.tensor_tensor(out=hT[:, oc, sl], in0=po[:], in1=hT[:, oc, sl], op=ALU.add),
                     reads=[bpo, bh], writes=[bh])


def load_consts(C, T):
    S = C.S
    k = {}
    idf = C.sb([128, 128], F32); bidf = Buf()
    S.dma("sp", idf[:], T["ident"], writes=[bidf])
    k["identb"] = C.sb([128, 128], BF16); k["bident"] = Buf()
    S.op("dve", lambda e: e.tensor_copy(out=k["identb"][:], in_=idf[:]), reads=[bidf], writes=[k["bident"]])
    k["ones_f"] = C.sb([128, 128], F32); k["bones"] = Buf()
    S.op("pool", lambda e: e.memset(k["ones_f"][:], 1.0), writes=[k["bones"]])
    k["idf"] = idf; k["bidf"] = bidf
    return k


def phase_attn_inproj(nc, T):
    C = Ctx(nc); S = C.S
    K = load_consts(C, T)
    banks = [(C.ps([128, 512], F32), Buf()) for _ in range(8)]
    xT = C.sb([128, 8, NT], F32); bx = Buf()
    xsrc = T["xT0"].rearrange("(k p) n -> p k n", p=128)
    for tt in range(4):
        S.dma("sp", xT[:, :, tt * 512:(tt + 1) * 512], xsrc[:, :, tt * 512:(tt + 1) * 512], writes=[bx])
    cs = C.sb([128, 2, NT], F32); bcs = Buf()
    S.dma("act", cs[:], T["cs_tab"], writes=[bcs])
    wn = C.sb([128, 8], F32); bwn = Buf()
    S.dma("act", wn[:], T["attn_norm"], writes=[bwn])
    xn = C.sb([128, 8, NT], BF16); bxn = Buf()
    rstd = C.sb([128, NT], F32); brstd = Buf()
    sqb = [(C.sb([128, 512], F32), Buf()) for _ in range(2)]
    rmsnorm_fm(C, xT, bx, wn, bwn, xn, bxn, K["ones_f"], K["bones"], banks[0:2], sqb=sqb, rstd=rstd, brstd=brstd)

    kcut = int(os.environ.get("KCUT", "9"))
    if kcut <= 1:
        C.close()
        return
    wb = [(C.sb([128, 8, 512], BF16), Buf()) for _ in range(2)]
    wsw = [(C.sb([128, 8, 512], BF16), Buf()) for _ in range(2)]
    ob = [(C.sb([128, NT], BF16), Buf()) for _ in range(2)]
    t1 = [(C.sb([128, 512], F32), Buf()) for _ in range(2)]
    t2 = [(C.sb([128, 512], F32), Buf()) for _ in range(2)]
    vst = [(C.sb([128, 4, 129], BF16), Buf()) for _ in range(2)]
    vss = [(C.sb([128, 512], BF16), Buf()) for _ in range(2)]
    for v, bv in vst:
        S.op("pool", lambda e, v=v: e.memset(v[:], 1.0), writes=[bv])
    wsrc = T["w_in0"].rearrange("(k p) c -> p k c", p=128)
    swsrc = T["w_sw0"].rearrange("(k p) c -> p k c", p=128)
    dst_fm = {0: ("QT", 0), 1: ("KTd", 0), 3: ("QT", 512), 4: ("KTs", 0)}
    nchunk = 0
    for g in range(6):
        w, bw = wb[g % 2]
        for k in range(8):
            S.dma("pool", w[:, k, :], wsrc[:, k, g * 512:(g + 1) * 512], writes=[bw])
        if g < 2:
            w2, bw2 = wsw[g % 2]
            for k in range(8):
                S.dma("pool", w2[:, k, :], swsrc[:, k, g * 512:(g + 1) * 512], writes=[bw2])
        if g in dst_fm:
            dname, roff = dst_fm[g]
            for cc in range(4):
                o, bo = ob[nchunk % 2]; nchunk += 1
                for tt in range(4):
                    sl = slice(tt * 512, (tt + 1) * 512)
                    p1, bp1 = banks[2 + (2 * tt) % 4]
                    for k in range(8):
                        S.op("pe", lambda e, p1=p1, w=w, k=k, cc=cc, sl=sl: e.matmul(p1[:], lhsT=w[:, k, cc * 128:(cc + 1) * 128], rhs=xn[:, k, sl],
                                                                                     start=(k == 0), stop=(k == 7)),
                             reads=[bw, bxn], writes=[bp1])
                    if g < 2:
                        p2, bp2 = banks[2 + (2 * tt + 1) % 4]
                        for k in range(8):
                            S.op("pe", lambda e, p2=p2, w2=w2, k=k, cc=cc, sl=sl: e.matmul(p2[:], lhsT=w2[:, k, cc * 128:(cc + 1) * 128], rhs=xn[:, k, sl],
                                                                                           start=(k == 0), stop=(k == 7)),
                                 reads=[bw2, bxn], writes=[bp2])
                        a1, ba1 = t1[tt % 2]
                        a2, ba2 = t2[tt % 2]
                        S.op("dve", lambda e, a1=a1, p1=p1, sl=sl: e.tensor_tensor(out=a1[:], in0=p1[:], in1=cs[:, 0, sl], op=ALU.mult),
                             reads=[bp1, bcs], writes=[ba1])
                        S.op("dve", lambda e, a2=a2, p2=p2, sl=sl: e.tensor_tensor(out=a2[:], in0=p2[:], in1=cs[:, 1, sl], op=ALU.mult),
                             reads=[bp2, bcs], writes=[ba2])
                        S.op("pool", lambda e, o=o, a1=a1, a2=a2, sl=sl: e.tensor_tensor(out=o[:, sl], in0=a1[:], in1=a2[:], op=ALU.add),
                             reads=[ba1, ba2], writes=[bo])
                    else:
                        sc = 0.125 if g == 3 else 1.0
                        S.op("act", lambda e, o=o, p1=p1, sl=sl, sc=sc: e.activation(out=o[:, sl], in_=p1[:], func=AF.Copy, scale=sc),
                             reads=[bp1], writes=[bo])
                if dname == "QT":
                    r0 = roff + cc * 128
                    S.dma("sp", T["QT"][r0:r0 + 128, :], o[:], reads=[bo], writes=[T["b_QT"]])
                else:
                    ch, r0 = cc // 2, (cc % 2) * 128
                    S.dma("sp", T[dname][ch][r0:r0 + 128, :], o[:], reads=[bo], writes=[T["b_" + dname][ch]])
                    if cc % 2 == 1:
                        collective(S, "AllGather", T[dname][ch], T[dname + "_all"][ch], reads=[T["b_" + dname][ch]], writes=[T["b_" + dname + "_all"][ch]])
        else:
            for tb in range(16):
                p1, bp1 = banks[2 + tb % 4]
                for k in range(8):
                    S.op("pe", lambda e, p1=p1, w=w, k=k, tb=tb: e.matmul(p1[:], lhsT=xn[:, k, tb * 128:(tb + 1) * 128], rhs=w[:, k, :],
                                                                          start=(k == 0), stop=(k == 7)),
                         reads=[bw, bxn], writes=[bp1])
                if g == 2:
                    v, bv = vst[tb % 2]
                    S.op("act", lambda e, v=v, p1=p1: e.activation(out=v[:, :, 0:128], in_=p1[:].rearrange("p (h c) -> p h c", c=128), func=AF.Copy),
                         reads=[bp1], writes=[bv])
                    ch, r0 = tb // 4, (tb % 4) * 128
                    S.dma("sp", T["Vd"][ch][r0:r0 + 128, :], v[:].rearrange("p h c -> p (h c)"), reads=[bv], writes=[T["b_Vd"][ch]])
                    if tb % 4 == 3:
                        collective(S, "AllGather", T["Vd"][ch], T["Vd_all"][ch], reads=[T["b_Vd"][ch]], writes=[T["b_Vd_all"][ch]])
                else:
                    v, bv = vss[tb % 2]
                    S.op("act", lambda e, v=v, p1=p1: e.activation(out=v[:], in_=p1[:], func=AF.Copy), reads=[bp1], writes=[bv])
                    ch, r0 = tb // 8, (tb % 8) * 128
                    S.dma("sp", T["Vs"][ch][r0:r0 + 128, :], v[:], reads=[bv], writes=[T["b_Vs"][ch]])
                    if tb % 8 == 7:
                        collective(S, "AllGather", T["Vs"][ch], T["Vs_all"][ch], reads=[T["b_Vs"][ch]], writes=[T["b_Vs_all"][ch]])
    C.close()


def tile_iters(J):
    out = []
    for kb in range(16 * J + 16):
        i0 = max(0, kb // 4 - 4 * J)
        z = kb - 16 * J if kb >= 16 * J else None
        out.append((kb, kb % 4, kb // 4, i0, z))
    return out


def phase_attn_core(nc, T):
    C = Ctx(nc); S = C.S
    K = load_consts(C, T)
    identb, bident = K["identb"], K["bident"]
    ones_f, bones = K["ones_f"], K["bones"]
    psall = C.ps([128, 8, 512], F32)
    banks = [(psall[:, i, :], Buf()) for i in range(8)]
    nuf = C.sb([128, 128], F32); bnuf = Buf()
    S.dma("sp", nuf[:], T["nu"], writes=[bnuf])
    NU = C.sb([128, 128], BF16); bNU = Buf()
    S.op("dve", lambda e: e.tensor_copy(out=NU[:], in_=nuf[:]), reads=[bnuf], writes=[bNU])
    onesb = C.sb([128, 128], BF16); bonesb = Buf()
    S.op("pool", lambda e: e.memset(onesb[:], 1.0), writes=[bonesb])
    nmI = C.sb([128, 16, 512], BF16); bnmI = Buf()
    nmS = C.sb([128, 16, 512], BF16); bnmS = Buf()
    for z in range(0, 16, 4):
        S.dma("pool", nmI[:, z:z + 4, :], T["nm_incl"][:, z:z + 4, :], writes=[bnmI])
        S.dma("pool", nmS[:, z:z + 4, :], T["nm_strict"][:, z:z + 4, :], writes=[bnmS])
    lv = C.sb([128, 4, 64], F32); blv = Buf()
    for i, nm in enumerate(("lq1", "lk1", "lq2", "lk2")):
        S.dma("sp", lv[:, i:i + 1, :], T[nm].partition_broadcast(128), writes=[blv])
    lt = C.sb([128, 2, 64], F32); blt = Buf()
    S.op("dve", lambda e: e.tensor_tensor(out=lt[:, 0, :], in0=lv[:, 0, :], in1=lv[:, 1, :], op=ALU.mult), reads=[blv], writes=[blt])
    S.op("dve", lambda e: e.tensor_tensor(out=lt[:, 1, :], in0=lv[:, 2, :], in1=lv[:, 3, :], op=ALU.mult), reads=[blv], writes=[blt])
    ls = C.sb([128, 4], F32); bls = Buf()
    S.op("dve", lambda e: e.reduce_sum(out=ls[:, 0:1], in_=lt[:, 0, :], axis=mybir.AxisListType.X), reads=[blt], writes=[bls])
    S.op("dve", lambda e: e.reduce_sum(out=ls[:, 1:2], in_=lt[:, 1, :], axis=mybir.AxisListType.X), reads=[blt], writes=[bls])
    S.op("act", lambda e: e.activation(out=ls[:, 0:2], in_=ls[:, 0:2], func=AF.Exp), reads=[bls], writes=[bls])
    S.op("dve", lambda e: e.tensor_tensor(out=ls[:, 2:3], in0=ls[:, 1:2], in1=ls[:, 0:1], op=ALU.subtract), reads=[bls], writes=[bls])
    S.op("dve", lambda e: e.tensor_scalar(out=ls[:, 3:4], in0=ls[:, 2:3], scalar1=-0.2, scalar2=None, op0=ALU.add), reads=[bls], writes=[bls])
    negl = ls[:, 3:4]
    sl_ = C.sb([128, 1], F32); bsl = Buf()
    S.dma("sp", sl_[:], T["subln"].rearrange("o v -> v o"), writes=[bsl])
    S.op("dve", lambda e: e.tensor_scalar(out=sl_[:], in0=sl_[:], scalar1=0.8, scalar2=None, op0=ALU.mult), reads=[bsl], writes=[bsl])

    kbuf = [(C.sb([128, 4, NT], BF16), Buf()) for _ in range(2)]
    vbuf = [(C.sb([128, 64 * 129], BF16), Buf()) for _ in range(2)]
    qz = [[(C.sb([128, NT], BF16), Buf()) for _ in range(2)] for _ in range(2)]
    for b_ in range(2):
        S.op("pool", lambda e, b_=b_: e.memset(qz[b_][0][0][64:128, :], 0.0), writes=[qz[b_][0][1]])
        S.op("pool", lambda e, b_=b_: e.memset(qz[b_][1][0][0:64, :], 0.0), writes=[qz[b_][1][1]])
    ostg = [(C.sb([128, 512], BF16), Buf()) for _ in range(2)]
    nload = 0
    nout = 0

    Eb = [(C.sb([128, 2, 512], BF16), Buf()) for _ in range(2)]
    ep = [(C.sb([128, 512], F32), Buf()) for _ in range(5)]
    ktd_all = [a.rearrange("(r x) n -> x r n", r=4) for a in T["KTd_all"]]
    vd_all = [a.rearrange("(r j t) (h c) -> t r j h c", r=4, j=4, h=4) for a in T["Vd_all"]]
    for H in range(4):
        kt, bkt = kbuf[nload % 2]; vv, bvv = vbuf[nload % 2]; qzz = qz[nload % 2]; nload += 1
        for r in range(4):
            S.dma("sp", kt[:, r, :], ktd_all[H // 2][(H % 2) * 128:(H % 2 + 1) * 128, r, :], reads=[T["b_KTd_all"][H // 2]], writes=[bkt])
            for vc in range(4):
                b0 = r * 16 + 4 * vc
                S.dma("sp", vv[:, b0 * 129:(b0 + 4) * 129].rearrange("p (j c) -> p j c", c=129), vd_all[vc][:, r, :, H, :],
                      reads=[T["b_Vd_all"][vc]], writes=[bvv])
        for m in range(2):
            S.dma("sp", qzz[m][0][m * 64:(m + 1) * 64, :], T["QT"][H * 128 + m * 64:H * 128 + (m + 1) * 64, :], reads=[T["b_QT"]], writes=[qzz[m][1]])
        v4 = vv[:].rearrange("p (b c) -> p b c", c=129)
        for J in range(4):
            its = tile_iters(J)
            q0 = J * 512
            for bk_i in (4, 5, 6, 7):
                S.op("dve", lambda e, bk_i=bk_i: e.memset(banks[bk_i][0], 0.0), writes=[banks[bk_i][1]])

            def stage1(it, slot):
                kb, r, j, i0, z = it
                w0 = i0 * 128
                for m in range(2):
                    ps, bps = banks[2 * slot + m]
                    S.op("pe", lambda e, ps=ps, m=m, r=r, j=j, w0=w0: e.matmul(
                        ps[:, w0:512], lhsT=kt[:, r, j * 128:(j + 1) * 128], rhs=qzz[m][0][:, q0 + w0:q0 + 512],
                        start=True, stop=(z is None)), reads=[bkt, qzz[m][1]], writes=[bps])
                    if z is not None:
                        S.op("pe", lambda e, ps=ps, z=z, w0=w0: e.matmul(ps[:, w0:512], lhsT=identb[:], rhs=nmI[:, z, w0:512], start=False, stop=True),
                             reads=[bident, bnmI], writes=[bps])

            def stage2(it, slot):
                kb, r, j, i0, z = it
                w0 = i0 * 128
                E, bE = Eb[slot]
                S.op("act", lambda e, E=E, slot=slot, w0=w0: e.activation(out=E[:, :, w0:512], in_=psall[:, 2 * slot:2 * slot + 2, w0:512], func=AF.Exp, scale=0.125),
                     reads=[banks[2 * slot][1], banks[2 * slot + 1][1]], writes=[bE])
                for m in range(2):
                    S.op("pe", lambda e, E=E, m=m, r=r, j=j, w0=w0: e.matmul(banks[4 + m][0][:, w0:512], lhsT=v4[:, r * 16 + j, 0:128], rhs=E[:, m, w0:512],
                                                                             start=False, stop=False, skip_group_check=True),
                         reads=[bE, bvv], writes=[banks[4 + m][1]])
                    S.op("pe", lambda e, E=E, m=m, w0=w0: e.matmul(banks[6 + m][0][:, w0:512], lhsT=onesb[:], rhs=E[:, m, w0:512],
                                                                   start=False, stop=False, skip_group_check=True),
                         reads=[bE, bonesb], writes=[banks[6 + m][1]])

            for n in range(len(its) + 1):
                if n < len(its):
                    stage1(its[n], n % 2)
                if n >= 1:
                    stage2(its[n - 1], (n - 1) % 2)
            (r1, br1), (t1, bt1), (t2, bt2), (od, bod), (sq, bsq) = ep
            S.op("dve", lambda e: e.reciprocal(out=r1[:], in_=banks[6][0]), reads=[banks[6][1]], writes=[br1])
            S.op("dve", lambda e: e.tensor_tensor(out=t1[:], in0=banks[4][0], in1=r1[:], op=ALU.mult), reads=[banks[4][1], br1], writes=[bt1])
            S.op("dve", lambda e: e.reciprocal(out=r1[:], in_=banks[7][0]), reads=[banks[7][1], bt1], writes=[br1])
            S.op("dve", lambda e: e.tensor_tensor(out=t2[:], in0=banks[5][0], in1=r1[:], op=ALU.mult), reads=[banks[5][1], br1], writes=[bt2])
            S.op("dve", lambda e: e.scalar_tensor_tensor(out=od[:], in0=t2[:], scalar=negl, in1=t1[:], op0=ALU.mult, op1=ALU.add),
                 reads=[bt1, bt2, bls], writes=[bod])
            S.op("act", lambda e: e.activation(out=sq[:], in_=od[:], func=AF.Square), reads=[bod], writes=[bsq])
            pss, bpss = banks[0]
            S.op("pe", lambda e: e.matmul(pss, lhsT=ones_f[:], rhs=sq[:], start=True, stop=True), reads=[bones, bsq], writes=[bpss])
            S.op("act", lambda e: e.activation(out=t1[:], in_=pss, func=AF.Ln, scale=1.0 / 128, bias=1e-5), reads=[bpss, bod], writes=[bt1])
            S.op("act", lambda e: e.activation(out=t1[:], in_=t1[:], func=AF.Exp, scale=-0.5), reads=[bt1], writes=[bt1])
            o_, bo_ = ostg[nout % 2]; nout += 1
            S.op("dve", lambda e, o_=o_: e.scalar_tensor_tensor(out=o_[:], in0=od[:], scalar=sl_[:, 0:1], in1=t1[:], op0=ALU.mult, op1=ALU.mult),
                 reads=[bod, bsl, bt1], writes=[bo_])
            S.dma("sp", T["mixT"][H * 128:(H + 1) * 128, q0:q0 + 512], o_[:], reads=[bo_], writes=[T["b_mixT"]])

    eb = [(C.sb([128, 2, 512], F32), Buf()) for _ in range(2)]
    spb = [(C.sb([128, 2, 512], F32), Buf()) for _ in range(2)]
    hib = [(C.sb([128, 2, 512], BF16), Buf()) for _ in range(2)]
    lob = [(C.sb([128, 2, 512], BF16), Buf()) for _ in range(2)]
    Ab = [(C.sb([128, 2, 512], BF16), Buf()) for _ in range(2)]
    fb = [(C.sb([128, 512], F32), Buf()) for _ in range(2)]
    OT = C.sb([128, 512], F32); bOT = Buf()
    kts_all = [a.rearrange("(r x) n -> x r n", r=4) for a in T["KTs_all"]]
    vs_all = [a.rearrange("(r j t) (pr c) -> t r j pr c", r=4, j=8, pr=4) for a in T["Vs_all"]]
    for pr in range(4):
        kt, bkt = kbuf[nload % 2]; vv, bvv = vbuf[nload % 2]; qzz = qz[nload % 2]; nload += 1
        for r in range(4):
            S.dma("sp", kt[:, r, :], kts_all[pr // 2][(pr % 2) * 128:(pr % 2 + 1) * 128, r, :], reads=[T["b_KTs_all"][pr // 2]], writes=[bkt])
            for vc in range(2):
                b0 = r * 16 + 8 * vc
                S.dma("sp", vv[:, b0 * 128:(b0 + 8) * 128].rearrange("p (j c) -> p j c", c=128), vs_all[vc][:, r, :, pr, :],
                      reads=[T["b_Vs_all"][vc]], writes=[bvv])
        for m in range(2):
            S.dma("sp", qzz[m][0][m * 64:(m + 1) * 64, :], T["QT"][512 + pr * 128 + m * 64:512 + pr * 128 + (m + 1) * 64, :], reads=[T["b_QT"]], writes=[qzz[m][1]])
        v4 = vv[:, 0:64 * 128].rearrange("p (b c) -> p b c", c=128)
        for J in range(4):
            its = tile_iters(J)
            q0 = J * 512
            S.op("pool", lambda e: e.memset(OT[:], 0.0), writes=[bOT])

            def stage1(it, slot):
                kb, r, j, i0, z = it
                w0 = i0 * 128
                ee, bee = eb[slot]; sp_, bsp = spb[slot]; hi, bhi = hib[slot]; lo, blo = lob[slot]
                for hh in range(2):
                    p0 = hh * 64
                    ps, bps = banks[2 * slot + hh]
                    S.op("pe", lambda e, ps=ps, r=r, j=j, w0=w0, hh=hh: e.matmul(
                        ps[:, w0:512], lhsT=kt[:, r, j * 128:(j + 1) * 128], rhs=qzz[hh][0][:, q0 + w0:q0 + 512],
                        start=True, stop=(z is None)), reads=[bkt, qzz[hh][1]], writes=[bps])
                    if z is not None:
                        S.op("pe", lambda e, ps=ps, z=z, w0=w0: e.matmul(ps[:, w0:512], lhsT=identb[:], rhs=nmS[:, z, w0:512], start=False, stop=True),
                             reads=[bident, bnmS], writes=[bps])
                zz = psall[:, 2 * slot:2 * slot + 2, w0:512]
                bz = [banks[2 * slot][1], banks[2 * slot + 1][1]]
                S.op("act", lambda e, ee=ee, zz=zz, w0=w0: e.activation(out=ee[:, :, w0:512], in_=zz, func=AF.Exp), reads=bz, writes=[bee])
                S.op("act", lambda e, ee=ee, sp_=sp_, w0=w0: e.activation(out=sp_[:, :, w0:512], in_=ee[:, :, w0:512], func=AF.Ln, bias=1.0),
                     reads=[bee], writes=[bsp])
                S.op("dve", lambda e, hi=hi, sp_=sp_, w0=w0: e.tensor_copy(out=hi[:, :, w0:512], in_=sp_[:, :, w0:512]), reads=[bsp], writes=[bhi])
                S.op("dve", lambda e, lo=lo, hi=hi, sp_=sp_, w0=w0: e.tensor_tensor(out=lo[:, :, w0:512], in0=sp_[:, :, w0:512], in1=hi[:, :, w0:512], op=ALU.subtract),
                     reads=[bsp, bhi], writes=[blo])

            def stage2(it, slot):
                kb, r, j, i0, z = it
                w0 = i0 * 128
                hi, bhi = hib[slot]; lo, blo = lob[slot]; A, bA = Ab[slot]; f, bf_ = fb[slot]
                bz = [banks[2 * slot][1], banks[2 * slot + 1][1]]
                for hh in range(2):
                    ps, bps = banks[2 * slot + hh]
                    S.op("pe", lambda e, ps=ps, hi=hi, hh=hh, w0=w0: e.matmul(ps[:, w0:512], lhsT=NU[:], rhs=hi[:, hh, w0:512], start=False, stop=False, skip_group_check=True),
                         reads=[bNU, bhi], writes=[bps])
                    S.op("pe", lambda e, ps=ps, lo=lo, hh=hh, w0=w0: e.matmul(ps[:, w0:512], lhsT=NU[:], rhs=lo[:, hh, w0:512], start=False, stop=True, skip_group_check=True),
                         reads=[bNU, blo], writes=[bps])
                zz = psall[:, 2 * slot:2 * slot + 2, w0:512]
                S.op("act", lambda e, A=A, zz=zz, w0=w0: e.activation(out=A[:, :, w0:512], in_=zz, func=AF.Exp), reads=bz, writes=[bA])
                pP, bpP = banks[4 + slot]
                pC, bpC = banks[6 + slot]
                for hh in range(2):
                    p0 = hh * 64
                    S.op("pe", lambda e, pP=pP, A=A, hh=hh, p0=p0, r=r, j=j, w0=w0: e.matmul(pP[p0:p0 + 64, w0:512], lhsT=v4[:, r * 16 + j, p0:p0 + 64], rhs=A[:, hh, w0:512],
                                                                                            start=True, stop=True),
                         reads=[bA, bvv], writes=[bpP])
                    S.op("pe", lambda e, pC=pC, hi=hi, hh=hh, p0=p0, w0=w0: e.matmul(pC[p0:p0 + 64, w0:512], lhsT=onesb[:, 0:64], rhs=hi[:, hh, w0:512], start=True, stop=False),
                         reads=[bhi, bonesb], writes=[bpC])
                    S.op("pe", lambda e, pC=pC, lo=lo, hh=hh, p0=p0, w0=w0: e.matmul(pC[p0:p0 + 64, w0:512], lhsT=onesb[:, 0:64], rhs=lo[:, hh, w0:512], start=False, stop=True),
                         reads=[blo, bonesb], writes=[bpC])
                S.op("act", lambda e, f=f, pC=pC, w0=w0: e.activation(out=f[:, w0:512], in_=pC[:, w0:512], func=AF.Exp, scale=-1.0), reads=[bpC], writes=[bf_])
                S.op("dve", lambda e, f=f, w0=w0: e.tensor_tensor(out=OT[:, w0:512], in0=OT[:, w0:512], in1=f[:, w0:512], op=ALU.mult), reads=[bOT, bf_], writes=[bOT])
                S.op("dve", lambda e, pP=pP, w0=w0: e.tensor_tensor(out=OT[:, w0:512], in0=pP[:, w0:512], in1=OT[:, w0:512], op=ALU.add), reads=[bOT, bpP], writes=[bOT])

            for n in range(len(its) + 1):
                if n < len(its):
                    stage1(its[n], n % 2)
                if n >= 1:
                    stage2(its[n - 1], (n - 1) % 2)
            o_, bo_ = ostg[nout % 2]; nout += 1
            S.op("act", lambda e, o_=o_: e.activation(out=o_[:], in_=OT[:], func=AF.Copy), reads=[bOT], writes=[bo_])
            S.dma("sp", T["mixT"][(4 + pr) * 128:(5 + pr) * 128, q0:q0 + 512], o_[:], reads=[bo_], writes=[T["b_mixT"]])
    C.close()


def phase_attn_out_ffn(nc, T, stage):
    C = Ctx(nc); S = C.S
    K = load_consts(C, T)
    banks = [(C.ps([128, 512], F32), Buf()) for _ in range(8)]
    hT = C.sb([128, 8, NT], F32); bh = Buf()
    xsrc = T["xT0"].rearrange("(k p) n -> p k n", p=128)
    for tt in range(4):
        S.dma("sp", hT[:, :, tt * 512:(tt + 1) * 512], xsrc[:, :, tt * 512:(tt + 1) * 512], writes=[bh])
    with contextlib.ExitStack() as st2:
        mT = st2.enter_context(nc.sbuf_tensor("mTf", [128, 8, NT], BF16)); bmT = Buf()
        wo = st2.enter_context(nc.sbuf_tensor("wo", [128, 8, D], BF16)); bwo = Buf()
        msrc = T["mixT"].rearrange("(c p) n -> p c n", p=128)
        for c in range(8):
            S.dma("act", mT[:, c, :], msrc[:, c, :], reads=[T["b_mixT"]], writes=[bmT])
            S.dma("pool", wo[:, c, :], T["w_out0"][c * 128:(c + 1) * 128, :], writes=[bwo])
        for tt in range(4):
            sl = slice(tt * 512, (tt + 1) * 512)
            for oc in range(8):
                po, bpo = banks[oc % 4]
                for c in range(8):
                    S.op("pe", lambda e, po=po, c=c, oc=oc, sl=sl: e.matmul(po[:], lhsT=wo[:, c, oc * 128:(oc + 1) * 128], rhs=mT[:, c, sl],
                                                                            start=(c == 0), stop=(c == 7)),
                         reads=[bwo, bmT], writes=[bpo])
                S.op("dve", lambda e, po=po, oc=oc, sl=sl: e.tensor_tensor(out=hT[:, oc, sl], in0=po[:], in1=hT[:, oc, sl], op=ALU.add),
                     reads=[bpo, bh], writes=[bh])
        if stage == "mix":
            dst = T["dbg"].rearrange("(k p) n -> p k n", p=128)
            for c in range(8):
                S.op("dve", lambda e, c=c: e.tensor_copy(out=hT[:, c, :], in_=mT[:, c, :]), reads=[bmT, bh], writes=[bh])
            for tt in range(4):
                S.dma("sp", dst[:, :, tt * 512:(tt + 1) * 512], hT[:, :, tt * 512:(tt + 1) * 512], reads=[bh])
            C.S.emit()
            st2.close(); C.st.close()
            return
        if stage == "attn":
            dst = T["dbg"].rearrange("(k p) n -> p k n", p=128)
            for tt in range(4):
                S.dma("sp", dst[:, :, tt * 512:(tt + 1) * 512], hT[:, :, tt * 512:(tt + 1) * 512], reads=[bh])
            C.S.emit()
            st2.close(); C.st.close()
            return
        C.S.emit()
    C.S = Sched(nc); S = C.S
    ffn_fm(C, T, 0, hT, bh, K["ones_f"], K["bones"], banks)
    if stage == "ffn0":
        dst = T["dbg"].rearrange("(k p) n -> p k n", p=128)
        for tt in range(4):
            S.dma("sp", dst[:, :, tt * 512:(tt + 1) * 512], hT[:, :, tt * 512:(tt + 1) * 512], reads=[bh])
        C.close()
        return
    zc = C.sb([128, 4], F32); bzc = Buf()
    S.op("pool", lambda e: e.memset(zc[:], 0.0), writes=[bzc])
    for ch in range(16):
        k, hf = ch // 2, ch % 2
        S.dma("sp", T["HX"][ch][:, 0:3], zc[0:64, 0:3], reads=[bzc], writes=[T["b_HX"][ch]])
        S.dma("sp", T["HX"][ch][:, 3:3 + NT], hT[hf * 64:(hf + 1) * 64, k, :], reads=[bh], writes=[T["b_HX"][ch]])
        collective(S, "AllGather", T["HX"][ch], T["HXA"][ch], reads=[T["b_HX"][ch]], writes=[T["b_HXA"][ch]])
    C.close()


def load_h1_contig(C, T, x1, bx1, halo):
    S = C.S
    off = 3 if halo else 0
    cache = {}

    def rank_of(e):
        if "c" not in cache:
            cache["c"] = e.partition_id() % 4
        return cache["c"]
    for ch in range(16):
        k, hf = ch // 2, ch % 2
        p0 = hf * 64
        for r in range(4):
            dstv = x1[p0:p0 + 64, k, off:off + NT].rearrange("p (m r t) -> p m r t", m=4, r=4)[:, :, r, :]

            def fn(e, dstv=dstv, ch=ch, r=r):
                c = rank_of(e)
                src = T["HXA"][ch][r * 64:(r + 1) * 64, bass.ds(c * 512 + 3, 512)]
                return e.dma_start(out=dstv, in_=src.rearrange("p (m t) -> p m t", m=4))
            S.dma("sp", None, None, reads=[T["b_HXA"][ch]], writes=[bx1], fn=fn)
        if halo:
            def fn2(e, ch=ch, p0=p0, k=k):
                c = rank_of(e)
                src = T["HXA"][ch][3 * 64:4 * 64, bass.ds(c * 512, 3)]
                return e.dma_start(out=x1[p0:p0 + 64, k, 0:3], in_=src)
            S.dma("sp", None, None, reads=[T["b_HXA"][ch]], writes=[bx1], fn=fn2)


def phase_ssd_inproj(nc, T):
    C = Ctx(nc); S = C.S
    K = load_consts(C, T)
    identb, bident = K["identb"], K["bident"]
    banks = [(C.ps([128, 512], F32), Buf()) for _ in range(8)]
    x1 = C.sb([128, 8, 3 + NT], F32); bx1 = Buf()
    load_h1_contig(C, T, x1, bx1, True)
    wn = C.sb([128, 8], F32); bwn = Buf()
    S.dma("sp", wn[:], T["ssd_norm"], writes=[bwn])
    xn = C.sb([128, 8, 3 + NT], BF16); bxn = Buf()
    rstd = C.sb([128, 3 + NT], F32); brstd = Buf()
    sqb = [(C.sb([128, 512], F32), Buf()) for _ in range(2)]
    slices = [slice(0, 3)] + [slice(3 + tt * 512, 3 + (tt + 1) * 512) for tt in range(4)]
    rmsnorm_fm(C, x1, bx1, wn, bwn, xn, bxn, K["ones_f"], K["bones"], banks[0:2], sqb=sqb, rstd=rstd, brstd=brstd, slices=slices)

    cw = C.sb([128, 32, 4], F32); bcw = Buf()
    cb = C.sb([128, 32], F32); bcb = Buf()
    S.dma("sp", cw[:], T["conv_w"], writes=[bcw])
    S.dma("sp", cb[:], T["conv_b"], writes=[bcb])
    wb = [(C.sb([128, 8, 512], BF16), Buf()) for _ in range(2)]
    wsrc = T["ssd_w_in"].rearrange("(k p) c -> p k c", p=128)
    ub = [(C.sb([128, 3 + NT], F32), Buf()) for _ in range(2)]
    accb = [(C.sb([128, NT], F32), Buf()) for _ in range(2)]
    xcb = [(C.sb([128, NT], BF16), Buf()) for _ in range(2)]
    ctmp = C.sb([128, NT], F32); bctmp = Buf()
    tst = [(C.sb([128, 8, 128], BF16), Buf()) for _ in range(2)]
    pTs = [(banks[6][0][:].bitcast(BF16), banks[6][1]), (banks[7][0][:].bitcast(BF16), banks[7][1])]
    nst = 0
    for g in range(8):
        w, bw = wb[g % 2]
        for k in range(8):
            S.dma("pool", w[:, k, :], wsrc[:, k, 2048 + g * 512:2048 + (g + 1) * 512], writes=[bw])
        for c4 in range(4):
            cc = g * 4 + c4
            u, bu = ub[cc % 2]; acc, bacc = accb[cc % 2]; xc, bxc = xcb[cc % 2]
            veng = "dve"
            for ti, sl in enumerate(slices):
                wd_ = sl.stop - sl.start
                p1, bp1 = banks[2 + ti % 4]
                for k in range(8):
                    S.op("pe", lambda e, p1=p1, w=w, k=k, c4=c4, sl=sl, wd_=wd_: e.matmul(p1[:, 0:wd_], lhsT=w[:, k, c4 * 128:(c4 + 1) * 128], rhs=xn[:, k, sl],
                                                                                         start=(k == 0), stop=(k == 7)),
                         reads=[bw, bxn], writes=[bp1])
                S.op("act", lambda e, u=u, p1=p1, sl=sl, wd_=wd_: e.activation(out=u[:, sl], in_=p1[:, 0:wd_], func=AF.Copy), reads=[bp1], writes=[bu])
            S.op(veng, lambda e, acc=acc, u=u, cc=cc: e.tensor_scalar(out=acc[:], in0=u[:, 0:NT], scalar1=cw[:, cc, 0:1], scalar2=None, op0=ALU.mult),
                 reads=[bu, bcw], writes=[bacc])
            for tap in range(1, 4):
                if veng == "dve":
                    S.op(veng, lambda e, acc=acc, u=u, cc=cc, tap=tap: e.scalar_tensor_tensor(out=acc[:], in0=u[:, tap:tap + NT], scalar=cw[:, cc, tap:tap + 1],
                                                                                             in1=acc[:], op0=ALU.mult, op1=ALU.add),
                         reads=[bu, bcw, bacc], writes=[bacc])
                else:
                    S.op(veng, lambda e, u=u, cc=cc, tap=tap: e.tensor_scalar(out=ctmp[:], in0=u[:, tap:tap + NT], scalar1=cw[:, cc, tap:tap + 1], scalar2=None, op0=ALU.mult),
                         reads=[bu, bcw], writes=[bctmp])
                    S.op(veng, lambda e, acc=acc: e.tensor_tensor(out=acc[:], in0=acc[:], in1=ctmp[:], op=ALU.add), reads=[bacc, bctmp], writes=[bacc])
            S.op("act", lambda e, xc=xc, acc=acc, cc=cc: e.activation(out=xc[:], in_=acc[:], func=AF.Silu, bias=cb[:, cc:cc + 1]),
                 reads=[bacc, bcb], writes=[bxc])
            if cc >= 16:
                nm = "BT" if cc < 24 else "CT"
                gi = cc - 16 if cc < 24 else cc - 24
                S.dma("sp", T[nm][gi * 128:(gi + 1) * 128, :], xc[:], reads=[bxc], writes=[T["b_" + nm]])
            if cc < 24:
                dname = "XS" if cc < 16 else "BTOK"
                col0 = cc * 128 if cc < 16 else (cc - 16) * 128
                for half in range(2):
                    pT, bpT = pTs[nst % 2]
                    st_, bst = tst[nst % 2]; nst += 1
                    for tb8 in range(8):
                        tb = half * 8 + tb8
                        S.op("pe", lambda e, pT=pT, xc=xc, tb=tb, tb8=tb8: e.transpose(pT[:, tb8 * 128:(tb8 + 1) * 128], xc[:, tb * 128:(tb + 1) * 128], identb[:]),
                             reads=[bxc, bident], writes=[bpT])
                    S.op("dve" if nst % 2 == 0 else "act", (lambda e, st_=st_, pT=pT: e.tensor_copy(out=st_[:], in_=pT.rearrange("p (b c) -> p b c", c=128)))
                         if nst % 2 == 0 else (lambda e, st_=st_, pT=pT: e.activation(out=st_[:], in_=pT.rearrange("p (b c) -> p b c", c=128), func=AF.Copy)),
                         reads=[bpT], writes=[bst])
                    dstv = T[dname][half * 1024:(half + 1) * 1024, col0:col0 + 128].rearrange("(b t) c -> t b c", t=128)
                    S.dma("sp", dstv, st_[:], reads=[bst], writes=[T["b_" + dname]])
    zst = [(C.sb([128, 512], F32), Buf()) for _ in range(2)]
    nz = 0
    for g in range(4):
        w, bw = wb[g % 2]
        for k in range(8):
            S.dma("pool", w[:, k, :], wsrc[:, k, g * 512:(g + 1) * 512], writes=[bw])
        for tb in range(16):
            p1, bp1 = banks[2 + tb % 4]
            for k in range(8):
                S.op("pe", lambda e, p1=p1, w=w, k=k, tb=tb: e.matmul(p1[:], lhsT=xn[:, k, 3 + tb * 128:3 + (tb + 1) * 128], rhs=w[:, k, :],
                                                                      start=(k == 0), stop=(k == 7)),
                     reads=[bw, bxn], writes=[bp1])
            z_, bz = zst[nz % 2]; nz += 1
            S.op("act", lambda e, z_=z_, p1=p1: e.activation(out=z_[:], in_=p1[:], func=AF.Silu), reads=[bp1], writes=[bz])
            S.dma("sp", T["ZS"][tb * 128:(tb + 1) * 128, g * 512:(g + 1) * 512], z_[:], reads=[bz], writes=[T["b_ZS"]])
    wdt = C.sb([128, 8, 32], BF16); bwdt = Buf()
    for k in range(8):
        S.dma("pool", wdt[:, k, :], wsrc[:, k, 6144:6176], writes=[bwdt])
    hv = C.sb([128, 3, 32], F32); bhv = Buf()
    S.dma("sp", hv[:, 0:1, :], T["dt_bias"].partition_broadcast(128), writes=[bhv])
    S.dma("sp", hv[:, 1:2, :], T["a_log"].partition_broadcast(128), writes=[bhv])
    S.op("act", lambda e: e.activation(out=hv[:, 2, :], in_=hv[:, 1, :], func=AF.Exp), reads=[bhv], writes=[bhv])
    dst_ = [(C.sb([128, 64], F32), Buf()) for _ in range(2)]
    for tb in range(16):
        p1, bp1 = banks[2 + tb % 4]
        for k in range(8):
            S.op("pe", lambda e, p1=p1, k=k, tb=tb: e.matmul(p1[:, 0:32], lhsT=xn[:, k, 3 + tb * 128:3 + (tb + 1) * 128], rhs=wdt[:, k, :],
                                                             start=(k == 0), stop=(k == 7)),
                 reads=[bwdt, bxn], writes=[bp1])
        d_, bd = dst_[tb % 2]
        S.op("dve", lambda e, d_=d_, p1=p1: e.tensor_tensor(out=d_[:, 0:32], in0=p1[:, 0:32], in1=hv[:, 0, :], op=ALU.add), reads=[bp1, bhv], writes=[bd])
        S.op("act", lambda e, d_=d_: e.activation(out=d_[:, 0:32], in_=d_[:, 0:32], func=AF.Exp), reads=[bd], writes=[bd])
        S.op("act", lambda e, d_=d_: e.activation(out=d_[:, 0:32], in_=d_[:, 0:32], func=AF.Ln, bias=1.0), reads=[bd], writes=[bd])
        S.op("dve", lambda e, d_=d_: e.scalar_tensor_tensor(out=d_[:, 32:64], in0=d_[:, 0:32], scalar=-1.0, in1=hv[:, 2, :], op0=ALU.mult, op1=ALU.mult),
             reads=[bd, bhv], writes=[bd])
        S.dma("sp", T["DTD"][tb * 128:(tb + 1) * 128, :], d_[:], reads=[bd], writes=[T["b_DTD"]])
    C.close()


def ssd_consts(C, T):
    S = C.S
    k = {}
    for nm in ("tri_incl", "tri_gt", "ntri_incl"):
        k[nm] = C.sb([128, 128], F32); k["b_" + nm] = Buf()
        S.dma("sp", k[nm][:], T[nm], writes=[k["b_" + nm]])
    return k


def phase_ssd_states(nc, T):
    C = Ctx(nc); S = C.S
    K = load_consts(C, T)
    K2 = ssd_consts(C, T)
    banks = [(C.ps([128, 512], F32), Buf()) for _ in range(8)]
    Sloc = C.sb([128, 32, 64], F32); bS = Buf()
    S.op("pool", lambda e: e.memset(Sloc[:], 0.0), writes=[bS])
    ldsum = C.sb([128, 32], F32); bld = Buf()
    S.op("pool", lambda e: e.memset(ldsum[:], 0.0), writes=[bld])
    xsb = [(C.sb([128, 32, 64], BF16), Buf()) for _ in range(2)]
    btb = [(C.sb([128, 1024], BF16), Buf()) for _ in range(2)]
    dtb = [(C.sb([128, 64], F32), Buf()) for _ in range(2)]
    smb = [(C.sb([128, 3, 32], F32), Buf()) for _ in range(2)]
    xwb = [(C.sb([128, 32, 64], BF16), Buf()) for _ in range(2)]
    stb = [(C.sb([128, 2048], F32), Buf()) for _ in range(2)]
    for c in range(16):
        xs, bxs = xsb[c % 2]; bt, bbt = btb[c % 2]; dt_, bdt = dtb[c % 2]; sm, bsm = smb[c % 2]; xw, bxw = xwb[c % 2]; st_, bst = stb[c % 2]
        rows = slice(c * 128, (c + 1) * 128)
        S.dma("sp", xs[:].rearrange("p h c -> p (h c)"), T["XS"][rows, :], reads=[T["b_XS"]], writes=[bxs])
        S.dma("act", bt[:], T["BTOK"][rows, :], reads=[T["b_BTOK"]], writes=[bbt])
        S.dma("sp", dt_[:], T["DTD"][rows, :], reads=[T["b_DTD"]], writes=[bdt])
        pa, bpa = banks[c % 2]
        S.op("pe", lambda e, pa=pa, dt_=dt_: e.matmul(pa[:, 0:32], lhsT=K2["tri_gt"][:], rhs=dt_[:, 32:64], start=True, stop=True),
             reads=[K2["b_tri_gt"], bdt], writes=[bpa])
        S.op("pe", lambda e, pa=pa, dt_=dt_: e.matmul(pa[:, 32:64], lhsT=K["ones_f"][:], rhs=dt_[:, 32:64], start=True, stop=True),
             reads=[K["bones"], bdt], writes=[bpa])
        S.op("act", lambda e, sm=sm, pa=pa: e.activation(out=sm[:, 0:2, :], in_=pa[:, 0:64].rearrange("p (a h) -> p a h", a=2), func=AF.Exp),
             reads=[bpa], writes=[bsm])
        S.op("dve", lambda e, sm=sm, dt_=dt_: e.tensor_tensor(out=sm[:, 2, :], in0=sm[:, 0, :], in1=dt_[:, 0:32], op=ALU.mult), reads=[bsm, bdt], writes=[bsm])
        S.op("dve", lambda e, xw=xw, xs=xs, sm=sm: e.tensor_tensor(out=xw[:], in0=xs[:], in1=sm[:, 2, :].unsqueeze(2).to_broadcast([128, 32, 64]), op=ALU.mult),
             reads=[bxs, bsm], writes=[bxw])
        xwf = xw[:].rearrange("p h c -> p (h c)")
        for g in range(8):
            ps_, bps = banks[2 + g // 2]
            S.op("pe", lambda e, ps_=ps_, bt=bt, g=g, xwf=xwf: e.matmul(ps_[:, (g % 2) * 256:(g % 2 + 1) * 256], lhsT=bt[:, g * 128:(g + 1) * 128],
                                                                        rhs=xwf[:, g * 256:(g + 1) * 256], start=True, stop=True),
                 reads=[bbt, bxw], writes=[bps])
        for bq in range(4):
            ps_, bps = banks[2 + bq]
            S.op("act" if bq % 2 == 0 else "dve",
                 (lambda e, st_=st_, ps_=ps_, bq=bq: e.activation(out=st_[:, bq * 512:(bq + 1) * 512], in_=ps_[:], func=AF.Copy)) if bq % 2 == 0 else
                 (lambda e, st_=st_, ps_=ps_, bq=bq: e.tensor_copy(out=st_[:, bq * 512:(bq + 1) * 512], in_=ps_[:])),
                 reads=[bps], writes=[bst])
        S.dma("sp", T["ST"][c], st_[:], reads=[bst], writes=[T["b_ST"]])
        S.op("pool", lambda e, sm=sm: e.tensor_tensor(out=Sloc[:], in0=Sloc[:], in1=sm[:, 1, :].unsqueeze(2).to_broadcast([128, 32, 64]), op=ALU.mult),
             reads=[bS, bsm], writes=[bS])
        S.op("pool", lambda e, st_=st_: e.tensor_tensor(out=Sloc[:].rearrange("p h c -> p (h c)"), in0=Sloc[:].rearrange("p h c -> p (h c)"), in1=st_[:], op=ALU.add),
             reads=[bS, bst], writes=[bS])
        S.op("dve", lambda e, pa=pa: e.tensor_tensor(out=ldsum[:], in0=pa[:, 32:64], in1=ldsum[:], op=ALU.add), reads=[bpa, bld], writes=[bld])
    S.dma("sp", T["SXa"], Sloc[:].rearrange("p h c -> p (h c)"), reads=[bS], writes=[T["b_SXa"]])
    S.dma("sp", T["SXb"], ldsum[:], reads=[bld], writes=[T["b_SXb"]])
    collective(S, "AllGather", T["SXa"], T["SXa_all"], reads=[T["b_SXa"]], writes=[T["b_SXa_all"]])
    collective(S, "AllGather", T["SXb"], T["SXb_all"], reads=[T["b_SXb"]], writes=[T["b_SXb_all"]])
    C.close()


def phase_ssd_scan(nc, T):
    C = Ctx(nc); S = C.S
    K = load_consts(C, T)
    K2 = ssd_consts(C, T)
    identb, bident = K["identb"], K["bident"]
    identf, bidentf = K["idf"], K["bidf"]
    ones_f, bones = K["ones_f"], K["bones"]
    banks = [(C.ps([128, 512], F32), Buf()) for _ in range(8)]
    prev = C.sb([128, 32, 64], F32); bprev = Buf()
    prevb = C.sb([128, 2048], BF16); bprevb = Buf()
    msk = C.sb([128, 20], F32); bmsk = Buf()
    S.dma("sp", msk[:], T["selmask"], writes=[bmsk])
    ld = C.sb([128, 4, 32], F32); bldg = Buf()
    S.dma("sp", ld[:], T["SXb_all"].rearrange("(r p) h -> p r h", p=128), reads=[T["b_SXb_all"]], writes=[bldg])
    coef = C.sb([128, 4, 32], F32); bcoef = Buf()
    for r in range(4):
        S.op("dve", lambda e, r=r: e.tensor_scalar(out=coef[:, r, :], in0=ld[:, 0, :], scalar1=msk[:, 4 + 4 * r:5 + 4 * r], scalar2=None, op0=ALU.mult),
             reads=[bldg, bmsk], writes=[bcoef])
        for r2 in range(1, 4):
            S.op("dve", lambda e, r=r, r2=r2: e.scalar_tensor_tensor(out=coef[:, r, :], in0=ld[:, r2, :], scalar=msk[:, 4 + 4 * r + r2:5 + 4 * r + r2],
                                                                     in1=coef[:, r, :], op0=ALU.mult, op1=ALU.add),
                 reads=[bldg, bmsk, bcoef], writes=[bcoef])
        S.op("act", lambda e, r=r: e.activation(out=coef[:, r, :], in_=coef[:, r, :], func=AF.Exp), reads=[bcoef], writes=[bcoef])
        S.op("dve", lambda e, r=r: e.tensor_scalar(out=coef[:, r, :], in0=coef[:, r, :], scalar1=msk[:, r:r + 1], scalar2=None, op0=ALU.mult),
             reads=[bcoef, bmsk], writes=[bcoef])
    S.op("pool", lambda e: e.memset(prev[:], 0.0), writes=[bprev])
    sg_ = [(C.sb([128, 32, 64], F32), Buf()) for _ in range(2)]
    for r in range(4):
        t_, bt_ = sg_[r % 2]
        S.dma("sp", t_[:].rearrange("p h c -> p (h c)"), T["SXa_all"][r * 128:(r + 1) * 128, :], reads=[T["b_SXa_all"]], writes=[bt_])
        S.op("dve", lambda e, t_=t_, r=r: e.tensor_tensor(out=t_[:], in0=t_[:], in1=coef[:, r, :].unsqueeze(2).to_broadcast([128, 32, 64]), op=ALU.mult),
             reads=[bt_, bcoef], writes=[bt_])
        S.op("dve", lambda e, t_=t_: e.tensor_tensor(out=prev[:], in0=prev[:], in1=t_[:], op=ALU.add), reads=[bt_, bprev], writes=[bprev])
    prevf = prev[:].rearrange("p h c -> p (h c)")
    S.op("act", lambda e: e.activation(out=prevb[:], in_=prevf, func=AF.Copy), reads=[bprev], writes=[bprevb])
    hv = C.sb([128, 32], F32); bhv = Buf()
    S.dma("sp", hv[:].unsqueeze(1), T["ssd_d"].partition_broadcast(128), writes=[bhv])
    gw = C.sb([128, 16], F32); bgw = Buf()
    S.dma("sp", gw[:], T["gnorm"], writes=[bgw])
    negut = C.sb([128, 4, 128], F32); bneg = Buf()
    negutb = C.sb([128, 4, 128], BF16); bnegb = Buf()
    for r in range(4):
        S.dma("sp", negut[:, r, :], T["neg_ut"], writes=[bneg])
    S.op("dve", lambda e: e.tensor_copy(out=negutb[:], in_=negut[:]), reads=[bneg], writes=[bnegb])
    xsb = [(C.sb([128, 32, 64], BF16), Buf()) for _ in range(2)]
    dtb = [(C.sb([128, 64], F32), Buf()) for _ in range(2)]
    btb = [(C.sb([128, 8, 128], BF16), Buf()) for _ in range(2)]
    ctb = [(C.sb([128, 8, 128], BF16), Buf()) for _ in range(2)]
    zsb = [(C.sb([128, 2048], F32), Buf()) for _ in range(2)]
    stb = [(C.sb([128, 2048], F32), Buf()) for _ in range(2)]
    dth = C.sb([128, 2, 32], BF16); bdth = Buf()
    trib = C.sb([128, 128], BF16); ntrib = C.sb([128, 128], BF16); onesb = C.sb([128, 128], BF16); btb_ = Buf()
    S.op("dve", lambda e: e.tensor_copy(out=trib[:], in_=K2["tri_incl"][:]), reads=[K2["b_tri_incl"]], writes=[btb_])
    S.op("dve", lambda e: e.tensor_copy(out=ntrib[:], in_=K2["ntri_incl"][:]), reads=[K2["b_ntri_incl"]], writes=[btb_])
    S.op("dve", lambda e: e.memset(onesb[:], 1.0), writes=[btb_])
    smb = [(C.sb([128, 3, 32], F32), Buf()) for _ in range(2)]
    xdt = C.sb([128, 32, 64], BF16); bxdt = Buf()
    Dmb = [(C.sb([128, 4, 128], F32), Buf()) for _ in range(2)]
    MTb = [(C.sb([128, 4, 128], BF16), Buf()) for _ in range(2)]
    tyb = [(C.sb([128, 4, 64], F32), Buf()) for _ in range(2)]
    yc = C.sb([128, 8, 256], F32); byc = Buf()
    gy = C.sb([128, 8, 256], F32); bgy = Buf()
    ynb = C.sb([128, 2048], BF16); byn = Buf()
    nrm = C.sb([128, 3, 8], F32); bnrm = Buf()
    junk = C.sb([128, 256], F32); bjunk = Buf()
    yTs = [(C.sb([128, 16, 128], BF16), Buf()) for _ in range(2)]
    bt_src = T["BT"].rearrange("(g n) t -> n g t", n=128)
    ct_src = T["CT"].rearrange("(g n) t -> n g t", n=128)
    yt_dst = T["YT"].rearrange("(cc p) t -> p cc t", p=128)
    pT0 = banks[6][0][:].bitcast(BF16); pT1 = banks[7][0][:].bitcast(BF16)
    half_bufs = [[Buf(), Buf()] for _ in range(3)]
    for c in range(16):
        xs, bxs = xsb[c % 2]; dt_, bdt = dtb[c % 2]; bt, bbt = btb[c % 2]; ct, bct = ctb[c % 2]
        zs, bzs = zsb[c % 2]; st_, bst = stb[c % 2]; sm, bsm = smb[c % 2]
        rows = slice(c * 128, (c + 1) * 128)
        S.dma("sp", xs[:].rearrange("p h c -> p (h c)"), T["XS"][rows, :], reads=[T["b_XS"]], writes=[bxs])
        S.dma("sp", dt_[:], T["DTD"][rows, :], reads=[T["b_DTD"]], writes=[bdt])
        S.dma("act", bt[:], bt_src[:, :, rows], reads=[T["b_BT"]], writes=[bbt])
        S.dma("act", ct[:], ct_src[:, :, rows], reads=[T["b_CT"]], writes=[bct])
        S.dma("sp", zs[:], T["ZS"][rows, :], reads=[T["b_ZS"]], writes=[bzs])
        S.dma("sp", st_[:], T["ST"][c], reads=[T["b_ST"]], writes=[bst])
        pa, bpa = banks[0]
        S.op("pe", lambda e, dt_=dt_: e.matmul(pa[:, 0:32], lhsT=K2["tri_incl"][:], rhs=dt_[:, 32:64], start=True, stop=True),
             reads=[K2["b_tri_incl"], bdt], writes=[bpa])
        S.op("pe", lambda e, dt_=dt_: e.matmul(pa[:, 32:64], lhsT=ones_f[:], rhs=dt_[:, 32:64], start=True, stop=True),
             reads=[bones, bdt], writes=[bpa])
        S.op("act", lambda e, sm=sm: e.activation(out=sm[:, 0:2, :], in_=pa[:, 0:64].rearrange("p (a h) -> p a h", a=2), func=AF.Exp),
             reads=[bpa], writes=[bsm])
        S.op("dve", lambda e, dt_=dt_: e.tensor_copy(out=dth[:, 0, :], in_=dt_[:, 32:64]), reads=[bdt], writes=[bdth])
        S.op("dve", lambda e, dt_=dt_: e.tensor_tensor(out=dth[:, 1, :], in0=dt_[:, 32:64], in1=dth[:, 0, :], op=ALU.subtract), reads=[bdt, bdth], writes=[bdth])
        S.op("dve", lambda e, xs=xs, dt_=dt_: e.tensor_tensor(out=xdt[:], in0=xs[:], in1=dt_[:, 0:32].unsqueeze(2).to_broadcast([128, 32, 64]), op=ALU.mult),
             reads=[bxs, bdt], writes=[bxdt])
        for g in range(8):
            pseg, bpseg = banks[1 + g % 2]
            Dm, bDm = Dmb[g % 2]; MT, bMT = MTb[g % 2]; ty, bty = tyb[g % 2]
            hs = slice(4 * g, 4 * g + 4)
            psv = pseg[:].rearrange("p (h l) -> p h l", h=4)
            for x_ in range(2):
                S.op("pe", lambda e, psv=psv, x_=x_, hs=hs: e.matmul(psv, lhsT=ntrib[:], rhs=dth[:, x_, hs].unsqueeze(2).to_broadcast([128, 4, 128]),
                                                                     start=(x_ == 0), stop=False),
                     reads=[btb_, bdth], writes=[bpseg])
            for r in range(4):
                for x_ in range(2):
                    S.op("pe", lambda e, pseg=pseg, x_=x_, r=r, g=g: e.matmul(pseg[:, r * 128:(r + 1) * 128], lhsT=dth[:, x_, 4 * g + r:4 * g + r + 1].to_broadcast([128, 128]),
                                                                              rhs=trib[:], start=False, stop=False),
                         reads=[btb_, bdth], writes=[bpseg])
            S.op("pe", lambda e, pseg=pseg: e.matmul(pseg[:], lhsT=identb[:], rhs=negutb[:].rearrange("p h l -> p (h l)"), start=False, stop=True),
                 reads=[bident, bnegb], writes=[bpseg])
            S.op("act", lambda e, Dm=Dm, pseg=pseg: e.activation(out=Dm[:].rearrange("p h l -> p (h l)"), in_=pseg[:], func=AF.Exp), reads=[bpseg], writes=[bDm])
            pg = banks[3][0]; bpg = half_bufs[0][g % 2]
            gsl = slice((g % 2) * 128, (g % 2 + 1) * 128)
            S.op("pe", lambda e, bt=bt, ct=ct, g=g, gsl=gsl: e.matmul(pg[:, gsl], lhsT=bt[:, g, :], rhs=ct[:, g, :], start=True, stop=True),
                 reads=[bbt, bct], writes=[bpg])
            S.op("dve", lambda e, MT=MT, Dm=Dm, gsl=gsl: e.tensor_tensor(out=MT[:], in0=Dm[:], in1=pg[:, gsl].unsqueeze(1).to_broadcast([128, 4, 128]), op=ALU.mult),
                 reads=[bDm, bpg], writes=[bMT])
            pyd = banks[4][0]; bpyd = half_bufs[1][g % 2]
            pyo = banks[5][0]; bpyo = half_bufs[2][g % 2]
            ysl = slice((g % 2) * 256, (g % 2 + 1) * 256)
            for r in range(4):
                S.op("pe", lambda e, MT=MT, r=r, g=g, ysl=ysl: e.matmul(pyd[:, ysl.start + r * 64:ysl.start + (r + 1) * 64], lhsT=MT[:, r, :], rhs=xdt[:, 4 * g + r, :],
                                                                        start=True, stop=True),
                     reads=[bMT, bxdt], writes=[bpyd])
            S.op("pe", lambda e, ct=ct, g=g, ysl=ysl: e.matmul(pyo[:, ysl], lhsT=ct[:, g, :], rhs=prevb[:, g * 256:(g + 1) * 256], start=True, stop=True),
                 reads=[bct, bprevb], writes=[bpyo])
            S.op("dve", lambda e, ty=ty, sm=sm, hs=hs, ysl=ysl: e.tensor_tensor(out=ty[:], in0=pyo[:, ysl].rearrange("p (r c) -> p r c", r=4),
                                                                                in1=sm[:, 0, hs].unsqueeze(2).to_broadcast([128, 4, 64]), op=ALU.mult),
                 reads=[bpyo, bsm], writes=[bty])
            S.op("dve", lambda e, ty=ty, g=g, ysl=ysl: e.tensor_tensor(out=yc[:, g, :], in0=pyd[:, ysl], in1=ty[:].rearrange("p r c -> p (r c)"), op=ALU.add),
                 reads=[bpyd, bty], writes=[byc])
        ycv = yc[:].rearrange("p g (r c) -> p (g r) c", r=4)
        S.op("pool", lambda e, xs=xs: e.tensor_tensor(out=gy[:].rearrange("p g (r c) -> p (g r) c", r=4), in0=xs[:], in1=hv[:].unsqueeze(2).to_broadcast([128, 32, 64]), op=ALU.mult),
             reads=[bxs, bhv], writes=[bgy])
        S.op("pool", lambda e: e.tensor_tensor(out=yc[:], in0=yc[:], in1=gy[:], op=ALU.add), reads=[byc, bgy], writes=[byc])
        S.op("dve", lambda e, zs=zs: e.tensor_tensor(out=gy[:], in0=yc[:], in1=zs[:].rearrange("p (g c) -> p g c", g=8), op=ALU.mult), reads=[byc, bzs, bgy], writes=[bgy])
        for g in range(8):
            S.op("act", lambda e, g=g: e.activation(out=junk[:], in_=gy[:, g, :], func=AF.Square, accum_out=nrm[:, 0, g:g + 1]), reads=[bgy], writes=[bjunk, bnrm])
        S.op("act", lambda e: e.activation(out=nrm[:, 1, :], in_=nrm[:, 0, :], func=AF.Ln, scale=1.0 / 256, bias=1e-5), reads=[bnrm], writes=[bnrm])
        S.op("act", lambda e: e.activation(out=nrm[:, 2, :], in_=nrm[:, 1, :], func=AF.Exp, scale=-0.5), reads=[bnrm], writes=[bnrm])
        S.op("dve", lambda e: e.tensor_tensor(out=ynb[:].rearrange("p (g c) -> p g c", g=8), in0=gy[:], in1=nrm[:, 2, :].unsqueeze(2).to_broadcast([128, 8, 256]), op=ALU.mult),
             reads=[bgy, bnrm], writes=[byn])
        yT, byT = yTs[c % 2]
        for cc in range(16):
            pT = pT0 if cc < 8 else pT1
            S.op("pe", lambda e, pT=pT, cc=cc: e.transpose(pT[:, (cc % 8) * 128:(cc % 8 + 1) * 128], ynb[:, cc * 128:(cc + 1) * 128], identb[:]),
                 reads=[byn, bident], writes=[banks[6][1] if cc < 8 else banks[7][1]])
        S.op("dve", lambda e, yT=yT: e.tensor_tensor(out=yT[:, 0:8, :], in0=pT0.rearrange("p (b c) -> p b c", c=128), in1=gw[:, 0:8].unsqueeze(2).to_broadcast([128, 8, 128]), op=ALU.mult),
             reads=[banks[6][1], bgw], writes=[byT])
        S.op("dve", lambda e, yT=yT: e.tensor_tensor(out=yT[:, 8:16, :], in0=pT1.rearrange("p (b c) -> p b c", c=128), in1=gw[:, 8:16].unsqueeze(2).to_broadcast([128, 8, 128]), op=ALU.mult),
             reads=[banks[7][1], bgw], writes=[byT])
        S.dma("sp", yt_dst[:, :, rows], yT[:], reads=[byT], writes=[T["b_YT"]])
        S.op("pool", lambda e, sm=sm: e.tensor_tensor(out=prev[:], in0=prev[:], in1=sm[:, 1, :].unsqueeze(2).to_broadcast([128, 32, 64]), op=ALU.mult),
             reads=[bprev, bsm], writes=[bprev])
        S.op("pool", lambda e, st_=st_: e.tensor_tensor(out=prevf, in0=prevf, in1=st_[:], op=ALU.add), reads=[bprev, bst], writes=[bprev])
        S.op("act", lambda e: e.activation(out=prevb[:], in_=prevf, func=AF.Copy), reads=[bprev, bprevb], writes=[bprevb])
    C.close()


def phase_ssd_out_ffn(nc, T, stage):
    C = Ctx(nc); S = C.S
    K = load_consts(C, T)
    banks = [(C.ps([128, 512], F32), Buf()) for _ in range(8)]
    hT = C.sb([128, 8, NT], F32); bh = Buf()
    load_h1_contig(C, T, hT, bh, False)
    dst = T["dbg"].rearrange("(k p) n -> p k n", p=128)
    with contextlib.ExitStack() as st2:
        yT = st2.enter_context(nc.sbuf_tensor("yTf", [128, 16, NT], BF16)); byT = Buf()
        wo = st2.enter_context(nc.sbuf_tensor("wo1", [128, 16, D], BF16)); bwo = Buf()
        ysrc = T["YT"].rearrange("(cc p) t -> p cc t", p=128)
        for cc in range(16):
            S.dma("act", yT[:, cc, :], ysrc[:, cc, :], reads=[T["b_YT"]], writes=[byT])
            S.dma("pool", wo[:, cc, :], T["ssd_w_out"][cc * 128:(cc + 1) * 128, :], writes=[bwo])
        for tt in range(4):
            sl = slice(tt * 512, (tt + 1) * 512)
            for oc in range(8):
                po, bpo = banks[oc % 4]
                for cc in range(16):
                    S.op("pe", lambda e, po=po, cc=cc, oc=oc, sl=sl: e.matmul(po[:], lhsT=wo[:, cc, oc * 128:(oc + 1) * 128], rhs=yT[:, cc, sl],
                                                                              start=(cc == 0), stop=(cc == 15)),
                         reads=[bwo, byT], writes=[bpo])
                S.op("dve", lambda e, po=po, oc=oc, sl=sl: e.tensor_tensor(out=hT[:, oc, sl], in0=po[:], in1=hT[:, oc, sl], op=ALU.add),
                     reads=[bpo, bh], writes=[bh])
        if stage == "ssd":
            for tt in range(4):
                S.dma("sp", dst[:, :, tt * 512:(tt + 1) * 512], hT[:, :, tt * 512:(tt + 1) * 512], reads=[bh])
            C.S.emit()
            st2.close(); C.st.close()
            return
        C.S.emit()
    C.S = Sched(nc); S = C.S
    Cf = Ctx(nc); Cf.S = C.S
    ffn_fm(Cf, T, 1, hT, bh, K["ones_f"], K["bones"], banks)
    C.S.emit()
    Cf.st.close()
    C.S = Sched(nc); S = C.S
    if stage != "ffn1":
        fw = C.sb([128, 8], F32); bfw = Buf()
        S.dma("sp", fw[:], T["final_norm"], writes=[bfw])
        rstd = C.sb([128, NT], F32); brstd = Buf()
        sq = [(C.sb([128, 512], F32), Buf()) for _ in range(2)]
        for tt in range(4):
            sl = slice(tt * 512, (tt + 1) * 512)
            ps, bps = banks[tt % 2]
            for k in range(8):
                q_, bq_ = sq[k % 2]
                S.op("act", lambda e, q_=q_, k=k, sl=sl: e.activation(out=q_[:], in_=hT[:, k, sl], func=AF.Square), reads=[bh], writes=[bq_])
                S.op("pe", lambda e, ps=ps, q_=q_, k=k: e.matmul(ps[:], lhsT=K["ones_f"][:], rhs=q_[:], start=(k == 0), stop=(k == 7)),
                     reads=[bq_, K["bones"]], writes=[bps])
            S.op("act", lambda e, ps=ps, sl=sl: e.activation(out=rstd[:, sl], in_=ps[:], func=AF.Ln, scale=1.0 / D, bias=1e-6), reads=[bps], writes=[brstd])
            S.op("act", lambda e, sl=sl: e.activation(out=rstd[:, sl], in_=rstd[:, sl], func=AF.Exp, scale=-0.5), reads=[brstd], writes=[brstd])
            for k in range(8):
                S.op("dve", lambda e, k=k, sl=sl: e.scalar_tensor_tensor(out=hT[:, k, sl], in0=hT[:, k, sl], scalar=fw[:, k:k + 1], in1=rstd[:, sl],
                                                                         op0=ALU.mult, op1=ALU.mult),
                     reads=[bh, bfw, brstd], writes=[bh])
    for tt in range(4):
        S.dma("sp", dst[:, :, tt * 512:(tt + 1) * 512], hT[:, :, tt * 512:(tt + 1) * 512], reads=[bh])
    C.close()


def build_program(stage):
    Buf.ALL = []
    nc = bass.Bass("TRN2", target_bir_lowering=False)
    Sched.STATE = SemState(nc)
    T = {}

    def inp(name, shape, dt=F32):
        T[name] = nc.dram_tensor(name, list(shape), dt, kind="ExternalInput").ap()

    def scr(name, shape, dt):
        T[name] = nc.dram_tensor(name, list(shape), dt).ap()
        T["b_" + name] = Buf(name)

    inp("xT0", [D, NT]); inp("cs_tab", [128, 2, NT]); inp("attn_norm", [128, 8])
    inp("w_in0", [D, 3072]); inp("w_sw0", [D, 1024]); inp("w_out0", [D, D])
    inp("nm_incl", [128, 16, 512]); inp("nm_strict", [128, 16, 512])
    inp("ident", [128, 128]); inp("nu", [128, 128])
    for nm in ("lq1", "lk1", "lq2", "lk2"):
        inp(nm, [1, 64])
    inp("subln", [1, 128])
    inp("ffn_norm", [2, 128, 8]); inp("ffn_wg", [2, D, D_FF]); inp("ffn_wu", [2, D, D_FF]); inp("ffn_wd", [2, D_FF, D])
    scr("QT", [1536, NT], BF16)
    def scr_list(name, n, shape, dt):
        T[name] = [nc.dram_tensor("%s_%d" % (name, i), list(shape), dt).ap() for i in range(n)]
        T["b_" + name] = [Buf(name) for _ in range(n)]

    scr_list("KTd", 4, [256, NT], BF16); scr_list("KTd_all", 4, [1024, NT], BF16)
    scr_list("KTs", 2, [256, NT], BF16); scr_list("KTs_all", 2, [1024, NT], BF16)
    scr_list("Vd", 4, [512, 516], BF16); scr_list("Vd_all", 4, [2048, 516], BF16)
    scr_list("Vs", 2, [1024, 512], BF16); scr_list("Vs_all", 2, [4096, 512], BF16)
    scr("mixT", [D, NT], BF16)
    scr_list("HX", 16, [64, 3 + NT], F32); scr_list("HXA", 16, [256, 3 + NT], F32)
    inp("ssd_norm", [128, 8]); inp("ssd_w_in", [D, 6176]); inp("conv_w", [128, 32, 4]); inp("conv_b", [128, 32])
    inp("dt_bias", [1, 32]); inp("a_log", [1, 32]); inp("ssd_d", [1, 32]); inp("gnorm", [128, 16]); inp("ssd_w_out", [2048, D])
    inp("final_norm", [128, 8]); inp("selmask", [128, 20])
    inp("tri_incl", [128, 128]); inp("tri_gt", [128, 128]); inp("ntri_incl", [128, 128]); inp("neg_ut", [128, 128])
    scr("XS", [NT, 2048], BF16); scr("BTOK", [NT, 1024], BF16); scr("BT", [1024, NT], BF16); scr("CT", [1024, NT], BF16)
    scr("ZS", [NT, 2048], F32); scr("DTD", [NT, 64], F32)
    T["ST"] = [nc.dram_tensor("ST_%d" % i, [128, 2048], F32).ap() for i in range(16)]; T["b_ST"] = Buf("ST")
    scr("SXa", [128, 2048], F32); scr("SXa_all", [512, 2048], F32); scr("SXb", [128, 32], F32); scr("SXb_all", [512, 32], F32)
    scr("YT", [2048, NT], BF16)
    T["dbg"] = nc.dram_tensor("dbg", [D, NT], F32, kind="ExternalOutput").ap()

    nph = int(os.environ.get("KPH", "99"))
    phase_attn_inproj(nc, T)
    if nph >= 2:
        phase_attn_core(nc, T)
    if nph >= 3:
        phase_attn_out_ffn(nc, T, stage)
    if stage not in ("mix", "attn", "ffn0"):
        if nph >= 4:
            phase_ssd_inproj(nc, T)
        if nph >= 5:
            phase_ssd_states(nc, T)
        if nph >= 6:
            phase_ssd_scan(nc, T)
        if nph >= 7:
            phase_ssd_out_ffn(nc, T, stage)
    Sched.STATE.close()
    return nc


def rope_tables(q):
    half = 32
    inv_freq = (10000.0 ** (-np.arange(half, dtype=np.float32) / half)).astype(np.float32)
    n = np.arange(NT)
    pos = ((4 * (n // 128) + q) * 128 + (n % 128)).astype(np.float32)
    ang = pos[None, :] * inv_freq[:, None]
    cos = np.cos(ang).astype(np.float32)
    sin = np.sin(ang).astype(np.float32)
    tab = np.zeros((128, 2, NT), np.float32)
    for p in range(128):
        d = p % 64
        tab[p, 0] = cos[d % 32]
        tab[p, 1] = -sin[d % 32] if d < 32 else sin[d % 32]
    return tab


def neg_masks(q):
    jj = np.arange(128)[:, None]
    tt = np.arange(128)[None, :]
    out = []
    for strict in (False, True):
        keep_diag = (jj < tt) if strict else (jj <= tt)
        m = np.zeros((128, 16, 512), np.float32)
        for kbz in range(16):
            for i in range(4):
                z = kbz - 4 * i
                blk = m[:, kbz, i * 128:(i + 1) * 128]
                if z < 0:
                    continue
                if z > 3 or z > q:
                    blk[:] = NEG
                elif z == q:
                    blk[:] = np.where(keep_diag, 0.0, NEG)
        out.append(m)
    return out


def make_in_maps(inputs):
    f = lambda a: np.ascontiguousarray(np.asarray(a, dtype=np.float32))
    x = f(inputs["x"])
    w_in = f(inputs["attn_w_in"][0])
    qk = w_in[:, :1024].reshape(D, 16, 2, 32)
    w_sw = np.ascontiguousarray(qk[:, :, ::-1, :].reshape(D, 1024))
    ar = np.arange(128)
    common = {
        "w_in0": w_in, "w_sw0": w_sw, "w_out0": f(inputs["attn_w_out"][0]),
        "attn_norm": f(inputs["attn_norm"][0].reshape(8, 128).T),
        "ident": np.eye(128, dtype=np.float32),
        "nu": -(np.arange(128)[:, None] >= np.arange(128)[None, :]).astype(np.float32),
        "lq1": f(inputs["diff_lq1"]), "lk1": f(inputs["diff_lk1"]), "lq2": f(inputs["diff_lq2"]), "lk2": f(inputs["diff_lk2"]),
        "subln": f(inputs["diff_subln"]),
        "ffn_norm": f(np.stack([inputs["ffn_norm"][l].reshape(8, 128).T for l in range(2)])),
        "ffn_wg": f(inputs["ffn_w_gate"]), "ffn_wu": f(inputs["ffn_w_up"]), "ffn_wd": f(inputs["ffn_w_down"]),
        "ssd_norm": f(inputs["ssd_norm"][0].reshape(8, 128).T), "ssd_w_in": f(inputs["ssd_w_in"][0]),
        "conv_w": f(inputs["ssd_conv_w"][0].reshape(4, 32, 128).transpose(2, 1, 0)),
        "conv_b": f(inputs["ssd_conv_b"][0].reshape(32, 128).T),
        "dt_bias": f(inputs["ssd_dt_bias"]), "a_log": f(inputs["ssd_a_log"]), "ssd_d": f(inputs["ssd_d"]),
        "gnorm": f(inputs["ssd_gnorm"][0].reshape(16, 128).T), "ssd_w_out": f(inputs["ssd_w_out"][0]),
        "final_norm": f(inputs["final_norm"].reshape(8, 128).T),
        "tri_incl": (ar[:, None] <= ar[None, :]).astype(np.float32),
        "tri_gt": (ar[:, None] > ar[None, :]).astype(np.float32),
        "ntri_incl": -(ar[:, None] <= ar[None, :]).astype(np.float32),
        "neg_ut": np.where(ar[None, :] < ar[:, None], NEG, 0.0).astype(np.float32),
    }
    maps = []
    for c in range(8):
        b, q = c // 4, c % 4
        n = np.arange(NT)
        pos = (4 * (n // 128) + q) * 128 + (n % 128)
        m = dict(common)
        m["xT0"] = np.ascontiguousarray(x[b, pos, :].T)
        m["cs_tab"] = rope_tables(q)
        mi, ms = neg_masks(q)
        m["nm_incl"], m["nm_strict"] = mi, ms
        sel = np.zeros((128, 20), np.float32)
        for r in range(4):
            sel[:, r] = 1.0 if r < q else 0.0
            for r2 in range(4):
                sel[:, 4 + 4 * r + r2] = 1.0 if (r < r2 < q) else 0.0
        m["selmask"] = sel
        maps.append(m)
    return maps


def kernel(**inputs):
    stage = os.environ.get("KSTAGE", "final")
    nc = build_program(stage)
    maps = make_in_maps(inputs)
    res = run_bass_kernel_spmd(nc, maps, core_ids=list(range(8)))
    out = np.zeros((2, SEQ, D), np.float32)
    for c in range(8):
        b, q = c // 4, c % 4
        o = res.results[c]["dbg"]
        if stage in ("mix", "attn", "ffn0"):
            n = np.arange(NT)
            pos = (4 * (n // 128) + q) * 128 + (n % 128)
            out[b, pos, :] = o.T
        else:
            out[b, q * NT:(q + 1) * NT, :] = o.T
    return out
```

```python
import contextlib
import math
import os

import numpy as np
import concourse.bass as bass
import concourse.mybir as mybir
from concourse.bass_utils import run_bass_kernel_spmd

F32 = mybir.dt.float32
BF16 = mybir.dt.bfloat16
AF = mybir.ActivationFunctionType
ALU = mybir.AluOpType

D = 1024
SEQ = 8192
NT = 2048
NEG = -30000.0
GROUPS = [[0, 1, 2, 3], [4, 5, 6, 7]]
D_FF = 2816
FH = D_FF // 2


class Buf:
    __slots__ = ("name", "last_w", "rd_c", "rd_d")
    ALL = []

    def __init__(self, name=""):
        self.name = name
        self.last_w = None
        self.rd_c = {}
        self.rd_d = []
        Buf.ALL.append(self)

    @staticmethod
    def reset_all():
        for b in Buf.ALL:
            b.last_w = None
            b.rd_c = {}
            b.rd_d = []


class _Rec:
    def __init__(self):
        self.call = None

    def __getattr__(self, name):
        def f(*a, **k):
            self.call = (name, a, k)
            return None
        return f


def _freeze(fn):
    r = _Rec()
    fn(r)
    name, a, k = r.call
    return lambda e: getattr(e, name)(*a, **k)


class SemState:
    def __init__(self, nc):
        self.nc = nc
        self.st = contextlib.ExitStack()
        self.sems = {}
        self.base = {e: 0 for e in Sched.ENGS}
        self.semval = {}
        self.bar_count = 0

    def sem(self, key):
        if key not in self.sems:
            name = key if isinstance(key, str) else "_".join(str(k) for k in key)
            self.sems[key] = self.st.enter_context(self.nc.semaphore("s_" + name))
        return self.sems[key]

    def close(self):
        self.st.close()


class Sched:
    ENGS = ("sp", "act", "dve", "pool", "pe")
    STATE = None

    def __init__(self, nc, n_dma_sems=6):
        self.nc = nc
        self.state = Sched.STATE
        self.ops = {e: [] for e in self.ENGS}
        self.nops = {e: 0 for e in self.ENGS}
        self.known = {e: {} for e in self.ENGS}
        self.needed = {e: set() for e in self.ENGS}
        self.n_dma_sems = n_dma_sems
        self.dma_ring = {e: 0 for e in self.ENGS}
        self.semval = self.state.semval
        self.dma_keys = []

    def _wait(self, eng, ev):
        if ev is None:
            return
        if ev[0] == "c":
            _, src, idx = ev
            if src == "pe" and eng == "pe":
                return
            if self.known[eng].get(src, 0) >= idx:
                return
            self.known[eng][src] = idx
            self.needed[src].add(idx)
            self.ops[eng].append(("wait", ev))
        else:
            _, key, val = ev
            if self.known[eng].get(key, 0) >= val:
                return
            self.known[eng][key] = val
            self.ops[eng].append(("wait", ev))

    def _deps(self, eng, reads, writes):
        for b in reads:
            self._wait(eng, b.last_w)
        for b in writes:
            self._wait(eng, b.last_w)
            for src, idx in list(b.rd_c.items()):
                self._wait(eng, ("c", src, idx))
            for ev in b.rd_d:
                self._wait(eng, ev)

    def _commit(self, ev, reads, writes):
        for b in reads:
            if ev[0] == "c":
                if b.rd_c.get(ev[1], 0) < ev[2]:
                    b.rd_c[ev[1]] = ev[2]
            else:
                b.rd_d.append(ev)
        for b in writes:
            b.last_w = ev
            b.rd_c = {}
            b.rd_d = []

    def op(self, eng, fn, reads=(), writes=()):
        self._deps(eng, reads, writes)
        self.nops[eng] += 1
        idx = self.nops[eng]
        ev = ("c", eng, idx)
        self.ops[eng].append(("op", idx, _freeze(fn)))
        self._commit(ev, reads, writes)
        return ev

    def dma(self, eng, out, in_, reads=(), writes=(), inc=16, fn=None, ring=None):
        if ring is None:
            i = self.dma_ring[eng]
            self.dma_ring[eng] = i + 1
            key = ("dsem", eng, i % self.n_dma_sems)
        else:
            key = ring
        if key not in self.semval:
            self.semval[key] = 0
        if key not in self.dma_keys:
            self.dma_keys.append(key)
        prev = self.semval[key]
        if prev > 0:
            self._wait(eng, ("d", key, prev))
        self._deps(eng, reads, writes)
        self.semval[key] = prev + inc
        ev = ("d", key, prev + inc)
        if fn is None:
            fn = lambda e, out=out, in_=in_: e.dma_start(out=out, in_=in_)
        self.ops[eng].append(("dma", key, fn, inc))
        self._commit(ev, reads, writes)
        return ev

    def drain(self):
        for key in self.dma_keys:
            self._wait(key[1], ("d", key, self.semval[key]))

    def emit(self):
        nc = self.nc
        stt = self.state
        self.drain()
        sems = {}
        for e in self.ENGS:
            sems[e] = stt.sem("eng_" + e)
        for k in self.dma_keys:
            sems[k] = stt.sem(k)
        bar = stt.sem("bar")
        rank = {}
        for e in self.ENGS:
            if self.nops[e] > 0:
                self.needed[e].add(self.nops[e])
            rank[e] = {idx: stt.base[e] + r + 1 for r, idx in enumerate(sorted(self.needed[e]))}
        bar_target = stt.bar_count + 5
        with nc.Block() as block:
            def run(ename):
                def body(eng):
                    for ent in self.ops[ename]:
                        if ent[0] == "wait":
                            ev = ent[1]
                            if ev[0] == "c":
                                eng.wait_ge(sems[ev[1]], rank[ev[1]][ev[2]])
                            else:
                                eng.wait_ge(sems[ev[1]], ev[2])
                        elif ent[0] == "op":
                            ins = ent[2](eng)
                            if ent[1] in rank[ename]:
                                ins.then_inc(sems[ename], 1)
                        else:
                            ins = ent[2](eng)
                            ins.then_inc(sems[ent[1]], ent[3])
                    if self.nops[ename] > 0:
                        eng.wait_ge(sems[ename], rank[ename][self.nops[ename]])
                    eng.sem_inc(bar, 1)
                    eng.wait_ge(bar, bar_target)
                return body

            block.sync(run("sp"))
            block.scalar(run("act"))
            block.vector(run("dve"))
            block.gpsimd(run("pool"))
            block.tensor(run("pe"))
        if os.environ.get("KVERB"):
            print("emit: ops", {e: self.nops[e] for e in self.ENGS}, "entries", {e: len(self.ops[e]) for e in self.ENGS}, flush=True)
        for e in self.ENGS:
            stt.base[e] += len(self.needed[e])
        stt.bar_count = bar_target
        Buf.reset_all()


class Ctx:
    def __init__(self, nc):
        self.nc = nc
        self.S = Sched(nc)
        self.st = contextlib.ExitStack()
        self.n = 0

    CNT = [0]

    def sb(self, shape, dt=F32, name=None):
        Ctx.CNT[0] += 1
        return self.st.enter_context(self.nc.sbuf_tensor(name or ("t%d" % Ctx.CNT[0]), list(shape), dt))

    def ps(self, shape, dt=F32, name=None):
        Ctx.CNT[0] += 1
        return self.st.enter_context(self.nc.psum_tensor(name or ("p%d" % Ctx.CNT[0]), list(shape), dt))

    def close(self):
        self.S.emit()
        self.st.close()


COLL_N = [0]


def collective(S, kind, src, dst, reads, writes):
    def fn(e):
        return e.collective_compute(kind, ALU.bypass, replica_groups=GROUPS, ins=[src], outs=[dst])
    COLL_N[0] += 1
    return S.dma("pool", None, None, reads=reads, writes=writes, inc=1, fn=fn, ring=("csem", "pool", COLL_N[0] % 4))


def rmsnorm_fm(C, xT, bx, wn, bwn, xn, bxn, ones_f, bones, pss, eps=1e-6, ntok=NT, sqb=None, rstd=None, brstd=None, slices=None, out_f32=False):
    S = C.S
    if slices is None:
        slices = [slice(tt * 512, (tt + 1) * 512) for tt in range(ntok // 512)]
    for tt, sl in enumerate(slices):
        wd_ = sl.stop - sl.start
        ps, bps = pss[tt % len(pss)]
        for k in range(8):
            sq, bsq = sqb[k % len(sqb)]
            S.op("act", lambda e, sq=sq, k=k, sl=sl: e.activation(out=sq[:, 0:wd_], in_=xT[:, k, sl], func=AF.Square),
                 reads=[bx], writes=[bsq])
            S.op("pe", lambda e, ps=ps, sq=sq, k=k: e.matmul(ps[:, 0:wd_], lhsT=ones_f[:], rhs=sq[:, 0:wd_], start=(k == 0), stop=(k == 7)),
                 reads=[bsq, bones], writes=[bps])
        S.op("act", lambda e, ps=ps, sl=sl: e.activation(out=rstd[:, sl], in_=ps[:, 0:wd_], func=AF.Ln, scale=1.0 / D, bias=eps),
             reads=[bps], writes=[brstd])
        S.op("act", lambda e, sl=sl: e.activation(out=rstd[:, sl], in_=rstd[:, sl], func=AF.Exp, scale=-0.5),
             reads=[brstd], writes=[brstd])
        for k in range(8):
            S.op("dve", lambda e, k=k, sl=sl: e.scalar_tensor_tensor(out=xn[:, k, sl], in0=xT[:, k, sl], scalar=wn[:, k:k + 1],
                                                                     in1=rstd[:, sl], op0=ALU.mult, op1=ALU.mult),
                 reads=[bx, bwn, brstd], writes=[bxn])


def ffn_fm(C, T, layer, hT, bh, ones_f, bones, banks):
    S = C.S
    xn = C.sb([128, 8, NT], BF16); bxn = Buf()
    rstd = C.sb([128, NT], F32); brstd = Buf()
    wn = C.sb([128, 8], F32); bwn = Buf()
    S.dma("sp", wn[:], T["ffn_norm"][layer], writes=[bwn])
    sqb = [(C.sb([128, 512], F32), Buf()) for _ in range(2)]
    rmsnorm_fm(C, hT, bh, wn, bwn, xn, bxn, ones_f, bones, banks[0:2], sqb=sqb, rstd=rstd, brstd=brstd)
    wg = C.sb([128, 8, FH], BF16); bwg = Buf()
    wu = C.sb([128, 8, FH], BF16); bwu = Buf()
    wd = C.sb([128, 11, D], BF16); bwd = Buf()
    act = [(C.sb([128, 11, 512], BF16), Buf()) for _ in range(2)]
    sg = [(C.sb([128, 512], F32), Buf()) for _ in range(2)]
    for half in range(2):
        c0 = half * FH
        for k in range(8):
            S.dma("pool", wg[:, k, :], T["ffn_wg"][layer][k * 128:(k + 1) * 128, c0:c0 + FH], writes=[bwg])
            S.dma("pool", wu[:, k, :], T["ffn_wu"][layer][k * 128:(k + 1) * 128, c0:c0 + FH], writes=[bwu])
        for fc in range(11):
            S.dma("pool", wd[:, fc, :], T["ffn_wd"][layer][c0 + fc * 128:c0 + (fc + 1) * 128, :], writes=[bwd])
        for tt in range(NT // 512):
            sl = slice(tt * 512, (tt + 1) * 512)
            a, ba = act[tt % 2]
            for fc in range(11):
                pg, bpg = banks[(2 * fc) % 4]
                pu, bpu = banks[(2 * fc + 1) % 4]
                for k in range(8):
                    S.op("pe", lambda e, pg=pg, k=k, fc=fc, sl=sl: e.matmul(pg[:], lhsT=wg[:, k, fc * 128:(fc + 1) * 128], rhs=xn[:, k, sl],
                                                                            start=(k == 0), stop=(k == 7)),
                         reads=[bwg, bxn], writes=[bpg])
                for k in range(8):
                    S.op("pe", lambda e, pu=pu, k=k, fc=fc, sl=sl: e.matmul(pu[:], lhsT=wu[:, k, fc * 128:(fc + 1) * 128], rhs=xn[:, k, sl],
                                                                            start=(k == 0), stop=(k == 7)),
                         reads=[bwu, bxn], writes=[bpu])
                s_, bs_ = sg[fc % 2]
                S.op("act", lambda e, s_=s_, pg=pg: e.activation(out=s_[:], in_=pg[:], func=AF.Silu), reads=[bpg], writes=[bs_])
                S.op("dve", lambda e, a=a, fc=fc, s_=s_, pu=pu: e.tensor_tensor(out=a[:, fc, :], in0=pu[:], in1=s_[:], op=ALU.mult),
                     reads=[bpu, bs_], writes=[ba])
            for oc in range(8):
                po, bpo = banks[4 + oc % 4]
                for fc in range(11):
                    S.op("pe", lambda e, po=po, fc=fc, oc=oc, a=a: e.matmul(po[:], lhsT=wd[:, fc, oc * 128:(oc + 1) * 128], rhs=a[:, fc, :],
                                                                            start=(fc == 0), stop=(fc == 10)),
                         reads=[bwd, ba], writes=[bpo])
                S.op("dve", lambda e, po=po, oc=oc, sl=sl: e.tensor_tensor(out=hT[:, oc, sl], in0=po[:], in1=hT[:, oc, sl], op=ALU.add),
                     reads=[bpo, bh], writes=[bh])


def load_consts(C, T):
    S = C.S
    k = {}
    idf = C.sb([128, 128], F32); bidf = Buf()
    S.dma("sp", idf[:], T["ident"], writes=[bidf])
    k["identb"] = C.sb([128, 128], BF16); k["bident"] = Buf()
    S.op("dve", lambda e: e.tensor_copy(out=k["identb"][:], in_=idf[:]), reads=[bidf], writes=[k["bident"]])
    k["ones_f"] = C.sb([128, 128], F32); k["bones"] = Buf()
    S.op("pool", lambda e: e.memset(k["ones_f"][:], 1.0), writes=[k["bones"]])
    k["idf"] = idf; k["bidf"] = bidf
    return k


def phase_attn_inproj(nc, T):
    C = Ctx(nc); S = C.S
    K = load_consts(C, T)
    banks = [(C.ps([128, 512], F32), Buf()) for _ in range(8)]
    xT = C.sb([128, 8, NT], F32); bx = Buf()
    xsrc = T["xT0"].rearrange("(k p) n -> p k n", p=128)
    for tt in range(4):
        S.dma("sp", xT[:, :, tt * 512:(tt + 1) * 512], xsrc[:, :, tt * 512:(tt + 1) * 512], writes=[bx])
    cs = C.sb([128, 2, NT], F32); bcs = Buf()
    S.dma("act", cs[:], T["cs_tab"], writes=[bcs])
    wn = C.sb([128, 8], F32); bwn = Buf()
    S.dma("act", wn[:], T["attn_norm"], writes=[bwn])
    xn = C.sb([128, 8, NT], BF16); bxn = Buf()
    rstd = C.sb([128, NT], F32); brstd = Buf()
    sqb = [(C.sb([128, 512], F32), Buf()) for _ in range(2)]
    rmsnorm_fm(C, xT, bx, wn, bwn, xn, bxn, K["ones_f"], K["bones"], banks[0:2], sqb=sqb, rstd=rstd, brstd=brstd)

    kcut = int(os.environ.get("KCUT", "9"))
    if kcut <= 1:
        C.close()
        return
    wb = [(C.sb([128, 8, 512], BF16), Buf()) for _ in range(2)]
    wsw = [(C.sb([128, 8, 512], BF16), Buf()) for _ in range(2)]
    ob = [(C.sb([128, NT], BF16), Buf()) for _ in range(2)]
    t1 = [(C.sb([128, 512], F32), Buf()) for _ in range(2)]
    t2 = [(C.sb([128, 512], F32), Buf()) for _ in range(2)]
    vst = [(C.sb([128, 4, 129], BF16), Buf()) for _ in range(2)]
    vss = [(C.sb([128, 512], BF16), Buf()) for _ in range(2)]
    for v, bv in vst:
        S.op("pool", lambda e, v=v: e.memset(v[:], 1.0), writes=[bv])
    wsrc = T["w_in0"].rearrange("(k p) c -> p k c", p=128)
    swsrc = T["w_sw0"].rearrange("(k p) c -> p k c", p=128)
    dst_fm = {0: ("QT", 0), 1: ("KTd", 0), 3: ("QT", 512), 4: ("KTs", 0)}
    nchunk = 0
    for g in range(6):
        w, bw = wb[g % 2]
        for k in range(8):
            S.dma("pool", w[:, k, :], wsrc[:, k, g * 512:(g + 1) * 512], writes=[bw])
        if g < 2:
            w2, bw2 = wsw[g % 2]
            for k in range(8):
                S.dma("pool", w2[:, k, :], swsrc[:, k, g * 512:(g + 1) * 512], writes=[bw2])
        if g in dst_fm:
            dname, roff = dst_fm[g]
            for cc in range(4):
                o, bo = ob[nchunk % 2]; nchunk += 1
                for tt in range(4):
                    sl = slice(tt * 512, (tt + 1) * 512)
                    p1, bp1 = banks[2 + (2 * tt) % 4]
                    for k in range(8):
                        S.op("pe", lambda e, p1=p1, w=w, k=k, cc=cc, sl=sl: e.matmul(p1[:], lhsT=w[:, k, cc * 128:(cc + 1) * 128], rhs=xn[:, k, sl],
                                                                                     start=(k == 0), stop=(k == 7)),
                             reads=[bw, bxn], writes=[bp1])
                    if g < 2:
                        p2, bp2 = banks[2 + (2 * tt + 1) % 4]
                        for k in range(8):
                            S.op("pe", lambda e, p2=p2, w2=w2, k=k, cc=cc, sl=sl: e.matmul(p2[:], lhsT=w2[:, k, cc * 128:(cc + 1) * 128], rhs=xn[:, k, sl],
                                                                                           start=(k == 0), stop=(k == 7)),
                                 reads=[bw2, bxn], writes=[bp2])
                        a1, ba1 = t1[tt % 2]
                        a2, ba2 = t2[tt % 2]
                        S.op("dve", lambda e, a1=a1, p1=p1, sl=sl: e.tensor_tensor(out=a1[:], in0=p1[:], in1=cs[:, 0, sl], op=ALU.mult),
                             reads=[bp1, bcs], writes=[ba1])
                        S.op("dve", lambda e, a2=a2, p2=p2, sl=sl: e.tensor_tensor(out=a2[:], in0=p2[:], in1=cs[:, 1, sl], op=ALU.mult),
                             reads=[bp2, bcs], writes=[ba2])
                        S.op("pool", lambda e, o=o, a1=a1, a2=a2, sl=sl: e.tensor_tensor(out=o[:, sl], in0=a1[:], in1=a2[:], op=ALU.add),
                             reads=[ba1, ba2], writes=[bo])
                    else:
                        sc = 0.125 if g == 3 else 1.0
                        S.op("act", lambda e, o=o, p1=p1, sl=sl, sc=sc: e.activation(out=o[:, sl], in_=p1[:], func=AF.Copy, scale=sc),
                             reads=[bp1], writes=[bo])
                if dname == "QT":
                    r0 = roff + cc * 128
                    S.dma("sp", T["QT"][r0:r0 + 128, :], o[:], reads=[bo], writes=[T["b_QT"]])
                else:
                    ch, r0 = cc // 2, (cc % 2) * 128
                    S.dma("sp", T[dname][ch][r0:r0 + 128, :], o[:], reads=[bo], writes=[T["b_" + dname][ch]])
                    if cc % 2 == 1:
                        collective(S, "AllGather", T[dname][ch], T[dname + "_all"][ch], reads=[T["b_" + dname][ch]], writes=[T["b_" + dname + "_all"][ch]])
        else:
            for tb in range(16):
                p1, bp1 = banks[2 + tb % 4]
                for k in range(8):
                    S.op("pe", lambda e, p1=p1, w=w, k=k, tb=tb: e.matmul(p1[:], lhsT=xn[:, k, tb * 128:(tb + 1) * 128], rhs=w[:, k, :],
                                                                          start=(k == 0), stop=(k == 7)),
                         reads=[bw, bxn], writes=[bp1])
                if g == 2:
                    v, bv = vst[tb % 2]
                    S.op("act", lambda e, v=v, p1=p1: e.activation(out=v[:, :, 0:128], in_=p1[:].rearrange("p (h c) -> p h c", c=128), func=AF.Copy),
                         reads=[bp1], writes=[bv])
                    ch, r0 = tb // 4, (tb % 4) * 128
                    S.dma("sp", T["Vd"][ch][r0:r0 + 128, :], v[:].rearrange("p h c -> p (h c)"), reads=[bv], writes=[T["b_Vd"][ch]])
                    if tb % 4 == 3:
                        collective(S, "AllGather", T["Vd"][ch], T["Vd_all"][ch], reads=[T["b_Vd"][ch]], writes=[T["b_Vd_all"][ch]])
                else:
                    v, bv = vss[tb % 2]
                    S.op("act", lambda e, v=v, p1=p1: e.activation(out=v[:], in_=p1[:], func=AF.Copy), reads=[bp1], writes=[bv])
                    ch, r0 = tb // 8, (tb % 8) * 128
                    S.dma("sp", T["Vs"][ch][r0:r0 + 128, :], v[:], reads=[bv], writes=[T["b_Vs"][ch]])
                    if tb % 8 == 7:
                        collective(S, "AllGather", T["Vs"][ch], T["Vs_all"][ch], reads=[T["b_Vs"][ch]], writes=[T["b_Vs_all"][ch]])
    C.close()


def tile_iters(J):
    out = []
    for kb in range(16 * J + 16):
        i0 = max(0, kb // 4 - 4 * J)
        z = kb - 16 * J if kb >= 16 * J else None
        out.append((kb, kb % 4, kb // 4, i0, z))
    return out


def phase_attn_core(nc, T):
    C = Ctx(nc); S = C.S
    K = load_consts(C, T)
    identb, bident = K["identb"], K["bident"]
    ones_f, bones = K["ones_f"], K["bones"]
    psall = C.ps([128, 8, 512], F32)
    banks = [(psall[:, i, :], Buf()) for i in range(8)]
    nuf = C.sb([128, 128], F32); bnuf = Buf()
    S.dma("sp", nuf[:], T["nu"], writes=[bnuf])
    NU = C.sb([128, 128], BF16); bNU = Buf()
    S.op("dve", lambda e: e.tensor_copy(out=NU[:], in_=nuf[:]), reads=[bnuf], writes=[bNU])
    onesb = C.sb([128, 128], BF16); bonesb = Buf()
    S.op("pool", lambda e: e.memset(onesb[:], 1.0), writes=[bonesb])
    nmI = C.sb([128, 16, 512], BF16); bnmI = Buf()
    nmS = C.sb([128, 16, 512], BF16); bnmS = Buf()
    for z in range(0, 16, 4):
        S.dma("pool", nmI[:, z:z + 4, :], T["nm_incl"][:, z:z + 4, :], writes=[bnmI])
        S.dma("pool", nmS[:, z:z + 4, :], T["nm_strict"][:, z:z + 4, :], writes=[bnmS])
    lv = C.sb([128, 4, 64], F32); blv = Buf()
    for i, nm in enumerate(("lq1", "lk1", "lq2", "lk2")):
        S.dma("sp", lv[:, i:i + 1, :], T[nm].partition_broadcast(128), writes=[blv])
    lt = C.sb([128, 2, 64], F32); blt = Buf()
    S.op("dve", lambda e: e.tensor_tensor(out=lt[:, 0, :], in0=lv[:, 0, :], in1=lv[:, 1, :], op=ALU.mult), reads=[blv], writes=[blt])
    S.op("dve", lambda e: e.tensor_tensor(out=lt[:, 1, :], in0=lv[:, 2, :], in1=lv[:, 3, :], op=ALU.mult), reads=[blv], writes=[blt])
    ls = C.sb([128, 4], F32); bls = Buf()
    S.op("dve", lambda e: e.reduce_sum(out=ls[:, 0:1], in_=lt[:, 0, :], axis=mybir.AxisListType.X), reads=[blt], writes=[bls])
    S.op("dve", lambda e: e.reduce_sum(out=ls[:, 1:2], in_=lt[:, 1, :], axis=mybir.AxisListType.X), reads=[blt], writes=[bls])
    S.op("act", lambda e: e.activation(out=ls[:, 0:2], in_=ls[:, 0:2], func=AF.Exp), reads=[bls], writes=[bls])
    S.op("dve", lambda e: e.tensor_tensor(out=ls[:, 2:3], in0=ls[:, 1:2], in1=ls[:, 0:1], op=ALU.subtract), reads=[bls], writes=[bls])
    S.op("dve", lambda e: e.tensor_scalar(out=ls[:, 3:4], in0=ls[:, 2:3], scalar1=-0.2, scalar2=None, op0=ALU.add), reads=[bls], writes=[bls])
    negl = ls[:, 3:4]
    sl_ = C.sb([128, 1], F32); bsl = Buf()
    S.dma("sp", sl_[:], T["subln"].rearrange("o v -> v o"), writes=[bsl])
    S.op("dve", lambda e: e.tensor_scalar(out=sl_[:], in0=sl_[:], scalar1=0.8, scalar2=None, op0=ALU.mult), reads=[bsl], writes=[bsl])

    kbuf = [(C.sb([128, 4, NT], BF16), Buf()) for _ in range(2)]
    vbuf = [(C.sb([128, 64 * 129], BF16), Buf()) for _ in range(2)]
    qz = [[(C.sb([128, NT], BF16), Buf()) for _ in range(2)] for _ in range(2)]
    for b_ in range(2):
        S.op("pool", lambda e, b_=b_: e.memset(qz[b_][0][0][64:128, :], 0.0), writes=[qz[b_][0][1]])
        S.op("pool", lambda e, b_=b_: e.memset(qz[b_][1][0][0:64, :], 0.0), writes=[qz[b_][1][1]])
    ostg = [(C.sb([128, 512], BF16), Buf()) for _ in range(2)]
    nload = 0
    nout = 0

    Eb = [(C.sb([128, 2, 512], BF16), Buf()) for _ in range(2)]
    ep = [(C.sb([128, 512], F32), Buf()) for _ in range(5)]
    ktd_all = [a.rearrange("(r x) n -> x r n", r=4) for a in T["KTd_all"]]
    vd_all = [a.rearrange("(r j t) (h c) -> t r j h c", r=4, j=4, h=4) for a in T["Vd_all"]]
    for H in range(4):
        kt, bkt = kbuf[nload % 2]; vv, bvv = vbuf[nload % 2]; qzz = qz[nload % 2]; nload += 1
        for r in range(4):
            S.dma("sp", kt[:, r, :], ktd_all[H // 2][(H % 2) * 128:(H % 2 + 1) * 128, r, :], reads=[T["b_KTd_all"][H // 2]], writes=[bkt])
            for vc in range(4):
                b0 = r * 16 + 4 * vc
                S.dma("sp", vv[:, b0 * 129:(b0 + 4) * 129].rearrange("p (j c) -> p j c", c=129), vd_all[vc][:, r, :, H, :],
                      reads=[T["b_Vd_all"][vc]], writes=[bvv])
        for m in range(2):
            S.dma("sp", qzz[m][0][m * 64:(m + 1) * 64, :], T["QT"][H * 128 + m * 64:H * 128 + (m + 1) * 64, :], reads=[T["b_QT"]], writes=[qzz[m][1]])
        v4 = vv[:].rearrange("p (b c) -> p b c", c=129)
        for J in range(4):
            its = tile_iters(J)
            q0 = J * 512
            for bk_i in (4, 5, 6, 7):
                S.op("dve", lambda e, bk_i=bk_i: e.memset(banks[bk_i][0], 0.0), writes=[banks[bk_i][1]])

            def stage1(it, slot):
                kb, r, j, i0, z = it
                w0 = i0 * 128
                for m in range(2):
                    ps, bps = banks[2 * slot + m]
                    S.op("pe", lambda e, ps=ps, m=m, r=r, j=j, w0=w0: e.matmul(
                        ps[:, w0:512], lhsT=kt[:, r, j * 128:(j + 1) * 128], rhs=qzz[m][0][:, q0 + w0:q0 + 512],
                        start=True, stop=(z is None)), reads=[bkt, qzz[m][1]], writes=[bps])
                    if z is not None:
                        S.op("pe", lambda e, ps=ps, z=z, w0=w0: e.matmul(ps[:, w0:512], lhsT=identb[:], rhs=nmI[:, z, w0:512], start=False, stop=True),
                             reads=[bident, bnmI], writes=[bps])

            def stage2(it, slot):
                kb, r, j, i0, z = it
                w0 = i0 * 128
                E, bE = Eb[slot]
                S.op("act", lambda e, E=E, slot=slot, w0=w0: e.activation(out=E[:, :, w0:512], in_=psall[:, 2 * slot:2 * slot + 2, w0:512], func=AF.Exp, scale=0.125),
                     reads=[banks[2 * slot][1], banks[2 * slot + 1][1]], writes=[bE])
                for m in range(2):
                    S.op("pe", lambda e, E=E, m=m, r=r, j=j, w0=w0: e.matmul(banks[4 + m][0][:, w0:512], lhsT=v4[:, r * 16 + j, 0:128], rhs=E[:, m, w0:512],
                                                                             start=False, stop=False, skip_group_check=True),
                         reads=[bE, bvv], writes=[banks[4 + m][1]])
                    S.op("pe", lambda e, E=E, m=m, w0=w0: e.matmul(banks[6 + m][0][:, w0:512], lhsT=onesb[:], rhs=E[:, m, w0:512],
                                                                   start=False, stop=False, skip_group_check=True),
                         reads=[bE, bonesb], writes=[banks[6 + m][1]])

            for n in range(len(its) + 1):
                if n < len(its):
                    stage1(its[n], n % 2)
                if n >= 1:
                    stage2(its[n - 1], (n - 1) % 2)
            (r1, br1), (t1, bt1), (t2, bt2), (od, bod), (sq, bsq) = ep
            S.op("dve", lambda e: e.reciprocal(out=r1[:], in_=banks[6][0]), reads=[banks[6][1]], writes=[br1])
            S.op("dve", lambda e: e.tensor_tensor(out=t1[:], in0=banks[4][0], in1=r1[:], op=ALU.mult), reads=[banks[4][1], br1], writes=[bt1])
            S.op("dve", lambda e: e.reciprocal(out=r1[:], in_=banks[7][0]), reads=[banks[7][1], bt1], writes=[br1])
            S.op("dve", lambda e: e.tensor_tensor(out=t2[:], in0=banks[5][0], in1=r1[:], op=ALU.mult), reads=[banks[5][1], br1], writes=[bt2])
            S.op("dve", lambda e: e.scalar_tensor_tensor(out=od[:], in0=t2[:], scalar=negl, in1=t1[:], op0=ALU.mult, op1=ALU.add),
                 reads=[bt1, bt2, bls], writes=[bod])
            S.op("act", lambda e: e.activation(out=sq[:], in_=od[:], func=AF.Square), reads=[bod], writes=[bsq])
            pss, bpss = banks[0]
            S.op("pe", lambda e: e.matmul(pss, lhsT=ones_f[:], rhs=sq[:], start=True, stop=True), reads=[bones, bsq], writes=[bpss])
            S.op("act", lambda e: e.activation(out=t1[:], in_=pss, func=AF.Ln, scale=1.0 / 128, bias=1e-5), reads=[bpss, bod], writes=[bt1])
            S.op("act", lambda e: e.activation(out=t1[:], in_=t1[:], func=AF.Exp, scale=-0.5), reads=[bt1], writes=[bt1])
            o_, bo_ = ostg[nout % 2]; nout += 1
            S.op("dve", lambda e, o_=o_: e.scalar_tensor_tensor(out=o_[:], in0=od[:], scalar=sl_[:, 0:1], in1=t1[:], op0=ALU.mult, op1=ALU.mult),
                 reads=[bod, bsl, bt1], writes=[bo_])
            S.dma("sp", T["mixT"][H * 128:(H + 1) * 128, q0:q0 + 512], o_[:], reads=[bo_], writes=[T["b_mixT"]])

    eb = [(C.sb([128, 2, 512], F32), Buf()) for _ in range(2)]
    spb = [(C.sb([128, 2, 512], F32), Buf()) for _ in range(2)]
    hib = [(C.sb([128, 2, 512], BF16), Buf()) for _ in range(2)]
    lob = [(C.sb([128, 2, 512], BF16), Buf()) for _ in range(2)]
    Ab = [(C.sb([128, 2, 512], BF16), Buf()) for _ in range(2)]
    fb = [(C.sb([128, 512], F32), Buf()) for _ in range(2)]
    OT = C.sb([128, 512], F32); bOT = Buf()
    kts_all = [a.rearrange("(r x) n -> x r n", r=4) for a in T["KTs_all"]]
    vs_all = [a.rearrange("(r j t) (pr c) -> t r j pr c", r=4, j=8, pr=4) for a in T["Vs_all"]]
    for pr in range(4):
        kt, bkt = kbuf[nload % 2]; vv, bvv = vbuf[nload % 2]; qzz = qz[nload % 2]; nload += 1
        for r in range(4):
            S.dma("sp", kt[:, r, :], kts_all[pr // 2][(pr % 2) * 128:(pr % 2 + 1) * 128, r, :], reads=[T["b_KTs_all"][pr // 2]], writes=[bkt])
            for vc in range(2):
                b0 = r * 16 + 8 * vc
                S.dma("sp", vv[:, b0 * 128:(b0 + 8) * 128].rearrange("p (j c) -> p j c", c=128), vs_all[vc][:, r, :, pr, :],
                      reads=[T["b_Vs_all"][vc]], writes=[bvv])
        for m in range(2):
            S.dma("sp", qzz[m][0][m * 64:(m + 1) * 64, :], T["QT"][512 + pr * 128 + m * 64:512 + pr * 128 + (m + 1) * 64, :], reads=[T["b_QT"]], writes=[qzz[m][1]])
        v4 = vv[:, 0:64 * 128].rearrange("p (b c) -> p b c", c=128)
        for J in range(4):
            its = tile_iters(J)
            q0 = J * 512
            S.op("pool", lambda e: e.memset(OT[:], 0.0), writes=[bOT])

            def stage1(it, slot):
                kb, r, j, i0, z = it
                w0 = i0 * 128
                ee, bee = eb[slot]; sp_, bsp = spb[slot]; hi, bhi = hib[slot]; lo, blo = lob[slot]
                for hh in range(2):
                    p0 = hh * 64
                    ps, bps = banks[2 * slot + hh]
                    S.op("pe", lambda e, ps=ps, r=r, j=j, w0=w0, hh=hh: e.matmul(
                        ps[:, w0:512], lhsT=kt[:, r, j * 128:(j + 1) * 128], rhs=qzz[hh][0][:, q0 + w0:q0 + 512],
                        start=True, stop=(z is None)), reads=[bkt, qzz[hh][1]], writes=[bps])
                    if z is not None:
                        S.op("pe", lambda e, ps=ps, z=z, w0=w0: e.matmul(ps[:, w0:512], lhsT=identb[:], rhs=nmS[:, z, w0:512], start=False, stop=True),
                             reads=[bident, bnmS], writes=[bps])
                zz = psall[:, 2 * slot:2 * slot + 2, w0:512]
                bz = [banks[2 * slot][1], banks[2 * slot + 1][1]]
                S.op("act", lambda e, ee=ee, zz=zz, w0=w0: e.activation(out=ee[:, :, w0:512], in_=zz, func=AF.Exp), reads=bz, writes=[bee])
                S.op("act", lambda e, ee=ee, sp_=sp_, w0=w0: e.activation(out=sp_[:, :, w0:512], in_=ee[:, :, w0:512], func=AF.Ln, bias=1.0),
                     reads=[bee], writes=[bsp])
                S.op("dve", lambda e, hi=hi, sp_=sp_, w0=w0: e.tensor_copy(out=hi[:, :, w0:512], in_=sp_[:, :, w0:512]), reads=[bsp], writes=[bhi])
                S.op("dve", lambda e, lo=lo, hi=hi, sp_=sp_, w0=w0: e.tensor_tensor(out=lo[:, :, w0:512], in0=sp_[:, :, w0:512], in1=hi[:, :, w0:512], op=ALU.subtract),
                     reads=[bsp, bhi], writes=[blo])

            def stage2(it, slot):
                kb, r, j, i0, z = it
                w0 = i0 * 128
                hi, bhi = hib[slot]; lo, blo = lob[slot]; A, bA = Ab[slot]; f, bf_ = fb[slot]
                bz = [banks[2 * slot][1], banks[2 * slot + 1][1]]
                for hh in range(2):
                    ps, bps = banks[2 * slot + hh]
                    S.op("pe", lambda e, ps=ps, hi=hi, hh=hh, w0=w0: e.matmul(ps[:, w0:512], lhsT=NU[:], rhs=hi[:, hh, w0:512], start=False, stop=False, skip_group_check=True),
                         reads=[bNU, bhi], writes=[bps])
                    S.op("pe", lambda e, ps=ps, lo=lo, hh=hh, w0=w0: e.matmul(ps[:, w0:512], lhsT=NU[:], rhs=lo[:, hh, w0:512], start=False, stop=True, skip_group_check=True),
                         reads=[bNU, blo], writes=[bps])
                pC, bpC = banks[6 + slot]
                for hh in range(2):
                    p0 = hh * 64
                    S.op("pe", lambda e, pC=pC, hi=hi, hh=hh, p0=p0, w0=w0: e.matmul(pC[p0:p0 + 64, w0:512], lhsT=onesb[:, 0:64], rhs=hi[:, hh, w0:512], start=True, stop=False),
                         reads=[bhi, bonesb], writes=[bpC])
                    S.op("pe", lambda e, pC=pC, lo=lo, hh=hh, p0=p0, w0=w0: e.matmul(pC[p0:p0 + 64, w0:512], lhsT=onesb[:, 0:64], rhs=lo[:, hh, w0:512], start=False, stop=True),
                         reads=[blo, bonesb], writes=[bpC])
                zz = psall[:, 2 * slot:2 * slot + 2, w0:512]
                S.op("act", lambda e, A=A, zz=zz, w0=w0: e.activation(out=A[:, :, w0:512], in_=zz, func=AF.Exp), reads=bz, writes=[bA])
                S.op("act", lambda e, f=f, pC=pC, w0=w0: e.activation(out=f[:, w0:512], in_=pC[:, w0:512], func=AF.Exp, scale=-1.0), reads=[bpC], writes=[bf_])

            def stage3(it, slot):
                kb, r, j, i0, z = it
                w0 = i0 * 128
                A, bA = Ab[slot]; f, bf_ = fb[slot]
                pP, bpP = banks[4 + slot]
                for hh in range(2):
                    p0 = hh * 64
                    S.op("pe", lambda e, pP=pP, A=A, hh=hh, p0=p0, r=r, j=j, w0=w0: e.matmul(pP[p0:p0 + 64, w0:512], lhsT=v4[:, r * 16 + j, p0:p0 + 64], rhs=A[:, hh, w0:512],
                                                                                            start=True, stop=True),
                         reads=[bA, bvv], writes=[bpP])
                S.op("dve", lambda e, f=f, w0=w0: e.tensor_tensor(out=OT[:, w0:512], in0=OT[:, w0:512], in1=f[:, w0:512], op=ALU.mult), reads=[bOT, bf_], writes=[bOT])
                S.op("dve", lambda e, pP=pP, w0=w0: e.tensor_tensor(out=OT[:, w0:512], in0=pP[:, w0:512], in1=OT[:, w0:512], op=ALU.add), reads=[bOT, bpP], writes=[bOT])

            for n in range(len(its) + 2):
                if n < len(its):
                    stage1(its[n], n % 2)
                if 1 <= n <= len(its):
                    stage2(its[n - 1], (n - 1) % 2)
                if n >= 2:
                    stage3(its[n - 2], (n - 2) % 2)
            o_, bo_ = ostg[nout % 2]; nout += 1
            S.op("act", lambda e, o_=o_: e.activation(out=o_[:], in_=OT[:], func=AF.Copy), reads=[bOT], writes=[bo_])
            S.dma("sp", T["mixT"][(4 + pr) * 128:(5 + pr) * 128, q0:q0 + 512], o_[:], reads=[bo_], writes=[T["b_mixT"]])
    C.close()


def phase_attn_out_ffn(nc, T, stage):
    C = Ctx(nc); S = C.S
    K = load_consts(C, T)
    banks = [(C.ps([128, 512], F32), Buf()) for _ in range(8)]
    hT = C.sb([128, 8, NT], F32); bh = Buf()
    xsrc = T["xT0"].rearrange("(k p) n -> p k n", p=128)
    for tt in range(4):
        S.dma("sp", hT[:, :, tt * 512:(tt + 1) * 512], xsrc[:, :, tt * 512:(tt + 1) * 512], writes=[bh])
    with contextlib.ExitStack() as st2:
        mT = st2.enter_context(nc.sbuf_tensor("mTf", [128, 8, NT], BF16)); bmT = Buf()
        wo = st2.enter_context(nc.sbuf_tensor("wo", [128, 8, D], BF16)); bwo = Buf()
        msrc = T["mixT"].rearrange("(c p) n -> p c n", p=128)
        for c in range(8):
            S.dma("act", mT[:, c, :], msrc[:, c, :], reads=[T["b_mixT"]], writes=[bmT])
            S.dma("pool", wo[:, c, :], T["w_out0"][c * 128:(c + 1) * 128, :], writes=[bwo])
        for tt in range(4):
            sl = slice(tt * 512, (tt + 1) * 512)
            for oc in range(8):
                po, bpo = banks[oc % 4]
                for c in range(8):
                    S.op("pe", lambda e, po=po, c=c, oc=oc, sl=sl: e.matmul(po[:], lhsT=wo[:, c, oc * 128:(oc + 1) * 128], rhs=mT[:, c, sl],
                                                                            start=(c == 0), stop=(c == 7)),
                         reads=[bwo, bmT], writes=[bpo])
                S.op("dve", lambda e, po=po, oc=oc, sl=sl: e.tensor_tensor(out=hT[:, oc, sl], in0=po[:], in1=hT[:, oc, sl], op=ALU.add),
                     reads=[bpo, bh], writes=[bh])
        if stage == "mix":
            dst = T["dbg"].rearrange("(k p) n -> p k n", p=128)
            for c in range(8):
                S.op("dve", lambda e, c=c: e.tensor_copy(out=hT[:, c, :], in_=mT[:, c, :]), reads=[bmT, bh], writes=[bh])
            for tt in range(4):
                S.dma("sp", dst[:, :, tt * 512:(tt + 1) * 512], hT[:, :, tt * 512:(tt + 1) * 512], reads=[bh])
            C.S.emit()
            st2.close(); C.st.close()
            return
        if stage == "attn":
            dst = T["dbg"].rearrange("(k p) n -> p k n", p=128)
            for tt in range(4):
                S.dma("sp", dst[:, :, tt * 512:(tt + 1) * 512], hT[:, :, tt * 512:(tt + 1) * 512], reads=[bh])
            C.S.emit()
            st2.close(); C.st.close()
            return
        C.S.emit()
    C.S = Sched(nc); S = C.S
    ffn_fm(C, T, 0, hT, bh, K["ones_f"], K["bones"], banks)
    if stage == "ffn0":
        dst = T["dbg"].rearrange("(k p) n -> p k n", p=128)
        for tt in range(4):
            S.dma("sp", dst[:, :, tt * 512:(tt + 1) * 512], hT[:, :, tt * 512:(tt + 1) * 512], reads=[bh])
        C.close()
        return
    zc = C.sb([128, 4], F32); bzc = Buf()
    S.op("pool", lambda e: e.memset(zc[:], 0.0), writes=[bzc])
    for ch in range(16):
        k, hf = ch // 2, ch % 2
        S.dma("sp", T["HX"][ch][:, 0:3], zc[0:64, 0:3], reads=[bzc], writes=[T["b_HX"][ch]])
        S.dma("sp", T["HX"][ch][:, 3:3 + NT], hT[hf * 64:(hf + 1) * 64, k, :], reads=[bh], writes=[T["b_HX"][ch]])
        collective(S, "AllGather", T["HX"][ch], T["HXA"][ch], reads=[T["b_HX"][ch]], writes=[T["b_HXA"][ch]])
    C.close()


def load_h1_contig(C, T, x1, bx1, halo):
    S = C.S
    off = 3 if halo else 0
    cache = {}

    def rank_of(e):
        if "c" not in cache:
            cache["c"] = e.partition_id() % 4
        return cache["c"]
    for ch in range(16):
        k, hf = ch // 2, ch % 2
        p0 = hf * 64
        for r in range(4):
            dstv = x1[p0:p0 + 64, k, off:off + NT].rearrange("p (m r t) -> p m r t", m=4, r=4)[:, :, r, :]

            def fn(e, dstv=dstv, ch=ch, r=r):
                c = rank_of(e)
                src = T["HXA"][ch][r * 64:(r + 1) * 64, bass.ds(c * 512 + 3, 512)]
                return e.dma_start(out=dstv, in_=src.rearrange("p (m t) -> p m t", m=4))
            S.dma("sp", None, None, reads=[T["b_HXA"][ch]], writes=[bx1], fn=fn)
        if halo:
            def fn2(e, ch=ch, p0=p0, k=k):
                c = rank_of(e)
                src = T["HXA"][ch][3 * 64:4 * 64, bass.ds(c * 512, 3)]
                return e.dma_start(out=x1[p0:p0 + 64, k, 0:3], in_=src)
            S.dma("sp", None, None, reads=[T["b_HXA"][ch]], writes=[bx1], fn=fn2)


def phase_ssd_inproj(nc, T):
    C = Ctx(nc); S = C.S
    K = load_consts(C, T)
    identb, bident = K["identb"], K["bident"]
    banks = [(C.ps([128, 512], F32), Buf()) for _ in range(8)]
    x1 = C.sb([128, 8, 3 + NT], F32); bx1 = Buf()
    load_h1_contig(C, T, x1, bx1, True)
    wn = C.sb([128, 8], F32); bwn = Buf()
    S.dma("sp", wn[:], T["ssd_norm"], writes=[bwn])
    xn = C.sb([128, 8, 3 + NT], BF16); bxn = Buf()
    rstd = C.sb([128, 3 + NT], F32); brstd = Buf()
    sqb = [(C.sb([128, 512], F32), Buf()) for _ in range(2)]
    slices = [slice(0, 3)] + [slice(3 + tt * 512, 3 + (tt + 1) * 512) for tt in range(4)]
    rmsnorm_fm(C, x1, bx1, wn, bwn, xn, bxn, K["ones_f"], K["bones"], banks[0:2], sqb=sqb, rstd=rstd, brstd=brstd, slices=slices)

    cw = C.sb([128, 32, 4], F32); bcw = Buf()
    cb = C.sb([128, 32], F32); bcb = Buf()
    S.dma("sp", cw[:], T["conv_w"], writes=[bcw])
    S.dma("sp", cb[:], T["conv_b"], writes=[bcb])
    wb = [(C.sb([128, 8, 512], BF16), Buf()) for _ in range(2)]
    wsrc = T["ssd_w_in"].rearrange("(k p) c -> p k c", p=128)
    ub = [(C.sb([128, 3 + NT], F32), Buf()) for _ in range(2)]
    accb = [(C.sb([128, NT], F32), Buf()) for _ in range(2)]
    xcb = [(C.sb([128, NT], BF16), Buf()) for _ in range(2)]
    ctmp = C.sb([128, NT], F32); bctmp = Buf()
    tst = [(C.sb([128, 8, 128], BF16), Buf()) for _ in range(2)]
    pTs = [(banks[6][0][:].bitcast(BF16), banks[6][1]), (banks[7][0][:].bitcast(BF16), banks[7][1])]
    nst = 0
    for g in range(8):
        w, bw = wb[g % 2]
        for k in range(8):
            S.dma("pool", w[:, k, :], wsrc[:, k, 2048 + g * 512:2048 + (g + 1) * 512], writes=[bw])
        for c4 in range(4):
            cc = g * 4 + c4
            u, bu = ub[cc % 2]; acc, bacc = accb[cc % 2]; xc, bxc = xcb[cc % 2]
            veng = "dve"
            for ti, sl in enumerate(slices):
                wd_ = sl.stop - sl.start
                p1, bp1 = banks[2 + ti % 4]
                for k in range(8):
                    S.op("pe", lambda e, p1=p1, w=w, k=k, c4=c4, sl=sl, wd_=wd_: e.matmul(p1[:, 0:wd_], lhsT=w[:, k, c4 * 128:(c4 + 1) * 128], rhs=xn[:, k, sl],
                                                                                         start=(k == 0), stop=(k == 7)),
                         reads=[bw, bxn], writes=[bp1])
                S.op("act", lambda e, u=u, p1=p1, sl=sl, wd_=wd_: e.activation(out=u[:, sl], in_=p1[:, 0:wd_], func=AF.Copy), reads=[bp1], writes=[bu])
            S.op(veng, lambda e, acc=acc, u=u, cc=cc: e.tensor_scalar(out=acc[:], in0=u[:, 0:NT], scalar1=cw[:, cc, 0:1], scalar2=None, op0=ALU.mult),
                 reads=[bu, bcw], writes=[bacc])
            for tap in range(1, 4):
                if veng == "dve":
                    S.op(veng, lambda e, acc=acc, u=u, cc=cc, tap=tap: e.scalar_tensor_tensor(out=acc[:], in0=u[:, tap:tap + NT], scalar=cw[:, cc, tap:tap + 1],
                                                                                             in1=acc[:], op0=ALU.mult, op1=ALU.add),
                         reads=[bu, bcw, bacc], writes=[bacc])
                else:
                    S.op(veng, lambda e, u=u, cc=cc, tap=tap: e.tensor_scalar(out=ctmp[:], in0=u[:, tap:tap + NT], scalar1=cw[:, cc, tap:tap + 1], scalar2=None, op0=ALU.mult),
                         reads=[bu, bcw], writes=[bctmp])
                    S.op(veng, lambda e, acc=acc: e.tensor_tensor(out=acc[:], in0=acc[:], in1=ctmp[:], op=ALU.add), reads=[bacc, bctmp], writes=[bacc])
            S.op("act", lambda e, xc=xc, acc=acc, cc=cc: e.activation(out=xc[:], in_=acc[:], func=AF.Silu, bias=cb[:, cc:cc + 1]),
                 reads=[bacc, bcb], writes=[bxc])
            if cc >= 16:
                nm = "BT" if cc < 24 else "CT"
                gi = cc - 16 if cc < 24 else cc - 24
                S.dma("sp", T[nm][gi * 128:(gi + 1) * 128, :], xc[:], reads=[bxc], writes=[T["b_" + nm]])
            if cc < 24:
                dname = "XS" if cc < 16 else "BTOK"
                col0 = cc * 128 if cc < 16 else (cc - 16) * 128
                for half in range(2):
                    pT, bpT = pTs[nst % 2]
                    st_, bst = tst[nst % 2]; nst += 1
                    for tb8 in range(8):
                        tb = half * 8 + tb8
                        S.op("pe", lambda e, pT=pT, xc=xc, tb=tb, tb8=tb8: e.transpose(pT[:, tb8 * 128:(tb8 + 1) * 128], xc[:, tb * 128:(tb + 1) * 128], identb[:]),
                             reads=[bxc, bident], writes=[bpT])
                    S.op("dve" if nst % 2 == 0 else "act", (lambda e, st_=st_, pT=pT: e.tensor_copy(out=st_[:], in_=pT.rearrange("p (b c) -> p b c", c=128)))
                         if nst % 2 == 0 else (lambda e, st_=st_, pT=pT: e.activation(out=st_[:], in_=pT.rearrange("p (b c) -> p b c", c=128), func=AF.Copy)),
                         reads=[bpT], writes=[bst])
                    dstv = T[dname][half * 1024:(half + 1) * 1024, col0:col0 + 128].rearrange("(b t) c -> t b c", t=128)
                    S.dma("sp", dstv, st_[:], reads=[bst], writes=[T["b_" + dname]])
    zst = [(C.sb([128, 512], F32), Buf()) for _ in range(2)]
    nz = 0
    for g in range(4):
        w, bw = wb[g % 2]
        for k in range(8):
            S.dma("pool", w[:, k, :], wsrc[:, k, g * 512:(g + 1) * 512], writes=[bw])
        for tb in range(16):
            p1, bp1 = banks[2 + tb % 4]
            for k in range(8):
                S.op("pe", lambda e, p1=p1, w=w, k=k, tb=tb: e.matmul(p1[:], lhsT=xn[:, k, 3 + tb * 128:3 + (tb + 1) * 128], rhs=w[:, k, :],
                                                                      start=(k == 0), stop=(k == 7)),
                     reads=[bw, bxn], writes=[bp1])
            z_, bz = zst[nz % 2]; nz += 1
            S.op("act", lambda e, z_=z_, p1=p1: e.activation(out=z_[:], in_=p1[:], func=AF.Silu), reads=[bp1], writes=[bz])
            S.dma("sp", T["ZS"][tb * 128:(tb + 1) * 128, g * 512:(g + 1) * 512], z_[:], reads=[bz], writes=[T["b_ZS"]])
    wdt = C.sb([128, 8, 32], BF16); bwdt = Buf()
    for k in range(8):
        S.dma("pool", wdt[:, k, :], wsrc[:, k, 6144:6176], writes=[bwdt])
    hv = C.sb([128, 3, 32], F32); bhv = Buf()
    S.dma("sp", hv[:, 0:1, :], T["dt_bias"].partition_broadcast(128), writes=[bhv])
    S.dma("sp", hv[:, 1:2, :], T["a_log"].partition_broadcast(128), writes=[bhv])
    S.op("act", lambda e: e.activation(out=hv[:, 2, :], in_=hv[:, 1, :], func=AF.Exp), reads=[bhv], writes=[bhv])
    dst_ = [(C.sb([128, 64], F32), Buf()) for _ in range(2)]
    for tb in range(16):
        p1, bp1 = banks[2 + tb % 4]
        for k in range(8):
            S.op("pe", lambda e, p1=p1, k=k, tb=tb: e.matmul(p1[:, 0:32], lhsT=xn[:, k, 3 + tb * 128:3 + (tb + 1) * 128], rhs=wdt[:, k, :],
                                                             start=(k == 0), stop=(k == 7)),
                 reads=[bwdt, bxn], writes=[bp1])
        d_, bd = dst_[tb % 2]
        S.op("dve", lambda e, d_=d_, p1=p1: e.tensor_tensor(out=d_[:, 0:32], in0=p1[:, 0:32], in1=hv[:, 0, :], op=ALU.add), reads=[bp1, bhv], writes=[bd])
        S.op("act", lambda e, d_=d_: e.activation(out=d_[:, 0:32], in_=d_[:, 0:32], func=AF.Exp), reads=[bd], writes=[bd])
        S.op("act", lambda e, d_=d_: e.activation(out=d_[:, 0:32], in_=d_[:, 0:32], func=AF.Ln, bias=1.0), reads=[bd], writes=[bd])
        S.op("dve", lambda e, d_=d_: e.scalar_tensor_tensor(out=d_[:, 32:64], in0=d_[:, 0:32], scalar=-1.0, in1=hv[:, 2, :], op0=ALU.mult, op1=ALU.mult),
             reads=[bd, bhv], writes=[bd])
        S.dma("sp", T["DTD"][tb * 128:(tb + 1) * 128, :], d_[:], reads=[bd], writes=[T["b_DTD"]])
    C.close()


def ssd_consts(C, T):
    S = C.S
    k = {}
    for nm in ("tri_incl", "tri_gt", "ntri_incl"):
        k[nm] = C.sb([128, 128], F32); k["b_" + nm] = Buf()
        S.dma("sp", k[nm][:], T[nm], writes=[k["b_" + nm]])
    return k


def phase_ssd_states(nc, T):
    C = Ctx(nc); S = C.S
    K = load_consts(C, T)
    K2 = ssd_consts(C, T)
    banks = [(C.ps([128, 512], F32), Buf()) for _ in range(8)]
    Sloc = C.sb([128, 32, 64], F32); bS = Buf()
    S.op("pool", lambda e: e.memset(Sloc[:], 0.0), writes=[bS])
    ldsum = C.sb([128, 32], F32); bld = Buf()
    S.op("pool", lambda e: e.memset(ldsum[:], 0.0), writes=[bld])
    xsb = [(C.sb([128, 32, 64], BF16), Buf()) for _ in range(2)]
    btb = [(C.sb([128, 1024], BF16), Buf()) for _ in range(2)]
    dtb = [(C.sb([128, 64], F32), Buf()) for _ in range(2)]
    smb = [(C.sb([128, 3, 32], F32), Buf()) for _ in range(2)]
    xwb = [(C.sb([128, 32, 64], BF16), Buf()) for _ in range(2)]
    stb = [(C.sb([128, 2048], F32), Buf()) for _ in range(2)]
    for c in range(16):
        xs, bxs = xsb[c % 2]; bt, bbt = btb[c % 2]; dt_, bdt = dtb[c % 2]; sm, bsm = smb[c % 2]; xw, bxw = xwb[c % 2]; st_, bst = stb[c % 2]
        rows = slice(c * 128, (c + 1) * 128)
        S.dma("sp", xs[:].rearrange("p h c -> p (h c)"), T["XS"][rows, :], reads=[T["b_XS"]], writes=[bxs])
        S.dma("act", bt[:], T["BTOK"][rows, :], reads=[T["b_BTOK"]], writes=[bbt])
        S.dma("sp", dt_[:], T["DTD"][rows, :], reads=[T["b_DTD"]], writes=[bdt])
        pa, bpa = banks[c % 2]
        S.op("pe", lambda e, pa=pa, dt_=dt_: e.matmul(pa[:, 0:32], lhsT=K2["tri_gt"][:], rhs=dt_[:, 32:64], start=True, stop=True),
             reads=[K2["b_tri_gt"], bdt], writes=[bpa])
        S.op("pe", lambda e, pa=pa, dt_=dt_: e.matmul(pa[:, 32:64], lhsT=K["ones_f"][:], rhs=dt_[:, 32:64], start=True, stop=True),
             reads=[K["bones"], bdt], writes=[bpa])
        S.op("act", lambda e, sm=sm, pa=pa: e.activation(out=sm[:, 0:2, :], in_=pa[:, 0:64].rearrange("p (a h) -> p a h", a=2), func=AF.Exp),
             reads=[bpa], writes=[bsm])
        S.op("dve", lambda e, sm=sm, dt_=dt_: e.tensor_tensor(out=sm[:, 2, :], in0=sm[:, 0, :], in1=dt_[:, 0:32], op=ALU.mult), reads=[bsm, bdt], writes=[bsm])
        S.op("dve", lambda e, xw=xw, xs=xs, sm=sm: e.tensor_tensor(out=xw[:], in0=xs[:], in1=sm[:, 2, :].unsqueeze(2).to_broadcast([128, 32, 64]), op=ALU.mult),
             reads=[bxs, bsm], writes=[bxw])
        xwf = xw[:].rearrange("p h c -> p (h c)")
        for g in range(8):
            ps_, bps = banks[2 + g // 2]
            S.op("pe", lambda e, ps_=ps_, bt=bt, g=g, xwf=xwf: e.matmul(ps_[:, (g % 2) * 256:(g % 2 + 1) * 256], lhsT=bt[:, g * 128:(g + 1) * 128],
                                                                        rhs=xwf[:, g * 256:(g + 1) * 256], start=True, stop=True),
                 reads=[bbt, bxw], writes=[bps])
        for bq in range(4):
            ps_, bps = banks[2 + bq]
            S.op("act" if bq % 2 == 0 else "dve",
                 (lambda e, st_=st_, ps_=ps_, bq=bq: e.activation(out=st_[:, bq * 512:(bq + 1) * 512], in_=ps_[:], func=AF.Copy)) if bq % 2 == 0 else
                 (lambda e, st_=st_, ps_=ps_, bq=bq: e.tensor_copy(out=st_[:, bq * 512:(bq + 1) * 512], in_=ps_[:])),
                 reads=[bps], writes=[bst])
        S.dma("sp", T["ST"][c], st_[:], reads=[bst], writes=[T["b_ST"]])
        S.op("pool", lambda e, sm=sm: e.tensor_tensor(out=Sloc[:], in0=Sloc[:], in1=sm[:, 1, :].unsqueeze(2).to_broadcast([128, 32, 64]), op=ALU.mult),
             reads=[bS, bsm], writes=[bS])
        S.op("pool", lambda e, st_=st_: e.tensor_tensor(out=Sloc[:].rearrange("p h c -> p (h c)"), in0=Sloc[:].rearrange("p h c -> p (h c)"), in1=st_[:], op=ALU.add),
             reads=[bS, bst], writes=[bS])
        S.op("dve", lambda e, pa=pa: e.tensor_tensor(out=ldsum[:], in0=pa[:, 32:64], in1=ldsum[:], op=ALU.add), reads=[bpa, bld], writes=[bld])
    S.dma("sp", T["SXa"], Sloc[:].rearrange("p h c -> p (h c)"), reads=[bS], writes=[T["b_SXa"]])
    S.dma("sp", T["SXb"], ldsum[:], reads=[bld], writes=[T["b_SXb"]])
    collective(S, "AllGather", T["SXa"], T["SXa_all"], reads=[T["b_SXa"]], writes=[T["b_SXa_all"]])
    collective(S, "AllGather", T["SXb"], T["SXb_all"], reads=[T["b_SXb"]], writes=[T["b_SXb_all"]])
    C.close()


def phase_ssd_scan(nc, T):
    C = Ctx(nc); S = C.S
    K = load_consts(C, T)
    K2 = ssd_consts(C, T)
    identb, bident = K["identb"], K["bident"]
    identf, bidentf = K["idf"], K["bidf"]
    ones_f, bones = K["ones_f"], K["bones"]
    banks = [(C.ps([128, 512], F32), Buf()) for _ in range(8)]
    prev = C.sb([128, 32, 64], F32); bprev = Buf()
    prevb = C.sb([128, 2048], BF16); bprevb = Buf()
    msk = C.sb([128, 20], F32); bmsk = Buf()
    S.dma("sp", msk[:], T["selmask"], writes=[bmsk])
    ld = C.sb([128, 4, 32], F32); bldg = Buf()
    S.dma("sp", ld[:], T["SXb_all"].rearrange("(r p) h -> p r h", p=128), reads=[T["b_SXb_all"]], writes=[bldg])
    coef = C.sb([128, 4, 32], F32); bcoef = Buf()
    for r in range(4):
        S.op("dve", lambda e, r=r: e.tensor_scalar(out=coef[:, r, :], in0=ld[:, 0, :], scalar1=msk[:, 4 + 4 * r:5 + 4 * r], scalar2=None, op0=ALU.mult),
             reads=[bldg, bmsk], writes=[bcoef])
        for r2 in range(1, 4):
            S.op("dve", lambda e, r=r, r2=r2: e.scalar_tensor_tensor(out=coef[:, r, :], in0=ld[:, r2, :], scalar=msk[:, 4 + 4 * r + r2:5 + 4 * r + r2],
                                                                     in1=coef[:, r, :], op0=ALU.mult, op1=ALU.add),
                 reads=[bldg, bmsk, bcoef], writes=[bcoef])
        S.op("act", lambda e, r=r: e.activation(out=coef[:, r, :], in_=coef[:, r, :], func=AF.Exp), reads=[bcoef], writes=[bcoef])
        S.op("dve", lambda e, r=r: e.tensor_scalar(out=coef[:, r, :], in0=coef[:, r, :], scalar1=msk[:, r:r + 1], scalar2=None, op0=ALU.mult),
             reads=[bcoef, bmsk], writes=[bcoef])
    S.op("pool", lambda e: e.memset(prev[:], 0.0), writes=[bprev])
    sg_ = [(C.sb([128, 32, 64], F32), Buf()) for _ in range(2)]
    for r in range(4):
        t_, bt_ = sg_[r % 2]
        S.dma("sp", t_[:].rearrange("p h c -> p (h c)"), T["SXa_all"][r * 128:(r + 1) * 128, :], reads=[T["b_SXa_all"]], writes=[bt_])
        S.op("dve", lambda e, t_=t_, r=r: e.tensor_tensor(out=t_[:], in0=t_[:], in1=coef[:, r, :].unsqueeze(2).to_broadcast([128, 32, 64]), op=ALU.mult),
             reads=[bt_, bcoef], writes=[bt_])
        S.op("dve", lambda e, t_=t_: e.tensor_tensor(out=prev[:], in0=prev[:], in1=t_[:], op=ALU.add), reads=[bt_, bprev], writes=[bprev])
    prevf = prev[:].rearrange("p h c -> p (h c)")
    S.op("act", lambda e: e.activation(out=prevb[:], in_=prevf, func=AF.Copy), reads=[bprev], writes=[bprevb])
    hv = C.sb([128, 32], F32); bhv = Buf()
    S.dma("sp", hv[:].unsqueeze(1), T["ssd_d"].partition_broadcast(128), writes=[bhv])
    gw = C.sb([128, 16], F32); bgw = Buf()
    S.dma("sp", gw[:], T["gnorm"], writes=[bgw])
    negut = C.sb([128, 4, 128], F32); bneg = Buf()
    negutb = C.sb([128, 4, 128], BF16); bnegb = Buf()
    for r in range(4):
        S.dma("sp", negut[:, r, :], T["neg_ut"], writes=[bneg])
    S.op("dve", lambda e: e.tensor_copy(out=negutb[:], in_=negut[:]), reads=[bneg], writes=[bnegb])
    xsb = [(C.sb([128, 32, 64], BF16), Buf()) for _ in range(2)]
    dtb = [(C.sb([128, 64], F32), Buf()) for _ in range(2)]
    btb = [(C.sb([128, 8, 128], BF16), Buf()) for _ in range(2)]
    ctb = [(C.sb([128, 8, 128], BF16), Buf()) for _ in range(2)]
    zsb = [(C.sb([128, 2048], F32), Buf()) for _ in range(2)]
    stb = [(C.sb([128, 2048], F32), Buf()) for _ in range(2)]
    dth = C.sb([128, 2, 32], BF16); bdth = Buf()
    trib = C.sb([128, 128], BF16); ntrib = C.sb([128, 128], BF16); onesb = C.sb([128, 128], BF16); btb_ = Buf()
    S.op("dve", lambda e: e.tensor_copy(out=trib[:], in_=K2["tri_incl"][:]), reads=[K2["b_tri_incl"]], writes=[btb_])
    S.op("dve", lambda e: e.tensor_copy(out=ntrib[:], in_=K2["ntri_incl"][:]), reads=[K2["b_ntri_incl"]], writes=[btb_])
    S.op("dve", lambda e: e.memset(onesb[:], 1.0), writes=[btb_])
    smb = [(C.sb([128, 3, 32], F32), Buf()) for _ in range(2)]
    xdt = C.sb([128, 32, 64], BF16); bxdt = Buf()
    Dmb = [(C.sb([128, 4, 128], F32), Buf()) for _ in range(2)]
    MTb = [(C.sb([128, 4, 128], BF16), Buf()) for _ in range(2)]
    tyb = [(C.sb([128, 4, 64], F32), Buf()) for _ in range(2)]
    yc = C.sb([128, 8, 256], F32); byc = Buf()
    gy = C.sb([128, 8, 256], F32); bgy = Buf()
    ynb = C.sb([128, 2048], BF16); byn = Buf()
    nrm = C.sb([128, 3, 8], F32); bnrm = Buf()
    junk = C.sb([128, 256], F32); bjunk = Buf()
    yTs = [(C.sb([128, 16, 128], BF16), Buf()) for _ in range(2)]
    bt_src = T["BT"].rearrange("(g n) t -> n g t", n=128)
    ct_src = T["CT"].rearrange("(g n) t -> n g t", n=128)
    yt_dst = T["YT"].rearrange("(cc p) t -> p cc t", p=128)
    pT0 = banks[6][0][:].bitcast(BF16); pT1 = banks[7][0][:].bitcast(BF16)
    half_bufs = [[Buf(), Buf()] for _ in range(3)]
    for c in range(16):
        xs, bxs = xsb[c % 2]; dt_, bdt = dtb[c % 2]; bt, bbt = btb[c % 2]; ct, bct = ctb[c % 2]
        zs, bzs = zsb[c % 2]; st_, bst = stb[c % 2]; sm, bsm = smb[c % 2]
        rows = slice(c * 128, (c + 1) * 128)
        S.dma("sp", xs[:].rearrange("p h c -> p (h c)"), T["XS"][rows, :], reads=[T["b_XS"]], writes=[bxs])
        S.dma("sp", dt_[:], T["DTD"][rows, :], reads=[T["b_DTD"]], writes=[bdt])
        S.dma("act", bt[:], bt_src[:, :, rows], reads=[T["b_BT"]], writes=[bbt])
        S.dma("act", ct[:], ct_src[:, :, rows], reads=[T["b_CT"]], writes=[bct])
        S.dma("sp", zs[:], T["ZS"][rows, :], reads=[T["b_ZS"]], writes=[bzs])
        S.dma("sp", st_[:], T["ST"][c], reads=[T["b_ST"]], writes=[bst])
        pa, bpa = banks[0]
        S.op("pe", lambda e, dt_=dt_: e.matmul(pa[:, 0:32], lhsT=K2["tri_incl"][:], rhs=dt_[:, 32:64], start=True, stop=True),
             reads=[K2["b_tri_incl"], bdt], writes=[bpa])
        S.op("pe", lambda e, dt_=dt_: e.matmul(pa[:, 32:64], lhsT=ones_f[:], rhs=dt_[:, 32:64], start=True, stop=True),
             reads=[bones, bdt], writes=[bpa])
        S.op("act", lambda e, sm=sm: e.activation(out=sm[:, 0:2, :], in_=pa[:, 0:64].rearrange("p (a h) -> p a h", a=2), func=AF.Exp),
             reads=[bpa], writes=[bsm])
        S.op("dve", lambda e, dt_=dt_: e.tensor_copy(out=dth[:, 0, :], in_=dt_[:, 32:64]), reads=[bdt], writes=[bdth])
        S.op("dve", lambda e, dt_=dt_: e.tensor_tensor(out=dth[:, 1, :], in0=dt_[:, 32:64], in1=dth[:, 0, :], op=ALU.subtract), reads=[bdt, bdth], writes=[bdth])
        S.op("dve", lambda e, xs=xs, dt_=dt_: e.tensor_tensor(out=xdt[:], in0=xs[:], in1=dt_[:, 0:32].unsqueeze(2).to_broadcast([128, 32, 64]), op=ALU.mult),
             reads=[bxs, bdt], writes=[bxdt])
        for g in range(8):
            pseg, bpseg = banks[1 + g % 2]
            Dm, bDm = Dmb[g % 2]; MT, bMT = MTb[g % 2]; ty, bty = tyb[g % 2]
            hs = slice(4 * g, 4 * g + 4)
            psv = pseg[:].rearrange("p (h l) -> p h l", h=4)
            for x_ in range(2):
                S.op("pe", lambda e, psv=psv, x_=x_, hs=hs: e.matmul(psv, lhsT=ntrib[:], rhs=dth[:, x_, hs].unsqueeze(2).to_broadcast([128, 4, 128]),
                                                                     start=(x_ == 0), stop=False),
                     reads=[btb_, bdth], writes=[bpseg])
            for r in range(4):
                for x_ in range(2):
                    S.op("pe", lambda e, pseg=pseg, x_=x_, r=r, g=g: e.matmul(pseg[:, r * 128:(r + 1) * 128], lhsT=dth[:, x_, 4 * g + r:4 * g + r + 1].to_broadcast([128, 128]),
                                                                              rhs=trib[:], start=False, stop=False),
                         reads=[btb_, bdth], writes=[bpseg])
            S.op("pe", lambda e, pseg=pseg: e.matmul(pseg[:], lhsT=identb[:], rhs=negutb[:].rearrange("p h l -> p (h l)"), start=False, stop=True),
                 reads=[bident, bnegb], writes=[bpseg])
            S.op("act", lambda e, Dm=Dm, pseg=pseg: e.activation(out=Dm[:].rearrange("p h l -> p (h l)"), in_=pseg[:], func=AF.Exp), reads=[bpseg], writes=[bDm])
            pg = banks[3][0]; bpg = half_bufs[0][g % 2]
            gsl = slice((g % 2) * 128, (g % 2 + 1) * 128)
            S.op("pe", lambda e, bt=bt, ct=ct, g=g, gsl=gsl: e.matmul(pg[:, gsl], lhsT=bt[:, g, :], rhs=ct[:, g, :], start=True, stop=True),
                 reads=[bbt, bct], writes=[bpg])
            S.op("dve", lambda e, MT=MT, Dm=Dm, gsl=gsl: e.tensor_tensor(out=MT[:], in0=Dm[:], in1=pg[:, gsl].unsqueeze(1).to_broadcast([128, 4, 128]), op=ALU.mult),
                 reads=[bDm, bpg], writes=[bMT])
            pyd = banks[4][0]; bpyd = half_bufs[1][g % 2]
            pyo = banks[5][0]; bpyo = half_bufs[2][g % 2]
            ysl = slice((g % 2) * 256, (g % 2 + 1) * 256)
            for r in range(4):
                S.op("pe", lambda e, MT=MT, r=r, g=g, ysl=ysl: e.matmul(pyd[:, ysl.start + r * 64:ysl.start + (r + 1) * 64], lhsT=MT[:, r, :], rhs=xdt[:, 4 * g + r, :],
                                                                        start=True, stop=True),
                     reads=[bMT, bxdt], writes=[bpyd])
            S.op("pe", lambda e, ct=ct, g=g, ysl=ysl: e.matmul(pyo[:, ysl], lhsT=ct[:, g, :], rhs=prevb[:, g * 256:(g + 1) * 256], start=True, stop=True),
                 reads=[bct, bprevb], writes=[bpyo])
            S.op("dve", lambda e, ty=ty, sm=sm, hs=hs, ysl=ysl: e.tensor_tensor(out=ty[:], in0=pyo[:, ysl].rearrange("p (r c) -> p r c", r=4),
                                                                                in1=sm[:, 0, hs].unsqueeze(2).to_broadcast([128, 4, 64]), op=ALU.mult),
                 reads=[bpyo, bsm], writes=[bty])
            S.op("dve", lambda e, ty=ty, g=g, ysl=ysl: e.tensor_tensor(out=yc[:, g, :], in0=pyd[:, ysl], in1=ty[:].rearrange("p r c -> p (r c)"), op=ALU.add),
                 reads=[bpyd, bty], writes=[byc])
        ycv = yc[:].rearrange("p g (r c) -> p (g r) c", r=4)
        S.op("pool", lambda e, xs=xs: e.tensor_tensor(out=gy[:].rearrange("p g (r c) -> p (g r) c", r=4), in0=xs[:], in1=hv[:].unsqueeze(2).to_broadcast([128, 32, 64]), op=ALU.mult),
             reads=[bxs, bhv], writes=[bgy])
        S.op("pool", lambda e: e.tensor_tensor(out=yc[:], in0=yc[:], in1=gy[:], op=ALU.add), reads=[byc, bgy], writes=[byc])
        S.op("dve", lambda e, zs=zs: e.tensor_tensor(out=gy[:], in0=yc[:], in1=zs[:].rearrange("p (g c) -> p g c", g=8), op=ALU.mult), reads=[byc, bzs, bgy], writes=[bgy])
        for g in range(8):
            S.op("act", lambda e, g=g: e.activation(out=junk[:], in_=gy[:, g, :], func=AF.Square, accum_out=nrm[:, 0, g:g + 1]), reads=[bgy], writes=[bjunk, bnrm])
        S.op("act", lambda e: e.activation(out=nrm[:, 1, :], in_=nrm[:, 0, :], func=AF.Ln, scale=1.0 / 256, bias=1e-5), reads=[bnrm], writes=[bnrm])
        S.op("act", lambda e: e.activation(out=nrm[:, 2, :], in_=nrm[:, 1, :], func=AF.Exp, scale=-0.5), reads=[bnrm], writes=[bnrm])
        S.op("dve", lambda e: e.tensor_tensor(out=ynb[:].rearrange("p (g c) -> p g c", g=8), in0=gy[:], in1=nrm[:, 2, :].unsqueeze(2).to_broadcast([128, 8, 256]), op=ALU.mult),
             reads=[bgy, bnrm], writes=[byn])
        yT, byT = yTs[c % 2]
        for cc in range(16):
            pT = pT0 if cc < 8 else pT1
            S.op("pe", lambda e, pT=pT, cc=cc: e.transpose(pT[:, (cc % 8) * 128:(cc % 8 + 1) * 128], ynb[:, cc * 128:(cc + 1) * 128], identb[:]),
                 reads=[byn, bident], writes=[banks[6][1] if cc < 8 else banks[7][1]])
        S.op("dve", lambda e, yT=yT: e.tensor_tensor(out=yT[:, 0:8, :], in0=pT0.rearrange("p (b c) -> p b c", c=128), in1=gw[:, 0:8].unsqueeze(2).to_broadcast([128, 8, 128]), op=ALU.mult),
             reads=[banks[6][1], bgw], writes=[byT])
        S.op("dve", lambda e, yT=yT: e.tensor_tensor(out=yT[:, 8:16, :], in0=pT1.rearrange("p (b c) -> p b c", c=128), in1=gw[:, 8:16].unsqueeze(2).to_broadcast([128, 8, 128]), op=ALU.mult),
             reads=[banks[7][1], bgw], writes=[byT])
        S.dma("sp", yt_dst[:, :, rows], yT[:], reads=[byT], writes=[T["b_YT"]])
        S.op("pool", lambda e, sm=sm: e.tensor_tensor(out=prev[:], in0=prev[:], in1=sm[:, 1, :].unsqueeze(2).to_broadcast([128, 32, 64]), op=ALU.mult),
             reads=[bprev, bsm], writes=[bprev])
        S.op("pool", lambda e, st_=st_: e.tensor_tensor(out=prevf, in0=prevf, in1=st_[:], op=ALU.add), reads=[bprev, bst], writes=[bprev])
        S.op("act", lambda e: e.activation(out=prevb[:], in_=prevf, func=AF.Copy), reads=[bprev, bprevb], writes=[bprevb])
    C.close()


def phase_ssd_out_ffn(nc, T, stage):
    C = Ctx(nc); S = C.S
    K = load_consts(C, T)
    banks = [(C.ps([128, 512], F32), Buf()) for _ in range(8)]
    hT = C.sb([128, 8, NT], F32); bh = Buf()
    load_h1_contig(C, T, hT, bh, False)
    dst = T["dbg"].rearrange("(k p) n -> p k n", p=128)
    with contextlib.ExitStack() as st2:
        yT = st2.enter_context(nc.sbuf_tensor("yTf", [128, 16, NT], BF16)); byT = Buf()
        wo = st2.enter_context(nc.sbuf_tensor("wo1", [128, 16, D], BF16)); bwo = Buf()
        ysrc = T["YT"].rearrange("(cc p) t -> p cc t", p=128)
        for cc in range(16):
            S.dma("act", yT[:, cc, :], ysrc[:, cc, :], reads=[T["b_YT"]], writes=[byT])
            S.dma("pool", wo[:, cc, :], T["ssd_w_out"][cc * 128:(cc + 1) * 128, :], writes=[bwo])
        for tt in range(4):
            sl = slice(tt * 512, (tt + 1) * 512)
            for oc in range(8):
                po, bpo = banks[oc % 4]
                for cc in range(16):
                    S.op("pe", lambda e, po=po, cc=cc, oc=oc, sl=sl: e.matmul(po[:], lhsT=wo[:, cc, oc * 128:(oc + 1) * 128], rhs=yT[:, cc, sl],
                                                                              start=(cc == 0), stop=(cc == 15)),
                         reads=[bwo, byT], writes=[bpo])
                S.op("dve", lambda e, po=po, oc=oc, sl=sl: e.tensor_tensor(out=hT[:, oc, sl], in0=po[:], in1=hT[:, oc, sl], op=ALU.add),
                     reads=[bpo, bh], writes=[bh])
        if stage == "ssd":
            for tt in range(4):
                S.dma("sp", dst[:, :, tt * 512:(tt + 1) * 512], hT[:, :, tt * 512:(tt + 1) * 512], reads=[bh])
            C.S.emit()
            st2.close(); C.st.close()
            return
        C.S.emit()
    C.S = Sched(nc); S = C.S
    Cf = Ctx(nc); Cf.S = C.S
    ffn_fm(Cf, T, 1, hT, bh, K["ones_f"], K["bones"], banks)
    C.S.emit()
    Cf.st.close()
    C.S = Sched(nc); S = C.S
    if stage != "ffn1":
        fw = C.sb([128, 8], F32); bfw = Buf()
        S.dma("sp", fw[:], T["final_norm"], writes=[bfw])
        rstd = C.sb([128, NT], F32); brstd = Buf()
        sq = [(C.sb([128, 512], F32), Buf()) for _ in range(2)]
        for tt in range(4):
            sl = slice(tt * 512, (tt + 1) * 512)
            ps, bps = banks[tt % 2]
            for k in range(8):
                q_, bq_ = sq[k % 2]
                S.op("act", lambda e, q_=q_, k=k, sl=sl: e.activation(out=q_[:], in_=hT[:, k, sl], func=AF.Square), reads=[bh], writes=[bq_])
                S.op("pe", lambda e, ps=ps, q_=q_, k=k: e.matmul(ps[:], lhsT=K["ones_f"][:], rhs=q_[:], start=(k == 0), stop=(k == 7)),
                     reads=[bq_, K["bones"]], writes=[bps])
            S.op("act", lambda e, ps=ps, sl=sl: e.activation(out=rstd[:, sl], in_=ps[:], func=AF.Ln, scale=1.0 / D, bias=1e-6), reads=[bps], writes=[brstd])
            S.op("act", lambda e, sl=sl: e.activation(out=rstd[:, sl], in_=rstd[:, sl], func=AF.Exp, scale=-0.5), reads=[brstd], writes=[brstd])
            for k in range(8):
                S.op("dve", lambda e, k=k, sl=sl: e.scalar_tensor_tensor(out=hT[:, k, sl], in0=hT[:, k, sl], scalar=fw[:, k:k + 1], in1=rstd[:, sl],
                                                                         op0=ALU.mult, op1=ALU.mult),
                     reads=[bh, bfw, brstd], writes=[bh])
    for tt in range(4):
        S.dma("sp", dst[:, :, tt * 512:(tt + 1) * 512], hT[:, :, tt * 512:(tt + 1) * 512], reads=[bh])
    C.close()


def build_program(stage):
    Buf.ALL = []
    nc = bass.Bass("TRN2", target_bir_lowering=False)
    Sched.STATE = SemState(nc)
    T = {}

    def inp(name, shape, dt=F32):
        T[name] = nc.dram_tensor(name, list(shape), dt, kind="ExternalInput").ap()

    def scr(name, shape, dt):
        T[name] = nc.dram_tensor(name, list(shape), dt).ap()
        T["b_" + name] = Buf(name)

    inp("xT0", [D, NT]); inp("cs_tab", [128, 2, NT]); inp("attn_norm", [128, 8])
    inp("w_in0", [D, 3072]); inp("w_sw0", [D, 1024]); inp("w_out0", [D, D])
    inp("nm_incl", [128, 16, 512]); inp("nm_strict", [128, 16, 512])
    inp("ident", [128, 128]); inp("nu", [128, 128])
    for nm in ("lq1", "lk1", "lq2", "lk2"):
        inp(nm, [1, 64])
    inp("subln", [1, 128])
    inp("ffn_norm", [2, 128, 8]); inp("ffn_wg", [2, D, D_FF]); inp("ffn_wu", [2, D, D_FF]); inp("ffn_wd", [2, D_FF, D])
    scr("QT", [1536, NT], BF16)
    def scr_list(name, n, shape, dt):
        T[name] = [nc.dram_tensor("%s_%d" % (name, i), list(shape), dt).ap() for i in range(n)]
        T["b_" + name] = [Buf(name) for _ in range(n)]

    scr_list("KTd", 4, [256, NT], BF16); scr_list("KTd_all", 4, [1024, NT], BF16)
    scr_list("KTs", 2, [256, NT], BF16); scr_list("KTs_all", 2, [1024, NT], BF16)
    scr_list("Vd", 4, [512, 516], BF16); scr_list("Vd_all", 4, [2048, 516], BF16)
    scr_list("Vs", 2, [1024, 512], BF16); scr_list("Vs_all", 2, [4096, 512], BF16)
    scr("mixT", [D, NT], BF16)
    scr_list("HX", 16, [64, 3 + NT], F32); scr_list("HXA", 16, [256, 3 + NT], F32)
    inp("ssd_norm", [128, 8]); inp("ssd_w_in", [D, 6176]); inp("conv_w", [128, 32, 4]); inp("conv_b", [128, 32])
    inp("dt_bias", [1, 32]); inp("a_log", [1, 32]); inp("ssd_d", [1, 32]); inp("gnorm", [128, 16]); inp("ssd_w_out", [2048, D])
    inp("final_norm", [128, 8]); inp("selmask", [128, 20])
    inp("tri_incl", [128, 128]); inp("tri_gt", [128, 128]); inp("ntri_incl", [128, 128]); inp("neg_ut", [128, 128])
    scr("XS", [NT, 2048], BF16); scr("BTOK", [NT, 1024], BF16); scr("BT", [1024, NT], BF16); scr("CT", [1024, NT], BF16)
    scr("ZS", [NT, 2048], F32); scr("DTD", [NT, 64], F32)
    T["ST"] = [nc.dram_tensor("ST_%d" % i, [128, 2048], F32).ap() for i in range(16)]; T["b_ST"] = Buf("ST")
    scr("SXa", [128, 2048], F32); scr("SXa_all", [512, 2048], F32); scr("SXb", [128, 32], F32); scr("SXb_all", [512, 32], F32)
    scr("YT", [2048, NT], BF16)
    T["dbg"] = nc.dram_tensor("dbg", [D, NT], F32, kind="ExternalOutput").ap()

    nph = int(os.environ.get("KPH", "99"))
    phase_attn_inproj(nc, T)
    if nph >= 2:
        phase_attn_core(nc, T)
    if nph >= 3:
        phase_attn_out_ffn(nc, T, stage)
    if stage not in ("mix", "attn", "ffn0"):
        if nph >= 4:
            phase_ssd_inproj(nc, T)
        if nph >= 5:
            phase_ssd_states(nc, T)
        if nph >= 6:
            phase_ssd_scan(nc, T)
        if nph >= 7:
            phase_ssd_out_ffn(nc, T, stage)
    Sched.STATE.close()
    return nc


def rope_tables(q):
    half = 32
    inv_freq = (10000.0 ** (-np.arange(half, dtype=np.float32) / half)).astype(np.float32)
    n = np.arange(NT)
    pos = ((4 * (n // 128) + q) * 128 + (n % 128)).astype(np.float32)
    ang = pos[None, :] * inv_freq[:, None]
    cos = np.cos(ang).astype(np.float32)
    sin = np.sin(ang).astype(np.float32)
    tab = np.zeros((128, 2, NT), np.float32)
    for p in range(128):
        d = p % 64
        tab[p, 0] = cos[d % 32]
        tab[p, 1] = -sin[d % 32] if d < 32 else sin[d % 32]
    return tab


def neg_masks(q):
    jj = np.arange(128)[:, None]
    tt = np.arange(128)[None, :]
    out = []
    for strict in (False, True):
        keep_diag = (jj < tt) if strict else (jj <= tt)
        m = np.zeros((128, 16, 512), np.float32)
        for kbz in range(16):
            for i in range(4):
                z = kbz - 4 * i
                blk = m[:, kbz, i * 128:(i + 1) * 128]
                if z < 0:
                    continue
                if z > 3 or z > q:
                    blk[:] = NEG
                elif z == q:
                    blk[:] = np.where(keep_diag, 0.0, NEG)
        out.append(m)
    return out


def make_in_maps(inputs):
    f = lambda a: np.ascontiguousarray(np.asarray(a, dtype=np.float32))
    x = f(inputs["x"])
    w_in = f(inputs["attn_w_in"][0])
    qk = w_in[:, :1024].reshape(D, 16, 2, 32)
    w_sw = np.ascontiguousarray(qk[:, :, ::-1, :].reshape(D, 1024))
    ar = np.arange(128)
    common = {
        "w_in0": w_in, "w_sw0": w_sw, "w_out0": f(inputs["attn_w_out"][0]),
        "attn_norm": f(inputs["attn_norm"][0].reshape(8, 128).T),
        "ident": np.eye(128, dtype=np.float32),
        "nu": -(np.arange(128)[:, None] >= np.arange(128)[None, :]).astype(np.float32),
        "lq1": f(inputs["diff_lq1"]), "lk1": f(inputs["diff_lk1"]), "lq2": f(inputs["diff_lq2"]), "lk2": f(inputs["diff_lk2"]),
        "subln": f(inputs["diff_subln"]),
        "ffn_norm": f(np.stack([inputs["ffn_norm"][l].reshape(8, 128).T for l in range(2)])),
        "ffn_wg": f(inputs["ffn_w_gate"]), "ffn_wu": f(inputs["ffn_w_up"]), "ffn_wd": f(inputs["ffn_w_down"]),
        "ssd_norm": f(inputs["ssd_norm"][0].reshape(8, 128).T), "ssd_w_in": f(inputs["ssd_w_in"][0]),
        "conv_w": f(inputs["ssd_conv_w"][0].reshape(4, 32, 128).transpose(2, 1, 0)),
        "conv_b": f(inputs["ssd_conv_b"][0].reshape(32, 128).T),
        "dt_bias": f(inputs["ssd_dt_bias"]), "a_log": f(inputs["ssd_a_log"]), "ssd_d": f(inputs["ssd_d"]),
        "gnorm": f(inputs["ssd_gnorm"][0].reshape(16, 128).T), "ssd_w_out": f(inputs["ssd_w_out"][0]),
        "final_norm": f(inputs["final_norm"].reshape(8, 128).T),
        "tri_incl": (ar[:, None] <= ar[None, :]).astype(np.float32),
        "tri_gt": (ar[:, None] > ar[None, :]).astype(np.float32),
        "ntri_incl": -(ar[:, None] <= ar[None, :]).astype(np.float32),
        "neg_ut": np.where(ar[None, :] < ar[:, None], NEG, 0.0).astype(np.float32),
    }
    maps = []
    for c in range(8):
        b, q = c // 4, c % 4
        n = np.arange(NT)
        pos = (4 * (n // 128) + q) * 128 + (n % 128)
        m = dict(common)
        m["xT0"] = np.ascontiguousarray(x[b, pos, :].T)
        m["cs_tab"] = rope_tables(q)
        mi, ms = neg_masks(q)
        m["nm_incl"], m["nm_strict"] = mi, ms
        sel = np.zeros((128, 20), np.float32)
        for r in range(4):
            sel[:, r] = 1.0 if r < q else 0.0
            for r2 in range(4):
                sel[:, 4 + 4 * r + r2] = 1.0 if (r < r2 < q) else 0.0
        m["selmask"] = sel
        maps.append(m)
    return maps


def kernel(**inputs):
    stage = os.environ.get("KSTAGE", "final")
    nc = build_program(stage)
    maps = make_in_maps(inputs)
    res = run_bass_kernel_spmd(nc, maps, core_ids=list(range(8)))
    out = np.zeros((2, SEQ, D), np.float32)
    for c in range(8):
        b, q = c // 4, c % 4
        o = res.results[c]["dbg"]
        if stage in ("mix", "attn", "ffn0"):
            n = np.arange(NT)
            pos = (4 * (n // 128) + q) * 128 + (n % 128)
            out[b, pos, :] = o.T
        else:
            out[b, q * NT:(q + 1) * NT, :] = o.T
    return out
```

```python
import contextlib
import math
import os

import numpy as np
import concourse.bass as bass
import concourse.mybir as mybir
from concourse.bass_utils import run_bass_kernel_spmd

F32 = mybir.dt.float32
BF16 = mybir.dt.bfloat16
AF = mybir.ActivationFunctionType
ALU = mybir.AluOpType

D = 1024
SEQ = 8192
NT = 2048
NEG = -30000.0
GROUPS = [[0, 1, 2, 3], [4, 5, 6, 7]]
D_FF = 2816
FH = D_FF // 2


class Buf:
    __slots__ = ("name", "last_w", "rd_c", "rd_d")
    ALL = []

    def __init__(self, name=""):
        self.name = name
        self.last_w = None
        self.rd_c = {}
        self.rd_d = []
        Buf.ALL.append(self)

    @staticmethod
    def reset_all():
        for b in Buf.ALL:
            b.last_w = None
            b.rd_c = {}
            b.rd_d = []


class _Rec:
    def __init__(self):
        self.call = None

    def __getattr__(self, name):
        def f(*a, **k):
            self.call = (name, a, k)
            return None
        return f


def _freeze(fn):
    r = _Rec()
    fn(r)
    name, a, k = r.call
    return lambda e: getattr(e, name)(*a, **k)


class SemState:
    def __init__(self, nc):
        self.nc = nc
        self.st = contextlib.ExitStack()
        self.sems = {}
        self.base = {e: 0 for e in Sched.ENGS}
        self.semval = {}
        self.bar_count = 0

    def sem(self, key):
        if key not in self.sems:
            name = key if isinstance(key, str) else "_".join(str(k) for k in key)
            self.sems[key] = self.st.enter_context(self.nc.semaphore("s_" + name))
        return self.sems[key]

    def close(self):
        self.st.close()


class Sched:
    ENGS = ("sp", "act", "dve", "pool", "pe")
    STATE = None

    def __init__(self, nc, n_dma_sems=6):
        self.nc = nc
        self.state = Sched.STATE
        self.ops = {e: [] for e in self.ENGS}
        self.nops = {e: 0 for e in self.ENGS}
        self.known = {e: {} for e in self.ENGS}
        self.needed = {e: set() for e in self.ENGS}
        self.n_dma_sems = n_dma_sems
        self.dma_ring = {e: 0 for e in self.ENGS}
        self.semval = self.state.semval
        self.dma_keys = []

    def _wait(self, eng, ev):
        if ev is None:
            return
        if ev[0] == "c":
            _, src, idx = ev
            if src == "pe" and eng == "pe":
                return
            if self.known[eng].get(src, 0) >= idx:
                return
            self.known[eng][src] = idx
            self.needed[src].add(idx)
            self.ops[eng].append(("wait", ev))
        else:
            _, key, val = ev
            if self.known[eng].get(key, 0) >= val:
                return
            self.known[eng][key] = val
            self.ops[eng].append(("wait", ev))

    def _deps(self, eng, reads, writes):
        for b in reads:
            self._wait(eng, b.last_w)
        for b in writes:
            self._wait(eng, b.last_w)
            for src, idx in list(b.rd_c.items()):
                self._wait(eng, ("c", src, idx))
            for ev in b.rd_d:
                self._wait(eng, ev)

    def _commit(self, ev, reads, writes):
        for b in reads:
            if ev[0] == "c":
                if b.rd_c.get(ev[1], 0) < ev[2]:
                    b.rd_c[ev[1]] = ev[2]
            else:
                b.rd_d.append(ev)
        for b in writes:
            b.last_w = ev
            b.rd_c = {}
            b.rd_d = []

    def op(self, eng, fn, reads=(), writes=()):
        self._deps(eng, reads, writes)
        self.nops[eng] += 1
        idx = self.nops[eng]
        ev = ("c", eng, idx)
        self.ops[eng].append(("op", idx, _freeze(fn)))
        self._commit(ev, reads, writes)
        return ev

    def dma(self, eng, out, in_, reads=(), writes=(), inc=16, fn=None, ring=None):
        if ring is None:
            i = self.dma_ring[eng]
            self.dma_ring[eng] = i + 1
            key = ("dsem", eng, i % self.n_dma_sems)
        else:
            key = ring
        if key not in self.semval:
            self.semval[key] = 0
        if key not in self.dma_keys:
            self.dma_keys.append(key)
        prev = self.semval[key]
        if prev > 0:
            self._wait(eng, ("d", key, prev))
        self._deps(eng, reads, writes)
        self.semval[key] = prev + inc
        ev = ("d", key, prev + inc)
        if fn is None:
            fn = lambda e, out=out, in_=in_: e.dma_start(out=out, in_=in_)
        self.ops[eng].append(("dma", key, fn, inc))
        self._commit(ev, reads, writes)
        return ev

    def drain(self):
        for key in self.dma_keys:
            self._wait(key[1], ("d", key, self.semval[key]))

    def emit(self):
        nc = self.nc
        stt = self.state
        self.drain()
        sems = {}
        for e in self.ENGS:
            sems[e] = stt.sem("eng_" + e)
        for k in self.dma_keys:
            sems[k] = stt.sem(k)
        bar = stt.sem("bar")
        rank = {}
        for e in self.ENGS:
            if self.nops[e] > 0:
                self.needed[e].add(self.nops[e])
            rank[e] = {idx: stt.base[e] + r + 1 for r, idx in enumerate(sorted(self.needed[e]))}
        bar_target = stt.bar_count + 5
        with nc.Block() as block:
            def run(ename):
                def body(eng):
                    for ent in self.ops[ename]:
                        if ent[0] == "wait":
                            ev = ent[1]
                            if ev[0] == "c":
                                eng.wait_ge(sems[ev[1]], rank[ev[1]][ev[2]])
                            else:
                                eng.wait_ge(sems[ev[1]], ev[2])
                        elif ent[0] == "op":
                            ins = ent[2](eng)
                            if ent[1] in rank[ename]:
                                ins.then_inc(sems[ename], 1)
                        else:
                            ins = ent[2](eng)
                            ins.then_inc(sems[ent[1]], ent[3])
                    if self.nops[ename] > 0:
                        eng.wait_ge(sems[ename], rank[ename][self.nops[ename]])
                    eng.sem_inc(bar, 1)
                    eng.wait_ge(bar, bar_target)
                return body

            block.sync(run("sp"))
            block.scalar(run("act"))
            block.vector(run("dve"))
            block.gpsimd(run("pool"))
            block.tensor(run("pe"))
        if os.environ.get("KVERB"):
            print("emit: ops", {e: self.nops[e] for e in self.ENGS}, "entries", {e: len(self.ops[e]) for e in self.ENGS}, flush=True)
        for e in self.ENGS:
            stt.base[e] += len(self.needed[e])
        stt.bar_count = bar_target
        Buf.reset_all()


class Ctx:
    def __init__(self, nc):
        self.nc = nc
        self.S = Sched(nc)
        self.st = contextlib.ExitStack()
        self.n = 0

    CNT = [0]

    def sb(self, shape, dt=F32, name=None):
        Ctx.CNT[0] += 1
        return self.st.enter_context(self.nc.sbuf_tensor(name or ("t%d" % Ctx.CNT[0]), list(shape), dt))

    def ps(self, shape, dt=F32, name=None):
        Ctx.CNT[0] += 1
        return self.st.enter_context(self.nc.psum_tensor(name or ("p%d" % Ctx.CNT[0]), list(shape), dt))

    def close(self):
        self.S.emit()
        self.st.close()


COLL_N = [0]


def collective(S, kind, src, dst, reads, writes):
    def fn(e):
        return e.collective_compute(kind, ALU.bypass, replica_groups=GROUPS, ins=[src], outs=[dst])
    COLL_N[0] += 1
    return S.dma("pool", None, None, reads=reads, writes=writes, inc=1, fn=fn, ring=("csem", "pool", COLL_N[0] % 4))


def rmsnorm_fm(C, xT, bx, wn, bwn, xn, bxn, ones_f, bones, pss, eps=1e-6, ntok=NT, sqb=None, rstd=None, brstd=None, slices=None, out_f32=False):
    S = C.S
    if slices is None:
        slices = [slice(tt * 512, (tt + 1) * 512) for tt in range(ntok // 512)]
    for tt, sl in enumerate(slices):
        wd_ = sl.stop - sl.start
        ps, bps = pss[tt % len(pss)]
        for k in range(8):
            sq, bsq = sqb[k % len(sqb)]
            S.op("act", lambda e, sq=sq, k=k, sl=sl: e.activation(out=sq[:, 0:wd_], in_=xT[:, k, sl], func=AF.Square),
                 reads=[bx], writes=[bsq])
            S.op("pe", lambda e, ps=ps, sq=sq, k=k: e.matmul(ps[:, 0:wd_], lhsT=ones_f[:], rhs=sq[:, 0:wd_], start=(k == 0), stop=(k == 7)),
                 reads=[bsq, bones], writes=[bps])
        S.op("act", lambda e, ps=ps, sl=sl: e.activation(out=rstd[:, sl], in_=ps[:, 0:wd_], func=AF.Ln, scale=1.0 / D, bias=eps),
             reads=[bps], writes=[brstd])
        S.op("act", lambda e, sl=sl: e.activation(out=rstd[:, sl], in_=rstd[:, sl], func=AF.Exp, scale=-0.5),
             reads=[brstd], writes=[brstd])
        for k in range(8):
            S.op("dve", lambda e, k=k, sl=sl: e.scalar_tensor_tensor(out=xn[:, k, sl], in0=xT[:, k, sl], scalar=wn[:, k:k + 1],
                                                                     in1=rstd[:, sl], op0=ALU.mult, op1=ALU.mult),
                 reads=[bx, bwn, brstd], writes=[bxn])


def ffn_fm(C, T, layer, hT, bh, ones_f, bones, banks):
    S = C.S
    xn = C.sb([128, 8, NT], BF16); bxn = Buf()
    rstd = C.sb([128, NT], F32); brstd = Buf()
    wn = C.sb([128, 8], F32); bwn = Buf()
    S.dma("sp", wn[:], T["ffn_norm"][layer], writes=[bwn])
    sqb = [(C.sb([128, 512], F32), Buf()) for _ in range(2)]
    rmsnorm_fm(C, hT, bh, wn, bwn, xn, bxn, ones_f, bones, banks[0:2], sqb=sqb, rstd=rstd, brstd=brstd)
    wg = C.sb([128, 8, FH], BF16); bwg = Buf()
    wu = C.sb([128, 8, FH], BF16); bwu = Buf()
    wd = C.sb([128, 11, D], BF16); bwd = Buf()
    act = [(C.sb([128, 11, 512], BF16), Buf()) for _ in range(2)]
    sg = [(C.sb([128, 512], F32), Buf()) for _ in range(2)]
    for half in range(2):
        c0 = half * FH
        for k in range(8):
            S.dma("pool", wg[:, k, :], T["ffn_wg"][layer][k * 128:(k + 1) * 128, c0:c0 + FH], writes=[bwg])
            S.dma("pool", wu[:, k, :], T["ffn_wu"][layer][k * 128:(k + 1) * 128, c0:c0 + FH], writes=[bwu])
        for fc in range(11):
            S.dma("pool", wd[:, fc, :], T["ffn_wd"][layer][c0 + fc * 128:c0 + (fc + 1) * 128, :], writes=[bwd])
        for tt in range(NT // 512):
            sl = slice(tt * 512, (tt + 1) * 512)
            a, ba = act[tt % 2]
            for fc in range(11):
                pg, bpg = banks[(2 * fc) % 4]
                pu, bpu = banks[(2 * fc + 1) % 4]
                for k in range(8):
                    S.op("pe", lambda e, pg=pg, k=k, fc=fc, sl=sl: e.matmul(pg[:], lhsT=wg[:, k, fc * 128:(fc + 1) * 128], rhs=xn[:, k, sl],
                                                                            start=(k == 0), stop=(k == 7)),
                         reads=[bwg, bxn], writes=[bpg])
                for k in range(8):
                    S.op("pe", lambda e, pu=pu, k=k, fc=fc, sl=sl: e.matmul(pu[:], lhsT=wu[:, k, fc * 128:(fc + 1) * 128], rhs=xn[:, k, sl],
                                                                            start=(k == 0), stop=(k == 7)),
                         reads=[bwu, bxn], writes=[bpu])
                s_, bs_ = sg[fc % 2]
                S.op("act", lambda e, s_=s_, pg=pg: e.activation(out=s_[:], in_=pg[:], func=AF.Silu), reads=[bpg], writes=[bs_])
                S.op("dve", lambda e, a=a, fc=fc, s_=s_, pu=pu: e.tensor_tensor(out=a[:, fc, :], in0=pu[:], in1=s_[:], op=ALU.mult),
                     reads=[bpu, bs_], writes=[ba])
            for oc in range(8):
                po, bpo = banks[4 + oc % 4]
                for fc in range(11):
                    S.op("pe", lambda e, po=po, fc=fc, oc=oc, a=a: e.matmul(po[:], lhsT=wd[:, fc, oc * 128:(oc + 1) * 128], rhs=a[:, fc, :],
                                                                            start=(fc == 0), stop=(fc == 10)),
                         reads=[bwd, ba], writes=[bpo])
                S.op("dve", lambda e, po=po, oc=oc, sl=sl: e.tensor_tensor(out=hT[:, oc, sl], in0=po[:], in1=hT[:, oc, sl], op=ALU.add),
                     reads=[bpo, bh], writes=[bh])


def load_consts(C, T):
    S = C.S
    k = {}
    idf = C.sb([128, 128], F32); bidf = Buf()
    S.dma("sp", idf[:], T["ident"], writes=[bidf])
    k["identb"] = C.sb([128, 128], BF16); k["bident"] = Buf()
    S.op("dve", lambda e: e.tensor_copy(out=k["identb"][:], in_=idf[:]), reads=[bidf], writes=[k["bident"]])
    k["ones_f"] = C.sb([128, 128], F32); k["bones"] = Buf()
    S.op("pool", lambda e: e.memset(k["ones_f"][:], 1.0), writes=[k["bones"]])
    k["idf"] = idf; k["bidf"] = bidf
    return k


def phase_attn_inproj(nc, T):
    C = Ctx(nc); S = C.S
    K = load_consts(C, T)
    banks = [(C.ps([128, 512], F32), Buf()) for _ in range(8)]
    xT = C.sb([128, 8, NT], F32); bx = Buf()
    xsrc = T["xT0"].rearrange("(k p) n -> p k n", p=128)
    for tt in range(4):
        S.dma("sp", xT[:, :, tt * 512:(tt + 1) * 512], xsrc[:, :, tt * 512:(tt + 1) * 512], writes=[bx])
    cs = C.sb([128, 2, NT], F32); bcs = Buf()
    S.dma("act", cs[:], T["cs_tab"], writes=[bcs])
    wn = C.sb([128, 8], F32); bwn = Buf()
    S.dma("act", wn[:], T["attn_norm"], writes=[bwn])
    xn = C.sb([128, 8, NT], BF16); bxn = Buf()
    rstd = C.sb([128, NT], F32); brstd = Buf()
    sqb = [(C.sb([128, 512], F32), Buf()) for _ in range(2)]
    rmsnorm_fm(C, xT, bx, wn, bwn, xn, bxn, K["ones_f"], K["bones"], banks[0:2], sqb=sqb, rstd=rstd, brstd=brstd)

    kcut = int(os.environ.get("KCUT", "9"))
    if kcut <= 1:
        C.close()
        return
    wb = [(C.sb([128, 8, 512], BF16), Buf()) for _ in range(2)]
    wsw = [(C.sb([128, 8, 512], BF16), Buf()) for _ in range(2)]
    ob = [(C.sb([128, NT], BF16), Buf()) for _ in range(2)]
    t1 = [(C.sb([128, 512], F32), Buf()) for _ in range(2)]
    t2 = [(C.sb([128, 512], F32), Buf()) for _ in range(2)]
    vst = [(C.sb([128, 4, 129], BF16), Buf()) for _ in range(2)]
    vss = [(C.sb([128, 512], BF16), Buf()) for _ in range(2)]
    for v, bv in vst:
        S.op("pool", lambda e, v=v: e.memset(v[:], 1.0), writes=[bv])
    wsrc = T["w_in0"].rearrange("(k p) c -> p k c", p=128)
    swsrc = T["w_sw0"].rearrange("(k p) c -> p k c", p=128)
    dst_fm = {0: ("QT", 0), 1: ("KTd", 0), 3: ("QT", 512), 4: ("KTs", 0)}
    nchunk = 0
    for gi, g in enumerate((1, 2, 4, 5, 0, 3)):
        w, bw = wb[gi % 2]
        for k in range(8):
            S.dma("pool", w[:, k, :], wsrc[:, k, g * 512:(g + 1) * 512], writes=[bw])
        if g < 2:
            w2, bw2 = wsw[g % 2]
            for k in range(8):
                S.dma("pool", w2[:, k, :], swsrc[:, k, g * 512:(g + 1) * 512], writes=[bw2])
        if g in dst_fm:
            dname, roff = dst_fm[g]
            for cc in range(4):
                o, bo = ob[nchunk % 2]; nchunk += 1
                for tt in range(4):
                    sl = slice(tt * 512, (tt + 1) * 512)
                    p1, bp1 = banks[2 + (2 * tt) % 4]
                    for k in range(8):
                        S.op("pe", lambda e, p1=p1, w=w, k=k, cc=cc, sl=sl: e.matmul(p1[:], lhsT=w[:, k, cc * 128:(cc + 1) * 128], rhs=xn[:, k, sl],
                                                                                     start=(k == 0), stop=(k == 7)),
                             reads=[bw, bxn], writes=[bp1])
                    if g < 2:
                        p2, bp2 = banks[2 + (2 * tt + 1) % 4]
                        for k in range(8):
                            S.op("pe", lambda e, p2=p2, w2=w2, k=k, cc=cc, sl=sl: e.matmul(p2[:], lhsT=w2[:, k, cc * 128:(cc + 1) * 128], rhs=xn[:, k, sl],
                                                                                           start=(k == 0), stop=(k == 7)),
                                 reads=[bw2, bxn], writes=[bp2])
                        a1, ba1 = t1[tt % 2]
                        a2, ba2 = t2[tt % 2]
                        S.op("dve", lambda e, a1=a1, p1=p1, sl=sl: e.tensor_tensor(out=a1[:], in0=p1[:], in1=cs[:, 0, sl], op=ALU.mult),
                             reads=[bp1, bcs], writes=[ba1])
                        S.op("dve", lambda e, a2=a2, p2=p2, sl=sl: e.tensor_tensor(out=a2[:], in0=p2[:], in1=cs[:, 1, sl], op=ALU.mult),
                             reads=[bp2, bcs], writes=[ba2])
                        S.op("pool", lambda e, o=o, a1=a1, a2=a2, sl=sl: e.tensor_tensor(out=o[:, sl], in0=a1[:], in1=a2[:], op=ALU.add),
                             reads=[ba1, ba2], writes=[bo])
                    else:
                        sc = 0.125 if g == 3 else 1.0
                        S.op("act", lambda e, o=o, p1=p1, sl=sl, sc=sc: e.activation(out=o[:, sl], in_=p1[:], func=AF.Copy, scale=sc),
                             reads=[bp1], writes=[bo])
                if dname == "QT":
                    r0 = roff + cc * 128
                    S.dma("sp", T["QT"][r0:r0 + 128, :], o[:], reads=[bo], writes=[T["b_QT"]])
                else:
                    ch, r0 = cc // 2, (cc % 2) * 128
                    S.dma("sp", T[dname][ch][r0:r0 + 128, :], o[:], reads=[bo], writes=[T["b_" + dname][ch]])
                    if cc % 2 == 1:
                        collective(S, "AllGather", T[dname][ch], T[dname + "_all"][ch], reads=[T["b_" + dname][ch]], writes=[T["b_" + dname + "_all"][ch]])
        else:
            for tb in range(16):
                p1, bp1 = banks[2 + tb % 4]
                for k in range(8):
                    S.op("pe", lambda e, p1=p1, w=w, k=k, tb=tb: e.matmul(p1[:], lhsT=xn[:, k, tb * 128:(tb + 1) * 128], rhs=w[:, k, :],
                                                                          start=(k == 0), stop=(k == 7)),
                         reads=[bw, bxn], writes=[bp1])
                if g == 2:
                    v, bv = vst[tb % 2]
                    S.op("act", lambda e, v=v, p1=p1: e.activation(out=v[:, :, 0:128], in_=p1[:].rearrange("p (h c) -> p h c", c=128), func=AF.Copy),
                         reads=[bp1], writes=[bv])
                    ch, r0 = tb // 4, (tb % 4) * 128
                    S.dma("sp", T["Vd"][ch][r0:r0 + 128, :], v[:].rearrange("p h c -> p (h c)"), reads=[bv], writes=[T["b_Vd"][ch]])
                    if tb % 4 == 3:
                        collective(S, "AllGather", T["Vd"][ch], T["Vd_all"][ch], reads=[T["b_Vd"][ch]], writes=[T["b_Vd_all"][ch]])
                else:
                    v, bv = vss[tb % 2]
                    S.op("act", lambda e, v=v, p1=p1: e.activation(out=v[:], in_=p1[:], func=AF.Copy), reads=[bp1], writes=[bv])
                    ch, r0 = tb // 8, (tb % 8) * 128
                    S.dma("sp", T["Vs"][ch][r0:r0 + 128, :], v[:], reads=[bv], writes=[T["b_Vs"][ch]])
                    if tb % 8 == 7:
                        collective(S, "AllGather", T["Vs"][ch], T["Vs_all"][ch], reads=[T["b_Vs"][ch]], writes=[T["b_Vs_all"][ch]])
    C.close()


def tile_iters(J):
    out = []
    for kb in range(16 * J + 16):
        i0 = max(0, kb // 4 - 4 * J)
        z = kb - 16 * J if kb >= 16 * J else None
        out.append((kb, kb % 4, kb // 4, i0, z))
    return out


def phase_attn_core(nc, T):
    C = Ctx(nc); S = C.S
    K = load_consts(C, T)
    identb, bident = K["identb"], K["bident"]
    ones_f, bones = K["ones_f"], K["bones"]
    psall = C.ps([128, 8, 512], F32)
    banks = [(psall[:, i, :], Buf()) for i in range(8)]
    nuf = C.sb([128, 128], F32); bnuf = Buf()
    S.dma("sp", nuf[:], T["nu"], writes=[bnuf])
    NU = C.sb([128, 128], BF16); bNU = Buf()
    S.op("dve", lambda e: e.tensor_copy(out=NU[:], in_=nuf[:]), reads=[bnuf], writes=[bNU])
    onesb = C.sb([128, 128], BF16); bonesb = Buf()
    S.op("pool", lambda e: e.memset(onesb[:], 1.0), writes=[bonesb])
    nmI = C.sb([128, 16, 512], BF16); bnmI = Buf()
    nmS = C.sb([128, 16, 512], BF16); bnmS = Buf()
    for z in range(0, 16, 4):
        S.dma("pool", nmI[:, z:z + 4, :], T["nm_incl"][:, z:z + 4, :], writes=[bnmI])
        S.dma("pool", nmS[:, z:z + 4, :], T["nm_strict"][:, z:z + 4, :], writes=[bnmS])
    lv = C.sb([128, 4, 64], F32); blv = Buf()
    for i, nm in enumerate(("lq1", "lk1", "lq2", "lk2")):
        S.dma("sp", lv[:, i:i + 1, :], T[nm].partition_broadcast(128), writes=[blv])
    lt = C.sb([128, 2, 64], F32); blt = Buf()
    S.op("dve", lambda e: e.tensor_tensor(out=lt[:, 0, :], in0=lv[:, 0, :], in1=lv[:, 1, :], op=ALU.mult), reads=[blv], writes=[blt])
    S.op("dve", lambda e: e.tensor_tensor(out=lt[:, 1, :], in0=lv[:, 2, :], in1=lv[:, 3, :], op=ALU.mult), reads=[blv], writes=[blt])
    ls = C.sb([128, 4], F32); bls = Buf()
    S.op("dve", lambda e: e.reduce_sum(out=ls[:, 0:1], in_=lt[:, 0, :], axis=mybir.AxisListType.X), reads=[blt], writes=[bls])
    S.op("dve", lambda e: e.reduce_sum(out=ls[:, 1:2], in_=lt[:, 1, :], axis=mybir.AxisListType.X), reads=[blt], writes=[bls])
    S.op("act", lambda e: e.activation(out=ls[:, 0:2], in_=ls[:, 0:2], func=AF.Exp), reads=[bls], writes=[bls])
    S.op("dve", lambda e: e.tensor_tensor(out=ls[:, 2:3], in0=ls[:, 1:2], in1=ls[:, 0:1], op=ALU.subtract), reads=[bls], writes=[bls])
    S.op("dve", lambda e: e.tensor_scalar(out=ls[:, 3:4], in0=ls[:, 2:3], scalar1=-0.2, scalar2=None, op0=ALU.add), reads=[bls], writes=[bls])
    negl = ls[:, 3:4]
    sl_ = C.sb([128, 1], F32); bsl = Buf()
    S.dma("sp", sl_[:], T["subln"].rearrange("o v -> v o"), writes=[bsl])
    S.op("dve", lambda e: e.tensor_scalar(out=sl_[:], in0=sl_[:], scalar1=0.8, scalar2=None, op0=ALU.mult), reads=[bsl], writes=[bsl])

    kbuf = [(C.sb([128, 4, NT], BF16), Buf()) for _ in range(2)]
    vbuf = [(C.sb([128, 64 * 129], BF16), Buf()) for _ in range(2)]
    qz = [[(C.sb([128, NT], BF16), Buf()) for _ in range(2)] for _ in range(2)]
    for b_ in range(2):
        S.op("pool", lambda e, b_=b_: e.memset(qz[b_][0][0][64:128, :], 0.0), writes=[qz[b_][0][1]])
        S.op("pool", lambda e, b_=b_: e.memset(qz[b_][1][0][0:64, :], 0.0), writes=[qz[b_][1][1]])
    ostg = [(C.sb([128, 512], BF16), Buf()) for _ in range(2)]
    nload = 0
    nout = 0

    Eb = [(C.sb([128, 2, 512], BF16), Buf()) for _ in range(2)]
    ep = [(C.sb([128, 512], F32), Buf()) for _ in range(5)]
    ktd_all = [a.rearrange("(r x) n -> x r n", r=4) for a in T["KTd_all"]]
    vd_all = [a.rearrange("(r j t) (h c) -> t r j h c", r=4, j=4, h=4) for a in T["Vd_all"]]
    for H in range(4):
        kt, bkt = kbuf[nload % 2]; vv, bvv = vbuf[nload % 2]; qzz = qz[nload % 2]; nload += 1
        for r in range(4):
            S.dma("sp", kt[:, r, :], ktd_all[H // 2][(H % 2) * 128:(H % 2 + 1) * 128, r, :], reads=[T["b_KTd_all"][H // 2]], writes=[bkt])
            for vc in range(4):
                b0 = r * 16 + 4 * vc
                S.dma("sp", vv[:, b0 * 129:(b0 + 4) * 129].rearrange("p (j c) -> p j c", c=129), vd_all[vc][:, r, :, H, :],
                      reads=[T["b_Vd_all"][vc]], writes=[bvv])
        for m in range(2):
            S.dma("sp", qzz[m][0][m * 64:(m + 1) * 64, :], T["QT"][H * 128 + m * 64:H * 128 + (m + 1) * 64, :], reads=[T["b_QT"]], writes=[qzz[m][1]])
        v4 = vv[:].rearrange("p (b c) -> p b c", c=129)
        for J in range(4):
            its = tile_iters(J)
            q0 = J * 512
            for bk_i in (4, 5, 6, 7):
                S.op("dve", lambda e, bk_i=bk_i: e.memset(banks[bk_i][0], 0.0), writes=[banks[bk_i][1]])

            def stage1(it, slot):
                kb, r, j, i0, z = it
                w0 = i0 * 128
                for m in range(2):
                    ps, bps = banks[2 * slot + m]
                    S.op("pe", lambda e, ps=ps, m=m, r=r, j=j, w0=w0: e.matmul(
                        ps[:, w0:512], lhsT=kt[:, r, j * 128:(j + 1) * 128], rhs=qzz[m][0][:, q0 + w0:q0 + 512],
                        start=True, stop=(z is None)), reads=[bkt, qzz[m][1]], writes=[bps])
                    if z is not None:
                        S.op("pe", lambda e, ps=ps, z=z, w0=w0: e.matmul(ps[:, w0:512], lhsT=identb[:], rhs=nmI[:, z, w0:512], start=False, stop=True),
                             reads=[bident, bnmI], writes=[bps])

            def stage2(it, slot):
                kb, r, j, i0, z = it
                w0 = i0 * 128
                E, bE = Eb[slot]
                S.op("act", lambda e, E=E, slot=slot, w0=w0: e.activation(out=E[:, :, w0:512], in_=psall[:, 2 * slot:2 * slot + 2, w0:512], func=AF.Exp, scale=0.125),
                     reads=[banks[2 * slot][1], banks[2 * slot + 1][1]], writes=[bE])

            def stage3(it, slot):
                kb, r, j, i0, z = it
                w0 = i0 * 128
                E, bE = Eb[slot]
                for m in range(2):
                    S.op("pe", lambda e, E=E, m=m, r=r, j=j, w0=w0: e.matmul(banks[4 + m][0][:, w0:512], lhsT=v4[:, r * 16 + j, 0:128], rhs=E[:, m, w0:512],
                                                                             start=False, stop=False, skip_group_check=True),
                         reads=[bE, bvv], writes=[banks[4 + m][1]])
                    S.op("pe", lambda e, E=E, m=m, w0=w0: e.matmul(banks[6 + m][0][:, w0:512], lhsT=onesb[:], rhs=E[:, m, w0:512],
                                                                   start=False, stop=False, skip_group_check=True),
                         reads=[bE, bonesb], writes=[banks[6 + m][1]])

            for n in range(len(its) + 2):
                if n < len(its):
                    stage1(its[n], n % 2)
                if 1 <= n <= len(its):
                    stage2(its[n - 1], (n - 1) % 2)
                if n >= 2:
                    stage3(its[n - 2], (n - 2) % 2)
            (r1, br1), (t1, bt1), (t2, bt2), (od, bod), (sq, bsq) = ep
            S.op("dve", lambda e: e.reciprocal(out=r1[:], in_=banks[6][0]), reads=[banks[6][1]], writes=[br1])
            S.op("dve", lambda e: e.tensor_tensor(out=t1[:], in0=banks[4][0], in1=r1[:], op=ALU.mult), reads=[banks[4][1], br1], writes=[bt1])
            S.op("dve", lambda e: e.reciprocal(out=r1[:], in_=banks[7][0]), reads=[banks[7][1], bt1], writes=[br1])
            S.op("dve", lambda e: e.tensor_tensor(out=t2[:], in0=banks[5][0], in1=r1[:], op=ALU.mult), reads=[banks[5][1], br1], writes=[bt2])
            S.op("dve", lambda e: e.scalar_tensor_tensor(out=od[:], in0=t2[:], scalar=negl, in1=t1[:], op0=ALU.mult, op1=ALU.add),
                 reads=[bt1, bt2, bls], writes=[bod])
            S.op("act", lambda e: e.activation(out=sq[:], in_=od[:], func=AF.Square), reads=[bod], writes=[bsq])
            pss, bpss = banks[0]
            S.op("pe", lambda e: e.matmul(pss, lhsT=ones_f[:], rhs=sq[:], start=True, stop=True), reads=[bones, bsq], writes=[bpss])
            S.op("act", lambda e: e.activation(out=t1[:], in_=pss, func=AF.Ln, scale=1.0 / 128, bias=1e-5), reads=[bpss, bod], writes=[bt1])
            S.op("act", lambda e: e.activation(out=t1[:], in_=t1[:], func=AF.Exp, scale=-0.5), reads=[bt1], writes=[bt1])
            o_, bo_ = ostg[nout % 2]; nout += 1
            S.op("dve", lambda e, o_=o_: e.scalar_tensor_tensor(out=o_[:], in0=od[:], scalar=sl_[:, 0:1], in1=t1[:], op0=ALU.mult, op1=ALU.mult),
                 reads=[bod, bsl, bt1], writes=[bo_])
            S.dma("sp", T["mixT"][H * 128:(H + 1) * 128, q0:q0 + 512], o_[:], reads=[bo_], writes=[T["b_mixT"]])

    eb = [(C.sb([128, 2, 512], F32), Buf()) for _ in range(2)]
    spb = [(C.sb([128, 2, 512], F32), Buf()) for _ in range(2)]
    hib = [(C.sb([128, 2, 512], BF16), Buf()) for _ in range(2)]
    lob = [(C.sb([128, 2, 512], BF16), Buf()) for _ in range(2)]
    Ab = [(C.sb([128, 2, 512], BF16), Buf()) for _ in range(2)]
    fb = [(C.sb([128, 512], F32), Buf()) for _ in range(2)]
    OT = C.sb([128, 512], F32); bOT = Buf()
    kts_all = [a.rearrange("(r x) n -> x r n", r=4) for a in T["KTs_all"]]
    vs_all = [a.rearrange("(r j t) (pr c) -> t r j pr c", r=4, j=8, pr=4) for a in T["Vs_all"]]
    for pr in range(4):
        kt, bkt = kbuf[nload % 2]; vv, bvv = vbuf[nload % 2]; qzz = qz[nload % 2]; nload += 1
        for r in range(4):
            S.dma("sp", kt[:, r, :], kts_all[pr // 2][(pr % 2) * 128:(pr % 2 + 1) * 128, r, :], reads=[T["b_KTs_all"][pr // 2]], writes=[bkt])
            for vc in range(2):
                b0 = r * 16 + 8 * vc
                S.dma("sp", vv[:, b0 * 128:(b0 + 8) * 128].rearrange("p (j c) -> p j c", c=128), vs_all[vc][:, r, :, pr, :],
                      reads=[T["b_Vs_all"][vc]], writes=[bvv])
        for m in range(2):
            S.dma("sp", qzz[m][0][m * 64:(m + 1) * 64, :], T["QT"][512 + pr * 128 + m * 64:512 + pr * 128 + (m + 1) * 64, :], reads=[T["b_QT"]], writes=[qzz[m][1]])
        v4 = vv[:, 0:64 * 128].rearrange("p (b c) -> p b c", c=128)
        for J in range(4):
            its = tile_iters(J)
            q0 = J * 512
            S.op("pool", lambda e: e.memset(OT[:], 0.0), writes=[bOT])

            def stage1(it, slot):
                kb, r, j, i0, z = it
                w0 = i0 * 128
                ee, bee = eb[slot]; sp_, bsp = spb[slot]; hi, bhi = hib[slot]; lo, blo = lob[slot]
                for hh in range(2):
                    p0 = hh * 64
                    ps, bps = banks[2 * slot + hh]
                    S.op("pe", lambda e, ps=ps, r=r, j=j, w0=w0, hh=hh: e.matmul(
                        ps[:, w0:512], lhsT=kt[:, r, j * 128:(j + 1) * 128], rhs=qzz[hh][0][:, q0 + w0:q0 + 512],
                        start=True, stop=(z is None)), reads=[bkt, qzz[hh][1]], writes=[bps])
                    if z is not None:
                        S.op("pe", lambda e, ps=ps, z=z, w0=w0: e.matmul(ps[:, w0:512], lhsT=identb[:], rhs=nmS[:, z, w0:512], start=False, stop=True),
                             reads=[bident, bnmS], writes=[bps])
                zz = psall[:, 2 * slot:2 * slot + 2, w0:512]
                bz = [banks[2 * slot][1], banks[2 * slot + 1][1]]
                S.op("act", lambda e, ee=ee, zz=zz, w0=w0: e.activation(out=ee[:, :, w0:512], in_=zz, func=AF.Exp), reads=bz, writes=[bee])
                S.op("act", lambda e, ee=ee, sp_=sp_, w0=w0: e.activation(out=sp_[:, :, w0:512], in_=ee[:, :, w0:512], func=AF.Ln, bias=1.0),
                     reads=[bee], writes=[bsp])
                S.op("dve", lambda e, hi=hi, sp_=sp_, w0=w0: e.tensor_copy(out=hi[:, :, w0:512], in_=sp_[:, :, w0:512]), reads=[bsp], writes=[bhi])
                S.op("dve", lambda e, lo=lo, hi=hi, sp_=sp_, w0=w0: e.tensor_tensor(out=lo[:, :, w0:512], in0=sp_[:, :, w0:512], in1=hi[:, :, w0:512], op=ALU.subtract),
                     reads=[bsp, bhi], writes=[blo])

            def stage2(it, slot):
                kb, r, j, i0, z = it
                w0 = i0 * 128
                hi, bhi = hib[slot]; lo, blo = lob[slot]; A, bA = Ab[slot]; f, bf_ = fb[slot]
                bz = [banks[2 * slot][1], banks[2 * slot + 1][1]]
                for hh in range(2):
                    ps, bps = banks[2 * slot + hh]
                    S.op("pe", lambda e, ps=ps, hi=hi, hh=hh, w0=w0: e.matmul(ps[:, w0:512], lhsT=NU[:], rhs=hi[:, hh, w0:512], start=False, stop=False, skip_group_check=True),
                         reads=[bNU, bhi], writes=[bps])
                    S.op("pe", lambda e, ps=ps, lo=lo, hh=hh, w0=w0: e.matmul(ps[:, w0:512], lhsT=NU[:], rhs=lo[:, hh, w0:512], start=False, stop=True, skip_group_check=True),
                         reads=[bNU, blo], writes=[bps])
                pC, bpC = banks[6 + slot]
                for hh in range(2):
                    p0 = hh * 64
                    S.op("pe", lambda e, pC=pC, hi=hi, hh=hh, p0=p0, w0=w0: e.matmul(pC[p0:p0 + 64, w0:512], lhsT=onesb[:, 0:64], rhs=hi[:, hh, w0:512], start=True, stop=False),
                         reads=[bhi, bonesb], writes=[bpC])
                    S.op("pe", lambda e, pC=pC, lo=lo, hh=hh, p0=p0, w0=w0: e.matmul(pC[p0:p0 + 64, w0:512], lhsT=onesb[:, 0:64], rhs=lo[:, hh, w0:512], start=False, stop=True),
                         reads=[blo, bonesb], writes=[bpC])
                zz = psall[:, 2 * slot:2 * slot + 2, w0:512]
                S.op("act", lambda e, A=A, zz=zz, w0=w0: e.activation(out=A[:, :, w0:512], in_=zz, func=AF.Exp), reads=bz, writes=[bA])
                S.op("act", lambda e, f=f, pC=pC, w0=w0: e.activation(out=f[:, w0:512], in_=pC[:, w0:512], func=AF.Exp, scale=-1.0), reads=[bpC], writes=[bf_])

            def stage3(it, slot):
                kb, r, j, i0, z = it
                w0 = i0 * 128
                A, bA = Ab[slot]; f, bf_ = fb[slot]
                pP, bpP = banks[4 + slot]
                for hh in range(2):
                    p0 = hh * 64
                    S.op("pe", lambda e, pP=pP, A=A, hh=hh, p0=p0, r=r, j=j, w0=w0: e.matmul(pP[p0:p0 + 64, w0:512], lhsT=v4[:, r * 16 + j, p0:p0 + 64], rhs=A[:, hh, w0:512],
                                                                                            start=True, stop=True),
                         reads=[bA, bvv], writes=[bpP])
                S.op("dve", lambda e, f=f, w0=w0: e.tensor_tensor(out=OT[:, w0:512], in0=OT[:, w0:512], in1=f[:, w0:512], op=ALU.mult), reads=[bOT, bf_], writes=[bOT])
                S.op("dve", lambda e, pP=pP, w0=w0: e.tensor_tensor(out=OT[:, w0:512], in0=pP[:, w0:512], in1=OT[:, w0:512], op=ALU.add), reads=[bOT, bpP], writes=[bOT])

            for n in range(len(its) + 2):
                if n < len(its):
                    stage1(its[n], n % 2)
                if 1 <= n <= len(its):
                    stage2(its[n - 1], (n - 1) % 2)
                if n >= 2:
                    stage3(its[n - 2], (n - 2) % 2)
            o_, bo_ = ostg[nout % 2]; nout += 1
            S.op("act", lambda e, o_=o_: e.activation(out=o_[:], in_=OT[:], func=AF.Copy), reads=[bOT], writes=[bo_])
            S.dma("sp", T["mixT"][(4 + pr) * 128:(5 + pr) * 128, q0:q0 + 512], o_[:], reads=[bo_], writes=[T["b_mixT"]])
    C.close()


def phase_attn_out_ffn(nc, T, stage):
    C = Ctx(nc); S = C.S
    K = load_consts(C, T)
    banks = [(C.ps([128, 512], F32), Buf()) for _ in range(8)]
    hT = C.sb([128, 8, NT], F32); bh = Buf()
    xsrc = T["xT0"].rearrange("(k p) n -> p k n", p=128)
    for tt in range(4):
        S.dma("sp", hT[:, :, tt * 512:(tt + 1) * 512], xsrc[:, :, tt * 512:(tt + 1) * 512], writes=[bh])
    with contextlib.ExitStack() as st2:
        mT = st2.enter_context(nc.sbuf_tensor("mTf", [128, 8, NT], BF16)); bmT = Buf()
        wo = st2.enter_context(nc.sbuf_tensor("wo", [128, 8, D], BF16)); bwo = Buf()
        msrc = T["mixT"].rearrange("(c p) n -> p c n", p=128)
        for c in range(8):
            S.dma("act", mT[:, c, :], msrc[:, c, :], reads=[T["b_mixT"]], writes=[bmT])
            S.dma("pool", wo[:, c, :], T["w_out0"][c * 128:(c + 1) * 128, :], writes=[bwo])
        for tt in range(4):
            sl = slice(tt * 512, (tt + 1) * 512)
            for oc in range(8):
                po, bpo = banks[oc % 4]
                for c in range(8):
                    S.op("pe", lambda e, po=po, c=c, oc=oc, sl=sl: e.matmul(po[:], lhsT=wo[:, c, oc * 128:(oc + 1) * 128], rhs=mT[:, c, sl],
                                                                            start=(c == 0), stop=(c == 7)),
                         reads=[bwo, bmT], writes=[bpo])
                S.op("dve", lambda e, po=po, oc=oc, sl=sl: e.tensor_tensor(out=hT[:, oc, sl], in0=po[:], in1=hT[:, oc, sl], op=ALU.add),
                     reads=[bpo, bh], writes=[bh])
        if stage == "mix":
            dst = T["dbg"].rearrange("(k p) n -> p k n", p=128)
            for c in range(8):
                S.op("dve", lambda e, c=c: e.tensor_copy(out=hT[:, c, :], in_=mT[:, c, :]), reads=[bmT, bh], writes=[bh])
            for tt in range(4):
                S.dma("sp", dst[:, :, tt * 512:(tt + 1) * 512], hT[:, :, tt * 512:(tt + 1) * 512], reads=[bh])
            C.S.emit()
            st2.close(); C.st.close()
            return
        if stage == "attn":
            dst = T["dbg"].rearrange("(k p) n -> p k n", p=128)
            for tt in range(4):
                S.dma("sp", dst[:, :, tt * 512:(tt + 1) * 512], hT[:, :, tt * 512:(tt + 1) * 512], reads=[bh])
            C.S.emit()
            st2.close(); C.st.close()
            return
        C.S.emit()
    C.S = Sched(nc); S = C.S
    ffn_fm(C, T, 0, hT, bh, K["ones_f"], K["bones"], banks)
    if stage == "ffn0":
        dst = T["dbg"].rearrange("(k p) n -> p k n", p=128)
        for tt in range(4):
            S.dma("sp", dst[:, :, tt * 512:(tt + 1) * 512], hT[:, :, tt * 512:(tt + 1) * 512], reads=[bh])
        C.close()
        return
    zc = C.sb([128, 4], F32); bzc = Buf()
    S.op("pool", lambda e: e.memset(zc[:], 0.0), writes=[bzc])
    for ch in range(16):
        k, hf = ch // 2, ch % 2
        S.dma("sp", T["HX"][ch][:, 0:3], zc[0:64, 0:3], reads=[bzc], writes=[T["b_HX"][ch]])
        S.dma("sp", T["HX"][ch][:, 3:3 + NT], hT[hf * 64:(hf + 1) * 64, k, :], reads=[bh], writes=[T["b_HX"][ch]])
        collective(S, "AllGather", T["HX"][ch], T["HXA"][ch], reads=[T["b_HX"][ch]], writes=[T["b_HXA"][ch]])
    C.close()


def load_h1_contig(C, T, x1, bx1, halo):
    S = C.S
    off = 3 if halo else 0
    cache = {}

    def rank_of(e):
        if "c" not in cache:
            cache["c"] = e.partition_id() % 4
        return cache["c"]
    for ch in range(16):
        k, hf = ch // 2, ch % 2
        p0 = hf * 64
        for r in range(4):
            dstv = x1[p0:p0 + 64, k, off:off + NT].rearrange("p (m r t) -> p m r t", m=4, r=4)[:, :, r, :]

            def fn(e, dstv=dstv, ch=ch, r=r):
                c = rank_of(e)
                src = T["HXA"][ch][r * 64:(r + 1) * 64, bass.ds(c * 512 + 3, 512)]
                return e.dma_start(out=dstv, in_=src.rearrange("p (m t) -> p m t", m=4))
            S.dma("sp", None, None, reads=[T["b_HXA"][ch]], writes=[bx1], fn=fn)
        if halo:
            def fn2(e, ch=ch, p0=p0, k=k):
                c = rank_of(e)
                src = T["HXA"][ch][3 * 64:4 * 64, bass.ds(c * 512, 3)]
                return e.dma_start(out=x1[p0:p0 + 64, k, 0:3], in_=src)
            S.dma("sp", None, None, reads=[T["b_HXA"][ch]], writes=[bx1], fn=fn2)


def phase_ssd_inproj(nc, T):
    C = Ctx(nc); S = C.S
    K = load_consts(C, T)
    identb, bident = K["identb"], K["bident"]
    banks = [(C.ps([128, 512], F32), Buf()) for _ in range(8)]
    x1 = C.sb([128, 8, 3 + NT], F32); bx1 = Buf()
    load_h1_contig(C, T, x1, bx1, True)
    wn = C.sb([128, 8], F32); bwn = Buf()
    S.dma("sp", wn[:], T["ssd_norm"], writes=[bwn])
    xn = C.sb([128, 8, 3 + NT], BF16); bxn = Buf()
    rstd = C.sb([128, 3 + NT], F32); brstd = Buf()
    sqb = [(C.sb([128, 512], F32), Buf()) for _ in range(2)]
    slices = [slice(0, 3)] + [slice(3 + tt * 512, 3 + (tt + 1) * 512) for tt in range(4)]
    rmsnorm_fm(C, x1, bx1, wn, bwn, xn, bxn, K["ones_f"], K["bones"], banks[0:2], sqb=sqb, rstd=rstd, brstd=brstd, slices=slices)

    cw = C.sb([128, 32, 4], F32); bcw = Buf()
    cb = C.sb([128, 32], F32); bcb = Buf()
    S.dma("sp", cw[:], T["conv_w"], writes=[bcw])
    S.dma("sp", cb[:], T["conv_b"], writes=[bcb])
    wb = [(C.sb([128, 8, 512], BF16), Buf()) for _ in range(2)]
    wsrc = T["ssd_w_in"].rearrange("(k p) c -> p k c", p=128)
    ub = [(C.sb([128, 3 + NT], F32), Buf()) for _ in range(2)]
    accb = [(C.sb([128, NT], F32), Buf()) for _ in range(2)]
    xcb = [(C.sb([128, NT], BF16), Buf()) for _ in range(2)]
    ctmp = C.sb([128, NT], F32); bctmp = Buf()
    tst = [(C.sb([128, 8, 128], BF16), Buf()) for _ in range(2)]
    pTs = [(banks[6][0][:].bitcast(BF16), banks[6][1]), (banks[7][0][:].bitcast(BF16), banks[7][1])]
    nst = 0
    for g in range(8):
        w, bw = wb[g % 2]
        for k in range(8):
            S.dma("pool", w[:, k, :], wsrc[:, k, 2048 + g * 512:2048 + (g + 1) * 512], writes=[bw])
        for c4 in range(4):
            cc = g * 4 + c4
            u, bu = ub[cc % 2]; acc, bacc = accb[cc % 2]; xc, bxc = xcb[cc % 2]
            veng = "dve"
            for ti, sl in enumerate(slices):
                wd_ = sl.stop - sl.start
                p1, bp1 = banks[2 + ti % 4]
                for k in range(8):
                    S.op("pe", lambda e, p1=p1, w=w, k=k, c4=c4, sl=sl, wd_=wd_: e.matmul(p1[:, 0:wd_], lhsT=w[:, k, c4 * 128:(c4 + 1) * 128], rhs=xn[:, k, sl],
                                                                                         start=(k == 0), stop=(k == 7)),
                         reads=[bw, bxn], writes=[bp1])
                S.op("act", lambda e, u=u, p1=p1, sl=sl, wd_=wd_: e.activation(out=u[:, sl], in_=p1[:, 0:wd_], func=AF.Copy), reads=[bp1], writes=[bu])
            S.op(veng, lambda e, acc=acc, u=u, cc=cc: e.tensor_scalar(out=acc[:], in0=u[:, 0:NT], scalar1=cw[:, cc, 0:1], scalar2=None, op0=ALU.mult),
                 reads=[bu, bcw], writes=[bacc])
            for tap in range(1, 4):
                if veng == "dve":
                    S.op(veng, lambda e, acc=acc, u=u, cc=cc, tap=tap: e.scalar_tensor_tensor(out=acc[:], in0=u[:, tap:tap + NT], scalar=cw[:, cc, tap:tap + 1],
                                                                                             in1=acc[:], op0=ALU.mult, op1=ALU.add),
                         reads=[bu, bcw, bacc], writes=[bacc])
                else:
                    S.op(veng, lambda e, u=u, cc=cc, tap=tap: e.tensor_scalar(out=ctmp[:], in0=u[:, tap:tap + NT], scalar1=cw[:, cc, tap:tap + 1], scalar2=None, op0=ALU.mult),
                         reads=[bu, bcw], writes=[bctmp])
                    S.op(veng, lambda e, acc=acc: e.tensor_tensor(out=acc[:], in0=acc[:], in1=ctmp[:], op=ALU.add), reads=[bacc, bctmp], writes=[bacc])
            S.op("act", lambda e, xc=xc, acc=acc, cc=cc: e.activation(out=xc[:], in_=acc[:], func=AF.Silu, bias=cb[:, cc:cc + 1]),
                 reads=[bacc, bcb], writes=[bxc])
            if cc >= 16:
                nm = "BT" if cc < 24 else "CT"
                gi = cc - 16 if cc < 24 else cc - 24
                S.dma("sp", T[nm][gi * 128:(gi + 1) * 128, :], xc[:], reads=[bxc], writes=[T["b_" + nm]])
            if cc < 24:
                dname = "XS" if cc < 16 else "BTOK"
                col0 = cc * 128 if cc < 16 else (cc - 16) * 128
                for half in range(2):
                    pT, bpT = pTs[nst % 2]
                    st_, bst = tst[nst % 2]; nst += 1
                    for tb8 in range(8):
                        tb = half * 8 + tb8
                        S.op("pe", lambda e, pT=pT, xc=xc, tb=tb, tb8=tb8: e.transpose(pT[:, tb8 * 128:(tb8 + 1) * 128], xc[:, tb * 128:(tb + 1) * 128], identb[:]),
                             reads=[bxc, bident], writes=[bpT])
                    S.op("dve" if nst % 2 == 0 else "act", (lambda e, st_=st_, pT=pT: e.tensor_copy(out=st_[:], in_=pT.rearrange("p (b c) -> p b c", c=128)))
                         if nst % 2 == 0 else (lambda e, st_=st_, pT=pT: e.activation(out=st_[:], in_=pT.rearrange("p (b c) -> p b c", c=128), func=AF.Copy)),
                         reads=[bpT], writes=[bst])
                    dstv = T[dname][half * 1024:(half + 1) * 1024, col0:col0 + 128].rearrange("(b t) c -> t b c", t=128)
                    S.dma("sp", dstv, st_[:], reads=[bst], writes=[T["b_" + dname]])
    zst = [(C.sb([128, 512], F32), Buf()) for _ in range(2)]
    nz = 0
    for g in range(4):
        w, bw = wb[g % 2]
        for k in range(8):
            S.dma("pool", w[:, k, :], wsrc[:, k, g * 512:(g + 1) * 512], writes=[bw])
        for tb in range(16):
            p1, bp1 = banks[2 + tb % 4]
            for k in range(8):
                S.op("pe", lambda e, p1=p1, w=w, k=k, tb=tb: e.matmul(p1[:], lhsT=xn[:, k, 3 + tb * 128:3 + (tb + 1) * 128], rhs=w[:, k, :],
                                                                      start=(k == 0), stop=(k == 7)),
                     reads=[bw, bxn], writes=[bp1])
            z_, bz = zst[nz % 2]; nz += 1
            S.op("act", lambda e, z_=z_, p1=p1: e.activation(out=z_[:], in_=p1[:], func=AF.Silu), reads=[bp1], writes=[bz])
            S.dma("sp", T["ZS"][tb * 128:(tb + 1) * 128, g * 512:(g + 1) * 512], z_[:], reads=[bz], writes=[T["b_ZS"]])
    wdt = C.sb([128, 8, 32], BF16); bwdt = Buf()
    for k in range(8):
        S.dma("pool", wdt[:, k, :], wsrc[:, k, 6144:6176], writes=[bwdt])
    hv = C.sb([128, 3, 32], F32); bhv = Buf()
    S.dma("sp", hv[:, 0:1, :], T["dt_bias"].partition_broadcast(128), writes=[bhv])
    S.dma("sp", hv[:, 1:2, :], T["a_log"].partition_broadcast(128), writes=[bhv])
    S.op("act", lambda e: e.activation(out=hv[:, 2, :], in_=hv[:, 1, :], func=AF.Exp), reads=[bhv], writes=[bhv])
    dst_ = [(C.sb([128, 64], F32), Buf()) for _ in range(2)]
    for tb in range(16):
        p1, bp1 = banks[2 + tb % 4]
        for k in range(8):
            S.op("pe", lambda e, p1=p1, k=k, tb=tb: e.matmul(p1[:, 0:32], lhsT=xn[:, k, 3 + tb * 128:3 + (tb + 1) * 128], rhs=wdt[:, k, :],
                                                             start=(k == 0), stop=(k == 7)),
                 reads=[bwdt, bxn], writes=[bp1])
        d_, bd = dst_[tb % 2]
        S.op("dve", lambda e, d_=d_, p1=p1: e.tensor_tensor(out=d_[:, 0:32], in0=p1[:, 0:32], in1=hv[:, 0, :], op=ALU.add), reads=[bp1, bhv], writes=[bd])
        S.op("act", lambda e, d_=d_: e.activation(out=d_[:, 0:32], in_=d_[:, 0:32], func=AF.Exp), reads=[bd], writes=[bd])
        S.op("act", lambda e, d_=d_: e.activation(out=d_[:, 0:32], in_=d_[:, 0:32], func=AF.Ln, bias=1.0), reads=[bd], writes=[bd])
        S.op("dve", lambda e, d_=d_: e.scalar_tensor_tensor(out=d_[:, 32:64], in0=d_[:, 0:32], scalar=-1.0, in1=hv[:, 2, :], op0=ALU.mult, op1=ALU.mult),
             reads=[bd, bhv], writes=[bd])
        S.dma("sp", T["DTD"][tb * 128:(tb + 1) * 128, :], d_[:], reads=[bd], writes=[T["b_DTD"]])
    C.close()


def ssd_consts(C, T):
    S = C.S
    k = {}
    for nm in ("tri_incl", "tri_gt", "ntri_incl"):
        k[nm] = C.sb([128, 128], F32); k["b_" + nm] = Buf()
        S.dma("sp", k[nm][:], T[nm], writes=[k["b_" + nm]])
    return k


def phase_ssd_states(nc, T):
    C = Ctx(nc); S = C.S
    K = load_consts(C, T)
    K2 = ssd_consts(C, T)
    banks = [(C.ps([128, 512], F32), Buf()) for _ in range(8)]
    Sloc = C.sb([128, 32, 64], F32); bS = Buf()
    S.op("pool", lambda e: e.memset(Sloc[:], 0.0), writes=[bS])
    ldsum = C.sb([128, 32], F32); bld = Buf()
    S.op("pool", lambda e: e.memset(ldsum[:], 0.0), writes=[bld])
    xsb = [(C.sb([128, 32, 64], BF16), Buf()) for _ in range(2)]
    btb = [(C.sb([128, 1024], BF16), Buf()) for _ in range(2)]
    dtb = [(C.sb([128, 64], F32), Buf()) for _ in range(2)]
    smb = [(C.sb([128, 3, 32], F32), Buf()) for _ in range(2)]
    xwb = [(C.sb([128, 32, 64], BF16), Buf()) for _ in range(2)]
    stb = [(C.sb([128, 2048], F32), Buf()) for _ in range(2)]
    for c in range(16):
        xs, bxs = xsb[c % 2]; bt, bbt = btb[c % 2]; dt_, bdt = dtb[c % 2]; sm, bsm = smb[c % 2]; xw, bxw = xwb[c % 2]; st_, bst = stb[c % 2]
        rows = slice(c * 128, (c + 1) * 128)
        S.dma("sp", xs[:].rearrange("p h c -> p (h c)"), T["XS"][rows, :], reads=[T["b_XS"]], writes=[bxs])
        S.dma("act", bt[:], T["BTOK"][rows, :], reads=[T["b_BTOK"]], writes=[bbt])
        S.dma("sp", dt_[:], T["DTD"][rows, :], reads=[T["b_DTD"]], writes=[bdt])
        pa, bpa = banks[c % 2]
        S.op("pe", lambda e, pa=pa, dt_=dt_: e.matmul(pa[:, 0:32], lhsT=K2["tri_gt"][:], rhs=dt_[:, 32:64], start=True, stop=True),
             reads=[K2["b_tri_gt"], bdt], writes=[bpa])
        S.op("pe", lambda e, pa=pa, dt_=dt_: e.matmul(pa[:, 32:64], lhsT=K["ones_f"][:], rhs=dt_[:, 32:64], start=True, stop=True),
             reads=[K["bones"], bdt], writes=[bpa])
        S.op("act", lambda e, sm=sm, pa=pa: e.activation(out=sm[:, 0:2, :], in_=pa[:, 0:64].rearrange("p (a h) -> p a h", a=2), func=AF.Exp),
             reads=[bpa], writes=[bsm])
        S.op("dve", lambda e, sm=sm, dt_=dt_: e.tensor_tensor(out=sm[:, 2, :], in0=sm[:, 0, :], in1=dt_[:, 0:32], op=ALU.mult), reads=[bsm, bdt], writes=[bsm])
        S.op("dve", lambda e, xw=xw, xs=xs, sm=sm: e.tensor_tensor(out=xw[:], in0=xs[:], in1=sm[:, 2, :].unsqueeze(2).to_broadcast([128, 32, 64]), op=ALU.mult),
             reads=[bxs, bsm], writes=[bxw])
        xwf = xw[:].rearrange("p h c -> p (h c)")
        for g in range(8):
            ps_, bps = banks[2 + g // 2]
            S.op("pe", lambda e, ps_=ps_, bt=bt, g=g, xwf=xwf: e.matmul(ps_[:, (g % 2) * 256:(g % 2 + 1) * 256], lhsT=bt[:, g * 128:(g + 1) * 128],
                                                                        rhs=xwf[:, g * 256:(g + 1) * 256], start=True, stop=True),
                 reads=[bbt, bxw], writes=[bps])
        for bq in range(4):
            ps_, bps = banks[2 + bq]
            S.op("act" if bq % 2 == 0 else "dve",
                 (lambda e, st_=st_, ps_=ps_, bq=bq: e.activation(out=st_[:, bq * 512:(bq + 1) * 512], in_=ps_[:], func=AF.Copy)) if bq % 2 == 0 else
                 (lambda e, st_=st_, ps_=ps_, bq=bq: e.tensor_copy(out=st_[:, bq * 512:(bq + 1) * 512], in_=ps_[:])),
                 reads=[bps], writes=[bst])
        S.dma("sp", T["ST"][c], st_[:], reads=[bst], writes=[T["b_ST"]])
        S.op("pool", lambda e, sm=sm: e.tensor_tensor(out=Sloc[:], in0=Sloc[:], in1=sm[:, 1, :].unsqueeze(2).to_broadcast([128, 32, 64]), op=ALU.mult),
             reads=[bS, bsm], writes=[bS])
        S.op("pool", lambda e, st_=st_: e.tensor_tensor(out=Sloc[:].rearrange("p h c -> p (h c)"), in0=Sloc[:].rearrange("p h c -> p (h c)"), in1=st_[:], op=ALU.add),
             reads=[bS, bst], writes=[bS])
        S.op("dve", lambda e, pa=pa: e.tensor_tensor(out=ldsum[:], in0=pa[:, 32:64], in1=ldsum[:], op=ALU.add), reads=[bpa, bld], writes=[bld])
    S.dma("sp", T["SXa"], Sloc[:].rearrange("p h c -> p (h c)"), reads=[bS], writes=[T["b_SXa"]])
    S.dma("sp", T["SXb"], ldsum[:], reads=[bld], writes=[T["b_SXb"]])
    collective(S, "AllGather", T["SXa"], T["SXa_all"], reads=[T["b_SXa"]], writes=[T["b_SXa_all"]])
    collective(S, "AllGather", T["SXb"], T["SXb_all"], reads=[T["b_SXb"]], writes=[T["b_SXb_all"]])
    C.close()


def phase_ssd_scan(nc, T):
    C = Ctx(nc); S = C.S
    K = load_consts(C, T)
    K2 = ssd_consts(C, T)
    identb, bident = K["identb"], K["bident"]
    identf, bidentf = K["idf"], K["bidf"]
    ones_f, bones = K["ones_f"], K["bones"]
    banks = [(C.ps([128, 512], F32), Buf()) for _ in range(8)]
    prev = C.sb([128, 32, 64], F32); bprev = Buf()
    prevb = C.sb([128, 2048], BF16); bprevb = Buf()
    msk = C.sb([128, 20], F32); bmsk = Buf()
    S.dma("sp", msk[:], T["selmask"], writes=[bmsk])
    ld = C.sb([128, 4, 32], F32); bldg = Buf()
    S.dma("sp", ld[:], T["SXb_all"].rearrange("(r p) h -> p r h", p=128), reads=[T["b_SXb_all"]], writes=[bldg])
    coef = C.sb([128, 4, 32], F32); bcoef = Buf()
    for r in range(4):
        S.op("dve", lambda e, r=r: e.tensor_scalar(out=coef[:, r, :], in0=ld[:, 0, :], scalar1=msk[:, 4 + 4 * r:5 + 4 * r], scalar2=None, op0=ALU.mult),
             reads=[bldg, bmsk], writes=[bcoef])
        for r2 in range(1, 4):
            S.op("dve", lambda e, r=r, r2=r2: e.scalar_tensor_tensor(out=coef[:, r, :], in0=ld[:, r2, :], scalar=msk[:, 4 + 4 * r + r2:5 + 4 * r + r2],
                                                                     in1=coef[:, r, :], op0=ALU.mult, op1=ALU.add),
                 reads=[bldg, bmsk, bcoef], writes=[bcoef])
        S.op("act", lambda e, r=r: e.activation(out=coef[:, r, :], in_=coef[:, r, :], func=AF.Exp), reads=[bcoef], writes=[bcoef])
        S.op("dve", lambda e, r=r: e.tensor_scalar(out=coef[:, r, :], in0=coef[:, r, :], scalar1=msk[:, r:r + 1], scalar2=None, op0=ALU.mult),
             reads=[bcoef, bmsk], writes=[bcoef])
    S.op("pool", lambda e: e.memset(prev[:], 0.0), writes=[bprev])
    sg_ = [(C.sb([128, 32, 64], F32), Buf()) for _ in range(2)]
    for r in range(4):
        t_, bt_ = sg_[r % 2]
        S.dma("sp", t_[:].rearrange("p h c -> p (h c)"), T["SXa_all"][r * 128:(r + 1) * 128, :], reads=[T["b_SXa_all"]], writes=[bt_])
        S.op("dve", lambda e, t_=t_, r=r: e.tensor_tensor(out=t_[:], in0=t_[:], in1=coef[:, r, :].unsqueeze(2).to_broadcast([128, 32, 64]), op=ALU.mult),
             reads=[bt_, bcoef], writes=[bt_])
        S.op("dve", lambda e, t_=t_: e.tensor_tensor(out=prev[:], in0=prev[:], in1=t_[:], op=ALU.add), reads=[bt_, bprev], writes=[bprev])
    prevf = prev[:].rearrange("p h c -> p (h c)")
    S.op("act", lambda e: e.activation(out=prevb[:], in_=prevf, func=AF.Copy), reads=[bprev], writes=[bprevb])
    hv = C.sb([128, 32], F32); bhv = Buf()
    S.dma("sp", hv[:].unsqueeze(1), T["ssd_d"].partition_broadcast(128), writes=[bhv])
    gw = C.sb([128, 16], F32); bgw = Buf()
    S.dma("sp", gw[:], T["gnorm"], writes=[bgw])
    negut = C.sb([128, 4, 128], F32); bneg = Buf()
    negutb = C.sb([128, 4, 128], BF16); bnegb = Buf()
    for r in range(4):
        S.dma("sp", negut[:, r, :], T["neg_ut"], writes=[bneg])
    S.op("dve", lambda e: e.tensor_copy(out=negutb[:], in_=negut[:]), reads=[bneg], writes=[bnegb])
    xsb = [(C.sb([128, 32, 64], BF16), Buf()) for _ in range(2)]
    dtb = [(C.sb([128, 64], F32), Buf()) for _ in range(2)]
    btb = [(C.sb([128, 8, 128], BF16), Buf()) for _ in range(2)]
    ctb = [(C.sb([128, 8, 128], BF16), Buf()) for _ in range(2)]
    zsb = [(C.sb([128, 2048], F32), Buf()) for _ in range(2)]
    stb = [(C.sb([128, 2048], F32), Buf()) for _ in range(2)]
    dth = C.sb([128, 2, 32], BF16); bdth = Buf()
    trib = C.sb([128, 128], BF16); ntrib = C.sb([128, 128], BF16); onesb = C.sb([128, 128], BF16); btb_ = Buf()
    S.op("dve", lambda e: e.tensor_copy(out=trib[:], in_=K2["tri_incl"][:]), reads=[K2["b_tri_incl"]], writes=[btb_])
    S.op("dve", lambda e: e.tensor_copy(out=ntrib[:], in_=K2["ntri_incl"][:]), reads=[K2["b_ntri_incl"]], writes=[btb_])
    S.op("dve", lambda e: e.memset(onesb[:], 1.0), writes=[btb_])
    smb = [(C.sb([128, 3, 32], F32), Buf()) for _ in range(2)]
    xdt = C.sb([128, 32, 64], BF16); bxdt = Buf()
    Dmb = [(C.sb([128, 4, 128], F32), Buf()) for _ in range(2)]
    MTb = [(C.sb([128, 4, 128], BF16), Buf()) for _ in range(2)]
    tyb = [(C.sb([128, 4, 64], F32), Buf()) for _ in range(2)]
    yc = C.sb([128, 8, 256], F32); byc = Buf()
    gy = C.sb([128, 8, 256], F32); bgy = Buf()
    ynb = C.sb([128, 2048], BF16); byn = Buf()
    nrm = C.sb([128, 3, 8], F32); bnrm = Buf()
    junk = C.sb([128, 256], F32); bjunk = Buf()
    yTs = [(C.sb([128, 16, 128], BF16), Buf()) for _ in range(2)]
    bt_src = T["BT"].rearrange("(g n) t -> n g t", n=128)
    ct_src = T["CT"].rearrange("(g n) t -> n g t", n=128)
    yt_dst = T["YT"].rearrange("(cc p) t -> p cc t", p=128)
    pT0 = banks[6][0][:].bitcast(BF16); pT1 = banks[7][0][:].bitcast(BF16)
    half_bufs = [[Buf(), Buf()] for _ in range(3)]
    for c in range(16):
        xs, bxs = xsb[c % 2]; dt_, bdt = dtb[c % 2]; bt, bbt = btb[c % 2]; ct, bct = ctb[c % 2]
        zs, bzs = zsb[c % 2]; st_, bst = stb[c % 2]; sm, bsm = smb[c % 2]
        rows = slice(c * 128, (c + 1) * 128)
        S.dma("sp", xs[:].rearrange("p h c -> p (h c)"), T["XS"][rows, :], reads=[T["b_XS"]], writes=[bxs])
        S.dma("sp", dt_[:], T["DTD"][rows, :], reads=[T["b_DTD"]], writes=[bdt])
        S.dma("act", bt[:], bt_src[:, :, rows], reads=[T["b_BT"]], writes=[bbt])
        S.dma("act", ct[:], ct_src[:, :, rows], reads=[T["b_CT"]], writes=[bct])
        S.dma("sp", zs[:], T["ZS"][rows, :], reads=[T["b_ZS"]], writes=[bzs])
        S.dma("sp", st_[:], T["ST"][c], reads=[T["b_ST"]], writes=[bst])
        pa, bpa = banks[0]
        S.op("pe", lambda e, dt_=dt_: e.matmul(pa[:, 0:32], lhsT=K2["tri_incl"][:], rhs=dt_[:, 32:64], start=True, stop=True),
             reads=[K2["b_tri_incl"], bdt], writes=[bpa])
        S.op("pe", lambda e, dt_=dt_: e.matmul(pa[:, 32:64], lhsT=ones_f[:], rhs=dt_[:, 32:64], start=True, stop=True),
             reads=[bones, bdt], writes=[bpa])
        S.op("act", lambda e, sm=sm: e.activation(out=sm[:, 0:2, :], in_=pa[:, 0:64].rearrange("p (a h) -> p a h", a=2), func=AF.Exp),
             reads=[bpa], writes=[bsm])
        S.op("dve", lambda e, dt_=dt_: e.tensor_copy(out=dth[:, 0, :], in_=dt_[:, 32:64]), reads=[bdt], writes=[bdth])
        S.op("dve", lambda e, dt_=dt_: e.tensor_tensor(out=dth[:, 1, :], in0=dt_[:, 32:64], in1=dth[:, 0, :], op=ALU.subtract), reads=[bdt, bdth], writes=[bdth])
        S.op("dve", lambda e, xs=xs, dt_=dt_: e.tensor_tensor(out=xdt[:], in0=xs[:], in1=dt_[:, 0:32].unsqueeze(2).to_broadcast([128, 32, 64]), op=ALU.mult),
             reads=[bxs, bdt], writes=[bxdt])
        for g in range(8):
            pseg, bpseg = banks[1 + g % 2]
            Dm, bDm = Dmb[g % 2]; MT, bMT = MTb[g % 2]; ty, bty = tyb[g % 2]
            hs = slice(4 * g, 4 * g + 4)
            psv = pseg[:].rearrange("p (h l) -> p h l", h=4)
            for x_ in range(2):
                S.op("pe", lambda e, psv=psv, x_=x_, hs=hs: e.matmul(psv, lhsT=ntrib[:], rhs=dth[:, x_, hs].unsqueeze(2).to_broadcast([128, 4, 128]),
                                                                     start=(x_ == 0), stop=False),
                     reads=[btb_, bdth], writes=[bpseg])
            for r in range(4):
                for x_ in range(2):
                    S.op("pe", lambda e, pseg=pseg, x_=x_, r=r, g=g: e.matmul(pseg[:, r * 128:(r + 1) * 128], lhsT=dth[:, x_, 4 * g + r:4 * g + r + 1].to_broadcast([128, 128]),
                                                                              rhs=trib[:], start=False, stop=False),
                         reads=[btb_, bdth], writes=[bpseg])
            S.op("pe", lambda e, pseg=pseg: e.matmul(pseg[:], lhsT=identb[:], rhs=negutb[:].rearrange("p h l -> p (h l)"), start=False, stop=True),
                 reads=[bident, bnegb], writes=[bpseg])
            S.op("act", lambda e, Dm=Dm, pseg=pseg: e.activation(out=Dm[:].rearrange("p h l -> p (h l)"), in_=pseg[:], func=AF.Exp), reads=[bpseg], writes=[bDm])
            pg = banks[3][0]; bpg = half_bufs[0][g % 2]
            gsl = slice((g % 2) * 128, (g % 2 + 1) * 128)
            S.op("pe", lambda e, bt=bt, ct=ct, g=g, gsl=gsl: e.matmul(pg[:, gsl], lhsT=bt[:, g, :], rhs=ct[:, g, :], start=True, stop=True),
                 reads=[bbt, bct], writes=[bpg])
            S.op("dve", lambda e, MT=MT, Dm=Dm, gsl=gsl: e.tensor_tensor(out=MT[:], in0=Dm[:], in1=pg[:, gsl].unsqueeze(1).to_broadcast([128, 4, 128]), op=ALU.mult),
                 reads=[bDm, bpg], writes=[bMT])
            pyd = banks[4][0]; bpyd = half_bufs[1][g % 2]
            pyo = banks[5][0]; bpyo = half_bufs[2][g % 2]
            ysl = slice((g % 2) * 256, (g % 2 + 1) * 256)
            for r in range(4):
                S.op("pe", lambda e, MT=MT, r=r, g=g, ysl=ysl: e.matmul(pyd[:, ysl.start + r * 64:ysl.start + (r + 1) * 64], lhsT=MT[:, r, :], rhs=xdt[:, 4 * g + r, :],
                                                                        start=True, stop=True),
                     reads=[bMT, bxdt], writes=[bpyd])
            S.op("pe", lambda e, ct=ct, g=g, ysl=ysl: e.matmul(pyo[:, ysl], lhsT=ct[:, g, :], rhs=prevb[:, g * 256:(g + 1) * 256], start=True, stop=True),
                 reads=[bct, bprevb], writes=[bpyo])
            S.op("dve", lambda e, ty=ty, sm=sm, hs=hs, ysl=ysl: e.tensor_tensor(out=ty[:], in0=pyo[:, ysl].rearrange("p (r c) -> p r c", r=4),
                                                                                in1=sm[:, 0, hs].unsqueeze(2).to_broadcast([128, 4, 64]), op=ALU.mult),
                 reads=[bpyo, bsm], writes=[bty])
            S.op("dve", lambda e, ty=ty, g=g, ysl=ysl: e.tensor_tensor(out=yc[:, g, :], in0=pyd[:, ysl], in1=ty[:].rearrange("p r c -> p (r c)"), op=ALU.add),
                 reads=[bpyd, bty], writes=[byc])
        ycv = yc[:].rearrange("p g (r c) -> p (g r) c", r=4)
        S.op("pool", lambda e, xs=xs: e.tensor_tensor(out=gy[:].rearrange("p g (r c) -> p (g r) c", r=4), in0=xs[:], in1=hv[:].unsqueeze(2).to_broadcast([128, 32, 64]), op=ALU.mult),
             reads=[bxs, bhv], writes=[bgy])
        S.op("pool", lambda e: e.tensor_tensor(out=yc[:], in0=yc[:], in1=gy[:], op=ALU.add), reads=[byc, bgy], writes=[byc])
        S.op("dve", lambda e, zs=zs: e.tensor_tensor(out=gy[:], in0=yc[:], in1=zs[:].rearrange("p (g c) -> p g c", g=8), op=ALU.mult), reads=[byc, bzs, bgy], writes=[bgy])
        for g in range(8):
            S.op("act", lambda e, g=g: e.activation(out=junk[:], in_=gy[:, g, :], func=AF.Square, accum_out=nrm[:, 0, g:g + 1]), reads=[bgy], writes=[bjunk, bnrm])
        S.op("act", lambda e: e.activation(out=nrm[:, 1, :], in_=nrm[:, 0, :], func=AF.Ln, scale=1.0 / 256, bias=1e-5), reads=[bnrm], writes=[bnrm])
        S.op("act", lambda e: e.activation(out=nrm[:, 2, :], in_=nrm[:, 1, :], func=AF.Exp, scale=-0.5), reads=[bnrm], writes=[bnrm])
        S.op("dve", lambda e: e.tensor_tensor(out=ynb[:].rearrange("p (g c) -> p g c", g=8), in0=gy[:], in1=nrm[:, 2, :].unsqueeze(2).to_broadcast([128, 8, 256]), op=ALU.mult),
             reads=[bgy, bnrm], writes=[byn])
        yT, byT = yTs[c % 2]
        for cc in range(16):
            pT = pT0 if cc < 8 else pT1
            S.op("pe", lambda e, pT=pT, cc=cc: e.transpose(pT[:, (cc % 8) * 128:(cc % 8 + 1) * 128], ynb[:, cc * 128:(cc + 1) * 128], identb[:]),
                 reads=[byn, bident], writes=[banks[6][1] if cc < 8 else banks[7][1]])
        S.op("dve", lambda e, yT=yT: e.tensor_tensor(out=yT[:, 0:8, :], in0=pT0.rearrange("p (b c) -> p b c", c=128), in1=gw[:, 0:8].unsqueeze(2).to_broadcast([128, 8, 128]), op=ALU.mult),
             reads=[banks[6][1], bgw], writes=[byT])
        S.op("dve", lambda e, yT=yT: e.tensor_tensor(out=yT[:, 8:16, :], in0=pT1.rearrange("p (b c) -> p b c", c=128), in1=gw[:, 8:16].unsqueeze(2).to_broadcast([128, 8, 128]), op=ALU.mult),
             reads=[banks[7][1], bgw], writes=[byT])
        S.dma("sp", yt_dst[:, :, rows], yT[:], reads=[byT], writes=[T["b_YT"]])
        S.op("pool", lambda e, sm=sm: e.tensor_tensor(out=prev[:], in0=prev[:], in1=sm[:, 1, :].unsqueeze(2).to_broadcast([128, 32, 64]), op=ALU.mult),
             reads=[bprev, bsm], writes=[bprev])
        S.op("pool", lambda e, st_=st_: e.tensor_tensor(out=prevf, in0=prevf, in1=st_[:], op=ALU.add), reads=[bprev, bst], writes=[bprev])
        S.op("act", lambda e: e.activation(out=prevb[:], in_=prevf, func=AF.Copy), reads=[bprev, bprevb], writes=[bprevb])
    C.close()


def phase_ssd_out_ffn(nc, T, stage):
    C = Ctx(nc); S = C.S
    K = load_consts(C, T)
    banks = [(C.ps([128, 512], F32), Buf()) for _ in range(8)]
    hT = C.sb([128, 8, NT], F32); bh = Buf()
    load_h1_contig(C, T, hT, bh, False)
    dst = T["dbg"].rearrange("(k p) n -> p k n", p=128)
    with contextlib.ExitStack() as st2:
        yT = st2.enter_context(nc.sbuf_tensor("yTf", [128, 16, NT], BF16)); byT = Buf()
        wo = st2.enter_context(nc.sbuf_tensor("wo1", [128, 16, D], BF16)); bwo = Buf()
        ysrc = T["YT"].rearrange("(cc p) t -> p cc t", p=128)
        for cc in range(16):
            S.dma("act", yT[:, cc, :], ysrc[:, cc, :], reads=[T["b_YT"]], writes=[byT])
            S.dma("pool", wo[:, cc, :], T["ssd_w_out"][cc * 128:(cc + 1) * 128, :], writes=[bwo])
        for tt in range(4):
            sl = slice(tt * 512, (tt + 1) * 512)
            for oc in range(8):
                po, bpo = banks[oc % 4]
                for cc in range(16):
                    S.op("pe", lambda e, po=po, cc=cc, oc=oc, sl=sl: e.matmul(po[:], lhsT=wo[:, cc, oc * 128:(oc + 1) * 128], rhs=yT[:, cc, sl],
                                                                              start=(cc == 0), stop=(cc == 15)),
                         reads=[bwo, byT], writes=[bpo])
                S.op("dve", lambda e, po=po, oc=oc, sl=sl: e.tensor_tensor(out=hT[:, oc, sl], in0=po[:], in1=hT[:, oc, sl], op=ALU.add),
                     reads=[bpo, bh], writes=[bh])
        if stage == "ssd":
            for tt in range(4):
                S.dma("sp", dst[:, :, tt * 512:(tt + 1) * 512], hT[:, :, tt * 512:(tt + 1) * 512], reads=[bh])
            C.S.emit()
            st2.close(); C.st.close()
            return
        C.S.emit()
    C.S = Sched(nc); S = C.S
    Cf = Ctx(nc); Cf.S = C.S
    ffn_fm(Cf, T, 1, hT, bh, K["ones_f"], K["bones"], banks)
    C.S.emit()
    Cf.st.close()
    C.S = Sched(nc); S = C.S
    if stage != "ffn1":
        fw = C.sb([128, 8], F32); bfw = Buf()
        S.dma("sp", fw[:], T["final_norm"], writes=[bfw])
        rstd = C.sb([128, NT], F32); brstd = Buf()
        sq = [(C.sb([128, 512], F32), Buf()) for _ in range(2)]
        for tt in range(4):
            sl = slice(tt * 512, (tt + 1) * 512)
            ps, bps = banks[tt % 2]
            for k in range(8):
                q_, bq_ = sq[k % 2]
                S.op("act", lambda e, q_=q_, k=k, sl=sl: e.activation(out=q_[:], in_=hT[:, k, sl], func=AF.Square), reads=[bh], writes=[bq_])
                S.op("pe", lambda e, ps=ps, q_=q_, k=k: e.matmul(ps[:], lhsT=K["ones_f"][:], rhs=q_[:], start=(k == 0), stop=(k == 7)),
                     reads=[bq_, K["bones"]], writes=[bps])
            S.op("act", lambda e, ps=ps, sl=sl: e.activation(out=rstd[:, sl], in_=ps[:], func=AF.Ln, scale=1.0 / D, bias=1e-6), reads=[bps], writes=[brstd])
            S.op("act", lambda e, sl=sl: e.activation(out=rstd[:, sl], in_=rstd[:, sl], func=AF.Exp, scale=-0.5), reads=[brstd], writes=[brstd])
            for k in range(8):
                S.op("dve", lambda e, k=k, sl=sl: e.scalar_tensor_tensor(out=hT[:, k, sl], in0=hT[:, k, sl], scalar=fw[:, k:k + 1], in1=rstd[:, sl],
                                                                         op0=ALU.mult, op1=ALU.mult),
                     reads=[bh, bfw, brstd], writes=[bh])
    for tt in range(4):
        S.dma("sp", dst[:, :, tt * 512:(tt + 1) * 512], hT[:, :, tt * 512:(tt + 1) * 512], reads=[bh])
    C.close()


def build_program(stage):
    Buf.ALL = []
    nc = bass.Bass("TRN2", target_bir_lowering=False)
    Sched.STATE = SemState(nc)
    T = {}

    def inp(name, shape, dt=F32):
        T[name] = nc.dram_tensor(name, list(shape), dt, kind="ExternalInput").ap()

    def scr(name, shape, dt):
        T[name] = nc.dram_tensor(name, list(shape), dt).ap()
        T["b_" + name] = Buf(name)

    inp("xT0", [D, NT]); inp("cs_tab", [128, 2, NT]); inp("attn_norm", [128, 8])
    inp("w_in0", [D, 3072]); inp("w_sw0", [D, 1024]); inp("w_out0", [D, D])
    inp("nm_incl", [128, 16, 512]); inp("nm_strict", [128, 16, 512])
    inp("ident", [128, 128]); inp("nu", [128, 128])
    for nm in ("lq1", "lk1", "lq2", "lk2"):
        inp(nm, [1, 64])
    inp("subln", [1, 128])
    inp("ffn_norm", [2, 128, 8]); inp("ffn_wg", [2, D, D_FF]); inp("ffn_wu", [2, D, D_FF]); inp("ffn_wd", [2, D_FF, D])
    scr("QT", [1536, NT], BF16)
    def scr_list(name, n, shape, dt):
        T[name] = [nc.dram_tensor("%s_%d" % (name, i), list(shape), dt).ap() for i in range(n)]
        T["b_" + name] = [Buf(name) for _ in range(n)]

    scr_list("KTd", 4, [256, NT], BF16); scr_list("KTd_all", 4, [1024, NT], BF16)
    scr_list("KTs", 2, [256, NT], BF16); scr_list("KTs_all", 2, [1024, NT], BF16)
    scr_list("Vd", 4, [512, 516], BF16); scr_list("Vd_all", 4, [2048, 516], BF16)
    scr_list("Vs", 2, [1024, 512], BF16); scr_list("Vs_all", 2, [4096, 512], BF16)
    scr("mixT", [D, NT], BF16)
    scr_list("HX", 16, [64, 3 + NT], F32); scr_list("HXA", 16, [256, 3 + NT], F32)
    inp("ssd_norm", [128, 8]); inp("ssd_w_in", [D, 6176]); inp("conv_w", [128, 32, 4]); inp("conv_b", [128, 32])
    inp("dt_bias", [1, 32]); inp("a_log", [1, 32]); inp("ssd_d", [1, 32]); inp("gnorm", [128, 16]); inp("ssd_w_out", [2048, D])
    inp("final_norm", [128, 8]); inp("selmask", [128, 20])
    inp("tri_incl", [128, 128]); inp("tri_gt", [128, 128]); inp("ntri_incl", [128, 128]); inp("neg_ut", [128, 128])
    scr("XS", [NT, 2048], BF16); scr("BTOK", [NT, 1024], BF16); scr("BT", [1024, NT], BF16); scr("CT", [1024, NT], BF16)
    scr("ZS", [NT, 2048], F32); scr("DTD", [NT, 64], F32)
    T["ST"] = [nc.dram_tensor("ST_%d" % i, [128, 2048], F32).ap() for i in range(16)]; T["b_ST"] = Buf("ST")
    scr("SXa", [128, 2048], F32); scr("SXa_all", [512, 2048], F32); scr("SXb", [128, 32], F32); scr("SXb_all", [512, 32], F32)
    scr("YT", [2048, NT], BF16)
    T["dbg"] = nc.dram_tensor("dbg", [D, NT], F32, kind="ExternalOutput").ap()

    nph = int(os.environ.get("KPH", "99"))
    phase_attn_inproj(nc, T)
    if nph >= 2:
        phase_attn_core(nc, T)
    if nph >= 3:
        phase_attn_out_ffn(nc, T, stage)
    if stage not in ("mix", "attn", "ffn0"):
        if nph >= 4:
            phase_ssd_inproj(nc, T)
        if nph >= 5:
            phase_ssd_states(nc, T)
        if nph >= 6:
            phase_ssd_scan(nc, T)
        if nph >= 7:
            phase_ssd_out_ffn(nc, T, stage)
    Sched.STATE.close()
    return nc


def rope_tables(q):
    half = 32
    inv_freq = (10000.0 ** (-np.arange(half, dtype=np.float32) / half)).astype(np.float32)
    n = np.arange(NT)
    pos = ((4 * (n // 128) + q) * 128 + (n % 128)).astype(np.float32)
    ang = pos[None, :] * inv_freq[:, None]
    cos = np.cos(ang).astype(np.float32)
    sin = np.sin(ang).astype(np.float32)
    tab = np.zeros((128, 2, NT), np.float32)
    for p in range(128):
        d = p % 64
        tab[p, 0] = cos[d % 32]
        tab[p, 1] = -sin[d % 32] if d < 32 else sin[d % 32]
    return tab


def neg_masks(q):
    jj = np.arange(128)[:, None]
    tt = np.arange(128)[None, :]
    out = []
    for strict in (False, True):
        keep_diag = (jj < tt) if strict else (jj <= tt)
        m = np.zeros((128, 16, 512), np.float32)
        for kbz in range(16):
            for i in range(4):
                z = kbz - 4 * i
                blk = m[:, kbz, i * 128:(i + 1) * 128]
                if z < 0:
                    continue
                if z > 3 or z > q:
                    blk[:] = NEG
                elif z == q:
                    blk[:] = np.where(keep_diag, 0.0, NEG)
        out.append(m)
    return out


def make_in_maps(inputs):
    f = lambda a: np.ascontiguousarray(np.asarray(a, dtype=np.float32))
    x = f(inputs["x"])
    w_in = f(inputs["attn_w_in"][0])
    qk = w_in[:, :1024].reshape(D, 16, 2, 32)
    w_sw = np.ascontiguousarray(qk[:, :, ::-1, :].reshape(D, 1024))
    ar = np.arange(128)
    common = {
        "w_in0": w_in, "w_sw0": w_sw, "w_out0": f(inputs["attn_w_out"][0]),
        "attn_norm": f(inputs["attn_norm"][0].reshape(8, 128).T),
        "ident": np.eye(128, dtype=np.float32),
        "nu": -(np.arange(128)[:, None] >= np.arange(128)[None, :]).astype(np.float32),
        "lq1": f(inputs["diff_lq1"]), "lk1": f(inputs["diff_lk1"]), "lq2": f(inputs["diff_lq2"]), "lk2": f(inputs["diff_lk2"]),
        "subln": f(inputs["diff_subln"]),
        "ffn_norm": f(np.stack([inputs["ffn_norm"][l].reshape(8, 128).T for l in range(2)])),
        "ffn_wg": f(inputs["ffn_w_gate"]), "ffn_wu": f(inputs["ffn_w_up"]), "ffn_wd": f(inputs["ffn_w_down"]),
        "ssd_norm": f(inputs["ssd_norm"][0].reshape(8, 128).T), "ssd_w_in": f(inputs["ssd_w_in"][0]),
        "conv_w": f(inputs["ssd_conv_w"][0].reshape(4, 32, 128).transpose(2, 1, 0)),
        "conv_b": f(inputs["ssd_conv_b"][0].reshape(32, 128).T),
        "dt_bias": f(inputs["ssd_dt_bias"]), "a_log": f(inputs["ssd_a_log"]), "ssd_d": f(inputs["ssd_d"]),
        "gnorm": f(inputs["ssd_gnorm"][0].reshape(16, 128).T), "ssd_w_out": f(inputs["ssd_w_out"][0]),
        "final_norm": f(inputs["final_norm"].reshape(8, 128).T),
        "tri_incl": (ar[:, None] <= ar[None, :]).astype(np.float32),
        "tri_gt": (ar[:, None] > ar[None, :]).astype(np.float32),
        "ntri_incl": -(ar[:, None] <= ar[None, :]).astype(np.float32),
        "neg_ut": np.where(ar[None, :] < ar[:, None], NEG, 0.0).astype(np.float32),
    }
    maps = []
    for c in range(8):
        b, q = c // 4, c % 4
        n = np.arange(NT)
        pos = (4 * (n // 128) + q) * 128 + (n % 128)
        m = dict(common)
        m["xT0"] = np.ascontiguousarray(x[b, pos, :].T)
        m["cs_tab"] = rope_tables(q)
        mi, ms = neg_masks(q)
        m["nm_incl"], m["nm_strict"] = mi, ms
        sel = np.zeros((128, 20), np.float32)
        for r in range(4):
            sel[:, r] = 1.0 if r < q else 0.0
            for r2 in range(4):
                sel[:, 4 + 4 * r + r2] = 1.0 if (r < r2 < q) else 0.0
        m["selmask"] = sel
        maps.append(m)
    return maps


def kernel(**inputs):
    stage = os.environ.get("KSTAGE", "final")
    nc = build_program(stage)
    maps = make_in_maps(inputs)
    res = run_bass_kernel_spmd(nc, maps, core_ids=list(range(8)))
    out = np.zeros((2, SEQ, D), np.float32)
    for c in range(8):
        b, q = c // 4, c % 4
        o = res.results[c]["dbg"]
        if stage in ("mix", "attn", "ffn0"):
            n = np.arange(NT)
            pos = (4 * (n // 128) + q) * 128 + (n % 128)
            out[b, pos, :] = o.T
        else:
            out[b, q * NT:(q + 1) * NT, :] = o.T
    return out
```

```python
import contextlib
import math
import os

import numpy as np
import concourse.bass as bass
import concourse.mybir as mybir
from concourse.bass_utils import run_bass_kernel_spmd

F32 = mybir.dt.float32
BF16 = mybir.dt.bfloat16
AF = mybir.ActivationFunctionType
ALU = mybir.AluOpType

D = 1024
SEQ = 8192
NT = 2048
NEG = -30000.0
GROUPS = [[0, 1, 2, 3], [4, 5, 6, 7]]
D_FF = 2816
FH = D_FF // 2
NX = 3 + NT
TOKSL = [slice(0, 3)] + [slice(3 + t * 512, 3 + (t + 1) * 512) for t in range(4)]


class Buf:
    __slots__ = ("name", "last_w", "rd_c", "rd_d")
    ALL = []

    def __init__(self, name=""):
        self.name = name
        self.last_w = None
        self.rd_c = {}
        self.rd_d = []
        Buf.ALL.append(self)

    @staticmethod
    def reset_all():
        for b in Buf.ALL:
            b.last_w = None
            b.rd_c = {}
            b.rd_d = []


class _Rec:
    def __init__(self):
        self.call = None

    def __getattr__(self, name):
        def f(*a, **k):
            self.call = (name, a, k)
            return None
        return f


def _freeze(fn):
    r = _Rec()
    fn(r)
    name, a, k = r.call
    return lambda e: getattr(e, name)(*a, **k)


class SemState:
    def __init__(self, nc):
        self.nc = nc
        self.st = contextlib.ExitStack()
        self.sems = {}
        self.base = {e: 0 for e in Sched.ENGS}
        self.semval = {}
        self.bar_count = 0

    def sem(self, key):
        if key not in self.sems:
            name = key if isinstance(key, str) else "_".join(str(k) for k in key)
            self.sems[key] = self.st.enter_context(self.nc.semaphore("s_" + name))
        return self.sems[key]

    def close(self):
        self.st.close()


class Sched:
    ENGS = ("sp", "act", "dve", "pool", "pe")
    STATE = None

    def __init__(self, nc, n_dma_sems=6):
        self.nc = nc
        self.state = Sched.STATE
        self.ops = {e: [] for e in self.ENGS}
        self.nops = {e: 0 for e in self.ENGS}
        self.known = {e: {} for e in self.ENGS}
        self.needed = {e: set() for e in self.ENGS}
        self.n_dma_sems = n_dma_sems
        self.dma_ring = {e: 0 for e in self.ENGS}
        self.semval = self.state.semval
        self.dma_keys = []

    def _wait(self, eng, ev):
        if ev is None:
            return
        if ev[0] == "c":
            _, src, idx = ev
            if src == "pe" and eng == "pe":
                return
            if self.known[eng].get(src, 0) >= idx:
                return
            self.known[eng][src] = idx
            self.needed[src].add(idx)
            self.ops[eng].append(("wait", ev))
        else:
            _, key, val = ev
            if self.known[eng].get(key, 0) >= val:
                return
            self.known[eng][key] = val
            self.ops[eng].append(("wait", ev))

    def _deps(self, eng, reads, writes):
        for b in reads:
            self._wait(eng, b.last_w)
        for b in writes:
            self._wait(eng, b.last_w)
            for src, idx in list(b.rd_c.items()):
                self._wait(eng, ("c", src, idx))
            for ev in b.rd_d:
                self._wait(eng, ev)

    def _commit(self, ev, reads, writes):
        for b in reads:
            if ev[0] == "c":
                if b.rd_c.get(ev[1], 0) < ev[2]:
                    b.rd_c[ev[1]] = ev[2]
            else:
                b.rd_d.append(ev)
        for b in writes:
            b.last_w = ev
            b.rd_c = {}
            b.rd_d = []

    def op(self, eng, fn, reads=(), writes=()):
        self._deps(eng, reads, writes)
        self.nops[eng] += 1
        idx = self.nops[eng]
        ev = ("c", eng, idx)
        self.ops[eng].append(("op", idx, _freeze(fn)))
        self._commit(ev, reads, writes)
        return ev

    def dma(self, eng, out, in_, reads=(), writes=(), inc=16, fn=None, ring=None):
        if ring is None:
            i = self.dma_ring[eng]
            self.dma_ring[eng] = i + 1
            key = ("dsem", eng, i % self.n_dma_sems)
        else:
            key = ring
        if key not in self.semval:
            self.semval[key] = 0
        if key not in self.dma_keys:
            self.dma_keys.append(key)
        prev = self.semval[key]
        if prev > 0:
            self._wait(eng, ("d", key, prev))
        self._deps(eng, reads, writes)
        self.semval[key] = prev + inc
        ev = ("d", key, prev + inc)
        if fn is None:
            fn = lambda e, out=out, in_=in_: e.dma_start(out=out, in_=in_)
        self.ops[eng].append(("dma", key, fn, inc))
        self._commit(ev, reads, writes)
        return ev

    def drain(self):
        for key in self.dma_keys:
            self._wait(key[1], ("d", key, self.semval[key]))

    def emit(self):
        nc = self.nc
        stt = self.state
        self.drain()
        sems = {}
        for e in self.ENGS:
            sems[e] = stt.sem("eng_" + e)
        for k in self.dma_keys:
            sems[k] = stt.sem(k)
        bar = stt.sem("bar")
        rank = {}
        for e in self.ENGS:
            if self.nops[e] > 0:
                self.needed[e].add(self.nops[e])
            rank[e] = {idx: stt.base[e] + r + 1 for r, idx in enumerate(sorted(self.needed[e]))}
        bar_target = stt.bar_count + 5
        with nc.Block() as block:
            def run(ename):
                def body(eng):
                    for ent in self.ops[ename]:
                        if ent[0] == "wait":
                            ev = ent[1]
                            if ev[0] == "c":
                                eng.wait_ge(sems[ev[1]], rank[ev[1]][ev[2]])
                            else:
                                eng.wait_ge(sems[ev[1]], ev[2])
                        elif ent[0] == "op":
                            ins = ent[2](eng)
                            if ent[1] in rank[ename]:
                                ins.then_inc(sems[ename], 1)
                        else:
                            ins = ent[2](eng)
                            ins.then_inc(sems[ent[1]], ent[3])
                    if self.nops[ename] > 0:
                        eng.wait_ge(sems[ename], rank[ename][self.nops[ename]])
                    eng.sem_inc(bar, 1)
                    eng.wait_ge(bar, bar_target)
                return body

            block.sync(run("sp"))
            block.scalar(run("act"))
            block.vector(run("dve"))
            block.gpsimd(run("pool"))
            block.tensor(run("pe"))
        if os.environ.get("KVERB"):
            print("emit: ops", {e: self.nops[e] for e in self.ENGS}, "entries", {e: len(self.ops[e]) for e in self.ENGS}, flush=True)
        for e in self.ENGS:
            stt.base[e] += len(self.needed[e])
        stt.bar_count = bar_target
        Buf.reset_all()


class Ctx:
    def __init__(self, nc):
        self.nc = nc
        self.S = Sched(nc)
        self.st = contextlib.ExitStack()
        self.n = 0

    CNT = [0]

    def sb(self, shape, dt=F32, name=None):
        Ctx.CNT[0] += 1
        return self.st.enter_context(self.nc.sbuf_tensor(name or ("t%d" % Ctx.CNT[0]), list(shape), dt))

    def ps(self, shape, dt=F32, name=None):
        Ctx.CNT[0] += 1
        return self.st.enter_context(self.nc.psum_tensor(name or ("p%d" % Ctx.CNT[0]), list(shape), dt))

    def close(self):
        self.S.emit()
        self.st.close()


COLL_N = [0]


def collective(S, kind, src, dst, reads, writes):
    def fn(e):
        return e.collective_compute(kind, ALU.bypass, replica_groups=GROUPS, ins=[src], outs=[dst])
    COLL_N[0] += 1
    return S.dma("pool", None, None, reads=reads, writes=writes, inc=1, fn=fn, ring=("csem", "pool", COLL_N[0] % 4))


def rmsnorm_fm(C, xT, bx, wn, bwn, xn, bxn, ones_f, bones, pss, eps=1e-6, ntok=NT, sqb=None, rstd=None, brstd=None, slices=None, out_f32=False):
    S = C.S
    if slices is None:
        slices = [slice(tt * 512, (tt + 1) * 512) for tt in range(ntok // 512)]
    for tt, sl in enumerate(slices):
        wd_ = sl.stop - sl.start
        ps, bps = pss[tt % len(pss)]
        for k in range(8):
            sq, bsq = sqb[k % len(sqb)]
            S.op("act", lambda e, sq=sq, k=k, sl=sl: e.activation(out=sq[:, 0:wd_], in_=xT[:, k, sl], func=AF.Square),
                 reads=[bx], writes=[bsq])
            S.op("pe", lambda e, ps=ps, sq=sq, k=k: e.matmul(ps[:, 0:wd_], lhsT=ones_f[:], rhs=sq[:, 0:wd_], start=(k == 0), stop=(k == 7)),
                 reads=[bsq, bones], writes=[bps])
        S.op("act", lambda e, ps=ps, sl=sl: e.activation(out=rstd[:, sl], in_=ps[:, 0:wd_], func=AF.Ln, scale=1.0 / D, bias=eps),
             reads=[bps], writes=[brstd])
        S.op("act", lambda e, sl=sl: e.activation(out=rstd[:, sl], in_=rstd[:, sl], func=AF.Exp, scale=-0.5),
             reads=[brstd], writes=[brstd])
        for k in range(8):
            S.op("dve", lambda e, k=k, sl=sl: e.scalar_tensor_tensor(out=xn[:, k, sl], in0=xT[:, k, sl], scalar=wn[:, k:k + 1],
                                                                     in1=rstd[:, sl], op0=ALU.mult, op1=ALU.mult),
                 reads=[bx, bwn, brstd], writes=[bxn])


def ffn_fm(C, T, layer, hT, bh, ones_f, bones, banks, slices=None):
    S = C.S
    if slices is None:
        slices = [slice(tt * 512, (tt + 1) * 512) for tt in range(NT // 512)]
    ntot = slices[-1].stop
    xn = C.sb([128, 8, ntot], BF16); bxn = Buf()
    rstd = C.sb([128, ntot], F32); brstd = Buf()
    wn = C.sb([128, 8], F32); bwn = Buf()
    S.dma("sp", wn[:], T["ffn_norm"][layer], writes=[bwn])
    sqb = [(C.sb([128, 512], F32), Buf()) for _ in range(2)]
    rmsnorm_fm(C, hT, bh, wn, bwn, xn, bxn, ones_f, bones, banks[0:2], sqb=sqb, rstd=rstd, brstd=brstd, slices=slices)
    wg = C.sb([128, 8, FH], BF16); bwg = Buf()
    wu = C.sb([128, 8, FH], BF16); bwu = Buf()
    wd = C.sb([128, 11, D], BF16); bwd = Buf()
    act = [(C.sb([128, 11, 512], BF16), Buf()) for _ in range(2)]
    sg = [(C.sb([128, 512], F32), Buf()) for _ in range(2)]
    for half in range(2):
        c0 = half * FH
        for k in range(8):
            S.dma("pool", wg[:, k, :], T["ffn_wg"][layer][k * 128:(k + 1) * 128, c0:c0 + FH], writes=[bwg])
            S.dma("pool", wu[:, k, :], T["ffn_wu"][layer][k * 128:(k + 1) * 128, c0:c0 + FH], writes=[bwu])
        for fc in range(11):
            S.dma("pool", wd[:, fc, :], T["ffn_wd"][layer][c0 + fc * 128:c0 + (fc + 1) * 128, :], writes=[bwd])
        for tt, sl in enumerate(slices):
            wd_ = sl.stop - sl.start
            a, ba = act[tt % 2]
            for fc in range(11):
                pg, bpg = banks[(2 * fc) % 4]
                pu, bpu = banks[(2 * fc + 1) % 4]
                for k in range(8):
                    S.op("pe", lambda e, pg=pg, k=k, fc=fc, sl=sl: e.matmul(pg[:, 0:wd_], lhsT=wg[:, k, fc * 128:(fc + 1) * 128], rhs=xn[:, k, sl],
                                                                            start=(k == 0), stop=(k == 7)),
                         reads=[bwg, bxn], writes=[bpg])
                for k in range(8):
                    S.op("pe", lambda e, pu=pu, k=k, fc=fc, sl=sl: e.matmul(pu[:, 0:wd_], lhsT=wu[:, k, fc * 128:(fc + 1) * 128], rhs=xn[:, k, sl],
                                                                            start=(k == 0), stop=(k == 7)),
                         reads=[bwu, bxn], writes=[bpu])
                s_, bs_ = sg[fc % 2]
                S.op("act", lambda e, s_=s_, pg=pg: e.activation(out=s_[:, 0:wd_], in_=pg[:, 0:wd_], func=AF.Silu), reads=[bpg], writes=[bs_])
                S.op("dve", lambda e, a=a, fc=fc, s_=s_, pu=pu: e.tensor_tensor(out=a[:, fc, 0:wd_], in0=pu[:, 0:wd_], in1=s_[:, 0:wd_], op=ALU.mult),
                     reads=[bpu, bs_], writes=[ba])
            for oc in range(8):
                po, bpo = banks[4 + oc % 4]
                for fc in range(11):
                    S.op("pe", lambda e, po=po, fc=fc, oc=oc, a=a: e.matmul(po[:, 0:wd_], lhsT=wd[:, fc, oc * 128:(oc + 1) * 128], rhs=a[:, fc, 0:wd_],
                                                                            start=(fc == 0), stop=(fc == 10)),
                         reads=[bwd, ba], writes=[bpo])
                S.op("dve", lambda e, po=po, oc=oc, sl=sl: e.tensor_tensor(out=hT[:, oc, sl], in0=po[:, 0:wd_], in1=hT[:, oc, sl], op=ALU.add),
                     reads=[bpo, bh], writes=[bh])


def load_consts(C, T):
    S = C.S
    k = {}
    idf = C.sb([128, 128], F32); bidf = Buf()
    S.dma("sp", idf[:], T["ident"], writes=[bidf])
    k["identb"] = C.sb([128, 128], BF16); k["bident"] = Buf()
    S.op("dve", lambda e: e.tensor_copy(out=k["identb"][:], in_=idf[:]), reads=[bidf], writes=[k["bident"]])
    k["ones_f"] = C.sb([128, 128], F32); k["bones"] = Buf()
    S.op("pool", lambda e: e.memset(k["ones_f"][:], 1.0), writes=[k["bones"]])
    k["idf"] = idf; k["bidf"] = bidf
    return k


def phase_attn_inproj(nc, T):
    C = Ctx(nc); S = C.S
    K = load_consts(C, T)
    banks = [(C.ps([128, 512], F32), Buf()) for _ in range(8)]
    xT = C.sb([128, 8, NT], F32); bx = Buf()
    xsrc = T["xT0"].rearrange("(k p) n -> p k n", p=128)
    for tt in range(4):
        S.dma("sp", xT[:, :, tt * 512:(tt + 1) * 512], xsrc[:, :, tt * 512:(tt + 1) * 512], writes=[bx])
    cs = C.sb([128, 2, NT], F32); bcs = Buf()
    S.dma("act", cs[:], T["cs_tab"], writes=[bcs])
    wn = C.sb([128, 8], F32); bwn = Buf()
    S.dma("act", wn[:], T["attn_norm"], writes=[bwn])
    xn = C.sb([128, 8, NT], BF16); bxn = Buf()
    rstd = C.sb([128, NT], F32); brstd = Buf()
    sqb = [(C.sb([128, 512], F32), Buf()) for _ in range(2)]
    rmsnorm_fm(C, xT, bx, wn, bwn, xn, bxn, K["ones_f"], K["bones"], banks[0:2], sqb=sqb, rstd=rstd, brstd=brstd)

    kcut = int(os.environ.get("KCUT", "9"))
    if kcut <= 1:
        C.close()
        return
    wb = [(C.sb([128, 8, 512], BF16), Buf()) for _ in range(2)]
    wsw = [(C.sb([128, 8, 512], BF16), Buf()) for _ in range(2)]
    ob = [(C.sb([128, NT], BF16), Buf()) for _ in range(2)]
    t1 = [(C.sb([128, 512], F32), Buf()) for _ in range(2)]
    t2 = [(C.sb([128, 512], F32), Buf()) for _ in range(2)]
    vst = [(C.sb([128, 4, 129], BF16), Buf()) for _ in range(2)]
    vss = [(C.sb([128, 512], BF16), Buf()) for _ in range(2)]
    for v, bv in vst:
        S.op("pool", lambda e, v=v: e.memset(v[:], 1.0), writes=[bv])
    wsrc = T["w_in0"].rearrange("(k p) c -> p k c", p=128)
    swsrc = T["w_sw0"].rearrange("(k p) c -> p k c", p=128)
    dst_fm = {0: ("QT", 0), 1: ("KTd", 0), 3: ("QT", 512), 4: ("KTs", 0)}
    nchunk = 0
    for gi, g in enumerate((1, 2, 4, 5, 0, 3)):
        w, bw = wb[gi % 2]
        for k in range(8):
            S.dma("pool", w[:, k, :], wsrc[:, k, g * 512:(g + 1) * 512], writes=[bw])
        if g < 2:
            w2, bw2 = wsw[g % 2]
            for k in range(8):
                S.dma("pool", w2[:, k, :], swsrc[:, k, g * 512:(g + 1) * 512], writes=[bw2])
        if g in dst_fm:
            dname, roff = dst_fm[g]
            for cc in range(4):
                o, bo = ob[nchunk % 2]; nchunk += 1
                for tt in range(4):
                    sl = slice(tt * 512, (tt + 1) * 512)
                    p1, bp1 = banks[2 + (2 * tt) % 4]
                    for k in range(8):
                        S.op("pe", lambda e, p1=p1, w=w, k=k, cc=cc, sl=sl: e.matmul(p1[:], lhsT=w[:, k, cc * 128:(cc + 1) * 128], rhs=xn[:, k, sl],
                                                                                     start=(k == 0), stop=(k == 7)),
                             reads=[bw, bxn], writes=[bp1])
                    if g < 2:
                        p2, bp2 = banks[2 + (2 * tt + 1) % 4]
                        for k in range(8):
                            S.op("pe", lambda e, p2=p2, w2=w2, k=k, cc=cc, sl=sl: e.matmul(p2[:], lhsT=w2[:, k, cc * 128:(cc + 1) * 128], rhs=xn[:, k, sl],
                                                                                           start=(k == 0), stop=(k == 7)),
                                 reads=[bw2, bxn], writes=[bp2])
                        a1, ba1 = t1[tt % 2]
                        a2, ba2 = t2[tt % 2]
                        S.op("dve", lambda e, a1=a1, p1=p1, sl=sl: e.tensor_tensor(out=a1[:], in0=p1[:], in1=cs[:, 0, sl], op=ALU.mult),
                             reads=[bp1, bcs], writes=[ba1])
                        S.op("dve", lambda e, a2=a2, p2=p2, sl=sl: e.tensor_tensor(out=a2[:], in0=p2[:], in1=cs[:, 1, sl], op=ALU.mult),
                             reads=[bp2, bcs], writes=[ba2])
                        S.op("pool", lambda e, o=o, a1=a1, a2=a2, sl=sl: e.tensor_tensor(out=o[:, sl], in0=a1[:], in1=a2[:], op=ALU.add),
                             reads=[ba1, ba2], writes=[bo])
                    else:
                        sc = 0.125 if g == 3 else 1.0
                        S.op("act", lambda e, o=o, p1=p1, sl=sl, sc=sc: e.activation(out=o[:, sl], in_=p1[:], func=AF.Copy, scale=sc),
                             reads=[bp1], writes=[bo])
                if dname == "QT":
                    r0 = roff + cc * 128
                    S.dma("sp", T["QT"][r0:r0 + 128, :], o[:], reads=[bo], writes=[T["b_QT"]])
                else:
                    ch, r0 = cc // 2, (cc % 2) * 128
                    S.dma("sp", T[dname][ch][r0:r0 + 128, :], o[:], reads=[bo], writes=[T["b_" + dname][ch]])
                    if cc % 2 == 1:
                        collective(S, "AllGather", T[dname][ch], T[dname + "_all"][ch], reads=[T["b_" + dname][ch]], writes=[T["b_" + dname + "_all"][ch]])
        else:
            for tb in range(16):
                p1, bp1 = banks[2 + tb % 4]
                for k in range(8):
                    S.op("pe", lambda e, p1=p1, w=w, k=k, tb=tb: e.matmul(p1[:], lhsT=xn[:, k, tb * 128:(tb + 1) * 128], rhs=w[:, k, :],
                                                                          start=(k == 0), stop=(k == 7)),
                         reads=[bw, bxn], writes=[bp1])
                if g == 2:
                    v, bv = vst[tb % 2]
                    S.op("act", lambda e, v=v, p1=p1: e.activation(out=v[:, :, 0:128], in_=p1[:].rearrange("p (h c) -> p h c", c=128), func=AF.Copy),
                         reads=[bp1], writes=[bv])
                    ch, r0 = tb // 4, (tb % 4) * 128
                    S.dma("sp", T["Vd"][ch][r0:r0 + 128, :], v[:].rearrange("p h c -> p (h c)"), reads=[bv], writes=[T["b_Vd"][ch]])
                    if tb % 4 == 3:
                        collective(S, "AllGather", T["Vd"][ch], T["Vd_all"][ch], reads=[T["b_Vd"][ch]], writes=[T["b_Vd_all"][ch]])
                else:
                    v, bv = vss[tb % 2]
                    S.op("act", lambda e, v=v, p1=p1: e.activation(out=v[:], in_=p1[:], func=AF.Copy), reads=[bp1], writes=[bv])
                    ch, r0 = tb // 8, (tb % 8) * 128
                    S.dma("sp", T["Vs"][ch][r0:r0 + 128, :], v[:], reads=[bv], writes=[T["b_Vs"][ch]])
                    if tb % 8 == 7:
                        collective(S, "AllGather", T["Vs"][ch], T["Vs_all"][ch], reads=[T["b_Vs"][ch]], writes=[T["b_Vs_all"][ch]])
    C.close()


def tile_iters(J):
    out = []
    for kb in range(16 * J + 16):
        i0 = max(0, kb // 4 - 4 * J)
        z = kb - 16 * J if kb >= 16 * J else None
        out.append((kb, kb % 4, kb // 4, i0, z))
    return out


def phase_attn_core(nc, T):
    C = Ctx(nc); S = C.S
    K = load_consts(C, T)
    identb, bident = K["identb"], K["bident"]
    ones_f, bones = K["ones_f"], K["bones"]
    psall = C.ps([128, 8, 512], F32)
    banks = [(psall[:, i, :], Buf()) for i in range(8)]
    nuf = C.sb([128, 128], F32); bnuf = Buf()
    S.dma("sp", nuf[:], T["nu"], writes=[bnuf])
    NU = C.sb([128, 128], BF16); bNU = Buf()
    S.op("dve", lambda e: e.tensor_copy(out=NU[:], in_=nuf[:]), reads=[bnuf], writes=[bNU])
    onesb = C.sb([128, 128], BF16); bonesb = Buf()
    S.op("pool", lambda e: e.memset(onesb[:], 1.0), writes=[bonesb])
    nmI = C.sb([128, 16, 512], BF16); bnmI = Buf()
    nmS = C.sb([128, 16, 512], BF16); bnmS = Buf()
    for z in range(0, 16, 4):
        S.dma("pool", nmI[:, z:z + 4, :], T["nm_incl"][:, z:z + 4, :], writes=[bnmI])
        S.dma("pool", nmS[:, z:z + 4, :], T["nm_strict"][:, z:z + 4, :], writes=[bnmS])
    lv = C.sb([128, 4, 64], F32); blv = Buf()
    for i, nm in enumerate(("lq1", "lk1", "lq2", "lk2")):
        S.dma("sp", lv[:, i:i + 1, :], T[nm].partition_broadcast(128), writes=[blv])
    lt = C.sb([128, 2, 64], F32); blt = Buf()
    S.op("dve", lambda e: e.tensor_tensor(out=lt[:, 0, :], in0=lv[:, 0, :], in1=lv[:, 1, :], op=ALU.mult), reads=[blv], writes=[blt])
    S.op("dve", lambda e: e.tensor_tensor(out=lt[:, 1, :], in0=lv[:, 2, :], in1=lv[:, 3, :], op=ALU.mult), reads=[blv], writes=[blt])
    ls = C.sb([128, 4], F32); bls = Buf()
    S.op("dve", lambda e: e.reduce_sum(out=ls[:, 0:1], in_=lt[:, 0, :], axis=mybir.AxisListType.X), reads=[blt], writes=[bls])
    S.op("dve", lambda e: e.reduce_sum(out=ls[:, 1:2], in_=lt[:, 1, :], axis=mybir.AxisListType.X), reads=[blt], writes=[bls])
    S.op("act", lambda e: e.activation(out=ls[:, 0:2], in_=ls[:, 0:2], func=AF.Exp), reads=[bls], writes=[bls])
    S.op("dve", lambda e: e.tensor_tensor(out=ls[:, 2:3], in0=ls[:, 1:2], in1=ls[:, 0:1], op=ALU.subtract), reads=[bls], writes=[bls])
    S.op("dve", lambda e: e.tensor_scalar(out=ls[:, 3:4], in0=ls[:, 2:3], scalar1=-0.2, scalar2=None, op0=ALU.add), reads=[bls], writes=[bls])
    negl = ls[:, 3:4]
    sl_ = C.sb([128, 1], F32); bsl = Buf()
    S.dma("sp", sl_[:], T["subln"].rearrange("o v -> v o"), writes=[bsl])
    S.op("dve", lambda e: e.tensor_scalar(out=sl_[:], in0=sl_[:], scalar1=0.8, scalar2=None, op0=ALU.mult), reads=[bsl], writes=[bsl])

    zcb = C.sb([128, 4], BF16); bzcb = Buf()
    S.op("pool", lambda e: e.memset(zcb[:], 0.0), writes=[bzcb])
    for c_ in range(8):
        S.dma("sp", T["MX"][c_][:, 0:3], zcb[:, 0:3], reads=[bzcb], writes=[T["b_MX"][c_]])
    kbuf = [(C.sb([128, 4, NT], BF16), Buf()) for _ in range(2)]
    vbuf = [(C.sb([128, 64 * 129], BF16), Buf()) for _ in range(2)]
    qz = [[(C.sb([128, NT], BF16), Buf()) for _ in range(2)] for _ in range(2)]
    for b_ in range(2):
        S.op("pool", lambda e, b_=b_: e.memset(qz[b_][0][0][64:128, :], 0.0), writes=[qz[b_][0][1]])
        S.op("pool", lambda e, b_=b_: e.memset(qz[b_][1][0][0:64, :], 0.0), writes=[qz[b_][1][1]])
    ostg = [(C.sb([128, 512], BF16), Buf()) for _ in range(2)]
    nload = 0
    nout = 0

    Eb = [(C.sb([128, 2, 512], BF16), Buf()) for _ in range(2)]
    ep = [(C.sb([128, 512], F32), Buf()) for _ in range(5)]
    ktd_all = [a.rearrange("(r x) n -> x r n", r=4) for a in T["KTd_all"]]
    vd_all = [a.rearrange("(r j t) (h c) -> t r j h c", r=4, j=4, h=4) for a in T["Vd_all"]]
    for H in range(4):
        kt, bkt = kbuf[nload % 2]; vv, bvv = vbuf[nload % 2]; qzz = qz[nload % 2]; nload += 1
        for r in range(4):
            S.dma("sp", kt[:, r, :], ktd_all[H // 2][(H % 2) * 128:(H % 2 + 1) * 128, r, :], reads=[T["b_KTd_all"][H // 2]], writes=[bkt])
            for vc in range(4):
                b0 = r * 16 + 4 * vc
                S.dma("sp", vv[:, b0 * 129:(b0 + 4) * 129].rearrange("p (j c) -> p j c", c=129), vd_all[vc][:, r, :, H, :],
                      reads=[T["b_Vd_all"][vc]], writes=[bvv])
        for m in range(2):
            S.dma("sp", qzz[m][0][m * 64:(m + 1) * 64, :], T["QT"][H * 128 + m * 64:H * 128 + (m + 1) * 64, :], reads=[T["b_QT"]], writes=[qzz[m][1]])
        v4 = vv[:].rearrange("p (b c) -> p b c", c=129)
        for J in range(4):
            its = tile_iters(J)
            q0 = J * 512
            for bk_i in (4, 5, 6, 7):
                S.op("dve", lambda e, bk_i=bk_i: e.memset(banks[bk_i][0], 0.0), writes=[banks[bk_i][1]])

            def stage1(it, slot):
                kb, r, j, i0, z = it
                w0 = i0 * 128
                for m in range(2):
                    ps, bps = banks[2 * slot + m]
                    S.op("pe", lambda e, ps=ps, m=m, r=r, j=j, w0=w0: e.matmul(
                        ps[:, w0:512], lhsT=kt[:, r, j * 128:(j + 1) * 128], rhs=qzz[m][0][:, q0 + w0:q0 + 512],
                        start=True, stop=(z is None)), reads=[bkt, qzz[m][1]], writes=[bps])
                    if z is not None:
                        S.op("pe", lambda e, ps=ps, z=z, w0=w0: e.matmul(ps[:, w0:512], lhsT=identb[:], rhs=nmI[:, z, w0:512], start=False, stop=True),
                             reads=[bident, bnmI], writes=[bps])

            def stage2(it, slot):
                kb, r, j, i0, z = it
                w0 = i0 * 128
                E, bE = Eb[slot]
                S.op("act", lambda e, E=E, slot=slot, w0=w0: e.activation(out=E[:, :, w0:512], in_=psall[:, 2 * slot:2 * slot + 2, w0:512], func=AF.Exp, scale=0.125),
                     reads=[banks[2 * slot][1], banks[2 * slot + 1][1]], writes=[bE])

            def stage3(it, slot):
                kb, r, j, i0, z = it
                w0 = i0 * 128
                E, bE = Eb[slot]
                for m in range(2):
                    S.op("pe", lambda e, E=E, m=m, r=r, j=j, w0=w0: e.matmul(banks[4 + m][0][:, w0:512], lhsT=v4[:, r * 16 + j, 0:128], rhs=E[:, m, w0:512],
                                                                             start=False, stop=False, skip_group_check=True),
                         reads=[bE, bvv], writes=[banks[4 + m][1]])
                    S.op("pe", lambda e, E=E, m=m, w0=w0: e.matmul(banks[6 + m][0][:, w0:512], lhsT=onesb[:], rhs=E[:, m, w0:512],
                                                                   start=False, stop=False, skip_group_check=True),
                         reads=[bE, bonesb], writes=[banks[6 + m][1]])

            for n in range(len(its) + 2):
                if n < len(its):
                    stage1(its[n], n % 2)
                if 1 <= n <= len(its):
                    stage2(its[n - 1], (n - 1) % 2)
                if n >= 2:
                    stage3(its[n - 2], (n - 2) % 2)
            (r1, br1), (t1, bt1), (t2, bt2), (od, bod), (sq, bsq) = ep
            S.op("dve", lambda e: e.reciprocal(out=r1[:], in_=banks[6][0]), reads=[banks[6][1]], writes=[br1])
            S.op("dve", lambda e: e.tensor_tensor(out=t1[:], in0=banks[4][0], in1=r1[:], op=ALU.mult), reads=[banks[4][1], br1], writes=[bt1])
            S.op("dve", lambda e: e.reciprocal(out=r1[:], in_=banks[7][0]), reads=[banks[7][1], bt1], writes=[br1])
            S.op("dve", lambda e: e.tensor_tensor(out=t2[:], in0=banks[5][0], in1=r1[:], op=ALU.mult), reads=[banks[5][1], br1], writes=[bt2])
            S.op("dve", lambda e: e.scalar_tensor_tensor(out=od[:], in0=t2[:], scalar=negl, in1=t1[:], op0=ALU.mult, op1=ALU.add),
                 reads=[bt1, bt2, bls], writes=[bod])
            S.op("act", lambda e: e.activation(out=sq[:], in_=od[:], func=AF.Square), reads=[bod], writes=[bsq])
            pss, bpss = banks[0]
            S.op("pe", lambda e: e.matmul(pss, lhsT=ones_f[:], rhs=sq[:], start=True, stop=True), reads=[bones, bsq], writes=[bpss])
            S.op("act", lambda e: e.activation(out=t1[:], in_=pss, func=AF.Ln, scale=1.0 / 128, bias=1e-5), reads=[bpss, bod], writes=[bt1])
            S.op("act", lambda e: e.activation(out=t1[:], in_=t1[:], func=AF.Exp, scale=-0.5), reads=[bt1], writes=[bt1])
            o_, bo_ = ostg[nout % 2]; nout += 1
            S.op("dve", lambda e, o_=o_: e.scalar_tensor_tensor(out=o_[:], in0=od[:], scalar=sl_[:, 0:1], in1=t1[:], op0=ALU.mult, op1=ALU.mult),
                 reads=[bod, bsl, bt1], writes=[bo_])
            S.dma("sp", T["MX"][H][:, 3 + q0:3 + q0 + 512], o_[:], reads=[bo_], writes=[T["b_MX"][H]])
        collective(S, "AllGather", T["MX"][H], T["MXA"][H], reads=[T["b_MX"][H]], writes=[T["b_MXA"][H]])

    eb = [(C.sb([128, 2, 512], F32), Buf()) for _ in range(2)]
    spb = [(C.sb([128, 2, 512], F32), Buf()) for _ in range(2)]
    hib = [(C.sb([128, 2, 512], BF16), Buf()) for _ in range(2)]
    lob = [(C.sb([128, 2, 512], BF16), Buf()) for _ in range(2)]
    Ab = [(C.sb([128, 2, 512], BF16), Buf()) for _ in range(2)]
    fb = [(C.sb([128, 512], F32), Buf()) for _ in range(2)]
    OT = C.sb([128, 512], F32); bOT = Buf()
    kts_all = [a.rearrange("(r x) n -> x r n", r=4) for a in T["KTs_all"]]
    vs_all = [a.rearrange("(r j t) (pr c) -> t r j pr c", r=4, j=8, pr=4) for a in T["Vs_all"]]
    for pr in range(4):
        kt, bkt = kbuf[nload % 2]; vv, bvv = vbuf[nload % 2]; qzz = qz[nload % 2]; nload += 1
        for r in range(4):
            S.dma("sp", kt[:, r, :], kts_all[pr // 2][(pr % 2) * 128:(pr % 2 + 1) * 128, r, :], reads=[T["b_KTs_all"][pr // 2]], writes=[bkt])
            for vc in range(2):
                b0 = r * 16 + 8 * vc
                S.dma("sp", vv[:, b0 * 128:(b0 + 8) * 128].rearrange("p (j c) -> p j c", c=128), vs_all[vc][:, r, :, pr, :],
                      reads=[T["b_Vs_all"][vc]], writes=[bvv])
        for m in range(2):
            S.dma("sp", qzz[m][0][m * 64:(m + 1) * 64, :], T["QT"][512 + pr * 128 + m * 64:512 + pr * 128 + (m + 1) * 64, :], reads=[T["b_QT"]], writes=[qzz[m][1]])
        v4 = vv[:, 0:64 * 128].rearrange("p (b c) -> p b c", c=128)
        for J in range(4):
            its = tile_iters(J)
            q0 = J * 512
            S.op("pool", lambda e: e.memset(OT[:], 0.0), writes=[bOT])

            def stage1(it, slot):
                kb, r, j, i0, z = it
                w0 = i0 * 128
                ee, bee = eb[slot]; sp_, bsp = spb[slot]; hi, bhi = hib[slot]; lo, blo = lob[slot]
                for hh in range(2):
                    p0 = hh * 64
                    ps, bps = banks[2 * slot + hh]
                    S.op("pe", lambda e, ps=ps, r=r, j=j, w0=w0, hh=hh: e.matmul(
                        ps[:, w0:512], lhsT=kt[:, r, j * 128:(j + 1) * 128], rhs=qzz[hh][0][:, q0 + w0:q0 + 512],
                        start=True, stop=(z is None)), reads=[bkt, qzz[hh][1]], writes=[bps])
                    if z is not None:
                        S.op("pe", lambda e, ps=ps, z=z, w0=w0: e.matmul(ps[:, w0:512], lhsT=identb[:], rhs=nmS[:, z, w0:512], start=False, stop=True),
                             reads=[bident, bnmS], writes=[bps])
                zz = psall[:, 2 * slot:2 * slot + 2, w0:512]
                bz = [banks[2 * slot][1], banks[2 * slot + 1][1]]
                S.op("act", lambda e, ee=ee, zz=zz, w0=w0: e.activation(out=ee[:, :, w0:512], in_=zz, func=AF.Exp), reads=bz, writes=[bee])
                S.op("act", lambda e, ee=ee, sp_=sp_, w0=w0: e.activation(out=sp_[:, :, w0:512], in_=ee[:, :, w0:512], func=AF.Ln, bias=1.0),
                     reads=[bee], writes=[bsp])
                S.op("dve", lambda e, hi=hi, sp_=sp_, w0=w0: e.tensor_copy(out=hi[:, :, w0:512], in_=sp_[:, :, w0:512]), reads=[bsp], writes=[bhi])
                S.op("dve", lambda e, lo=lo, hi=hi, sp_=sp_, w0=w0: e.tensor_tensor(out=lo[:, :, w0:512], in0=sp_[:, :, w0:512], in1=hi[:, :, w0:512], op=ALU.subtract),
                     reads=[bsp, bhi], writes=[blo])

            def stage2(it, slot):
                kb, r, j, i0, z = it
                w0 = i0 * 128
                hi, bhi = hib[slot]; lo, blo = lob[slot]; A, bA = Ab[slot]; f, bf_ = fb[slot]
                bz = [banks[2 * slot][1], banks[2 * slot + 1][1]]
                for hh in range(2):
                    ps, bps = banks[2 * slot + hh]
                    S.op("pe", lambda e, ps=ps, hi=hi, hh=hh, w0=w0: e.matmul(ps[:, w0:512], lhsT=NU[:], rhs=hi[:, hh, w0:512], start=False, stop=False, skip_group_check=True),
                         reads=[bNU, bhi], writes=[bps])
                    S.op("pe", lambda e, ps=ps, lo=lo, hh=hh, w0=w0: e.matmul(ps[:, w0:512], lhsT=NU[:], rhs=lo[:, hh, w0:512], start=False, stop=True, skip_group_check=True),
                         reads=[bNU, blo], writes=[bps])
                pC, bpC = banks[6 + slot]
                for hh in range(2):
                    p0 = hh * 64
                    S.op("pe", lambda e, pC=pC, hi=hi, hh=hh, p0=p0, w0=w0: e.matmul(pC[p0:p0 + 64, w0:512], lhsT=onesb[:, 0:64], rhs=hi[:, hh, w0:512], start=True, stop=False),
                         reads=[bhi, bonesb], writes=[bpC])
                    S.op("pe", lambda e, pC=pC, lo=lo, hh=hh, p0=p0, w0=w0: e.matmul(pC[p0:p0 + 64, w0:512], lhsT=onesb[:, 0:64], rhs=lo[:, hh, w0:512], start=False, stop=True),
                         reads=[blo, bonesb], writes=[bpC])
                zz = psall[:, 2 * slot:2 * slot + 2, w0:512]
                S.op("act", lambda e, A=A, zz=zz, w0=w0: e.activation(out=A[:, :, w0:512], in_=zz, func=AF.Exp), reads=bz, writes=[bA])
                S.op("act", lambda e, f=f, pC=pC, w0=w0: e.activation(out=f[:, w0:512], in_=pC[:, w0:512], func=AF.Exp, scale=-1.0), reads=[bpC], writes=[bf_])

            def stage3(it, slot):
                kb, r, j, i0, z = it
                w0 = i0 * 128
                A, bA = Ab[slot]; f, bf_ = fb[slot]
                pP, bpP = banks[4 + slot]
                for hh in range(2):
                    p0 = hh * 64
                    S.op("pe", lambda e, pP=pP, A=A, hh=hh, p0=p0, r=r, j=j, w0=w0: e.matmul(pP[p0:p0 + 64, w0:512], lhsT=v4[:, r * 16 + j, p0:p0 + 64], rhs=A[:, hh, w0:512],
                                                                                            start=True, stop=True),
                         reads=[bA, bvv], writes=[bpP])
                S.op("dve", lambda e, f=f, w0=w0: e.tensor_tensor(out=OT[:, w0:512], in0=OT[:, w0:512], in1=f[:, w0:512], op=ALU.mult), reads=[bOT, bf_], writes=[bOT])
                S.op("dve", lambda e, pP=pP, w0=w0: e.tensor_tensor(out=OT[:, w0:512], in0=pP[:, w0:512], in1=OT[:, w0:512], op=ALU.add), reads=[bOT, bpP], writes=[bOT])

            for n in range(len(its) + 2):
                if n < len(its):
                    stage1(its[n], n % 2)
                if 1 <= n <= len(its):
                    stage2(its[n - 1], (n - 1) % 2)
                if n >= 2:
                    stage3(its[n - 2], (n - 2) % 2)
            o_, bo_ = ostg[nout % 2]; nout += 1
            S.op("act", lambda e, o_=o_: e.activation(out=o_[:], in_=OT[:], func=AF.Copy), reads=[bOT], writes=[bo_])
            S.dma("sp", T["MX"][4 + pr][:, 3 + q0:3 + q0 + 512], o_[:], reads=[bo_], writes=[T["b_MX"][4 + pr]])
        collective(S, "AllGather", T["MX"][4 + pr], T["MXA"][4 + pr], reads=[T["b_MX"][4 + pr]], writes=[T["b_MXA"][4 + pr]])
    C.close()


def phase_attn_out_ffn(nc, T, stage):
    C = Ctx(nc); S = C.S
    K = load_consts(C, T)
    banks = [(C.ps([128, 512], F32), Buf()) for _ in range(8)]
    hT = C.sb([128, 8, NX], F32); bh = Buf()
    xsrc = T["xT1"].rearrange("(k p) n -> p k n", p=128)
    for sl in TOKSL:
        S.dma("sp", hT[:, :, sl], xsrc[:, :, sl], writes=[bh])
    dst = T["dbg"].rearrange("(k p) n -> p k n", p=128)

    def dump():
        for tt in range(4):
            S.dma("sp", dst[:, :, tt * 512:(tt + 1) * 512], hT[:, :, 3 + tt * 512:3 + (tt + 1) * 512], reads=[bh])

    with contextlib.ExitStack() as st2:
        mT = st2.enter_context(nc.sbuf_tensor("mTf", [128, 8, NX], BF16)); bmT = Buf()
        wo = st2.enter_context(nc.sbuf_tensor("wo", [128, 8, D], BF16)); bwo = Buf()
        cache = {}

        def rank_of(e):
            if "c" not in cache:
                cache["c"] = e.partition_id() % 4
            return cache["c"]
        for c in range(8):
            for r in range(4):
                dstv = mT[:, c, 3:3 + NT].rearrange("p (m r t) -> p m r t", m=4, r=4)[:, :, r, :]

                def fn(e, dstv=dstv, c=c, r=r):
                    cid = rank_of(e)
                    src = T["MXA"][c][r * 128:(r + 1) * 128, bass.ds(cid * 512 + 3, 512)]
                    return e.dma_start(out=dstv, in_=src.rearrange("p (m t) -> p m t", m=4))
                S.dma("sp", None, None, reads=[T["b_MXA"][c]], writes=[bmT], fn=fn)

            def fn2(e, c=c):
                cid = rank_of(e)
                return e.dma_start(out=mT[:, c, 0:3], in_=T["MXA"][c][3 * 128:4 * 128, bass.ds(cid * 512, 3)])
            S.dma("sp", None, None, reads=[T["b_MXA"][c]], writes=[bmT], fn=fn2)
            S.dma("pool", wo[:, c, :], T["w_out0"][c * 128:(c + 1) * 128, :], writes=[bwo])
        for tt, sl in enumerate(TOKSL):
            wd_ = sl.stop - sl.start
            for oc in range(8):
                po, bpo = banks[oc % 4]
                for c in range(8):
                    S.op("pe", lambda e, po=po, c=c, oc=oc, sl=sl: e.matmul(po[:, 0:wd_], lhsT=wo[:, c, oc * 128:(oc + 1) * 128], rhs=mT[:, c, sl],
                                                                            start=(c == 0), stop=(c == 7)),
                         reads=[bwo, bmT], writes=[bpo])
                S.op("dve", lambda e, po=po, oc=oc, sl=sl: e.tensor_tensor(out=hT[:, oc, sl], in0=po[:, 0:wd_], in1=hT[:, oc, sl], op=ALU.add),
                     reads=[bpo, bh], writes=[bh])
        if stage == "mix":
            for c in range(8):
                S.op("dve", lambda e, c=c: e.tensor_copy(out=hT[:, c, :], in_=mT[:, c, :]), reads=[bmT, bh], writes=[bh])
        if stage in ("mix", "attn"):
            dump()
            C.S.emit()
            st2.close(); C.st.close()
            return
        C.S.emit()
    C.S = Sched(nc); S = C.S
    ffn_fm(C, T, 0, hT, bh, K["ones_f"], K["bones"], banks, slices=TOKSL)
    if stage == "ffn0":
        dump()
        C.close()
        return
    hdst = T["H1L"].rearrange("(k p) n -> p k n", p=128)
    for sl in TOKSL:
        S.dma("sp", hdst[:, :, sl], hT[:, :, sl], reads=[bh], writes=[T["b_H1L"]])
    C.close()


def load_h1_contig(C, T, x1, bx1, halo):
    S = C.S
    src = T["H1L"].rearrange("(k p) n -> p k n", p=128)
    if halo:
        for sl in TOKSL:
            S.dma("sp", x1[:, :, sl], src[:, :, sl], reads=[T["b_H1L"]], writes=[bx1])
    else:
        for tt in range(4):
            S.dma("sp", x1[:, :, tt * 512:(tt + 1) * 512], src[:, :, 3 + tt * 512:3 + (tt + 1) * 512], reads=[T["b_H1L"]], writes=[bx1])


def phase_ssd_inproj(nc, T):
    C = Ctx(nc); S = C.S
    K = load_consts(C, T)
    identb, bident = K["identb"], K["bident"]
    banks = [(C.ps([128, 512], F32), Buf()) for _ in range(8)]
    x1 = C.sb([128, 8, 3 + NT], F32); bx1 = Buf()
    load_h1_contig(C, T, x1, bx1, True)
    wn = C.sb([128, 8], F32); bwn = Buf()
    S.dma("sp", wn[:], T["ssd_norm"], writes=[bwn])
    xn = C.sb([128, 8, 3 + NT], BF16); bxn = Buf()
    rstd = C.sb([128, 3 + NT], F32); brstd = Buf()
    sqb = [(C.sb([128, 512], F32), Buf()) for _ in range(2)]
    slices = [slice(0, 3)] + [slice(3 + tt * 512, 3 + (tt + 1) * 512) for tt in range(4)]
    rmsnorm_fm(C, x1, bx1, wn, bwn, xn, bxn, K["ones_f"], K["bones"], banks[0:2], sqb=sqb, rstd=rstd, brstd=brstd, slices=slices)

    cw = C.sb([128, 32, 4], F32); bcw = Buf()
    cb = C.sb([128, 32], F32); bcb = Buf()
    S.dma("sp", cw[:], T["conv_w"], writes=[bcw])
    S.dma("sp", cb[:], T["conv_b"], writes=[bcb])
    wb = [(C.sb([128, 8, 512], BF16), Buf()) for _ in range(2)]
    wsrc = T["ssd_w_in"].rearrange("(k p) c -> p k c", p=128)
    ub = [(C.sb([128, 3 + NT], F32), Buf()) for _ in range(2)]
    accb = [(C.sb([128, NT], F32), Buf()) for _ in range(2)]
    xcb = [(C.sb([128, NT], BF16), Buf()) for _ in range(2)]
    ctmp = C.sb([128, NT], F32); bctmp = Buf()
    tst = [(C.sb([128, 8, 128], BF16), Buf()) for _ in range(2)]
    pTs = [(banks[6][0][:].bitcast(BF16), banks[6][1]), (banks[7][0][:].bitcast(BF16), banks[7][1])]
    nst = 0
    for g in range(8):
        w, bw = wb[g % 2]
        for k in range(8):
            S.dma("pool", w[:, k, :], wsrc[:, k, 2048 + g * 512:2048 + (g + 1) * 512], writes=[bw])
        for c4 in range(4):
            cc = g * 4 + c4
            u, bu = ub[cc % 2]; acc, bacc = accb[cc % 2]; xc, bxc = xcb[cc % 2]
            veng = "dve"
            for ti, sl in enumerate(slices):
                wd_ = sl.stop - sl.start
                p1, bp1 = banks[2 + ti % 4]
                for k in range(8):
                    S.op("pe", lambda e, p1=p1, w=w, k=k, c4=c4, sl=sl, wd_=wd_: e.matmul(p1[:, 0:wd_], lhsT=w[:, k, c4 * 128:(c4 + 1) * 128], rhs=xn[:, k, sl],
                                                                                         start=(k == 0), stop=(k == 7)),
                         reads=[bw, bxn], writes=[bp1])
                S.op("act", lambda e, u=u, p1=p1, sl=sl, wd_=wd_: e.activation(out=u[:, sl], in_=p1[:, 0:wd_], func=AF.Copy), reads=[bp1], writes=[bu])
            S.op(veng, lambda e, acc=acc, u=u, cc=cc: e.tensor_scalar(out=acc[:], in0=u[:, 0:NT], scalar1=cw[:, cc, 0:1], scalar2=None, op0=ALU.mult),
                 reads=[bu, bcw], writes=[bacc])
            for tap in range(1, 4):
                if veng == "dve":
                    S.op(veng, lambda e, acc=acc, u=u, cc=cc, tap=tap: e.scalar_tensor_tensor(out=acc[:], in0=u[:, tap:tap + NT], scalar=cw[:, cc, tap:tap + 1],
                                                                                             in1=acc[:], op0=ALU.mult, op1=ALU.add),
                         reads=[bu, bcw, bacc], writes=[bacc])
                else:
                    S.op(veng, lambda e, u=u, cc=cc, tap=tap: e.tensor_scalar(out=ctmp[:], in0=u[:, tap:tap + NT], scalar1=cw[:, cc, tap:tap + 1], scalar2=None, op0=ALU.mult),
                         reads=[bu, bcw], writes=[bctmp])
                    S.op(veng, lambda e, acc=acc: e.tensor_tensor(out=acc[:], in0=acc[:], in1=ctmp[:], op=ALU.add), reads=[bacc, bctmp], writes=[bacc])
            S.op("act", lambda e, xc=xc, acc=acc, cc=cc: e.activation(out=xc[:], in_=acc[:], func=AF.Silu, bias=cb[:, cc:cc + 1]),
                 reads=[bacc, bcb], writes=[bxc])
            if cc >= 16:
                nm = "BT" if cc < 24 else "CT"
                gi = cc - 16 if cc < 24 else cc - 24
                S.dma("sp", T[nm][gi * 128:(gi + 1) * 128, :], xc[:], reads=[bxc], writes=[T["b_" + nm]])
            if cc < 24:
                dname = "XS" if cc < 16 else "BTOK"
                col0 = cc * 128 if cc < 16 else (cc - 16) * 128
                for half in range(2):
                    pT, bpT = pTs[nst % 2]
                    st_, bst = tst[nst % 2]; nst += 1
                    for tb8 in range(8):
                        tb = half * 8 + tb8
                        S.op("pe", lambda e, pT=pT, xc=xc, tb=tb, tb8=tb8: e.transpose(pT[:, tb8 * 128:(tb8 + 1) * 128], xc[:, tb * 128:(tb + 1) * 128], identb[:]),
                             reads=[bxc, bident], writes=[bpT])
                    S.op("dve" if nst % 2 == 0 else "act", (lambda e, st_=st_, pT=pT: e.tensor_copy(out=st_[:], in_=pT.rearrange("p (b c) -> p b c", c=128)))
                         if nst % 2 == 0 else (lambda e, st_=st_, pT=pT: e.activation(out=st_[:], in_=pT.rearrange("p (b c) -> p b c", c=128), func=AF.Copy)),
                         reads=[bpT], writes=[bst])
                    dstv = T[dname][half * 1024:(half + 1) * 1024, col0:col0 + 128].rearrange("(b t) c -> t b c", t=128)
                    S.dma("sp", dstv, st_[:], reads=[bst], writes=[T["b_" + dname]])
    zst = [(C.sb([128, 512], F32), Buf()) for _ in range(2)]
    nz = 0
    for g in range(4):
        w, bw = wb[g % 2]
        for k in range(8):
            S.dma("pool", w[:, k, :], wsrc[:, k, g * 512:(g + 1) * 512], writes=[bw])
        for tb in range(16):
            p1, bp1 = banks[2 + tb % 4]
            for k in range(8):
                S.op("pe", lambda e, p1=p1, w=w, k=k, tb=tb: e.matmul(p1[:], lhsT=xn[:, k, 3 + tb * 128:3 + (tb + 1) * 128], rhs=w[:, k, :],
                                                                      start=(k == 0), stop=(k == 7)),
                     reads=[bw, bxn], writes=[bp1])
            z_, bz = zst[nz % 2]; nz += 1
            S.op("act", lambda e, z_=z_, p1=p1: e.activation(out=z_[:], in_=p1[:], func=AF.Silu), reads=[bp1], writes=[bz])
            S.dma("sp", T["ZS"][tb * 128:(tb + 1) * 128, g * 512:(g + 1) * 512], z_[:], reads=[bz], writes=[T["b_ZS"]])
    wdt = C.sb([128, 8, 32], BF16); bwdt = Buf()
    for k in range(8):
        S.dma("pool", wdt[:, k, :], wsrc[:, k, 6144:6176], writes=[bwdt])
    hv = C.sb([128, 3, 32], F32); bhv = Buf()
    S.dma("sp", hv[:, 0:1, :], T["dt_bias"].partition_broadcast(128), writes=[bhv])
    S.dma("sp", hv[:, 1:2, :], T["a_log"].partition_broadcast(128), writes=[bhv])
    S.op("act", lambda e: e.activation(out=hv[:, 2, :], in_=hv[:, 1, :], func=AF.Exp), reads=[bhv], writes=[bhv])
    dst_ = [(C.sb([128, 64], F32), Buf()) for _ in range(2)]
    for tb in range(16):
        p1, bp1 = banks[2 + tb % 4]
        for k in range(8):
            S.op("pe", lambda e, p1=p1, k=k, tb=tb: e.matmul(p1[:, 0:32], lhsT=xn[:, k, 3 + tb * 128:3 + (tb + 1) * 128], rhs=wdt[:, k, :],
                                                             start=(k == 0), stop=(k == 7)),
                 reads=[bwdt, bxn], writes=[bp1])
        d_, bd = dst_[tb % 2]
        S.op("dve", lambda e, d_=d_, p1=p1: e.tensor_tensor(out=d_[:, 0:32], in0=p1[:, 0:32], in1=hv[:, 0, :], op=ALU.add), reads=[bp1, bhv], writes=[bd])
        S.op("act", lambda e, d_=d_: e.activation(out=d_[:, 0:32], in_=d_[:, 0:32], func=AF.Exp), reads=[bd], writes=[bd])
        S.op("act", lambda e, d_=d_: e.activation(out=d_[:, 0:32], in_=d_[:, 0:32], func=AF.Ln, bias=1.0), reads=[bd], writes=[bd])
        S.op("dve", lambda e, d_=d_: e.scalar_tensor_tensor(out=d_[:, 32:64], in0=d_[:, 0:32], scalar=-1.0, in1=hv[:, 2, :], op0=ALU.mult, op1=ALU.mult),
             reads=[bd, bhv], writes=[bd])
        S.dma("sp", T["DTD"][tb * 128:(tb + 1) * 128, :], d_[:], reads=[bd], writes=[T["b_DTD"]])
    C.close()


def ssd_consts(C, T):
    S = C.S
    k = {}
    for nm in ("tri_incl", "tri_gt", "ntri_incl"):
        k[nm] = C.sb([128, 128], F32); k["b_" + nm] = Buf()
        S.dma("sp", k[nm][:], T[nm], writes=[k["b_" + nm]])
    return k


def phase_ssd_states(nc, T):
    C = Ctx(nc); S = C.S
    K = load_consts(C, T)
    K2 = ssd_consts(C, T)
    banks = [(C.ps([128, 512], F32), Buf()) for _ in range(8)]
    Sloc = C.sb([128, 32, 64], F32); bS = Buf()
    S.op("pool", lambda e: e.memset(Sloc[:], 0.0), writes=[bS])
    ldsum = C.sb([128, 32], F32); bld = Buf()
    S.op("pool", lambda e: e.memset(ldsum[:], 0.0), writes=[bld])
    xsb = [(C.sb([128, 32, 64], BF16), Buf()) for _ in range(2)]
    btb = [(C.sb([128, 1024], BF16), Buf()) for _ in range(2)]
    dtb = [(C.sb([128, 64], F32), Buf()) for _ in range(2)]
    smb = [(C.sb([128, 3, 32], F32), Buf()) for _ in range(2)]
    xwb = [(C.sb([128, 32, 64], BF16), Buf()) for _ in range(2)]
    stb = [(C.sb([128, 2048], F32), Buf()) for _ in range(2)]
    for c in range(16):
        xs, bxs = xsb[c % 2]; bt, bbt = btb[c % 2]; dt_, bdt = dtb[c % 2]; sm, bsm = smb[c % 2]; xw, bxw = xwb[c % 2]; st_, bst = stb[c % 2]
        rows = slice(c * 128, (c + 1) * 128)
        S.dma("sp", xs[:].rearrange("p h c -> p (h c)"), T["XS"][rows, :], reads=[T["b_XS"]], writes=[bxs])
        S.dma("act", bt[:], T["BTOK"][rows, :], reads=[T["b_BTOK"]], writes=[bbt])
        S.dma("sp", dt_[:], T["DTD"][rows, :], reads=[T["b_DTD"]], writes=[bdt])
        pa, bpa = banks[c % 2]
        S.op("pe", lambda e, pa=pa, dt_=dt_: e.matmul(pa[:, 0:32], lhsT=K2["tri_gt"][:], rhs=dt_[:, 32:64], start=True, stop=True),
             reads=[K2["b_tri_gt"], bdt], writes=[bpa])
        S.op("pe", lambda e, pa=pa, dt_=dt_: e.matmul(pa[:, 32:64], lhsT=K["ones_f"][:], rhs=dt_[:, 32:64], start=True, stop=True),
             reads=[K["bones"], bdt], writes=[bpa])
        S.op("act", lambda e, sm=sm, pa=pa: e.activation(out=sm[:, 0:2, :], in_=pa[:, 0:64].rearrange("p (a h) -> p a h", a=2), func=AF.Exp),
             reads=[bpa], writes=[bsm])
        S.op("dve", lambda e, sm=sm, dt_=dt_: e.tensor_tensor(out=sm[:, 2, :], in0=sm[:, 0, :], in1=dt_[:, 0:32], op=ALU.mult), reads=[bsm, bdt], writes=[bsm])
        S.op("dve", lambda e, xw=xw, xs=xs, sm=sm: e.tensor_tensor(out=xw[:], in0=xs[:], in1=sm[:, 2, :].unsqueeze(2).to_broadcast([128, 32, 64]), op=ALU.mult),
             reads=[bxs, bsm], writes=[bxw])
        xwf = xw[:].rearrange("p h c -> p (h c)")
        for g in range(8):
            ps_, bps = banks[2 + g // 2]
            S.op("pe", lambda e, ps_=ps_, bt=bt, g=g, xwf=xwf: e.matmul(ps_[:, (g % 2) * 256:(g % 2 + 1) * 256], lhsT=bt[:, g * 128:(g + 1) * 128],
                                                                        rhs=xwf[:, g * 256:(g + 1) * 256], start=True, stop=True),
                 reads=[bbt, bxw], writes=[bps])
        for bq in range(4):
            ps_, bps = banks[2 + bq]
            S.op("act" if bq % 2 == 0 else "dve",
                 (lambda e, st_=st_, ps_=ps_, bq=bq: e.activation(out=st_[:, bq * 512:(bq + 1) * 512], in_=ps_[:], func=AF.Copy)) if bq % 2 == 0 else
                 (lambda e, st_=st_, ps_=ps_, bq=bq: e.tensor_copy(out=st_[:, bq * 512:(bq + 1) * 512], in_=ps_[:])),
                 reads=[bps], writes=[bst])
        S.dma("sp", T["ST"][c], st_[:], reads=[bst], writes=[T["b_ST"]])
        S.op("pool", lambda e, sm=sm: e.tensor_tensor(out=Sloc[:], in0=Sloc[:], in1=sm[:, 1, :].unsqueeze(2).to_broadcast([128, 32, 64]), op=ALU.mult),
             reads=[bS, bsm], writes=[bS])
        S.op("pool", lambda e, st_=st_: e.tensor_tensor(out=Sloc[:].rearrange("p h c -> p (h c)"), in0=Sloc[:].rearrange("p h c -> p (h c)"), in1=st_[:], op=ALU.add),
             reads=[bS, bst], writes=[bS])
        S.op("dve", lambda e, pa=pa: e.tensor_tensor(out=ldsum[:], in0=pa[:, 32:64], in1=ldsum[:], op=ALU.add), reads=[bpa, bld], writes=[bld])
    S.dma("sp", T["SXa"], Sloc[:].rearrange("p h c -> p (h c)"), reads=[bS], writes=[T["b_SXa"]])
    S.dma("sp", T["SXb"], ldsum[:], reads=[bld], writes=[T["b_SXb"]])
    collective(S, "AllGather", T["SXa"], T["SXa_all"], reads=[T["b_SXa"]], writes=[T["b_SXa_all"]])
    collective(S, "AllGather", T["SXb"], T["SXb_all"], reads=[T["b_SXb"]], writes=[T["b_SXb_all"]])
    C.close()


def phase_ssd_scan(nc, T):
    C = Ctx(nc); S = C.S
    K = load_consts(C, T)
    K2 = ssd_consts(C, T)
    identb, bident = K["identb"], K["bident"]
    identf, bidentf = K["idf"], K["bidf"]
    ones_f, bones = K["ones_f"], K["bones"]
    banks = [(C.ps([128, 512], F32), Buf()) for _ in range(8)]
    prev = C.sb([128, 32, 64], F32); bprev = Buf()
    prevb = C.sb([128, 2048], BF16); bprevb = Buf()
    msk = C.sb([128, 20], F32); bmsk = Buf()
    S.dma("sp", msk[:], T["selmask"], writes=[bmsk])
    ld = C.sb([128, 4, 32], F32); bldg = Buf()
    S.dma("sp", ld[:], T["SXb_all"].rearrange("(r p) h -> p r h", p=128), reads=[T["b_SXb_all"]], writes=[bldg])
    coef = C.sb([128, 4, 32], F32); bcoef = Buf()
    for r in range(4):
        S.op("dve", lambda e, r=r: e.tensor_scalar(out=coef[:, r, :], in0=ld[:, 0, :], scalar1=msk[:, 4 + 4 * r:5 + 4 * r], scalar2=None, op0=ALU.mult),
             reads=[bldg, bmsk], writes=[bcoef])
        for r2 in range(1, 4):
            S.op("dve", lambda e, r=r, r2=r2: e.scalar_tensor_tensor(out=coef[:, r, :], in0=ld[:, r2, :], scalar=msk[:, 4 + 4 * r + r2:5 + 4 * r + r2],
                                                                     in1=coef[:, r, :], op0=ALU.mult, op1=ALU.add),
                 reads=[bldg, bmsk, bcoef], writes=[bcoef])
        S.op("act", lambda e, r=r: e.activation(out=coef[:, r, :], in_=coef[:, r, :], func=AF.Exp), reads=[bcoef], writes=[bcoef])
        S.op("dve", lambda e, r=r: e.tensor_scalar(out=coef[:, r, :], in0=coef[:, r, :], scalar1=msk[:, r:r + 1], scalar2=None, op0=ALU.mult),
             reads=[bcoef, bmsk], writes=[bcoef])
    S.op("pool", lambda e: e.memset(prev[:], 0.0), writes=[bprev])
    sg_ = [(C.sb([128, 32, 64], F32), Buf()) for _ in range(2)]
    for r in range(4):
        t_, bt_ = sg_[r % 2]
        S.dma("sp", t_[:].rearrange("p h c -> p (h c)"), T["SXa_all"][r * 128:(r + 1) * 128, :], reads=[T["b_SXa_all"]], writes=[bt_])
        S.op("dve", lambda e, t_=t_, r=r: e.tensor_tensor(out=t_[:], in0=t_[:], in1=coef[:, r, :].unsqueeze(2).to_broadcast([128, 32, 64]), op=ALU.mult),
             reads=[bt_, bcoef], writes=[bt_])
        S.op("dve", lambda e, t_=t_: e.tensor_tensor(out=prev[:], in0=prev[:], in1=t_[:], op=ALU.add), reads=[bt_, bprev], writes=[bprev])
    prevf = prev[:].rearrange("p h c -> p (h c)")
    S.op("act", lambda e: e.activation(out=prevb[:], in_=prevf, func=AF.Copy), reads=[bprev], writes=[bprevb])
    hv = C.sb([128, 32], F32); bhv = Buf()
    S.dma("sp", hv[:].unsqueeze(1), T["ssd_d"].partition_broadcast(128), writes=[bhv])
    gw = C.sb([128, 16], F32); bgw = Buf()
    S.dma("sp", gw[:], T["gnorm"], writes=[bgw])
    negut = C.sb([128, 4, 128], F32); bneg = Buf()
    negutb = C.sb([128, 4, 128], BF16); bnegb = Buf()
    for r in range(4):
        S.dma("sp", negut[:, r, :], T["neg_ut"], writes=[bneg])
    S.op("dve", lambda e: e.tensor_copy(out=negutb[:], in_=negut[:]), reads=[bneg], writes=[bnegb])
    xsb = [(C.sb([128, 32, 64], BF16), Buf()) for _ in range(2)]
    dtb = [(C.sb([128, 64], F32), Buf()) for _ in range(2)]
    btb = [(C.sb([128, 8, 128], BF16), Buf()) for _ in range(2)]
    ctb = [(C.sb([128, 8, 128], BF16), Buf()) for _ in range(2)]
    zsb = [(C.sb([128, 2048], F32), Buf()) for _ in range(2)]
    stb = [(C.sb([128, 2048], F32), Buf()) for _ in range(2)]
    dth = C.sb([128, 2, 32], BF16); bdth = Buf()
    trib = C.sb([128, 128], BF16); ntrib = C.sb([128, 128], BF16); onesb = C.sb([128, 128], BF16); btb_ = Buf()
    S.op("dve", lambda e: e.tensor_copy(out=trib[:], in_=K2["tri_incl"][:]), reads=[K2["b_tri_incl"]], writes=[btb_])
    S.op("dve", lambda e: e.tensor_copy(out=ntrib[:], in_=K2["ntri_incl"][:]), reads=[K2["b_ntri_incl"]], writes=[btb_])
    S.op("dve", lambda e: e.memset(onesb[:], 1.0), writes=[btb_])
    smb = [(C.sb([128, 3, 32], F32), Buf()) for _ in range(2)]
    xdt = C.sb([128, 32, 64], BF16); bxdt = Buf()
    Dmb = [(C.sb([128, 4, 128], F32), Buf()) for _ in range(2)]
    MTb = [(C.sb([128, 4, 128], BF16), Buf()) for _ in range(2)]
    tyb = [(C.sb([128, 4, 64], F32), Buf()) for _ in range(2)]
    yc = C.sb([128, 8, 256], F32); byc = Buf()
    gy = C.sb([128, 8, 256], F32); bgy = Buf()
    ynb = C.sb([128, 2048], BF16); byn = Buf()
    nrm = C.sb([128, 3, 8], F32); bnrm = Buf()
    junk = C.sb([128, 256], F32); bjunk = Buf()
    yTs = [(C.sb([128, 16, 128], BF16), Buf()) for _ in range(2)]
    bt_src = T["BT"].rearrange("(g n) t -> n g t", n=128)
    ct_src = T["CT"].rearrange("(g n) t -> n g t", n=128)
    yt_dst = T["YT"].rearrange("(cc p) t -> p cc t", p=128)
    pT0 = banks[6][0][:].bitcast(BF16); pT1 = banks[7][0][:].bitcast(BF16)
    half_bufs = [[Buf(), Buf()] for _ in range(3)]
    for c in range(16):
        xs, bxs = xsb[c % 2]; dt_, bdt = dtb[c % 2]; bt, bbt = btb[c % 2]; ct, bct = ctb[c % 2]
        zs, bzs = zsb[c % 2]; st_, bst = stb[c % 2]; sm, bsm = smb[c % 2]
        rows = slice(c * 128, (c + 1) * 128)
        S.dma("sp", xs[:].rearrange("p h c -> p (h c)"), T["XS"][rows, :], reads=[T["b_XS"]], writes=[bxs])
        S.dma("sp", dt_[:], T["DTD"][rows, :], reads=[T["b_DTD"]], writes=[bdt])
        S.dma("act", bt[:], bt_src[:, :, rows], reads=[T["b_BT"]], writes=[bbt])
        S.dma("act", ct[:], ct_src[:, :, rows], reads=[T["b_CT"]], writes=[bct])
        S.dma("sp", zs[:], T["ZS"][rows, :], reads=[T["b_ZS"]], writes=[bzs])
        S.dma("sp", st_[:], T["ST"][c], reads=[T["b_ST"]], writes=[bst])
        pa, bpa = banks[0]
        S.op("pe", lambda e, dt_=dt_: e.matmul(pa[:, 0:32], lhsT=K2["tri_incl"][:], rhs=dt_[:, 32:64], start=True, stop=True),
             reads=[K2["b_tri_incl"], bdt], writes=[bpa])
        S.op("pe", lambda e, dt_=dt_: e.matmul(pa[:, 32:64], lhsT=ones_f[:], rhs=dt_[:, 32:64], start=True, stop=True),
             reads=[bones, bdt], writes=[bpa])
        S.op("act", lambda e, sm=sm: e.activation(out=sm[:, 0:2, :], in_=pa[:, 0:64].rearrange("p (a h) -> p a h", a=2), func=AF.Exp),
             reads=[bpa], writes=[bsm])
        S.op("dve", lambda e, dt_=dt_: e.tensor_copy(out=dth[:, 0, :], in_=dt_[:, 32:64]), reads=[bdt], writes=[bdth])
        S.op("dve", lambda e, dt_=dt_: e.tensor_tensor(out=dth[:, 1, :], in0=dt_[:, 32:64], in1=dth[:, 0, :], op=ALU.subtract), reads=[bdt, bdth], writes=[bdth])
        S.op("dve", lambda e, xs=xs, dt_=dt_: e.tensor_tensor(out=xdt[:], in0=xs[:], in1=dt_[:, 0:32].unsqueeze(2).to_broadcast([128, 32, 64]), op=ALU.mult),
             reads=[bxs, bdt], writes=[bxdt])
        for g in range(8):
            pseg, bpseg = banks[1 + g % 2]
            Dm, bDm = Dmb[g % 2]; MT, bMT = MTb[g % 2]; ty, bty = tyb[g % 2]
            hs = slice(4 * g, 4 * g + 4)
            psv = pseg[:].rearrange("p (h l) -> p h l", h=4)
            for x_ in range(2):
                S.op("pe", lambda e, psv=psv, x_=x_, hs=hs: e.matmul(psv, lhsT=ntrib[:], rhs=dth[:, x_, hs].unsqueeze(2).to_broadcast([128, 4, 128]),
                                                                     start=(x_ == 0), stop=False),
                     reads=[btb_, bdth], writes=[bpseg])
            for r in range(4):
                for x_ in range(2):
                    S.op("pe", lambda e, pseg=pseg, x_=x_, r=r, g=g: e.matmul(pseg[:, r * 128:(r + 1) * 128], lhsT=dth[:, x_, 4 * g + r:4 * g + r + 1].to_broadcast([128, 128]),
                                                                              rhs=trib[:], start=False, stop=False),
                         reads=[btb_, bdth], writes=[bpseg])
            S.op("pe", lambda e, pseg=pseg: e.matmul(pseg[:], lhsT=identb[:], rhs=negutb[:].rearrange("p h l -> p (h l)"), start=False, stop=True),
                 reads=[bident, bnegb], writes=[bpseg])
            S.op("act", lambda e, Dm=Dm, pseg=pseg: e.activation(out=Dm[:].rearrange("p h l -> p (h l)"), in_=pseg[:], func=AF.Exp), reads=[bpseg], writes=[bDm])
            pg = banks[3][0]; bpg = half_bufs[0][g % 2]
            gsl = slice((g % 2) * 128, (g % 2 + 1) * 128)
            S.op("pe", lambda e, bt=bt, ct=ct, g=g, gsl=gsl: e.matmul(pg[:, gsl], lhsT=bt[:, g, :], rhs=ct[:, g, :], start=True, stop=True),
                 reads=[bbt, bct], writes=[bpg])
            S.op("dve", lambda e, MT=MT, Dm=Dm, gsl=gsl: e.tensor_tensor(out=MT[:], in0=Dm[:], in1=pg[:, gsl].unsqueeze(1).to_broadcast([128, 4, 128]), op=ALU.mult),
                 reads=[bDm, bpg], writes=[bMT])
            pyd = banks[4][0]; bpyd = half_bufs[1][g % 2]
            pyo = banks[5][0]; bpyo = half_bufs[2][g % 2]
            ysl = slice((g % 2) * 256, (g % 2 + 1) * 256)
            for r in range(4):
                S.op("pe", lambda e, MT=MT, r=r, g=g, ysl=ysl: e.matmul(pyd[:, ysl.start + r * 64:ysl.start + (r + 1) * 64], lhsT=MT[:, r, :], rhs=xdt[:, 4 * g + r, :],
                                                                        start=True, stop=True),
                     reads=[bMT, bxdt], writes=[bpyd])
            S.op("pe", lambda e, ct=ct, g=g, ysl=ysl: e.matmul(pyo[:, ysl], lhsT=ct[:, g, :], rhs=prevb[:, g * 256:(g + 1) * 256], start=True, stop=True),
                 reads=[bct, bprevb], writes=[bpyo])
            S.op("dve", lambda e, ty=ty, sm=sm, hs=hs, ysl=ysl: e.tensor_tensor(out=ty[:], in0=pyo[:, ysl].rearrange("p (r c) -> p r c", r=4),
                                                                                in1=sm[:, 0, hs].unsqueeze(2).to_broadcast([128, 4, 64]), op=ALU.mult),
                 reads=[bpyo, bsm], writes=[bty])
            S.op("dve", lambda e, ty=ty, g=g, ysl=ysl: e.tensor_tensor(out=yc[:, g, :], in0=pyd[:, ysl], in1=ty[:].rearrange("p r c -> p (r c)"), op=ALU.add),
                 reads=[bpyd, bty], writes=[byc])
        ycv = yc[:].rearrange("p g (r c) -> p (g r) c", r=4)
        S.op("pool", lambda e, xs=xs: e.tensor_tensor(out=gy[:].rearrange("p g (r c) -> p (g r) c", r=4), in0=xs[:], in1=hv[:].unsqueeze(2).to_broadcast([128, 32, 64]), op=ALU.mult),
             reads=[bxs, bhv], writes=[bgy])
        S.op("pool", lambda e: e.tensor_tensor(out=yc[:], in0=yc[:], in1=gy[:], op=ALU.add), reads=[byc, bgy], writes=[byc])
        S.op("dve", lambda e, zs=zs: e.tensor_tensor(out=gy[:], in0=yc[:], in1=zs[:].rearrange("p (g c) -> p g c", g=8), op=ALU.mult), reads=[byc, bzs, bgy], writes=[bgy])
        for g in range(8):
            S.op("act", lambda e, g=g: e.activation(out=junk[:], in_=gy[:, g, :], func=AF.Square, accum_out=nrm[:, 0, g:g + 1]), reads=[bgy], writes=[bjunk, bnrm])
        S.op("act", lambda e: e.activation(out=nrm[:, 1, :], in_=nrm[:, 0, :], func=AF.Ln, scale=1.0 / 256, bias=1e-5), reads=[bnrm], writes=[bnrm])
        S.op("act", lambda e: e.activation(out=nrm[:, 2, :], in_=nrm[:, 1, :], func=AF.Exp, scale=-0.5), reads=[bnrm], writes=[bnrm])
        S.op("dve", lambda e: e.tensor_tensor(out=ynb[:].rearrange("p (g c) -> p g c", g=8), in0=gy[:], in1=nrm[:, 2, :].unsqueeze(2).to_broadcast([128, 8, 256]), op=ALU.mult),
             reads=[bgy, bnrm], writes=[byn])
        yT, byT = yTs[c % 2]
        for cc in range(16):
            pT = pT0 if cc < 8 else pT1
            S.op("pe", lambda e, pT=pT, cc=cc: e.transpose(pT[:, (cc % 8) * 128:(cc % 8 + 1) * 128], ynb[:, cc * 128:(cc + 1) * 128], identb[:]),
                 reads=[byn, bident], writes=[banks[6][1] if cc < 8 else banks[7][1]])
        S.op("dve", lambda e, yT=yT: e.tensor_tensor(out=yT[:, 0:8, :], in0=pT0.rearrange("p (b c) -> p b c", c=128), in1=gw[:, 0:8].unsqueeze(2).to_broadcast([128, 8, 128]), op=ALU.mult),
             reads=[banks[6][1], bgw], writes=[byT])
        S.op("dve", lambda e, yT=yT: e.tensor_tensor(out=yT[:, 8:16, :], in0=pT1.rearrange("p (b c) -> p b c", c=128), in1=gw[:, 8:16].unsqueeze(2).to_broadcast([128, 8, 128]), op=ALU.mult),
             reads=[banks[7][1], bgw], writes=[byT])
        S.dma("sp", yt_dst[:, :, rows], yT[:], reads=[byT], writes=[T["b_YT"]])
        S.op("pool", lambda e, sm=sm: e.tensor_tensor(out=prev[:], in0=prev[:], in1=sm[:, 1, :].unsqueeze(2).to_broadcast([128, 32, 64]), op=ALU.mult),
             reads=[bprev, bsm], writes=[bprev])
        S.op("pool", lambda e, st_=st_: e.tensor_tensor(out=prevf, in0=prevf, in1=st_[:], op=ALU.add), reads=[bprev, bst], writes=[bprev])
        S.op("act", lambda e: e.activation(out=prevb[:], in_=prevf, func=AF.Copy), reads=[bprev, bprevb], writes=[bprevb])
    C.close()


def phase_ssd_out_ffn(nc, T, stage):
    C = Ctx(nc); S = C.S
    K = load_consts(C, T)
    banks = [(C.ps([128, 512], F32), Buf()) for _ in range(8)]
    hT = C.sb([128, 8, NT], F32); bh = Buf()
    load_h1_contig(C, T, hT, bh, False)
    dst = T["dbg"].rearrange("(k p) n -> p k n", p=128)
    with contextlib.ExitStack() as st2:
        yT = st2.enter_context(nc.sbuf_tensor("yTf", [128, 16, NT], BF16)); byT = Buf()
        wo = st2.enter_context(nc.sbuf_tensor("wo1", [128, 16, D], BF16)); bwo = Buf()
        ysrc = T["YT"].rearrange("(cc p) t -> p cc t", p=128)
        for cc in range(16):
            S.dma("act", yT[:, cc, :], ysrc[:, cc, :], reads=[T["b_YT"]], writes=[byT])
            S.dma("pool", wo[:, cc, :], T["ssd_w_out"][cc * 128:(cc + 1) * 128, :], writes=[bwo])
        for tt in range(4):
            sl = slice(tt * 512, (tt + 1) * 512)
            for oc in range(8):
                po, bpo = banks[oc % 4]
                for cc in range(16):
                    S.op("pe", lambda e, po=po, cc=cc, oc=oc, sl=sl: e.matmul(po[:], lhsT=wo[:, cc, oc * 128:(oc + 1) * 128], rhs=yT[:, cc, sl],
                                                                              start=(cc == 0), stop=(cc == 15)),
                         reads=[bwo, byT], writes=[bpo])
                S.op("dve", lambda e, po=po, oc=oc, sl=sl: e.tensor_tensor(out=hT[:, oc, sl], in0=po[:], in1=hT[:, oc, sl], op=ALU.add),
                     reads=[bpo, bh], writes=[bh])
        if stage == "ssd":
            for tt in range(4):
                S.dma("sp", dst[:, :, tt * 512:(tt + 1) * 512], hT[:, :, tt * 512:(tt + 1) * 512], reads=[bh])
            C.S.emit()
            st2.close(); C.st.close()
            return
        C.S.emit()
    C.S = Sched(nc); S = C.S
    Cf = Ctx(nc); Cf.S = C.S
    ffn_fm(Cf, T, 1, hT, bh, K["ones_f"], K["bones"], banks)
    C.S.emit()
    Cf.st.close()
    C.S = Sched(nc); S = C.S
    if stage != "ffn1":
        fw = C.sb([128, 8], F32); bfw = Buf()
        S.dma("sp", fw[:], T["final_norm"], writes=[bfw])
        rstd = C.sb([128, NT], F32); brstd = Buf()
        sq = [(C.sb([128, 512], F32), Buf()) for _ in range(2)]
        for tt in range(4):
            sl = slice(tt * 512, (tt + 1) * 512)
            ps, bps = banks[tt % 2]
            for k in range(8):
                q_, bq_ = sq[k % 2]
                S.op("act", lambda e, q_=q_, k=k, sl=sl: e.activation(out=q_[:], in_=hT[:, k, sl], func=AF.Square), reads=[bh], writes=[bq_])
                S.op("pe", lambda e, ps=ps, q_=q_, k=k: e.matmul(ps[:], lhsT=K["ones_f"][:], rhs=q_[:], start=(k == 0), stop=(k == 7)),
                     reads=[bq_, K["bones"]], writes=[bps])
            S.op("act", lambda e, ps=ps, sl=sl: e.activation(out=rstd[:, sl], in_=ps[:], func=AF.Ln, scale=1.0 / D, bias=1e-6), reads=[bps], writes=[brstd])
            S.op("act", lambda e, sl=sl: e.activation(out=rstd[:, sl], in_=rstd[:, sl], func=AF.Exp, scale=-0.5), reads=[brstd], writes=[brstd])
            for k in range(8):
                S.op("dve", lambda e, k=k, sl=sl: e.scalar_tensor_tensor(out=hT[:, k, sl], in0=hT[:, k, sl], scalar=fw[:, k:k + 1], in1=rstd[:, sl],
                                                                         op0=ALU.mult, op1=ALU.mult),
                     reads=[bh, bfw, brstd], writes=[bh])
    for tt in range(4):
        S.dma("sp", dst[:, :, tt * 512:(tt + 1) * 512], hT[:, :, tt * 512:(tt + 1) * 512], reads=[bh])
    C.close()


def build_program(stage):
    Buf.ALL = []
    nc = bass.Bass("TRN2", target_bir_lowering=False)
    Sched.STATE = SemState(nc)
    T = {}

    def inp(name, shape, dt=F32):
        T[name] = nc.dram_tensor(name, list(shape), dt, kind="ExternalInput").ap()

    def scr(name, shape, dt):
        T[name] = nc.dram_tensor(name, list(shape), dt).ap()
        T["b_" + name] = Buf(name)

    inp("xT0", [D, NT]); inp("cs_tab", [128, 2, NT]); inp("attn_norm", [128, 8])
    inp("w_in0", [D, 3072]); inp("w_sw0", [D, 1024]); inp("w_out0", [D, D])
    inp("nm_incl", [128, 16, 512]); inp("nm_strict", [128, 16, 512])
    inp("ident", [128, 128]); inp("nu", [128, 128])
    for nm in ("lq1", "lk1", "lq2", "lk2"):
        inp(nm, [1, 64])
    inp("subln", [1, 128])
    inp("ffn_norm", [2, 128, 8]); inp("ffn_wg", [2, D, D_FF]); inp("ffn_wu", [2, D, D_FF]); inp("ffn_wd", [2, D_FF, D])
    scr("QT", [1536, NT], BF16)
    def scr_list(name, n, shape, dt):
        T[name] = [nc.dram_tensor("%s_%d" % (name, i), list(shape), dt).ap() for i in range(n)]
        T["b_" + name] = [Buf(name) for _ in range(n)]

    scr_list("KTd", 4, [256, NT], BF16); scr_list("KTd_all", 4, [1024, NT], BF16)
    scr_list("KTs", 2, [256, NT], BF16); scr_list("KTs_all", 2, [1024, NT], BF16)
    scr_list("Vd", 4, [512, 516], BF16); scr_list("Vd_all", 4, [2048, 516], BF16)
    scr_list("Vs", 2, [1024, 512], BF16); scr_list("Vs_all", 2, [4096, 512], BF16)
    scr_list("MX", 8, [128, NX], BF16); scr_list("MXA", 8, [512, NX], BF16)
    scr("H1L", [D, NX], F32)
    inp("xT1", [D, NX])
    inp("ssd_norm", [128, 8]); inp("ssd_w_in", [D, 6176]); inp("conv_w", [128, 32, 4]); inp("conv_b", [128, 32])
    inp("dt_bias", [1, 32]); inp("a_log", [1, 32]); inp("ssd_d", [1, 32]); inp("gnorm", [128, 16]); inp("ssd_w_out", [2048, D])
    inp("final_norm", [128, 8]); inp("selmask", [128, 20])
    inp("tri_incl", [128, 128]); inp("tri_gt", [128, 128]); inp("ntri_incl", [128, 128]); inp("neg_ut", [128, 128])
    scr("XS", [NT, 2048], BF16); scr("BTOK", [NT, 1024], BF16); scr("BT", [1024, NT], BF16); scr("CT", [1024, NT], BF16)
    scr("ZS", [NT, 2048], F32); scr("DTD", [NT, 64], F32)
    T["ST"] = [nc.dram_tensor("ST_%d" % i, [128, 2048], F32).ap() for i in range(16)]; T["b_ST"] = Buf("ST")
    scr("SXa", [128, 2048], F32); scr("SXa_all", [512, 2048], F32); scr("SXb", [128, 32], F32); scr("SXb_all", [512, 32], F32)
    scr("YT", [2048, NT], BF16)
    T["dbg"] = nc.dram_tensor("dbg", [D, NT], F32, kind="ExternalOutput").ap()

    nph = int(os.environ.get("KPH", "99"))
    phase_attn_inproj(nc, T)
    if nph >= 2:
        phase_attn_core(nc, T)
    if nph >= 3:
        phase_attn_out_ffn(nc, T, stage)
    if stage not in ("mix", "attn", "ffn0"):
        if nph >= 4:
            phase_ssd_inproj(nc, T)
        if nph >= 5:
            phase_ssd_states(nc, T)
        if nph >= 6:
            phase_ssd_scan(nc, T)
        if nph >= 7:
            phase_ssd_out_ffn(nc, T, stage)
    Sched.STATE.close()
    return nc


def rope_tables(q):
    half = 32
    inv_freq = (10000.0 ** (-np.arange(half, dtype=np.float32) / half)).astype(np.float32)
    n = np.arange(NT)
    pos = ((4 * (n // 128) + q) * 128 + (n % 128)).astype(np.float32)
    ang = pos[None, :] * inv_freq[:, None]
    cos = np.cos(ang).astype(np.float32)
    sin = np.sin(ang).astype(np.float32)
    tab = np.zeros((128, 2, NT), np.float32)
    for p in range(128):
        d = p % 64
        tab[p, 0] = cos[d % 32]
        tab[p, 1] = -sin[d % 32] if d < 32 else sin[d % 32]
    return tab


def neg_masks(q):
    jj = np.arange(128)[:, None]
    tt = np.arange(128)[None, :]
    out = []
    for strict in (False, True):
        keep_diag = (jj < tt) if strict else (jj <= tt)
        m = np.zeros((128, 16, 512), np.float32)
        for kbz in range(16):
            for i in range(4):
                z = kbz - 4 * i
                blk = m[:, kbz, i * 128:(i + 1) * 128]
                if z < 0:
                    continue
                if z > 3 or z > q:
                    blk[:] = NEG
                elif z == q:
                    blk[:] = np.where(keep_diag, 0.0, NEG)
        out.append(m)
    return out


def make_in_maps(inputs):
    f = lambda a: np.ascontiguousarray(np.asarray(a, dtype=np.float32))
    x = f(inputs["x"])
    w_in = f(inputs["attn_w_in"][0])
    qk = w_in[:, :1024].reshape(D, 16, 2, 32)
    w_sw = np.ascontiguousarray(qk[:, :, ::-1, :].reshape(D, 1024))
    ar = np.arange(128)
    common = {
        "w_in0": w_in, "w_sw0": w_sw, "w_out0": f(inputs["attn_w_out"][0]),
        "attn_norm": f(inputs["attn_norm"][0].reshape(8, 128).T),
        "ident": np.eye(128, dtype=np.float32),
        "nu": -(np.arange(128)[:, None] >= np.arange(128)[None, :]).astype(np.float32),
        "lq1": f(inputs["diff_lq1"]), "lk1": f(inputs["diff_lk1"]), "lq2": f(inputs["diff_lq2"]), "lk2": f(inputs["diff_lk2"]),
        "subln": f(inputs["diff_subln"]),
        "ffn_norm": f(np.stack([inputs["ffn_norm"][l].reshape(8, 128).T for l in range(2)])),
        "ffn_wg": f(inputs["ffn_w_gate"]), "ffn_wu": f(inputs["ffn_w_up"]), "ffn_wd": f(inputs["ffn_w_down"]),
        "ssd_norm": f(inputs["ssd_norm"][0].reshape(8, 128).T), "ssd_w_in": f(inputs["ssd_w_in"][0]),
        "conv_w": f(inputs["ssd_conv_w"][0].reshape(4, 32, 128).transpose(2, 1, 0)),
        "conv_b": f(inputs["ssd_conv_b"][0].reshape(32, 128).T),
        "dt_bias": f(inputs["ssd_dt_bias"]), "a_log": f(inputs["ssd_a_log"]), "ssd_d": f(inputs["ssd_d"]),
        "gnorm": f(inputs["ssd_gnorm"][0].reshape(16, 128).T), "ssd_w_out": f(inputs["ssd_w_out"][0]),
        "final_norm": f(inputs["final_norm"].reshape(8, 128).T),
        "tri_incl": (ar[:, None] <= ar[None, :]).astype(np.float32),
        "tri_gt": (ar[:, None] > ar[None, :]).astype(np.float32),
        "ntri_incl": -(ar[:, None] <= ar[None, :]).astype(np.float32),
        "neg_ut": np.where(ar[None, :] < ar[:, None], NEG, 0.0).astype(np.float32),
    }
    maps = []
    for c in range(8):
        b, q = c // 4, c % 4
        n = np.arange(NT)
        pos = (4 * (n // 128) + q) * 128 + (n % 128)
        m = dict(common)
        m["xT0"] = np.ascontiguousarray(x[b, pos, :].T)
        x1 = np.zeros((D, NX), np.float32)
        x1[:, 3:] = x[b, q * NT:(q + 1) * NT, :].T
        if q > 0:
            x1[:, 0:3] = x[b, q * NT - 3:q * NT, :].T
        m["xT1"] = x1
        m["cs_tab"] = rope_tables(q)
        mi, ms = neg_masks(q)
        m["nm_incl"], m["nm_strict"] = mi, ms
        sel = np.zeros((128, 20), np.float32)
        for r in range(4):
            sel[:, r] = 1.0 if r < q else 0.0
            for r2 in range(4):
                sel[:, 4 + 4 * r + r2] = 1.0 if (r < r2 < q) else 0.0
        m["selmask"] = sel
        maps.append(m)
    return maps


def kernel(**inputs):
    stage = os.environ.get("KSTAGE", "final")
    nc = build_program(stage)
    maps = make_in_maps(inputs)
    res = run_bass_kernel_spmd(nc, maps, core_ids=list(range(8)))
    out = np.zeros((2, SEQ, D), np.float32)
    for c in range(8):
        b, q = c // 4, c % 4
        o = res.results[c]["dbg"]
        out[b, q * NT:(q + 1) * NT, :] = o.T
    return out
```

```python
import contextlib
import math
import os

import numpy as np
import concourse.bass as bass
import concourse.mybir as mybir
from concourse.bass_utils import run_bass_kernel_spmd

F32 = mybir.dt.float32
BF16 = mybir.dt.bfloat16
AF = mybir.ActivationFunctionType
ALU = mybir.AluOpType

D = 1024
SEQ = 8192
NT = 2048
NEG = -30000.0
GROUPS = [[0, 1, 2, 3], [4, 5, 6, 7]]
D_FF = 2816
FH = D_FF // 2
NX = 3 + NT
TOKSL = [slice(0, 3)] + [slice(3 + t * 512, 3 + (t + 1) * 512) for t in range(4)]


class Buf:
    __slots__ = ("name", "last_w", "rd_c", "rd_d")
    ALL = []

    def __init__(self, name=""):
        self.name = name
        self.last_w = None
        self.rd_c = {}
        self.rd_d = []
        Buf.ALL.append(self)

    @staticmethod
    def reset_all():
        for b in Buf.ALL:
            b.last_w = None
            b.rd_c = {}
            b.rd_d = []


class _Rec:
    def __init__(self):
        self.call = None

    def __getattr__(self, name):
        def f(*a, **k):
            self.call = (name, a, k)
            return None
        return f


def _freeze(fn):
    r = _Rec()
    fn(r)
    name, a, k = r.call
    return lambda e: getattr(e, name)(*a, **k)


class SemState:
    def __init__(self, nc):
        self.nc = nc
        self.st = contextlib.ExitStack()
        self.sems = {}
        self.base = {e: 0 for e in Sched.ENGS}
        self.semval = {}
        self.bar_count = 0

    def sem(self, key):
        if key not in self.sems:
            name = key if isinstance(key, str) else "_".join(str(k) for k in key)
            self.sems[key] = self.st.enter_context(self.nc.semaphore("s_" + name))
        return self.sems[key]

    def close(self):
        self.st.close()


class Sched:
    ENGS = ("sp", "act", "dve", "pool", "pe")
    STATE = None

    def __init__(self, nc, n_dma_sems=6):
        self.nc = nc
        self.state = Sched.STATE
        self.ops = {e: [] for e in self.ENGS}
        self.nops = {e: 0 for e in self.ENGS}
        self.known = {e: {} for e in self.ENGS}
        self.needed = {e: set() for e in self.ENGS}
        self.n_dma_sems = n_dma_sems
        self.dma_ring = {e: 0 for e in self.ENGS}
        self.semval = self.state.semval
        self.dma_keys = []

    def _wait(self, eng, ev):
        if ev is None:
            return
        if ev[0] == "c":
            _, src, idx = ev
            if src == "pe" and eng == "pe":
                return
            if self.known[eng].get(src, 0) >= idx:
                return
            self.known[eng][src] = idx
            self.needed[src].add(idx)
            self.ops[eng].append(("wait", ev))
        else:
            _, key, val = ev
            if self.known[eng].get(key, 0) >= val:
                return
            self.known[eng][key] = val
            self.ops[eng].append(("wait", ev))

    def _deps(self, eng, reads, writes):
        for b in reads:
            self._wait(eng, b.last_w)
        for b in writes:
            self._wait(eng, b.last_w)
            for src, idx in list(b.rd_c.items()):
                self._wait(eng, ("c", src, idx))
            for ev in b.rd_d:
                self._wait(eng, ev)

    def _commit(self, ev, reads, writes):
        for b in reads:
            if ev[0] == "c":
                if b.rd_c.get(ev[1], 0) < ev[2]:
                    b.rd_c[ev[1]] = ev[2]
            else:
                b.rd_d.append(ev)
        for b in writes:
            b.last_w = ev
            b.rd_c = {}
            b.rd_d = []

    def op(self, eng, fn, reads=(), writes=()):
        self._deps(eng, reads, writes)
        self.nops[eng] += 1
        idx = self.nops[eng]
        ev = ("c", eng, idx)
        self.ops[eng].append(("op", idx, _freeze(fn)))
        self._commit(ev, reads, writes)
        return ev

    def dma(self, eng, out, in_, reads=(), writes=(), inc=16, fn=None, ring=None):
        if ring is None:
            i = self.dma_ring[eng]
            self.dma_ring[eng] = i + 1
            key = ("dsem", eng, i % self.n_dma_sems)
        else:
            key = ring
        if key not in self.semval:
            self.semval[key] = 0
        if key not in self.dma_keys:
            self.dma_keys.append(key)
        prev = self.semval[key]
        if prev > 0:
            self._wait(eng, ("d", key, prev))
        self._deps(eng, reads, writes)
        self.semval[key] = prev + inc
        ev = ("d", key, prev + inc)
        if fn is None:
            fn = lambda e, out=out, in_=in_: e.dma_start(out=out, in_=in_)
        self.ops[eng].append(("dma", key, fn, inc))
        self._commit(ev, reads, writes)
        return ev

    def drain(self):
        for key in self.dma_keys:
            self._wait(key[1], ("d", key, self.semval[key]))

    def emit(self):
        nc = self.nc
        stt = self.state
        self.drain()
        sems = {}
        for e in self.ENGS:
            sems[e] = stt.sem("eng_" + e)
        for k in self.dma_keys:
            sems[k] = stt.sem(k)
        bar = stt.sem("bar")
        rank = {}
        for e in self.ENGS:
            if self.nops[e] > 0:
                self.needed[e].add(self.nops[e])
            rank[e] = {idx: stt.base[e] + r + 1 for r, idx in enumerate(sorted(self.needed[e]))}
        bar_target = stt.bar_count + 5
        with nc.Block() as block:
            def run(ename):
                def body(eng):
                    for ent in self.ops[ename]:
                        if ent[0] == "wait":
                            ev = ent[1]
                            if ev[0] == "c":
                                eng.wait_ge(sems[ev[1]], rank[ev[1]][ev[2]])
                            else:
                                eng.wait_ge(sems[ev[1]], ev[2])
                        elif ent[0] == "op":
                            ins = ent[2](eng)
                            if ent[1] in rank[ename]:
                                ins.then_inc(sems[ename], 1)
                        else:
                            ins = ent[2](eng)
                            ins.then_inc(sems[ent[1]], ent[3])
                    if self.nops[ename] > 0:
                        eng.wait_ge(sems[ename], rank[ename][self.nops[ename]])
                    eng.sem_inc(bar, 1)
                    eng.wait_ge(bar, bar_target)
                return body

            block.sync(run("sp"))
            block.scalar(run("act"))
            block.vector(run("dve"))
            block.gpsimd(run("pool"))
            block.tensor(run("pe"))
        if os.environ.get("KVERB"):
            print("emit: ops", {e: self.nops[e] for e in self.ENGS}, "entries", {e: len(self.ops[e]) for e in self.ENGS}, flush=True)
        for e in self.ENGS:
            stt.base[e] += len(self.needed[e])
        stt.bar_count = bar_target
        Buf.reset_all()


class Ctx:
    def __init__(self, nc):
        self.nc = nc
        self.S = Sched(nc)
        self.st = contextlib.ExitStack()
        self.n = 0

    CNT = [0]

    def sb(self, shape, dt=F32, name=None):
        Ctx.CNT[0] += 1
        return self.st.enter_context(self.nc.sbuf_tensor(name or ("t%d" % Ctx.CNT[0]), list(shape), dt))

    def ps(self, shape, dt=F32, name=None):
        Ctx.CNT[0] += 1
        return self.st.enter_context(self.nc.psum_tensor(name or ("p%d" % Ctx.CNT[0]), list(shape), dt))

    def close(self):
        self.S.emit()
        self.st.close()


COLL_N = [0]


def collective(S, kind, src, dst, reads, writes):
    def fn(e):
        return e.collective_compute(kind, ALU.bypass, replica_groups=GROUPS, ins=[src], outs=[dst])
    COLL_N[0] += 1
    return S.dma("pool", None, None, reads=reads, writes=writes, inc=1, fn=fn, ring=("csem", "pool", COLL_N[0] % 4))


def rmsnorm_fm(C, xT, bx, wn, bwn, xn, bxn, ones_f, bones, pss, eps=1e-6, ntok=NT, sqb=None, rstd=None, brstd=None, slices=None, out_f32=False):
    S = C.S
    if slices is None:
        slices = [slice(tt * 512, (tt + 1) * 512) for tt in range(ntok // 512)]
    for tt, sl in enumerate(slices):
        wd_ = sl.stop - sl.start
        ps, bps = pss[tt % len(pss)]
        for k in range(8):
            sq, bsq = sqb[k % len(sqb)]
            S.op("act", lambda e, sq=sq, k=k, sl=sl: e.activation(out=sq[:, 0:wd_], in_=xT[:, k, sl], func=AF.Square),
                 reads=[bx], writes=[bsq])
            S.op("pe", lambda e, ps=ps, sq=sq, k=k: e.matmul(ps[:, 0:wd_], lhsT=ones_f[:], rhs=sq[:, 0:wd_], start=(k == 0), stop=(k == 7)),
                 reads=[bsq, bones], writes=[bps])
        S.op("act", lambda e, ps=ps, sl=sl: e.activation(out=rstd[:, sl], in_=ps[:, 0:wd_], func=AF.Ln, scale=1.0 / D, bias=eps),
             reads=[bps], writes=[brstd])
        S.op("act", lambda e, sl=sl: e.activation(out=rstd[:, sl], in_=rstd[:, sl], func=AF.Exp, scale=-0.5),
             reads=[brstd], writes=[brstd])
        for k in range(8):
            S.op("dve", lambda e, k=k, sl=sl: e.scalar_tensor_tensor(out=xn[:, k, sl], in0=xT[:, k, sl], scalar=wn[:, k:k + 1],
                                                                     in1=rstd[:, sl], op0=ALU.mult, op1=ALU.mult),
                 reads=[bx, bwn, brstd], writes=[bxn])


def ffn_fm(C, T, layer, hT, bh, ones_f, bones, banks, slices=None):
    S = C.S
    if slices is None:
        slices = [slice(tt * 512, (tt + 1) * 512) for tt in range(NT // 512)]
    ntot = slices[-1].stop
    xn = C.sb([128, 8, ntot], BF16); bxn = Buf()
    rstd = C.sb([128, ntot], F32); brstd = Buf()
    wn = C.sb([128, 8], F32); bwn = Buf()
    S.dma("sp", wn[:], T["ffn_norm"][layer], writes=[bwn])
    sqb = [(C.sb([128, 512], F32), Buf()) for _ in range(2)]
    rmsnorm_fm(C, hT, bh, wn, bwn, xn, bxn, ones_f, bones, banks[0:2], sqb=sqb, rstd=rstd, brstd=brstd, slices=slices)
    wg = C.sb([128, 8, FH], BF16); bwg = Buf()
    wu = C.sb([128, 8, FH], BF16); bwu = Buf()
    wd = C.sb([128, 11, D], BF16); bwd = Buf()
    act = [(C.sb([128, 11, 512], BF16), Buf()) for _ in range(2)]
    sg = [(C.sb([128, 512], F32), Buf()) for _ in range(2)]
    for half in range(2):
        c0 = half * FH
        for k in range(8):
            S.dma("pool", wg[:, k, :], T["ffn_wg"][layer][k * 128:(k + 1) * 128, c0:c0 + FH], writes=[bwg])
            S.dma("pool", wu[:, k, :], T["ffn_wu"][layer][k * 128:(k + 1) * 128, c0:c0 + FH], writes=[bwu])
        for fc in range(11):
            S.dma("pool", wd[:, fc, :], T["ffn_wd"][layer][c0 + fc * 128:c0 + (fc + 1) * 128, :], writes=[bwd])
        for tt, sl in enumerate(slices):
            wd_ = sl.stop - sl.start
            a, ba = act[tt % 2]
            for fc in range(11):
                pg, bpg = banks[(2 * fc) % 4]
                pu, bpu = banks[(2 * fc + 1) % 4]
                for k in range(8):
                    S.op("pe", lambda e, pg=pg, k=k, fc=fc, sl=sl: e.matmul(pg[:, 0:wd_], lhsT=wg[:, k, fc * 128:(fc + 1) * 128], rhs=xn[:, k, sl],
                                                                            start=(k == 0), stop=(k == 7)),
                         reads=[bwg, bxn], writes=[bpg])
                for k in range(8):
                    S.op("pe", lambda e, pu=pu, k=k, fc=fc, sl=sl: e.matmul(pu[:, 0:wd_], lhsT=wu[:, k, fc * 128:(fc + 1) * 128], rhs=xn[:, k, sl],
                                                                            start=(k == 0), stop=(k == 7)),
                         reads=[bwu, bxn], writes=[bpu])
                s_, bs_ = sg[fc % 2]
                S.op("act", lambda e, s_=s_, pg=pg: e.activation(out=s_[:, 0:wd_], in_=pg[:, 0:wd_], func=AF.Silu), reads=[bpg], writes=[bs_])
                S.op("dve", lambda e, a=a, fc=fc, s_=s_, pu=pu: e.tensor_tensor(out=a[:, fc, 0:wd_], in0=pu[:, 0:wd_], in1=s_[:, 0:wd_], op=ALU.mult),
                     reads=[bpu, bs_], writes=[ba])
            for oc in range(8):
                po, bpo = banks[4 + oc % 4]
                for fc in range(11):
                    S.op("pe", lambda e, po=po, fc=fc, oc=oc, a=a: e.matmul(po[:, 0:wd_], lhsT=wd[:, fc, oc * 128:(oc + 1) * 128], rhs=a[:, fc, 0:wd_],
                                                                            start=(fc == 0), stop=(fc == 10)),
                         reads=[bwd, ba], writes=[bpo])
                S.op("dve", lambda e, po=po, oc=oc, sl=sl: e.tensor_tensor(out=hT[:, oc, sl], in0=po[:, 0:wd_], in1=hT[:, oc, sl], op=ALU.add),
                     reads=[bpo, bh], writes=[bh])


def load_consts(C, T):
    S = C.S
    k = {}
    idf = C.sb([128, 128], F32); bidf = Buf()
    S.dma("sp", idf[:], T["ident"], writes=[bidf])
    k["identb"] = C.sb([128, 128], BF16); k["bident"] = Buf()
    S.op("dve", lambda e: e.tensor_copy(out=k["identb"][:], in_=idf[:]), reads=[bidf], writes=[k["bident"]])
    k["ones_f"] = C.sb([128, 128], F32); k["bones"] = Buf()
    S.op("pool", lambda e: e.memset(k["ones_f"][:], 1.0), writes=[k["bones"]])
    k["idf"] = idf; k["bidf"] = bidf
    return k


def phase_attn_inproj(nc, T):
    C = Ctx(nc); S = C.S
    K = load_consts(C, T)
    banks = [(C.ps([128, 512], F32), Buf()) for _ in range(8)]
    xT = C.sb([128, 8, NT], F32); bx = Buf()
    xsrc = T["xT0"].rearrange("(k p) n -> p k n", p=128)
    for tt in range(4):
        S.dma("sp", xT[:, :, tt * 512:(tt + 1) * 512], xsrc[:, :, tt * 512:(tt + 1) * 512], writes=[bx])
    cs = C.sb([128, 2, NT], F32); bcs = Buf()
    S.dma("act", cs[:], T["cs_tab"], writes=[bcs])
    wn = C.sb([128, 8], F32); bwn = Buf()
    S.dma("act", wn[:], T["attn_norm"], writes=[bwn])
    xn = C.sb([128, 8, NT], BF16); bxn = Buf()
    rstd = C.sb([128, NT], F32); brstd = Buf()
    sqb = [(C.sb([128, 512], F32), Buf()) for _ in range(2)]
    rmsnorm_fm(C, xT, bx, wn, bwn, xn, bxn, K["ones_f"], K["bones"], banks[0:2], sqb=sqb, rstd=rstd, brstd=brstd)

    kcut = int(os.environ.get("KCUT", "9"))
    if kcut <= 1:
        C.close()
        return
    wb = [(C.sb([128, 8, 512], BF16), Buf()) for _ in range(2)]
    wsw = [(C.sb([128, 8, 512], BF16), Buf()) for _ in range(2)]
    ob = [(C.sb([128, NT], BF16), Buf()) for _ in range(2)]
    t1 = [(C.sb([128, 512], F32), Buf()) for _ in range(2)]
    t2 = [(C.sb([128, 512], F32), Buf()) for _ in range(2)]
    vst = [(C.sb([128, 4, 129], BF16), Buf()) for _ in range(2)]
    vss = [(C.sb([128, 512], BF16), Buf()) for _ in range(2)]
    for v, bv in vst:
        S.op("pool", lambda e, v=v: e.memset(v[:], 1.0), writes=[bv])
    wsrc = T["w_in0"].rearrange("(k p) c -> p k c", p=128)
    swsrc = T["w_sw0"].rearrange("(k p) c -> p k c", p=128)
    dst_fm = {0: ("QT", 0), 1: ("KTd", 0), 3: ("QT", 512), 4: ("KTs", 0)}
    nchunk = 0
    order = (1, 2, 4, 5, 0, 3)

    def load_w(gi):
        g = order[gi]
        w, bw = wb[gi % 2]
        for k in range(8):
            S.dma("pool", w[:, k, :], wsrc[:, k, g * 512:(g + 1) * 512], writes=[bw])
        if g < 2:
            w2, bw2 = wsw[g % 2]
            for k in range(8):
                S.dma("pool", w2[:, k, :], swsrc[:, k, g * 512:(g + 1) * 512], writes=[bw2])

    load_w(0)
    for gi, g in enumerate(order):
        if gi + 1 < len(order):
            load_w(gi + 1)
        w, bw = wb[gi % 2]
        if g < 2:
            w2, bw2 = wsw[g % 2]
        if g in dst_fm:
            dname, roff = dst_fm[g]
            for cc in range(4):
                o, bo = ob[nchunk % 2]; nchunk += 1
                for tt in range(4):
                    sl = slice(tt * 512, (tt + 1) * 512)
                    p1, bp1 = banks[2 + (2 * tt) % 4]
                    for k in range(8):
                        S.op("pe", lambda e, p1=p1, w=w, k=k, cc=cc, sl=sl: e.matmul(p1[:], lhsT=w[:, k, cc * 128:(cc + 1) * 128], rhs=xn[:, k, sl],
                                                                                     start=(k == 0), stop=(k == 7)),
                             reads=[bw, bxn], writes=[bp1])
                    if g < 2:
                        p2, bp2 = banks[2 + (2 * tt + 1) % 4]
                        for k in range(8):
                            S.op("pe", lambda e, p2=p2, w2=w2, k=k, cc=cc, sl=sl: e.matmul(p2[:], lhsT=w2[:, k, cc * 128:(cc + 1) * 128], rhs=xn[:, k, sl],
                                                                                           start=(k == 0), stop=(k == 7)),
                                 reads=[bw2, bxn], writes=[bp2])
                        a1, ba1 = t1[tt % 2]
                        a2, ba2 = t2[tt % 2]
                        S.op("dve", lambda e, a1=a1, p1=p1, sl=sl: e.tensor_tensor(out=a1[:], in0=p1[:], in1=cs[:, 0, sl], op=ALU.mult),
                             reads=[bp1, bcs], writes=[ba1])
                        S.op("dve", lambda e, a2=a2, p2=p2, sl=sl: e.tensor_tensor(out=a2[:], in0=p2[:], in1=cs[:, 1, sl], op=ALU.mult),
                             reads=[bp2, bcs], writes=[ba2])
                        S.op("pool", lambda e, o=o, a1=a1, a2=a2, sl=sl: e.tensor_tensor(out=o[:, sl], in0=a1[:], in1=a2[:], op=ALU.add),
                             reads=[ba1, ba2], writes=[bo])
                    else:
                        sc = 0.125 if g == 3 else 1.0
                        S.op("act", lambda e, o=o, p1=p1, sl=sl, sc=sc: e.activation(out=o[:, sl], in_=p1[:], func=AF.Copy, scale=sc),
                             reads=[bp1], writes=[bo])
                if dname == "QT":
                    r0 = roff + cc * 128
                    S.dma("sp", T["QT"][r0:r0 + 128, :], o[:], reads=[bo], writes=[T["b_QT"]])
                else:
                    ch, r0 = cc // 2, (cc % 2) * 128
                    S.dma("sp", T[dname][ch][r0:r0 + 128, :], o[:], reads=[bo], writes=[T["b_" + dname][ch]])
                    if cc % 2 == 1:
                        collective(S, "AllGather", T[dname][ch], T[dname + "_all"][ch], reads=[T["b_" + dname][ch]], writes=[T["b_" + dname + "_all"][ch]])
        else:
            for tb in range(16):
                p1, bp1 = banks[2 + tb % 4]
                for k in range(8):
                    S.op("pe", lambda e, p1=p1, w=w, k=k, tb=tb: e.matmul(p1[:], lhsT=xn[:, k, tb * 128:(tb + 1) * 128], rhs=w[:, k, :],
                                                                          start=(k == 0), stop=(k == 7)),
                         reads=[bw, bxn], writes=[bp1])
                if g == 2:
                    v, bv = vst[tb % 2]
                    S.op("act", lambda e, v=v, p1=p1: e.activation(out=v[:, :, 0:128], in_=p1[:].rearrange("p (h c) -> p h c", c=128), func=AF.Copy),
                         reads=[bp1], writes=[bv])
                    ch, r0 = tb // 4, (tb % 4) * 128
                    S.dma("sp", T["Vd"][ch][r0:r0 + 128, :], v[:].rearrange("p h c -> p (h c)"), reads=[bv], writes=[T["b_Vd"][ch]])
                    if tb % 4 == 3:
                        collective(S, "AllGather", T["Vd"][ch], T["Vd_all"][ch], reads=[T["b_Vd"][ch]], writes=[T["b_Vd_all"][ch]])
                else:
                    v, bv = vss[tb % 2]
                    S.op("act", lambda e, v=v, p1=p1: e.activation(out=v[:], in_=p1[:], func=AF.Copy), reads=[bp1], writes=[bv])
                    ch, r0 = tb // 8, (tb % 8) * 128
                    S.dma("sp", T["Vs"][ch][r0:r0 + 128, :], v[:], reads=[bv], writes=[T["b_Vs"][ch]])
                    if tb % 8 == 7:
                        collective(S, "AllGather", T["Vs"][ch], T["Vs_all"][ch], reads=[T["b_Vs"][ch]], writes=[T["b_Vs_all"][ch]])
    C.close()


def tile_iters(J):
    out = []
    for kb in range(16 * J + 16):
        i0 = max(0, kb // 4 - 4 * J)
        z = kb - 16 * J if kb >= 16 * J else None
        out.append((kb, kb % 4, kb // 4, i0, z))
    return out


def phase_attn_core(nc, T):
    C = Ctx(nc); S = C.S
    K = load_consts(C, T)
    identb, bident = K["identb"], K["bident"]
    ones_f, bones = K["ones_f"], K["bones"]
    psall = C.ps([128, 8, 512], F32)
    banks = [(psall[:, i, :], Buf()) for i in range(8)]
    nuf = C.sb([128, 128], F32); bnuf = Buf()
    S.dma("sp", nuf[:], T["nu"], writes=[bnuf])
    NU = C.sb([128, 128], BF16); bNU = Buf()
    S.op("dve", lambda e: e.tensor_copy(out=NU[:], in_=nuf[:]), reads=[bnuf], writes=[bNU])
    onesb = C.sb([128, 128], BF16); bonesb = Buf()
    S.op("pool", lambda e: e.memset(onesb[:], 1.0), writes=[bonesb])
    nmI = C.sb([128, 16, 512], BF16); bnmI = Buf()
    nmS = C.sb([128, 16, 512], BF16); bnmS = Buf()
    for z in range(0, 16, 4):
        S.dma("pool", nmI[:, z:z + 4, :], T["nm_incl"][:, z:z + 4, :], writes=[bnmI])
        S.dma("pool", nmS[:, z:z + 4, :], T["nm_strict"][:, z:z + 4, :], writes=[bnmS])
    lv = C.sb([128, 4, 64], F32); blv = Buf()
    for i, nm in enumerate(("lq1", "lk1", "lq2", "lk2")):
        S.dma("sp", lv[:, i:i + 1, :], T[nm].partition_broadcast(128), writes=[blv])
    lt = C.sb([128, 2, 64], F32); blt = Buf()
    S.op("dve", lambda e: e.tensor_tensor(out=lt[:, 0, :], in0=lv[:, 0, :], in1=lv[:, 1, :], op=ALU.mult), reads=[blv], writes=[blt])
    S.op("dve", lambda e: e.tensor_tensor(out=lt[:, 1, :], in0=lv[:, 2, :], in1=lv[:, 3, :], op=ALU.mult), reads=[blv], writes=[blt])
    ls = C.sb([128, 4], F32); bls = Buf()
    S.op("dve", lambda e: e.reduce_sum(out=ls[:, 0:1], in_=lt[:, 0, :], axis=mybir.AxisListType.X), reads=[blt], writes=[bls])
    S.op("dve", lambda e: e.reduce_sum(out=ls[:, 1:2], in_=lt[:, 1, :], axis=mybir.AxisListType.X), reads=[blt], writes=[bls])
    S.op("act", lambda e: e.activation(out=ls[:, 0:2], in_=ls[:, 0:2], func=AF.Exp), reads=[bls], writes=[bls])
    S.op("dve", lambda e: e.tensor_tensor(out=ls[:, 2:3], in0=ls[:, 1:2], in1=ls[:, 0:1], op=ALU.subtract), reads=[bls], writes=[bls])
    S.op("dve", lambda e: e.tensor_scalar(out=ls[:, 3:4], in0=ls[:, 2:3], scalar1=-0.2, scalar2=None, op0=ALU.add), reads=[bls], writes=[bls])
    negl = ls[:, 3:4]
    sl_ = C.sb([128, 1], F32); bsl = Buf()
    S.dma("sp", sl_[:], T["subln"].rearrange("o v -> v o"), writes=[bsl])
    S.op("dve", lambda e: e.tensor_scalar(out=sl_[:], in0=sl_[:], scalar1=0.8, scalar2=None, op0=ALU.mult), reads=[bsl], writes=[bsl])

    zcb = C.sb([128, 4], BF16); bzcb = Buf()
    S.op("pool", lambda e: e.memset(zcb[:], 0.0), writes=[bzcb])
    for c_ in range(8):
        S.dma("sp", T["MX"][c_][:, 0:3], zcb[:, 0:3], reads=[bzcb], writes=[T["b_MX"][c_]])
    kbuf = [(C.sb([128, 4, NT], BF16), Buf()) for _ in range(2)]
    vbuf = [(C.sb([128, 64 * 129], BF16), Buf()) for _ in range(2)]
    qz = [[(C.sb([128, NT], BF16), Buf()) for _ in range(2)] for _ in range(2)]
    for b_ in range(2):
        S.op("pool", lambda e, b_=b_: e.memset(qz[b_][0][0][64:128, :], 0.0), writes=[qz[b_][0][1]])
        S.op("pool", lambda e, b_=b_: e.memset(qz[b_][1][0][0:64, :], 0.0), writes=[qz[b_][1][1]])
    ostg = [(C.sb([128, 512], BF16), Buf()) for _ in range(2)]
    nload = 0
    nout = 0

    Eb = [(C.sb([128, 2, 512], BF16), Buf()) for _ in range(2)]
    ep = [(C.sb([128, 512], F32), Buf()) for _ in range(5)]
    ktd_all = [a.rearrange("(r x) n -> x r n", r=4) for a in T["KTd_all"]]
    vd_all = [a.rearrange("(r j t) (h c) -> t r j h c", r=4, j=4, h=4) for a in T["Vd_all"]]
    for H in range(4):
        kt, bkt = kbuf[nload % 2]; vv, bvv = vbuf[nload % 2]; qzz = qz[nload % 2]; nload += 1
        for r in range(4):
            S.dma("sp", kt[:, r, :], ktd_all[H // 2][(H % 2) * 128:(H % 2 + 1) * 128, r, :], reads=[T["b_KTd_all"][H // 2]], writes=[bkt])
            for vc in range(4):
                b0 = r * 16 + 4 * vc
                S.dma("sp", vv[:, b0 * 129:(b0 + 4) * 129].rearrange("p (j c) -> p j c", c=129), vd_all[vc][:, r, :, H, :],
                      reads=[T["b_Vd_all"][vc]], writes=[bvv])
        for m in range(2):
            S.dma("sp", qzz[m][0][m * 64:(m + 1) * 64, :], T["QT"][H * 128 + m * 64:H * 128 + (m + 1) * 64, :], reads=[T["b_QT"]], writes=[qzz[m][1]])
        v4 = vv[:].rearrange("p (b c) -> p b c", c=129)
        for J in range(4):
            its = tile_iters(J)
            q0 = J * 512
            for bk_i in (4, 5, 6, 7):
                S.op("dve", lambda e, bk_i=bk_i: e.memset(banks[bk_i][0], 0.0), writes=[banks[bk_i][1]])

            def stage1(it, slot):
                kb, r, j, i0, z = it
                w0 = i0 * 128
                for m in range(2):
                    ps, bps = banks[2 * slot + m]
                    S.op("pe", lambda e, ps=ps, m=m, r=r, j=j, w0=w0: e.matmul(
                        ps[:, w0:512], lhsT=kt[:, r, j * 128:(j + 1) * 128], rhs=qzz[m][0][:, q0 + w0:q0 + 512],
                        start=True, stop=(z is None)), reads=[bkt, qzz[m][1]], writes=[bps])
                    if z is not None:
                        S.op("pe", lambda e, ps=ps, z=z, w0=w0: e.matmul(ps[:, w0:512], lhsT=identb[:], rhs=nmI[:, z, w0:512], start=False, stop=True),
                             reads=[bident, bnmI], writes=[bps])

            def stage2(it, slot):
                kb, r, j, i0, z = it
                w0 = i0 * 128
                E, bE = Eb[slot]
                S.op("act", lambda e, E=E, slot=slot, w0=w0: e.activation(out=E[:, :, w0:512], in_=psall[:, 2 * slot:2 * slot + 2, w0:512], func=AF.Exp, scale=0.125),
                     reads=[banks[2 * slot][1], banks[2 * slot + 1][1]], writes=[bE])

            def stage3(it, slot):
                kb, r, j, i0, z = it
                w0 = i0 * 128
                E, bE = Eb[slot]
                for m in range(2):
                    S.op("pe", lambda e, E=E, m=m, r=r, j=j, w0=w0: e.matmul(banks[4 + m][0][:, w0:512], lhsT=v4[:, r * 16 + j, 0:128], rhs=E[:, m, w0:512],
                                                                             start=False, stop=False, skip_group_check=True),
                         reads=[bE, bvv], writes=[banks[4 + m][1]])
                    S.op("pe", lambda e, E=E, m=m, w0=w0: e.matmul(banks[6 + m][0][:, w0:512], lhsT=onesb[:], rhs=E[:, m, w0:512],
                                                                   start=False, stop=False, skip_group_check=True),
                         reads=[bE, bonesb], writes=[banks[6 + m][1]])

            for n in range(len(its) + 2):
                if n < len(its):
                    stage1(its[n], n % 2)
                if 1 <= n <= len(its):
                    stage2(its[n - 1], (n - 1) % 2)
                if n >= 2:
                    stage3(its[n - 2], (n - 2) % 2)
            (r1, br1), (t1, bt1), (t2, bt2), (od, bod), (sq, bsq) = ep
            S.op("dve", lambda e: e.reciprocal(out=r1[:], in_=banks[6][0]), reads=[banks[6][1]], writes=[br1])
            S.op("dve", lambda e: e.tensor_tensor(out=t1[:], in0=banks[4][0], in1=r1[:], op=ALU.mult), reads=[banks[4][1], br1], writes=[bt1])
            S.op("dve", lambda e: e.reciprocal(out=r1[:], in_=banks[7][0]), reads=[banks[7][1], bt1], writes=[br1])
            S.op("dve", lambda e: e.tensor_tensor(out=t2[:], in0=banks[5][0], in1=r1[:], op=ALU.mult), reads=[banks[5][1], br1], writes=[bt2])
            S.op("dve", lambda e: e.scalar_tensor_tensor(out=od[:], in0=t2[:], scalar=negl, in1=t1[:], op0=ALU.mult, op1=ALU.add),
                 reads=[bt1, bt2, bls], writes=[bod])
            S.op("act", lambda e: e.activation(out=sq[:], in_=od[:], func=AF.Square), reads=[bod], writes=[bsq])
            pss, bpss = banks[0]
            S.op("pe", lambda e: e.matmul(pss, lhsT=ones_f[:], rhs=sq[:], start=True, stop=True), reads=[bones, bsq], writes=[bpss])
            S.op("act", lambda e: e.activation(out=t1[:], in_=pss, func=AF.Ln, scale=1.0 / 128, bias=1e-5), reads=[bpss, bod], writes=[bt1])
            S.op("act", lambda e: e.activation(out=t1[:], in_=t1[:], func=AF.Exp, scale=-0.5), reads=[bt1], writes=[bt1])
            o_, bo_ = ostg[nout % 2]; nout += 1
            S.op("dve", lambda e, o_=o_: e.scalar_tensor_tensor(out=o_[:], in0=od[:], scalar=sl_[:, 0:1], in1=t1[:], op0=ALU.mult, op1=ALU.mult),
                 reads=[bod, bsl, bt1], writes=[bo_])
            S.dma("sp", T["MX"][H][:, 3 + q0:3 + q0 + 512], o_[:], reads=[bo_], writes=[T["b_MX"][H]])
        collective(S, "AllGather", T["MX"][H], T["MXA"][H], reads=[T["b_MX"][H]], writes=[T["b_MXA"][H]])

    eb = [(C.sb([128, 2, 512], F32), Buf()) for _ in range(2)]
    spb = [(C.sb([128, 2, 512], F32), Buf()) for _ in range(2)]
    hib = [(C.sb([128, 2, 512], BF16), Buf()) for _ in range(2)]
    lob = [(C.sb([128, 2, 512], BF16), Buf()) for _ in range(2)]
    Ab = [(C.sb([128, 2, 512], BF16), Buf()) for _ in range(2)]
    fb = [(C.sb([128, 512], F32), Buf()) for _ in range(2)]
    OT = C.sb([128, 512], F32); bOT = Buf()
    kts_all = [a.rearrange("(r x) n -> x r n", r=4) for a in T["KTs_all"]]
    vs_all = [a.rearrange("(r j t) (pr c) -> t r j pr c", r=4, j=8, pr=4) for a in T["Vs_all"]]
    for pr in range(4):
        kt, bkt = kbuf[nload % 2]; vv, bvv = vbuf[nload % 2]; qzz = qz[nload % 2]; nload += 1
        for r in range(4):
            S.dma("sp", kt[:, r, :], kts_all[pr // 2][(pr % 2) * 128:(pr % 2 + 1) * 128, r, :], reads=[T["b_KTs_all"][pr // 2]], writes=[bkt])
            for vc in range(2):
                b0 = r * 16 + 8 * vc
                S.dma("sp", vv[:, b0 * 128:(b0 + 8) * 128].rearrange("p (j c) -> p j c", c=128), vs_all[vc][:, r, :, pr, :],
                      reads=[T["b_Vs_all"][vc]], writes=[bvv])
        for m in range(2):
            S.dma("sp", qzz[m][0][m * 64:(m + 1) * 64, :], T["QT"][512 + pr * 128 + m * 64:512 + pr * 128 + (m + 1) * 64, :], reads=[T["b_QT"]], writes=[qzz[m][1]])
        v4 = vv[:, 0:64 * 128].rearrange("p (b c) -> p b c", c=128)
        for J in range(4):
            its = tile_iters(J)
            q0 = J * 512
            S.op("pool", lambda e: e.memset(OT[:], 0.0), writes=[bOT])

            def stage1(it, slot):
                kb, r, j, i0, z = it
                w0 = i0 * 128
                ee, bee = eb[slot]; sp_, bsp = spb[slot]; hi, bhi = hib[slot]; lo, blo = lob[slot]
                for hh in range(2):
                    p0 = hh * 64
                    ps, bps = banks[2 * slot + hh]
                    S.op("pe", lambda e, ps=ps, r=r, j=j, w0=w0, hh=hh: e.matmul(
                        ps[:, w0:512], lhsT=kt[:, r, j * 128:(j + 1) * 128], rhs=qzz[hh][0][:, q0 + w0:q0 + 512],
                        start=True, stop=(z is None)), reads=[bkt, qzz[hh][1]], writes=[bps])
                    if z is not None:
                        S.op("pe", lambda e, ps=ps, z=z, w0=w0: e.matmul(ps[:, w0:512], lhsT=identb[:], rhs=nmS[:, z, w0:512], start=False, stop=True),
                             reads=[bident, bnmS], writes=[bps])
                zz = psall[:, 2 * slot:2 * slot + 2, w0:512]
                bz = [banks[2 * slot][1], banks[2 * slot + 1][1]]
                S.op("act", lambda e, ee=ee, zz=zz, w0=w0: e.activation(out=ee[:, :, w0:512], in_=zz, func=AF.Exp), reads=bz, writes=[bee])
                S.op("act", lambda e, ee=ee, sp_=sp_, w0=w0: e.activation(out=sp_[:, :, w0:512], in_=ee[:, :, w0:512], func=AF.Ln, bias=1.0),
                     reads=[bee], writes=[bsp])
                S.op("dve", lambda e, hi=hi, sp_=sp_, w0=w0: e.tensor_copy(out=hi[:, :, w0:512], in_=sp_[:, :, w0:512]), reads=[bsp], writes=[bhi])
                S.op("dve", lambda e, lo=lo, hi=hi, sp_=sp_, w0=w0: e.tensor_tensor(out=lo[:, :, w0:512], in0=sp_[:, :, w0:512], in1=hi[:, :, w0:512], op=ALU.subtract),
                     reads=[bsp, bhi], writes=[blo])

            def stage2(it, slot):
                kb, r, j, i0, z = it
                w0 = i0 * 128
                hi, bhi = hib[slot]; lo, blo = lob[slot]; A, bA = Ab[slot]; f, bf_ = fb[slot]
                bz = [banks[2 * slot][1], banks[2 * slot + 1][1]]
                for hh in range(2):
                    ps, bps = banks[2 * slot + hh]
                    S.op("pe", lambda e, ps=ps, hi=hi, hh=hh, w0=w0: e.matmul(ps[:, w0:512], lhsT=NU[:], rhs=hi[:, hh, w0:512], start=False, stop=False, skip_group_check=True),
                         reads=[bNU, bhi], writes=[bps])
                    S.op("pe", lambda e, ps=ps, lo=lo, hh=hh, w0=w0: e.matmul(ps[:, w0:512], lhsT=NU[:], rhs=lo[:, hh, w0:512], start=False, stop=True, skip_group_check=True),
                         reads=[bNU, blo], writes=[bps])
                pC, bpC = banks[6 + slot]
                for hh in range(2):
                    p0 = hh * 64
                    S.op("pe", lambda e, pC=pC, hi=hi, hh=hh, p0=p0, w0=w0: e.matmul(pC[p0:p0 + 64, w0:512], lhsT=onesb[:, 0:64], rhs=hi[:, hh, w0:512], start=True, stop=False),
                         reads=[bhi, bonesb], writes=[bpC])
                    S.op("pe", lambda e, pC=pC, lo=lo, hh=hh, p0=p0, w0=w0: e.matmul(pC[p0:p0 + 64, w0:512], lhsT=onesb[:, 0:64], rhs=lo[:, hh, w0:512], start=False, stop=True),
                         reads=[blo, bonesb], writes=[bpC])
                zz = psall[:, 2 * slot:2 * slot + 2, w0:512]
                S.op("act", lambda e, A=A, zz=zz, w0=w0: e.activation(out=A[:, :, w0:512], in_=zz, func=AF.Exp), reads=bz, writes=[bA])
                S.op("act", lambda e, f=f, pC=pC, w0=w0: e.activation(out=f[:, w0:512], in_=pC[:, w0:512], func=AF.Exp, scale=-1.0), reads=[bpC], writes=[bf_])

            def stage3(it, slot):
                kb, r, j, i0, z = it
                w0 = i0 * 128
                A, bA = Ab[slot]; f, bf_ = fb[slot]
                pP, bpP = banks[4 + slot]
                for hh in range(2):
                    p0 = hh * 64
                    S.op("pe", lambda e, pP=pP, A=A, hh=hh, p0=p0, r=r, j=j, w0=w0: e.matmul(pP[p0:p0 + 64, w0:512], lhsT=v4[:, r * 16 + j, p0:p0 + 64], rhs=A[:, hh, w0:512],
                                                                                            start=True, stop=True),
                         reads=[bA, bvv], writes=[bpP])
                S.op("dve", lambda e, f=f, w0=w0: e.tensor_tensor(out=OT[:, w0:512], in0=OT[:, w0:512], in1=f[:, w0:512], op=ALU.mult), reads=[bOT, bf_], writes=[bOT])
                S.op("dve", lambda e, pP=pP, w0=w0: e.tensor_tensor(out=OT[:, w0:512], in0=pP[:, w0:512], in1=OT[:, w0:512], op=ALU.add), reads=[bOT, bpP], writes=[bOT])

            for n in range(len(its) + 2):
                if n < len(its):
                    stage1(its[n], n % 2)
                if 1 <= n <= len(its):
                    stage2(its[n - 1], (n - 1) % 2)
                if n >= 2:
                    stage3(its[n - 2], (n - 2) % 2)
            o_, bo_ = ostg[nout % 2]; nout += 1
            S.op("act", lambda e, o_=o_: e.activation(out=o_[:], in_=OT[:], func=AF.Copy), reads=[bOT], writes=[bo_])
            S.dma("sp", T["MX"][4 + pr][:, 3 + q0:3 + q0 + 512], o_[:], reads=[bo_], writes=[T["b_MX"][4 + pr]])
        collective(S, "AllGather", T["MX"][4 + pr], T["MXA"][4 + pr], reads=[T["b_MX"][4 + pr]], writes=[T["b_MXA"][4 + pr]])
    C.close()


def phase_attn_out_ffn(nc, T, stage):
    C = Ctx(nc); S = C.S
    K = load_consts(C, T)
    banks = [(C.ps([128, 512], F32), Buf()) for _ in range(8)]
    hT = C.sb([128, 8, NX], F32); bh = Buf()
    xsrc = T["xT1"].rearrange("(k p) n -> p k n", p=128)
    for sl in TOKSL:
        S.dma("sp", hT[:, :, sl], xsrc[:, :, sl], writes=[bh])
    dst = T["dbg"].rearrange("(k p) n -> p k n", p=128)

    def dump():
        for tt in range(4):
            S.dma("sp", dst[:, :, tt * 512:(tt + 1) * 512], hT[:, :, 3 + tt * 512:3 + (tt + 1) * 512], reads=[bh])

    with contextlib.ExitStack() as st2:
        mT = st2.enter_context(nc.sbuf_tensor("mTf", [128, 8, NX], BF16)); bmT = Buf()
        wo = st2.enter_context(nc.sbuf_tensor("wo", [128, 8, D], BF16)); bwo = Buf()
        cache = {}

        def rank_of(e):
            if "c" not in cache:
                cache["c"] = e.partition_id() % 4
            return cache["c"]
        for c in range(8):
            for r in range(4):
                dstv = mT[:, c, 3:3 + NT].rearrange("p (m r t) -> p m r t", m=4, r=4)[:, :, r, :]

                def fn(e, dstv=dstv, c=c, r=r):
                    cid = rank_of(e)
                    src = T["MXA"][c][r * 128:(r + 1) * 128, bass.ds(cid * 512 + 3, 512)]
                    return e.dma_start(out=dstv, in_=src.rearrange("p (m t) -> p m t", m=4))
                S.dma("sp", None, None, reads=[T["b_MXA"][c]], writes=[bmT], fn=fn)

            def fn2(e, c=c):
                cid = rank_of(e)
                return e.dma_start(out=mT[:, c, 0:3], in_=T["MXA"][c][3 * 128:4 * 128, bass.ds(cid * 512, 3)])
            S.dma("sp", None, None, reads=[T["b_MXA"][c]], writes=[bmT], fn=fn2)
            S.dma("pool", wo[:, c, :], T["w_out0"][c * 128:(c + 1) * 128, :], writes=[bwo])
        for tt, sl in enumerate(TOKSL):
            wd_ = sl.stop - sl.start
            for oc in range(8):
                po, bpo = banks[oc % 4]
                for c in range(8):
                    S.op("pe", lambda e, po=po, c=c, oc=oc, sl=sl: e.matmul(po[:, 0:wd_], lhsT=wo[:, c, oc * 128:(oc + 1) * 128], rhs=mT[:, c, sl],
                                                                            start=(c == 0), stop=(c == 7)),
                         reads=[bwo, bmT], writes=[bpo])
                S.op("dve", lambda e, po=po, oc=oc, sl=sl: e.tensor_tensor(out=hT[:, oc, sl], in0=po[:, 0:wd_], in1=hT[:, oc, sl], op=ALU.add),
                     reads=[bpo, bh], writes=[bh])
        if stage == "mix":
            for c in range(8):
                S.op("dve", lambda e, c=c: e.tensor_copy(out=hT[:, c, :], in_=mT[:, c, :]), reads=[bmT, bh], writes=[bh])
        if stage in ("mix", "attn"):
            dump()
            C.S.emit()
            st2.close(); C.st.close()
            return
        C.S.emit()
    C.S = Sched(nc); S = C.S
    ffn_fm(C, T, 0, hT, bh, K["ones_f"], K["bones"], banks, slices=TOKSL)
    if stage == "ffn0":
        dump()
        C.close()
        return
    hdst = T["H1L"].rearrange("(k p) n -> p k n", p=128)
    for sl in TOKSL:
        S.dma("sp", hdst[:, :, sl], hT[:, :, sl], reads=[bh], writes=[T["b_H1L"]])
    C.close()


def load_h1_contig(C, T, x1, bx1, halo):
    S = C.S
    src = T["H1L"].rearrange("(k p) n -> p k n", p=128)
    if halo:
        for sl in TOKSL:
            S.dma("sp", x1[:, :, sl], src[:, :, sl], reads=[T["b_H1L"]], writes=[bx1])
    else:
        for tt in range(4):
            S.dma("sp", x1[:, :, tt * 512:(tt + 1) * 512], src[:, :, 3 + tt * 512:3 + (tt + 1) * 512], reads=[T["b_H1L"]], writes=[bx1])


def phase_ssd_inproj(nc, T):
    C = Ctx(nc); S = C.S
    K = load_consts(C, T)
    identb, bident = K["identb"], K["bident"]
    banks = [(C.ps([128, 512], F32), Buf()) for _ in range(8)]
    x1 = C.sb([128, 8, 3 + NT], F32); bx1 = Buf()
    load_h1_contig(C, T, x1, bx1, True)
    wn = C.sb([128, 8], F32); bwn = Buf()
    S.dma("sp", wn[:], T["ssd_norm"], writes=[bwn])
    xn = C.sb([128, 8, 3 + NT], BF16); bxn = Buf()
    rstd = C.sb([128, 3 + NT], F32); brstd = Buf()
    sqb = [(C.sb([128, 512], F32), Buf()) for _ in range(2)]
    slices = [slice(0, 3)] + [slice(3 + tt * 512, 3 + (tt + 1) * 512) for tt in range(4)]
    rmsnorm_fm(C, x1, bx1, wn, bwn, xn, bxn, K["ones_f"], K["bones"], banks[0:2], sqb=sqb, rstd=rstd, brstd=brstd, slices=slices)

    cw = C.sb([128, 32, 4], F32); bcw = Buf()
    cb = C.sb([128, 32], F32); bcb = Buf()
    S.dma("sp", cw[:], T["conv_w"], writes=[bcw])
    S.dma("sp", cb[:], T["conv_b"], writes=[bcb])
    wb = [(C.sb([128, 8, 512], BF16), Buf()) for _ in range(2)]
    wsrc = T["ssd_w_in"].rearrange("(k p) c -> p k c", p=128)
    ub = [(C.sb([128, 3 + NT], F32), Buf()) for _ in range(2)]
    accb = [(C.sb([128, NT], F32), Buf()) for _ in range(2)]
    xcb = [(C.sb([128, NT], BF16), Buf()) for _ in range(2)]
    ctmp = C.sb([128, NT], F32); bctmp = Buf()
    tst = [(C.sb([128, 8, 128], BF16), Buf()) for _ in range(2)]
    pTs = [(banks[6][0][:].bitcast(BF16), banks[6][1]), (banks[7][0][:].bitcast(BF16), banks[7][1])]
    nst = 0
    for g in range(8):
        w, bw = wb[g % 2]
        for k in range(8):
            S.dma("pool", w[:, k, :], wsrc[:, k, 2048 + g * 512:2048 + (g + 1) * 512], writes=[bw])
        for c4 in range(4):
            cc = g * 4 + c4
            u, bu = ub[cc % 2]; acc, bacc = accb[cc % 2]; xc, bxc = xcb[cc % 2]
            veng = "dve"
            for ti, sl in enumerate(slices):
                wd_ = sl.stop - sl.start
                p1, bp1 = banks[2 + ti % 4]
                for k in range(8):
                    S.op("pe", lambda e, p1=p1, w=w, k=k, c4=c4, sl=sl, wd_=wd_: e.matmul(p1[:, 0:wd_], lhsT=w[:, k, c4 * 128:(c4 + 1) * 128], rhs=xn[:, k, sl],
                                                                                         start=(k == 0), stop=(k == 7)),
                         reads=[bw, bxn], writes=[bp1])
                S.op("act", lambda e, u=u, p1=p1, sl=sl, wd_=wd_: e.activation(out=u[:, sl], in_=p1[:, 0:wd_], func=AF.Copy), reads=[bp1], writes=[bu])
            S.op(veng, lambda e, acc=acc, u=u, cc=cc: e.tensor_scalar(out=acc[:], in0=u[:, 0:NT], scalar1=cw[:, cc, 0:1], scalar2=None, op0=ALU.mult),
                 reads=[bu, bcw], writes=[bacc])
            for tap in range(1, 4):
                if veng == "dve":
                    S.op(veng, lambda e, acc=acc, u=u, cc=cc, tap=tap: e.scalar_tensor_tensor(out=acc[:], in0=u[:, tap:tap + NT], scalar=cw[:, cc, tap:tap + 1],
                                                                                             in1=acc[:], op0=ALU.mult, op1=ALU.add),
                         reads=[bu, bcw, bacc], writes=[bacc])
                else:
                    S.op(veng, lambda e, u=u, cc=cc, tap=tap: e.tensor_scalar(out=ctmp[:], in0=u[:, tap:tap + NT], scalar1=cw[:, cc, tap:tap + 1], scalar2=None, op0=ALU.mult),
                         reads=[bu, bcw], writes=[bctmp])
                    S.op(veng, lambda e, acc=acc: e.tensor_tensor(out=acc[:], in0=acc[:], in1=ctmp[:], op=ALU.add), reads=[bacc, bctmp], writes=[bacc])
            S.op("act", lambda e, xc=xc, acc=acc, cc=cc: e.activation(out=xc[:], in_=acc[:], func=AF.Silu, bias=cb[:, cc:cc + 1]),
                 reads=[bacc, bcb], writes=[bxc])
            if cc >= 16:
                nm = "BT" if cc < 24 else "CT"
                gi = cc - 16 if cc < 24 else cc - 24
                S.dma("sp", T[nm][gi * 128:(gi + 1) * 128, :], xc[:], reads=[bxc], writes=[T["b_" + nm]])
            if cc < 24:
                dname = "XS" if cc < 16 else "BTOK"
                col0 = cc * 128 if cc < 16 else (cc - 16) * 128
                for half in range(2):
                    pT, bpT = pTs[nst % 2]
                    st_, bst = tst[nst % 2]; nst += 1
                    for tb8 in range(8):
                        tb = half * 8 + tb8
                        S.op("pe", lambda e, pT=pT, xc=xc, tb=tb, tb8=tb8: e.transpose(pT[:, tb8 * 128:(tb8 + 1) * 128], xc[:, tb * 128:(tb + 1) * 128], identb[:]),
                             reads=[bxc, bident], writes=[bpT])
                    S.op("dve" if nst % 2 == 0 else "act", (lambda e, st_=st_, pT=pT: e.tensor_copy(out=st_[:], in_=pT.rearrange("p (b c) -> p b c", c=128)))
                         if nst % 2 == 0 else (lambda e, st_=st_, pT=pT: e.activation(out=st_[:], in_=pT.rearrange("p (b c) -> p b c", c=128), func=AF.Copy)),
                         reads=[bpT], writes=[bst])
                    dstv = T[dname][half * 1024:(half + 1) * 1024, col0:col0 + 128].rearrange("(b t) c -> t b c", t=128)
                    S.dma("sp", dstv, st_[:], reads=[bst], writes=[T["b_" + dname]])
    zst = [(C.sb([128, 512], F32), Buf()) for _ in range(2)]
    nz = 0
    for g in range(4):
        w, bw = wb[g % 2]
        for k in range(8):
            S.dma("pool", w[:, k, :], wsrc[:, k, g * 512:(g + 1) * 512], writes=[bw])
        for tb in range(16):
            p1, bp1 = banks[2 + tb % 4]
            for k in range(8):
                S.op("pe", lambda e, p1=p1, w=w, k=k, tb=tb: e.matmul(p1[:], lhsT=xn[:, k, 3 + tb * 128:3 + (tb + 1) * 128], rhs=w[:, k, :],
                                                                      start=(k == 0), stop=(k == 7)),
                     reads=[bw, bxn], writes=[bp1])
            z_, bz = zst[nz % 2]; nz += 1
            S.op("act", lambda e, z_=z_, p1=p1: e.activation(out=z_[:], in_=p1[:], func=AF.Silu), reads=[bp1], writes=[bz])
            S.dma("sp", T["ZS"][tb * 128:(tb + 1) * 128, g * 512:(g + 1) * 512], z_[:], reads=[bz], writes=[T["b_ZS"]])
    wdt = C.sb([128, 8, 32], BF16); bwdt = Buf()
    for k in range(8):
        S.dma("pool", wdt[:, k, :], wsrc[:, k, 6144:6176], writes=[bwdt])
    hv = C.sb([128, 3, 32], F32); bhv = Buf()
    S.dma("sp", hv[:, 0:1, :], T["dt_bias"].partition_broadcast(128), writes=[bhv])
    S.dma("sp", hv[:, 1:2, :], T["a_log"].partition_broadcast(128), writes=[bhv])
    S.op("act", lambda e: e.activation(out=hv[:, 2, :], in_=hv[:, 1, :], func=AF.Exp), reads=[bhv], writes=[bhv])
    dst_ = [(C.sb([128, 64], F32), Buf()) for _ in range(2)]
    for tb in range(16):
        p1, bp1 = banks[2 + tb % 4]
        for k in range(8):
            S.op("pe", lambda e, p1=p1, k=k, tb=tb: e.matmul(p1[:, 0:32], lhsT=xn[:, k, 3 + tb * 128:3 + (tb + 1) * 128], rhs=wdt[:, k, :],
                                                             start=(k == 0), stop=(k == 7)),
                 reads=[bwdt, bxn], writes=[bp1])
        d_, bd = dst_[tb % 2]
        S.op("dve", lambda e, d_=d_, p1=p1: e.tensor_tensor(out=d_[:, 0:32], in0=p1[:, 0:32], in1=hv[:, 0, :], op=ALU.add), reads=[bp1, bhv], writes=[bd])
        S.op("act", lambda e, d_=d_: e.activation(out=d_[:, 0:32], in_=d_[:, 0:32], func=AF.Exp), reads=[bd], writes=[bd])
        S.op("act", lambda e, d_=d_: e.activation(out=d_[:, 0:32], in_=d_[:, 0:32], func=AF.Ln, bias=1.0), reads=[bd], writes=[bd])
        S.op("dve", lambda e, d_=d_: e.scalar_tensor_tensor(out=d_[:, 32:64], in0=d_[:, 0:32], scalar=-1.0, in1=hv[:, 2, :], op0=ALU.mult, op1=ALU.mult),
             reads=[bd, bhv], writes=[bd])
        S.dma("sp", T["DTD"][tb * 128:(tb + 1) * 128, :], d_[:], reads=[bd], writes=[T["b_DTD"]])
    C.close()


def ssd_consts(C, T):
    S = C.S
    k = {}
    for nm in ("tri_incl", "tri_gt", "ntri_incl"):
        k[nm] = C.sb([128, 128], F32); k["b_" + nm] = Buf()
        S.dma("sp", k[nm][:], T[nm], writes=[k["b_" + nm]])
    return k


def phase_ssd_states(nc, T):
    C = Ctx(nc); S = C.S
    K = load_consts(C, T)
    K2 = ssd_consts(C, T)
    banks = [(C.ps([128, 512], F32), Buf()) for _ in range(8)]
    Sloc = C.sb([128, 32, 64], F32); bS = Buf()
    S.op("pool", lambda e: e.memset(Sloc[:], 0.0), writes=[bS])
    ldsum = C.sb([128, 32], F32); bld = Buf()
    S.op("pool", lambda e: e.memset(ldsum[:], 0.0), writes=[bld])
    xsb = [(C.sb([128, 32, 64], BF16), Buf()) for _ in range(2)]
    btb = [(C.sb([128, 1024], BF16), Buf()) for _ in range(2)]
    dtb = [(C.sb([128, 64], F32), Buf()) for _ in range(2)]
    smb = [(C.sb([128, 3, 32], F32), Buf()) for _ in range(2)]
    xwb = [(C.sb([128, 32, 64], BF16), Buf()) for _ in range(2)]
    stb = [(C.sb([128, 2048], F32), Buf()) for _ in range(2)]
    for c in range(16):
        xs, bxs = xsb[c % 2]; bt, bbt = btb[c % 2]; dt_, bdt = dtb[c % 2]; sm, bsm = smb[c % 2]; xw, bxw = xwb[c % 2]; st_, bst = stb[c % 2]
        rows = slice(c * 128, (c + 1) * 128)
        S.dma("sp", xs[:].rearrange("p h c -> p (h c)"), T["XS"][rows, :], reads=[T["b_XS"]], writes=[bxs])
        S.dma("act", bt[:], T["BTOK"][rows, :], reads=[T["b_BTOK"]], writes=[bbt])
        S.dma("sp", dt_[:], T["DTD"][rows, :], reads=[T["b_DTD"]], writes=[bdt])
        pa, bpa = banks[c % 2]
        S.op("pe", lambda e, pa=pa, dt_=dt_: e.matmul(pa[:, 0:32], lhsT=K2["tri_gt"][:], rhs=dt_[:, 32:64], start=True, stop=True),
             reads=[K2["b_tri_gt"], bdt], writes=[bpa])
        S.op("pe", lambda e, pa=pa, dt_=dt_: e.matmul(pa[:, 32:64], lhsT=K["ones_f"][:], rhs=dt_[:, 32:64], start=True, stop=True),
             reads=[K["bones"], bdt], writes=[bpa])
        S.op("act", lambda e, sm=sm, pa=pa: e.activation(out=sm[:, 0:2, :], in_=pa[:, 0:64].rearrange("p (a h) -> p a h", a=2), func=AF.Exp),
             reads=[bpa], writes=[bsm])
        S.op("dve", lambda e, sm=sm, dt_=dt_: e.tensor_tensor(out=sm[:, 2, :], in0=sm[:, 0, :], in1=dt_[:, 0:32], op=ALU.mult), reads=[bsm, bdt], writes=[bsm])
        S.op("dve", lambda e, xw=xw, xs=xs, sm=sm: e.tensor_tensor(out=xw[:], in0=xs[:], in1=sm[:, 2, :].unsqueeze(2).to_broadcast([128, 32, 64]), op=ALU.mult),
             reads=[bxs, bsm], writes=[bxw])
        xwf = xw[:].rearrange("p h c -> p (h c)")
        for g in range(8):
            ps_, bps = banks[2 + g // 2]
            S.op("pe", lambda e, ps_=ps_, bt=bt, g=g, xwf=xwf: e.matmul(ps_[:, (g % 2) * 256:(g % 2 + 1) * 256], lhsT=bt[:, g * 128:(g + 1) * 128],
                                                                        rhs=xwf[:, g * 256:(g + 1) * 256], start=True, stop=True),
                 reads=[bbt, bxw], writes=[bps])
        for bq in range(4):
            ps_, bps = banks[2 + bq]
            S.op("act" if bq % 2 == 0 else "dve",
                 (lambda e, st_=st_, ps_=ps_, bq=bq: e.activation(out=st_[:, bq * 512:(bq + 1) * 512], in_=ps_[:], func=AF.Copy)) if bq % 2 == 0 else
                 (lambda e, st_=st_, ps_=ps_, bq=bq: e.tensor_copy(out=st_[:, bq * 512:(bq + 1) * 512], in_=ps_[:])),
                 reads=[bps], writes=[bst])
        S.dma("sp", T["ST"][c], st_[:], reads=[bst], writes=[T["b_ST"]])
        S.op("pool", lambda e, sm=sm: e.tensor_tensor(out=Sloc[:], in0=Sloc[:], in1=sm[:, 1, :].unsqueeze(2).to_broadcast([128, 32, 64]), op=ALU.mult),
             reads=[bS, bsm], writes=[bS])
        S.op("pool", lambda e, st_=st_: e.tensor_tensor(out=Sloc[:].rearrange("p h c -> p (h c)"), in0=Sloc[:].rearrange("p h c -> p (h c)"), in1=st_[:], op=ALU.add),
             reads=[bS, bst], writes=[bS])
        S.op("dve", lambda e, pa=pa: e.tensor_tensor(out=ldsum[:], in0=pa[:, 32:64], in1=ldsum[:], op=ALU.add), reads=[bpa, bld], writes=[bld])
    S.dma("sp", T["SXa"], Sloc[:].rearrange("p h c -> p (h c)"), reads=[bS], writes=[T["b_SXa"]])
    S.dma("sp", T["SXb"], ldsum[:], reads=[bld], writes=[T["b_SXb"]])
    collective(S, "AllGather", T["SXa"], T["SXa_all"], reads=[T["b_SXa"]], writes=[T["b_SXa_all"]])
    collective(S, "AllGather", T["SXb"], T["SXb_all"], reads=[T["b_SXb"]], writes=[T["b_SXb_all"]])
    C.close()


def phase_ssd_scan(nc, T):
    C = Ctx(nc); S = C.S
    K = load_consts(C, T)
    K2 = ssd_consts(C, T)
    identb, bident = K["identb"], K["bident"]
    identf, bidentf = K["idf"], K["bidf"]
    ones_f, bones = K["ones_f"], K["bones"]
    banks = [(C.ps([128, 512], F32), Buf()) for _ in range(8)]
    prev = C.sb([128, 32, 64], F32); bprev = Buf()
    prevb = C.sb([128, 2048], BF16); bprevb = Buf()
    msk = C.sb([128, 20], F32); bmsk = Buf()
    S.dma("sp", msk[:], T["selmask"], writes=[bmsk])
    ld = C.sb([128, 4, 32], F32); bldg = Buf()
    S.dma("sp", ld[:], T["SXb_all"].rearrange("(r p) h -> p r h", p=128), reads=[T["b_SXb_all"]], writes=[bldg])
    coef = C.sb([128, 4, 32], F32); bcoef = Buf()
    for r in range(4):
        S.op("dve", lambda e, r=r: e.tensor_scalar(out=coef[:, r, :], in0=ld[:, 0, :], scalar1=msk[:, 4 + 4 * r:5 + 4 * r], scalar2=None, op0=ALU.mult),
             reads=[bldg, bmsk], writes=[bcoef])
        for r2 in range(1, 4):
            S.op("dve", lambda e, r=r, r2=r2: e.scalar_tensor_tensor(out=coef[:, r, :], in0=ld[:, r2, :], scalar=msk[:, 4 + 4 * r + r2:5 + 4 * r + r2],
                                                                     in1=coef[:, r, :], op0=ALU.mult, op1=ALU.add),
                 reads=[bldg, bmsk, bcoef], writes=[bcoef])
        S.op("act", lambda e, r=r: e.activation(out=coef[:, r, :], in_=coef[:, r, :], func=AF.Exp), reads=[bcoef], writes=[bcoef])
        S.op("dve", lambda e, r=r: e.tensor_scalar(out=coef[:, r, :], in0=coef[:, r, :], scalar1=msk[:, r:r + 1], scalar2=None, op0=ALU.mult),
             reads=[bcoef, bmsk], writes=[bcoef])
    S.op("pool", lambda e: e.memset(prev[:], 0.0), writes=[bprev])
    sg_ = [(C.sb([128, 32, 64], F32), Buf()) for _ in range(2)]
    for r in range(4):
        t_, bt_ = sg_[r % 2]
        S.dma("sp", t_[:].rearrange("p h c -> p (h c)"), T["SXa_all"][r * 128:(r + 1) * 128, :], reads=[T["b_SXa_all"]], writes=[bt_])
        S.op("dve", lambda e, t_=t_, r=r: e.tensor_tensor(out=t_[:], in0=t_[:], in1=coef[:, r, :].unsqueeze(2).to_broadcast([128, 32, 64]), op=ALU.mult),
             reads=[bt_, bcoef], writes=[bt_])
        S.op("dve", lambda e, t_=t_: e.tensor_tensor(out=prev[:], in0=prev[:], in1=t_[:], op=ALU.add), reads=[bt_, bprev], writes=[bprev])
    prevf = prev[:].rearrange("p h c -> p (h c)")
    S.op("act", lambda e: e.activation(out=prevb[:], in_=prevf, func=AF.Copy), reads=[bprev], writes=[bprevb])
    hv = C.sb([128, 32], F32); bhv = Buf()
    S.dma("sp", hv[:].unsqueeze(1), T["ssd_d"].partition_broadcast(128), writes=[bhv])
    gw = C.sb([128, 16], F32); bgw = Buf()
    S.dma("sp", gw[:], T["gnorm"], writes=[bgw])
    negut = C.sb([128, 4, 128], F32); bneg = Buf()
    negutb = C.sb([128, 4, 128], BF16); bnegb = Buf()
    for r in range(4):
        S.dma("sp", negut[:, r, :], T["neg_ut"], writes=[bneg])
    S.op("dve", lambda e: e.tensor_copy(out=negutb[:], in_=negut[:]), reads=[bneg], writes=[bnegb])
    xsb = [(C.sb([128, 32, 64], BF16), Buf()) for _ in range(2)]
    dtb = [(C.sb([128, 64], F32), Buf()) for _ in range(2)]
    btb = [(C.sb([128, 8, 128], BF16), Buf()) for _ in range(2)]
    ctb = [(C.sb([128, 8, 128], BF16), Buf()) for _ in range(2)]
    zsb = [(C.sb([128, 2048], F32), Buf()) for _ in range(2)]
    stb = [(C.sb([128, 2048], F32), Buf()) for _ in range(2)]
    dth = C.sb([128, 2, 32], BF16); bdth = Buf()
    trib = C.sb([128, 128], BF16); ntrib = C.sb([128, 128], BF16); onesb = C.sb([128, 128], BF16); btb_ = Buf()
    S.op("dve", lambda e: e.tensor_copy(out=trib[:], in_=K2["tri_incl"][:]), reads=[K2["b_tri_incl"]], writes=[btb_])
    S.op("dve", lambda e: e.tensor_copy(out=ntrib[:], in_=K2["ntri_incl"][:]), reads=[K2["b_ntri_incl"]], writes=[btb_])
    S.op("dve", lambda e: e.memset(onesb[:], 1.0), writes=[btb_])
    smb = [(C.sb([128, 3, 32], F32), Buf()) for _ in range(2)]
    xdt = C.sb([128, 32, 64], BF16); bxdt = Buf()
    Dmb = [(C.sb([128, 4, 128], F32), Buf()) for _ in range(2)]
    MTb = [(C.sb([128, 4, 128], BF16), Buf()) for _ in range(2)]
    tyb = [(C.sb([128, 4, 64], F32), Buf()) for _ in range(2)]
    yc = C.sb([128, 8, 256], F32); byc = Buf()
    gy = C.sb([128, 8, 256], F32); bgy = Buf()
    ynb = C.sb([128, 2048], BF16); byn = Buf()
    nrm = C.sb([128, 3, 8], F32); bnrm = Buf()
    junk = C.sb([128, 256], F32); bjunk = Buf()
    yTs = [(C.sb([128, 16, 128], BF16), Buf()) for _ in range(2)]
    bt_src = T["BT"].rearrange("(g n) t -> n g t", n=128)
    ct_src = T["CT"].rearrange("(g n) t -> n g t", n=128)
    yt_dst = T["YT"].rearrange("(cc p) t -> p cc t", p=128)
    pT0 = banks[6][0][:].bitcast(BF16); pT1 = banks[7][0][:].bitcast(BF16)
    half_bufs = [[Buf(), Buf()] for _ in range(3)]
    for c in range(16):
        xs, bxs = xsb[c % 2]; dt_, bdt = dtb[c % 2]; bt, bbt = btb[c % 2]; ct, bct = ctb[c % 2]
        zs, bzs = zsb[c % 2]; st_, bst = stb[c % 2]; sm, bsm = smb[c % 2]
        rows = slice(c * 128, (c + 1) * 128)
        S.dma("sp", xs[:].rearrange("p h c -> p (h c)"), T["XS"][rows, :], reads=[T["b_XS"]], writes=[bxs])
        S.dma("sp", dt_[:], T["DTD"][rows, :], reads=[T["b_DTD"]], writes=[bdt])
        S.dma("act", bt[:], bt_src[:, :, rows], reads=[T["b_BT"]], writes=[bbt])
        S.dma("act", ct[:], ct_src[:, :, rows], reads=[T["b_CT"]], writes=[bct])
        S.dma("sp", zs[:], T["ZS"][rows, :], reads=[T["b_ZS"]], writes=[bzs])
        S.dma("sp", st_[:], T["ST"][c], reads=[T["b_ST"]], writes=[bst])
        pa, bpa = banks[0]
        S.op("pe", lambda e, dt_=dt_: e.matmul(pa[:, 0:32], lhsT=K2["tri_incl"][:], rhs=dt_[:, 32:64], start=True, stop=True),
             reads=[K2["b_tri_incl"], bdt], writes=[bpa])
        S.op("pe", lambda e, dt_=dt_: e.matmul(pa[:, 32:64], lhsT=ones_f[:], rhs=dt_[:, 32:64], start=True, stop=True),
             reads=[bones, bdt], writes=[bpa])
        S.op("act", lambda e, sm=sm: e.activation(out=sm[:, 0:2, :], in_=pa[:, 0:64].rearrange("p (a h) -> p a h", a=2), func=AF.Exp),
             reads=[bpa], writes=[bsm])
        S.op("dve", lambda e, dt_=dt_: e.tensor_copy(out=dth[:, 0, :], in_=dt_[:, 32:64]), reads=[bdt], writes=[bdth])
        S.op("dve", lambda e, dt_=dt_: e.tensor_tensor(out=dth[:, 1, :], in0=dt_[:, 32:64], in1=dth[:, 0, :], op=ALU.subtract), reads=[bdt, bdth], writes=[bdth])
        S.op("dve", lambda e, xs=xs, dt_=dt_: e.tensor_tensor(out=xdt[:], in0=xs[:], in1=dt_[:, 0:32].unsqueeze(2).to_broadcast([128, 32, 64]), op=ALU.mult),
             reads=[bxs, bdt], writes=[bxdt])
        for g in range(8):
            pseg, bpseg = banks[1 + g % 2]
            Dm, bDm = Dmb[g % 2]; MT, bMT = MTb[g % 2]; ty, bty = tyb[g % 2]
            hs = slice(4 * g, 4 * g + 4)
            psv = pseg[:].rearrange("p (h l) -> p h l", h=4)
            for x_ in range(2):
                S.op("pe", lambda e, psv=psv, x_=x_, hs=hs: e.matmul(psv, lhsT=ntrib[:], rhs=dth[:, x_, hs].unsqueeze(2).to_broadcast([128, 4, 128]),
                                                                     start=(x_ == 0), stop=False),
                     reads=[btb_, bdth], writes=[bpseg])
            for r in range(4):
                for x_ in range(2):
                    S.op("pe", lambda e, pseg=pseg, x_=x_, r=r, g=g: e.matmul(pseg[:, r * 128:(r + 1) * 128], lhsT=dth[:, x_, 4 * g + r:4 * g + r + 1].to_broadcast([128, 128]),
                                                                              rhs=trib[:], start=False, stop=False),
                         reads=[btb_, bdth], writes=[bpseg])
            S.op("pe", lambda e, pseg=pseg: e.matmul(pseg[:], lhsT=identb[:], rhs=negutb[:].rearrange("p h l -> p (h l)"), start=False, stop=True),
                 reads=[bident, bnegb], writes=[bpseg])
            S.op("act", lambda e, Dm=Dm, pseg=pseg: e.activation(out=Dm[:].rearrange("p h l -> p (h l)"), in_=pseg[:], func=AF.Exp), reads=[bpseg], writes=[bDm])
            pg = banks[3][0]; bpg = half_bufs[0][g % 2]
            gsl = slice((g % 2) * 128, (g % 2 + 1) * 128)
            S.op("pe", lambda e, bt=bt, ct=ct, g=g, gsl=gsl: e.matmul(pg[:, gsl], lhsT=bt[:, g, :], rhs=ct[:, g, :], start=True, stop=True),
                 reads=[bbt, bct], writes=[bpg])
            S.op("dve", lambda e, MT=MT, Dm=Dm, gsl=gsl: e.tensor_tensor(out=MT[:], in0=Dm[:], in1=pg[:, gsl].unsqueeze(1).to_broadcast([128, 4, 128]), op=ALU.mult),
                 reads=[bDm, bpg], writes=[bMT])
            pyd = banks[4][0]; bpyd = half_bufs[1][g % 2]
            pyo = banks[5][0]; bpyo = half_bufs[2][g % 2]
            ysl = slice((g % 2) * 256, (g % 2 + 1) * 256)
            for r in range(4):
                S.op("pe", lambda e, MT=MT, r=r, g=g, ysl=ysl: e.matmul(pyd[:, ysl.start + r * 64:ysl.start + (r + 1) * 64], lhsT=MT[:, r, :], rhs=xdt[:, 4 * g + r, :],
                                                                        start=True, stop=True),
                     reads=[bMT, bxdt], writes=[bpyd])
            S.op("pe", lambda e, ct=ct, g=g, ysl=ysl: e.matmul(pyo[:, ysl], lhsT=ct[:, g, :], rhs=prevb[:, g * 256:(g + 1) * 256], start=True, stop=True),
                 reads=[bct, bprevb], writes=[bpyo])
            S.op("dve", lambda e, ty=ty, sm=sm, hs=hs, ysl=ysl: e.tensor_tensor(out=ty[:], in0=pyo[:, ysl].rearrange("p (r c) -> p r c", r=4),
                                                                                in1=sm[:, 0, hs].unsqueeze(2).to_broadcast([128, 4, 64]), op=ALU.mult),
                 reads=[bpyo, bsm], writes=[bty])
            S.op("dve", lambda e, ty=ty, g=g, ysl=ysl: e.tensor_tensor(out=yc[:, g, :], in0=pyd[:, ysl], in1=ty[:].rearrange("p r c -> p (r c)"), op=ALU.add),
                 reads=[bpyd, bty], writes=[byc])
        ycv = yc[:].rearrange("p g (r c) -> p (g r) c", r=4)
        S.op("pool", lambda e, xs=xs: e.tensor_tensor(out=gy[:].rearrange("p g (r c) -> p (g r) c", r=4), in0=xs[:], in1=hv[:].unsqueeze(2).to_broadcast([128, 32, 64]), op=ALU.mult),
             reads=[bxs, bhv], writes=[bgy])
        S.op("pool", lambda e: e.tensor_tensor(out=yc[:], in0=yc[:], in1=gy[:], op=ALU.add), reads=[byc, bgy], writes=[byc])
        S.op("dve", lambda e, zs=zs: e.tensor_tensor(out=gy[:], in0=yc[:], in1=zs[:].rearrange("p (g c) -> p g c", g=8), op=ALU.mult), reads=[byc, bzs, bgy], writes=[bgy])
        for g in range(8):
            S.op("act", lambda e, g=g: e.activation(out=junk[:], in_=gy[:, g, :], func=AF.Square, accum_out=nrm[:, 0, g:g + 1]), reads=[bgy], writes=[bjunk, bnrm])
        S.op("act", lambda e: e.activation(out=nrm[:, 1, :], in_=nrm[:, 0, :], func=AF.Ln, scale=1.0 / 256, bias=1e-5), reads=[bnrm], writes=[bnrm])
        S.op("act", lambda e: e.activation(out=nrm[:, 2, :], in_=nrm[:, 1, :], func=AF.Exp, scale=-0.5), reads=[bnrm], writes=[bnrm])
        S.op("dve", lambda e: e.tensor_tensor(out=ynb[:].rearrange("p (g c) -> p g c", g=8), in0=gy[:], in1=nrm[:, 2, :].unsqueeze(2).to_broadcast([128, 8, 256]), op=ALU.mult),
             reads=[bgy, bnrm], writes=[byn])
        yT, byT = yTs[c % 2]
        for cc in range(16):
            pT = pT0 if cc < 8 else pT1
            S.op("pe", lambda e, pT=pT, cc=cc: e.transpose(pT[:, (cc % 8) * 128:(cc % 8 + 1) * 128], ynb[:, cc * 128:(cc + 1) * 128], identb[:]),
                 reads=[byn, bident], writes=[banks[6][1] if cc < 8 else banks[7][1]])
        S.op("dve", lambda e, yT=yT: e.tensor_tensor(out=yT[:, 0:8, :], in0=pT0.rearrange("p (b c) -> p b c", c=128), in1=gw[:, 0:8].unsqueeze(2).to_broadcast([128, 8, 128]), op=ALU.mult),
             reads=[banks[6][1], bgw], writes=[byT])
        S.op("dve", lambda e, yT=yT: e.tensor_tensor(out=yT[:, 8:16, :], in0=pT1.rearrange("p (b c) -> p b c", c=128), in1=gw[:, 8:16].unsqueeze(2).to_broadcast([128, 8, 128]), op=ALU.mult),
             reads=[banks[7][1], bgw], writes=[byT])
        S.dma("sp", yt_dst[:, :, rows], yT[:], reads=[byT], writes=[T["b_YT"]])
        S.op("pool", lambda e, sm=sm: e.tensor_tensor(out=prev[:], in0=prev[:], in1=sm[:, 1, :].unsqueeze(2).to_broadcast([128, 32, 64]), op=ALU.mult),
             reads=[bprev, bsm], writes=[bprev])
        S.op("pool", lambda e, st_=st_: e.tensor_tensor(out=prevf, in0=prevf, in1=st_[:], op=ALU.add), reads=[bprev, bst], writes=[bprev])
        S.op("act", lambda e: e.activation(out=prevb[:], in_=prevf, func=AF.Copy), reads=[bprev, bprevb], writes=[bprevb])
    C.close()


def phase_ssd_out_ffn(nc, T, stage):
    C = Ctx(nc); S = C.S
    K = load_consts(C, T)
    banks = [(C.ps([128, 512], F32), Buf()) for _ in range(8)]
    hT = C.sb([128, 8, NT], F32); bh = Buf()
    load_h1_contig(C, T, hT, bh, False)
    dst = T["dbg"].rearrange("(k p) n -> p k n", p=128)
    with contextlib.ExitStack() as st2:
        yT = st2.enter_context(nc.sbuf_tensor("yTf", [128, 16, NT], BF16)); byT = Buf()
        wo = st2.enter_context(nc.sbuf_tensor("wo1", [128, 16, D], BF16)); bwo = Buf()
        ysrc = T["YT"].rearrange("(cc p) t -> p cc t", p=128)
        for cc in range(16):
            S.dma("act", yT[:, cc, :], ysrc[:, cc, :], reads=[T["b_YT"]], writes=[byT])
            S.dma("pool", wo[:, cc, :], T["ssd_w_out"][cc * 128:(cc + 1) * 128, :], writes=[bwo])
        for tt in range(4):
            sl = slice(tt * 512, (tt + 1) * 512)
            for oc in range(8):
                po, bpo = banks[oc % 4]
                for cc in range(16):
                    S.op("pe", lambda e, po=po, cc=cc, oc=oc, sl=sl: e.matmul(po[:], lhsT=wo[:, cc, oc * 128:(oc + 1) * 128], rhs=yT[:, cc, sl],
                                                                              start=(cc == 0), stop=(cc == 15)),
                         reads=[bwo, byT], writes=[bpo])
                S.op("dve", lambda e, po=po, oc=oc, sl=sl: e.tensor_tensor(out=hT[:, oc, sl], in0=po[:], in1=hT[:, oc, sl], op=ALU.add),
                     reads=[bpo, bh], writes=[bh])
        if stage == "ssd":
            for tt in range(4):
                S.dma("sp", dst[:, :, tt * 512:(tt + 1) * 512], hT[:, :, tt * 512:(tt + 1) * 512], reads=[bh])
            C.S.emit()
            st2.close(); C.st.close()
            return
        C.S.emit()
    C.S = Sched(nc); S = C.S
    Cf = Ctx(nc); Cf.S = C.S
    ffn_fm(Cf, T, 1, hT, bh, K["ones_f"], K["bones"], banks)
    C.S.emit()
    Cf.st.close()
    C.S = Sched(nc); S = C.S
    if stage != "ffn1":
        fw = C.sb([128, 8], F32); bfw = Buf()
        S.dma("sp", fw[:], T["final_norm"], writes=[bfw])
        rstd = C.sb([128, NT], F32); brstd = Buf()
        sq = [(C.sb([128, 512], F32), Buf()) for _ in range(2)]
        for tt in range(4):
            sl = slice(tt * 512, (tt + 1) * 512)
            ps, bps = banks[tt % 2]
            for k in range(8):
                q_, bq_ = sq[k % 2]
                S.op("act", lambda e, q_=q_, k=k, sl=sl: e.activation(out=q_[:], in_=hT[:, k, sl], func=AF.Square), reads=[bh], writes=[bq_])
                S.op("pe", lambda e, ps=ps, q_=q_, k=k: e.matmul(ps[:], lhsT=K["ones_f"][:], rhs=q_[:], start=(k == 0), stop=(k == 7)),
                     reads=[bq_, K["bones"]], writes=[bps])
            S.op("act", lambda e, ps=ps, sl=sl: e.activation(out=rstd[:, sl], in_=ps[:], func=AF.Ln, scale=1.0 / D, bias=1e-6), reads=[bps], writes=[brstd])
            S.op("act", lambda e, sl=sl: e.activation(out=rstd[:, sl], in_=rstd[:, sl], func=AF.Exp, scale=-0.5), reads=[brstd], writes=[brstd])
            for k in range(8):
                S.op("dve", lambda e, k=k, sl=sl: e.scalar_tensor_tensor(out=hT[:, k, sl], in0=hT[:, k, sl], scalar=fw[:, k:k + 1], in1=rstd[:, sl],
                                                                         op0=ALU.mult, op1=ALU.mult),
                     reads=[bh, bfw, brstd], writes=[bh])
    for tt in range(4):
        S.dma("sp", dst[:, :, tt * 512:(tt + 1) * 512], hT[:, :, tt * 512:(tt + 1) * 512], reads=[bh])
    C.close()


def build_program(stage):
    Buf.ALL = []
    nc = bass.Bass("TRN2", target_bir_lowering=False)
    Sched.STATE = SemState(nc)
    T = {}

    def inp(name, shape, dt=F32):
        T[name] = nc.dram_tensor(name, list(shape), dt, kind="ExternalInput").ap()

    def scr(name, shape, dt):
        T[name] = nc.dram_tensor(name, list(shape), dt).ap()
        T["b_" + name] = Buf(name)

    inp("xT0", [D, NT]); inp("cs_tab", [128, 2, NT]); inp("attn_norm", [128, 8])
    inp("w_in0", [D, 3072]); inp("w_sw0", [D, 1024]); inp("w_out0", [D, D])
    inp("nm_incl", [128, 16, 512]); inp("nm_strict", [128, 16, 512])
    inp("ident", [128, 128]); inp("nu", [128, 128])
    for nm in ("lq1", "lk1", "lq2", "lk2"):
        inp(nm, [1, 64])
    inp("subln", [1, 128])
    inp("ffn_norm", [2, 128, 8]); inp("ffn_wg", [2, D, D_FF]); inp("ffn_wu", [2, D, D_FF]); inp("ffn_wd", [2, D_FF, D])
    scr("QT", [1536, NT], BF16)
    def scr_list(name, n, shape, dt):
        T[name] = [nc.dram_tensor("%s_%d" % (name, i), list(shape), dt).ap() for i in range(n)]
        T["b_" + name] = [Buf(name) for _ in range(n)]

    scr_list("KTd", 4, [256, NT], BF16); scr_list("KTd_all", 4, [1024, NT], BF16)
    scr_list("KTs", 2, [256, NT], BF16); scr_list("KTs_all", 2, [1024, NT], BF16)
    scr_list("Vd", 4, [512, 516], BF16); scr_list("Vd_all", 4, [2048, 516], BF16)
    scr_list("Vs", 2, [1024, 512], BF16); scr_list("Vs_all", 2, [4096, 512], BF16)
    scr_list("MX", 8, [128, NX], BF16); scr_list("MXA", 8, [512, NX], BF16)
    scr("H1L", [D, NX], F32)
    inp("xT1", [D, NX])
    inp("ssd_norm", [128, 8]); inp("ssd_w_in", [D, 6176]); inp("conv_w", [128, 32, 4]); inp("conv_b", [128, 32])
    inp("dt_bias", [1, 32]); inp("a_log", [1, 32]); inp("ssd_d", [1, 32]); inp("gnorm", [128, 16]); inp("ssd_w_out", [2048, D])
    inp("final_norm", [128, 8]); inp("selmask", [128, 20])
    inp("tri_incl", [128, 128]); inp("tri_gt", [128, 128]); inp("ntri_incl", [128, 128]); inp("neg_ut", [128, 128])
    scr("XS", [NT, 2048], BF16); scr("BTOK", [NT, 1024], BF16); scr("BT", [1024, NT], BF16); scr("CT", [1024, NT], BF16)
    scr("ZS", [NT, 2048], F32); scr("DTD", [NT, 64], F32)
    T["ST"] = [nc.dram_tensor("ST_%d" % i, [128, 2048], F32).ap() for i in range(16)]; T["b_ST"] = Buf("ST")
    scr("SXa", [128, 2048], F32); scr("SXa_all", [512, 2048], F32); scr("SXb", [128, 32], F32); scr("SXb_all", [512, 32], F32)
    scr("YT", [2048, NT], BF16)
    T["dbg"] = nc.dram_tensor("dbg", [D, NT], F32, kind="ExternalOutput").ap()

    nph = int(os.environ.get("KPH", "99"))
    phase_attn_inproj(nc, T)
    if nph >= 2:
        phase_attn_core(nc, T)
    if nph >= 3:
        phase_attn_out_ffn(nc, T, stage)
    if stage not in ("mix", "attn", "ffn0"):
        if nph >= 4:
            phase_ssd_inproj(nc, T)
        if nph >= 5:
            phase_ssd_states(nc, T)
        if nph >= 6:
            phase_ssd_scan(nc, T)
        if nph >= 7:
            phase_ssd_out_ffn(nc, T, stage)
    Sched.STATE.close()
    return nc


def rope_tables(q):
    half = 32
    inv_freq = (10000.0 ** (-np.arange(half, dtype=np.float32) / half)).astype(np.float32)
    n = np.arange(NT)
    pos = ((4 * (n // 128) + q) * 128 + (n % 128)).astype(np.float32)
    ang = pos[None, :] * inv_freq[:, None]
    cos = np.cos(ang).astype(np.float32)
    sin = np.sin(ang).astype(np.float32)
    tab = np.zeros((128, 2, NT), np.float32)
    for p in range(128):
        d = p % 64
        tab[p, 0] = cos[d % 32]
        tab[p, 1] = -sin[d % 32] if d < 32 else sin[d % 32]
    return tab


def neg_masks(q):
    jj = np.arange(128)[:, None]
    tt = np.arange(128)[None, :]
    out = []
    for strict in (False, True):
        keep_diag = (jj < tt) if strict else (jj <= tt)
        m = np.zeros((128, 16, 512), np.float32)
        for kbz in range(16):
            for i in range(4):
                z = kbz - 4 * i
                blk = m[:, kbz, i * 128:(i + 1) * 128]
                if z < 0:
                    continue
                if z > 3 or z > q:
                    blk[:] = NEG
                elif z == q:
                    blk[:] = np.where(keep_diag, 0.0, NEG)
        out.append(m)
    return out


def make_in_maps(inputs):
    f = lambda a: np.ascontiguousarray(np.asarray(a, dtype=np.float32))
    x = f(inputs["x"])
    w_in = f(inputs["attn_w_in"][0])
    qk = w_in[:, :1024].reshape(D, 16, 2, 32)
    w_sw = np.ascontiguousarray(qk[:, :, ::-1, :].reshape(D, 1024))
    ar = np.arange(128)
    common = {
        "w_in0": w_in, "w_sw0": w_sw, "w_out0": f(inputs["attn_w_out"][0]),
        "attn_norm": f(inputs["attn_norm"][0].reshape(8, 128).T),
        "ident": np.eye(128, dtype=np.float32),
        "nu": -(np.arange(128)[:, None] >= np.arange(128)[None, :]).astype(np.float32),
        "lq1": f(inputs["diff_lq1"]), "lk1": f(inputs["diff_lk1"]), "lq2": f(inputs["diff_lq2"]), "lk2": f(inputs["diff_lk2"]),
        "subln": f(inputs["diff_subln"]),
        "ffn_norm": f(np.stack([inputs["ffn_norm"][l].reshape(8, 128).T for l in range(2)])),
        "ffn_wg": f(inputs["ffn_w_gate"]), "ffn_wu": f(inputs["ffn_w_up"]), "ffn_wd": f(inputs["ffn_w_down"]),
        "ssd_norm": f(inputs["ssd_norm"][0].reshape(8, 128).T), "ssd_w_in": f(inputs["ssd_w_in"][0]),
        "conv_w": f(inputs["ssd_conv_w"][0].reshape(4, 32, 128).transpose(2, 1, 0)),
        "conv_b": f(inputs["ssd_conv_b"][0].reshape(32, 128).T),
        "dt_bias": f(inputs["ssd_dt_bias"]), "a_log": f(inputs["ssd_a_log"]), "ssd_d": f(inputs["ssd_d"]),
        "gnorm": f(inputs["ssd_gnorm"][0].reshape(16, 128).T), "ssd_w_out": f(inputs["ssd_w_out"][0]),
        "final_norm": f(inputs["final_norm"].reshape(8, 128).T),
        "tri_incl": (ar[:, None] <= ar[None, :]).astype(np.float32),
        "tri_gt": (ar[:, None] > ar[None, :]).astype(np.float32),
        "ntri_incl": -(ar[:, None] <= ar[None, :]).astype(np.float32),
        "neg_ut": np.where(ar[None, :] < ar[:, None], NEG, 0.0).astype(np.float32),
    }
    maps = []
    for c in range(8):
        b, q = c // 4, c % 4
        n = np.arange(NT)
        pos = (4 * (n // 128) + q) * 128 + (n % 128)
        m = dict(common)
        m["xT0"] = np.ascontiguousarray(x[b, pos, :].T)
        x1 = np.zeros((D, NX), np.float32)
        x1[:, 3:] = x[b, q * NT:(q + 1) * NT, :].T
        if q > 0:
            x1[:, 0:3] = x[b, q * NT - 3:q * NT, :].T
        m["xT1"] = x1
        m["cs_tab"] = rope_tables(q)
        mi, ms = neg_masks(q)
        m["nm_incl"], m["nm_strict"] = mi, ms
        sel = np.zeros((128, 20), np.float32)
        for r in range(4):
            sel[:, r] = 1.0 if r < q else 0.0
            for r2 in range(4):
                sel[:, 4 + 4 * r + r2] = 1.0 if (r < r2 < q) else 0.0
        m["selmask"] = sel
        maps.append(m)
    return maps


def kernel(**inputs):
    stage = os.environ.get("KSTAGE", "final")
    nc = build_program(stage)
    maps = make_in_maps(inputs)
    res = run_bass_kernel_spmd(nc, maps, core_ids=list(range(8)))
    out = np.zeros((2, SEQ, D), np.float32)
    for c in range(8):
        b, q = c // 4, c % 4
        o = res.results[c]["dbg"]
        out[b, q * NT:(q + 1) * NT, :] = o.T
    return out
```

```python
import contextlib
import math
import os

import numpy as np
import concourse.bass as bass
import concourse.mybir as mybir
from concourse.bass_utils import run_bass_kernel_spmd

F32 = mybir.dt.float32
BF16 = mybir.dt.bfloat16
AF = mybir.ActivationFunctionType
ALU = mybir.AluOpType

D = 1024
SEQ = 8192
NT = 2048
NEG = -30000.0
GROUPS = [[0, 1, 2, 3], [4, 5, 6, 7]]
D_FF = 2816
FH = D_FF // 2
NX = 3 + NT
TOKSL = [slice(0, 3)] + [slice(3 + t * 512, 3 + (t + 1) * 512) for t in range(4)]


class Buf:
    __slots__ = ("name", "last_w", "rd_c", "rd_d")
    ALL = []

    def __init__(self, name=""):
        self.name = name
        self.last_w = None
        self.rd_c = {}
        self.rd_d = []
        Buf.ALL.append(self)

    @staticmethod
    def reset_all():
        for b in Buf.ALL:
            b.last_w = None
            b.rd_c = {}
            b.rd_d = []


class _Rec:
    def __init__(self):
        self.call = None

    def __getattr__(self, name):
        def f(*a, **k):
            self.call = (name, a, k)
            return None
        return f


def _freeze(fn):
    r = _Rec()
    fn(r)
    name, a, k = r.call
    return lambda e: getattr(e, name)(*a, **k)


class SemState:
    def __init__(self, nc):
        self.nc = nc
        self.st = contextlib.ExitStack()
        self.sems = {}
        self.base = {e: 0 for e in Sched.ENGS}
        self.semval = {}
        self.bar_count = 0

    def sem(self, key):
        if key not in self.sems:
            name = key if isinstance(key, str) else "_".join(str(k) for k in key)
            self.sems[key] = self.st.enter_context(self.nc.semaphore("s_" + name))
        return self.sems[key]

    def close(self):
        self.st.close()


class Sched:
    ENGS = ("sp", "act", "dve", "pool", "pe")
    STATE = None

    def __init__(self, nc, n_dma_sems=6):
        self.nc = nc
        self.state = Sched.STATE
        self.ops = {e: [] for e in self.ENGS}
        self.nops = {e: 0 for e in self.ENGS}
        self.known = {e: {} for e in self.ENGS}
        self.needed = {e: set() for e in self.ENGS}
        self.n_dma_sems = n_dma_sems
        self.dma_ring = {e: 0 for e in self.ENGS}
        self.semval = self.state.semval
        self.dma_keys = []

    def _wait(self, eng, ev):
        if ev is None:
            return
        if ev[0] == "c":
            _, src, idx = ev
            if src == "pe" and eng == "pe":
                return
            if self.known[eng].get(src, 0) >= idx:
                return
            self.known[eng][src] = idx
            self.needed[src].add(idx)
            self.ops[eng].append(("wait", ev))
        else:
            _, key, val = ev
            if self.known[eng].get(key, 0) >= val:
                return
            self.known[eng][key] = val
            self.ops[eng].append(("wait", ev))

    def _deps(self, eng, reads, writes):
        for b in reads:
            self._wait(eng, b.last_w)
        for b in writes:
            self._wait(eng, b.last_w)
            for src, idx in list(b.rd_c.items()):
                self._wait(eng, ("c", src, idx))
            for ev in b.rd_d:
                self._wait(eng, ev)

    def _commit(self, ev, reads, writes):
        for b in reads:
            if ev[0] == "c":
                if b.rd_c.get(ev[1], 0) < ev[2]:
                    b.rd_c[ev[1]] = ev[2]
            else:
                b.rd_d.append(ev)
        for b in writes:
            b.last_w = ev
            b.rd_c = {}
            b.rd_d = []

    def op(self, eng, fn, reads=(), writes=()):
        self._deps(eng, reads, writes)
        self.nops[eng] += 1
        idx = self.nops[eng]
        ev = ("c", eng, idx)
        self.ops[eng].append(("op", idx, _freeze(fn)))
        self._commit(ev, reads, writes)
        return ev

    def dma(self, eng, out, in_, reads=(), writes=(), inc=16, fn=None, ring=None):
        if ring is None:
            i = self.dma_ring[eng]
            self.dma_ring[eng] = i + 1
            key = ("dsem", eng, i % self.n_dma_sems)
        else:
            key = ring
        if key not in self.semval:
            self.semval[key] = 0
        if key not in self.dma_keys:
            self.dma_keys.append(key)
        prev = self.semval[key]
        if prev > 0:
            self._wait(eng, ("d", key, prev))
        self._deps(eng, reads, writes)
        self.semval[key] = prev + inc
        ev = ("d", key, prev + inc)
        if fn is None:
            fn = lambda e, out=out, in_=in_: e.dma_start(out=out, in_=in_)
        self.ops[eng].append(("dma", key, fn, inc))
        self._commit(ev, reads, writes)
        return ev

    def drain(self):
        for key in self.dma_keys:
            self._wait(key[1], ("d", key, self.semval[key]))

    def emit(self):
        nc = self.nc
        stt = self.state
        self.drain()
        sems = {}
        for e in self.ENGS:
            sems[e] = stt.sem("eng_" + e)
        for k in self.dma_keys:
            sems[k] = stt.sem(k)
        bar = stt.sem("bar")
        rank = {}
        for e in self.ENGS:
            if self.nops[e] > 0:
                self.needed[e].add(self.nops[e])
            rank[e] = {idx: stt.base[e] + r + 1 for r, idx in enumerate(sorted(self.needed[e]))}
        bar_target = stt.bar_count + 5
        with nc.Block() as block:
            def run(ename):
                def body(eng):
                    for ent in self.ops[ename]:
                        if ent[0] == "wait":
                            ev = ent[1]
                            if ev[0] == "c":
                                eng.wait_ge(sems[ev[1]], rank[ev[1]][ev[2]])
                            else:
                                eng.wait_ge(sems[ev[1]], ev[2])
                        elif ent[0] == "op":
                            ins = ent[2](eng)
                            if ent[1] in rank[ename]:
                                ins.then_inc(sems[ename], 1)
                        else:
                            ins = ent[2](eng)
                            ins.then_inc(sems[ent[1]], ent[3])
                    if self.nops[ename] > 0:
                        eng.wait_ge(sems[ename], rank[ename][self.nops[ename]])
                    eng.sem_inc(bar, 1)
                    eng.wait_ge(bar, bar_target)
                return body

            block.sync(run("sp"))
            block.scalar(run("act"))
            block.vector(run("dve"))
            block.gpsimd(run("pool"))
            block.tensor(run("pe"))
        if os.environ.get("KVERB"):
            print("emit: ops", {e: self.nops[e] for e in self.ENGS}, "entries", {e: len(self.ops[e]) for e in self.ENGS}, flush=True)
        for e in self.ENGS:
            stt.base[e] += len(self.needed[e])
        stt.bar_count = bar_target
        Buf.reset_all()


class Ctx:
    def __init__(self, nc):
        self.nc = nc
        self.S = Sched(nc)
        self.st = contextlib.ExitStack()
        self.n = 0

    CNT = [0]

    def sb(self, shape, dt=F32, name=None):
        Ctx.CNT[0] += 1
        return self.st.enter_context(self.nc.sbuf_tensor(name or ("t%d" % Ctx.CNT[0]), list(shape), dt))

    def ps(self, shape, dt=F32, name=None):
        Ctx.CNT[0] += 1
        return self.st.enter_context(self.nc.psum_tensor(name or ("p%d" % Ctx.CNT[0]), list(shape), dt))

    def close(self):
        self.S.emit()
        self.st.close()


COLL_N = [0]


def collective(S, kind, src, dst, reads, writes):
    def fn(e):
        return e.collective_compute(kind, ALU.bypass, replica_groups=GROUPS, ins=[src], outs=[dst])
    COLL_N[0] += 1
    return S.dma("pool", None, None, reads=reads, writes=writes, inc=1, fn=fn, ring=("csem", "pool", COLL_N[0] % 4))


def rmsnorm_fm(C, xT, bx, wn, bwn, xn, bxn, ones_f, bones, pss, eps=1e-6, ntok=NT, sqb=None, rstd=None, brstd=None, slices=None, out_f32=False):
    S = C.S
    if slices is None:
        slices = [slice(tt * 512, (tt + 1) * 512) for tt in range(ntok // 512)]
    for tt, sl in enumerate(slices):
        wd_ = sl.stop - sl.start
        ps, bps = pss[tt % len(pss)]
        for k in range(8):
            sq, bsq = sqb[k % len(sqb)]
            S.op("act", lambda e, sq=sq, k=k, sl=sl: e.activation(out=sq[:, 0:wd_], in_=xT[:, k, sl], func=AF.Square),
                 reads=[bx], writes=[bsq])
            S.op("pe", lambda e, ps=ps, sq=sq, k=k: e.matmul(ps[:, 0:wd_], lhsT=ones_f[:], rhs=sq[:, 0:wd_], start=(k == 0), stop=(k == 7)),
                 reads=[bsq, bones], writes=[bps])
        S.op("act", lambda e, ps=ps, sl=sl: e.activation(out=rstd[:, sl], in_=ps[:, 0:wd_], func=AF.Ln, scale=1.0 / D, bias=eps),
             reads=[bps], writes=[brstd])
        S.op("act", lambda e, sl=sl: e.activation(out=rstd[:, sl], in_=rstd[:, sl], func=AF.Exp, scale=-0.5),
             reads=[brstd], writes=[brstd])
        for k in range(8):
            S.op("dve", lambda e, k=k, sl=sl: e.scalar_tensor_tensor(out=xn[:, k, sl], in0=xT[:, k, sl], scalar=wn[:, k:k + 1],
                                                                     in1=rstd[:, sl], op0=ALU.mult, op1=ALU.mult),
                 reads=[bx, bwn, brstd], writes=[bxn])


def ffn_fm(C, T, layer, hT, bh, ones_f, bones, banks, slices=None):
    S = C.S
    if slices is None:
        slices = [slice(tt * 512, (tt + 1) * 512) for tt in range(NT // 512)]
    ntot = slices[-1].stop
    xn = C.sb([128, 8, ntot], BF16); bxn = Buf()
    rstd = C.sb([128, ntot], F32); brstd = Buf()
    wn = C.sb([128, 8], F32); bwn = Buf()
    S.dma("sp", wn[:], T["ffn_norm"][layer], writes=[bwn])
    sqb = [(C.sb([128, 512], F32), Buf()) for _ in range(2)]
    rmsnorm_fm(C, hT, bh, wn, bwn, xn, bxn, ones_f, bones, banks[0:2], sqb=sqb, rstd=rstd, brstd=brstd, slices=slices)
    wg = C.sb([128, 8, FH], BF16); bwg = Buf()
    wu = C.sb([128, 8, FH], BF16); bwu = Buf()
    wd = C.sb([128, 11, D], BF16); bwd = Buf()
    act = [(C.sb([128, 11, 512], BF16), Buf()) for _ in range(2)]
    sg = [(C.sb([128, 512], F32), Buf()) for _ in range(2)]
    for half in range(2):
        c0 = half * FH
        for k in range(8):
            S.dma("pool", wg[:, k, :], T["ffn_wg"][layer][k * 128:(k + 1) * 128, c0:c0 + FH], writes=[bwg])
            S.dma("pool", wu[:, k, :], T["ffn_wu"][layer][k * 128:(k + 1) * 128, c0:c0 + FH], writes=[bwu])
        for fc in range(11):
            S.dma("pool", wd[:, fc, :], T["ffn_wd"][layer][c0 + fc * 128:c0 + (fc + 1) * 128, :], writes=[bwd])
        for tt, sl in enumerate(slices):
            wd_ = sl.stop - sl.start
            a, ba = act[tt % 2]
            for fc in range(11):
                pg, bpg = banks[(2 * fc) % 4]
                pu, bpu = banks[(2 * fc + 1) % 4]
                for k in range(8):
                    S.op("pe", lambda e, pg=pg, k=k, fc=fc, sl=sl: e.matmul(pg[:, 0:wd_], lhsT=wg[:, k, fc * 128:(fc + 1) * 128], rhs=xn[:, k, sl],
                                                                            start=(k == 0), stop=(k == 7)),
                         reads=[bwg, bxn], writes=[bpg])
                for k in range(8):
                    S.op("pe", lambda e, pu=pu, k=k, fc=fc, sl=sl: e.matmul(pu[:, 0:wd_], lhsT=wu[:, k, fc * 128:(fc + 1) * 128], rhs=xn[:, k, sl],
                                                                            start=(k == 0), stop=(k == 7)),
                         reads=[bwu, bxn], writes=[bpu])
                s_, bs_ = sg[fc % 2]
                S.op("act", lambda e, s_=s_, pg=pg: e.activation(out=s_[:, 0:wd_], in_=pg[:, 0:wd_], func=AF.Silu), reads=[bpg], writes=[bs_])
                S.op("dve", lambda e, a=a, fc=fc, s_=s_, pu=pu: e.tensor_tensor(out=a[:, fc, 0:wd_], in0=pu[:, 0:wd_], in1=s_[:, 0:wd_], op=ALU.mult),
                     reads=[bpu, bs_], writes=[ba])
            for oc in range(8):
                po, bpo = banks[4 + oc % 4]
                for fc in range(11):
                    S.op("pe", lambda e, po=po, fc=fc, oc=oc, a=a: e.matmul(po[:, 0:wd_], lhsT=wd[:, fc, oc * 128:(oc + 1) * 128], rhs=a[:, fc, 0:wd_],
                                                                            start=(fc == 0), stop=(fc == 10)),
                         reads=[bwd, ba], writes=[bpo])
                S.op("dve", lambda e, po=po, oc=oc, sl=sl: e.tensor_tensor(out=hT[:, oc, sl], in0=po[:, 0:wd_], in1=hT[:, oc, sl], op=ALU.add),
                     reads=[bpo, bh], writes=[bh])


def load_consts(C, T):
    S = C.S
    k = {}
    idf = C.sb([128, 128], F32); bidf = Buf()
    S.dma("sp", idf[:], T["ident"], writes=[bidf])
    k["identb"] = C.sb([128, 128], BF16); k["bident"] = Buf()
    S.op("dve", lambda e: e.tensor_copy(out=k["identb"][:], in_=idf[:]), reads=[bidf], writes=[k["bident"]])
    k["ones_f"] = C.sb([128, 128], F32); k["bones"] = Buf()
    S.op("pool", lambda e: e.memset(k["ones_f"][:], 1.0), writes=[k["bones"]])
    k["idf"] = idf; k["bidf"] = bidf
    return k


def phase_attn_inproj(nc, T):
    C = Ctx(nc); S = C.S
    K = load_consts(C, T)
    banks = [(C.ps([128, 512], F32), Buf()) for _ in range(8)]
    xT = C.sb([128, 8, NT], F32); bx = Buf()
    xsrc = T["xT0"].rearrange("(k p) n -> p k n", p=128)
    for tt in range(4):
        S.dma("sp", xT[:, :, tt * 512:(tt + 1) * 512], xsrc[:, :, tt * 512:(tt + 1) * 512], writes=[bx])
    cs = C.sb([128, 2, NT], F32); bcs = Buf()
    S.dma("act", cs[:], T["cs_tab"], writes=[bcs])
    wn = C.sb([128, 8], F32); bwn = Buf()
    S.dma("act", wn[:], T["attn_norm"], writes=[bwn])
    xn = C.sb([128, 8, NT], BF16); bxn = Buf()
    rstd = C.sb([128, NT], F32); brstd = Buf()
    sqb = [(C.sb([128, 512], F32), Buf()) for _ in range(2)]
    rmsnorm_fm(C, xT, bx, wn, bwn, xn, bxn, K["ones_f"], K["bones"], banks[0:2], sqb=sqb, rstd=rstd, brstd=brstd)

    kcut = int(os.environ.get("KCUT", "9"))
    if kcut <= 1:
        C.close()
        return
    wb = [(C.sb([128, 8, 512], BF16), Buf()) for _ in range(2)]
    wsw = [(C.sb([128, 8, 512], BF16), Buf()) for _ in range(2)]
    ob = [(C.sb([128, NT], BF16), Buf()) for _ in range(2)]
    t1 = [(C.sb([128, 512], F32), Buf()) for _ in range(2)]
    t2 = [(C.sb([128, 512], F32), Buf()) for _ in range(2)]
    vst = [(C.sb([128, 4, 129], BF16), Buf()) for _ in range(2)]
    vss = [(C.sb([128, 512], BF16), Buf()) for _ in range(2)]
    for v, bv in vst:
        S.op("pool", lambda e, v=v: e.memset(v[:], 1.0), writes=[bv])
    wsrc = T["w_in0"].rearrange("(k p) c -> p k c", p=128)
    swsrc = T["w_sw0"].rearrange("(k p) c -> p k c", p=128)
    dst_fm = {0: ("QT", 0), 1: ("KTd", 0), 3: ("QT", 512), 4: ("KTs", 0)}
    nchunk = 0
    order = (1, 2, 4, 5, 0, 3)

    def load_w(gi):
        g = order[gi]
        w, bw = wb[gi % 2]
        for k in range(8):
            S.dma("pool", w[:, k, :], wsrc[:, k, g * 512:(g + 1) * 512], writes=[bw])
        if g < 2:
            w2, bw2 = wsw[g % 2]
            for k in range(8):
                S.dma("pool", w2[:, k, :], swsrc[:, k, g * 512:(g + 1) * 512], writes=[bw2])

    load_w(0)
    for gi, g in enumerate(order):
        if gi + 1 < len(order):
            load_w(gi + 1)
        w, bw = wb[gi % 2]
        if g < 2:
            w2, bw2 = wsw[g % 2]
        if g in dst_fm:
            dname, roff = dst_fm[g]
            for cc in range(4):
                o, bo = ob[nchunk % 2]; nchunk += 1
                for tt in range(4):
                    sl = slice(tt * 512, (tt + 1) * 512)
                    p1, bp1 = banks[2 + (2 * tt) % 4]
                    for k in range(8):
                        S.op("pe", lambda e, p1=p1, w=w, k=k, cc=cc, sl=sl: e.matmul(p1[:], lhsT=w[:, k, cc * 128:(cc + 1) * 128], rhs=xn[:, k, sl],
                                                                                     start=(k == 0), stop=(k == 7)),
                             reads=[bw, bxn], writes=[bp1])
                    if g < 2:
                        p2, bp2 = banks[2 + (2 * tt + 1) % 4]
                        for k in range(8):
                            S.op("pe", lambda e, p2=p2, w2=w2, k=k, cc=cc, sl=sl: e.matmul(p2[:], lhsT=w2[:, k, cc * 128:(cc + 1) * 128], rhs=xn[:, k, sl],
                                                                                           start=(k == 0), stop=(k == 7)),
                                 reads=[bw2, bxn], writes=[bp2])
                        a1, ba1 = t1[tt % 2]
                        a2, ba2 = t2[tt % 2]
                        S.op("dve", lambda e, a1=a1, p1=p1, sl=sl: e.tensor_tensor(out=a1[:], in0=p1[:], in1=cs[:, 0, sl], op=ALU.mult),
                             reads=[bp1, bcs], writes=[ba1])
                        S.op("dve", lambda e, a2=a2, p2=p2, sl=sl: e.tensor_tensor(out=a2[:], in0=p2[:], in1=cs[:, 1, sl], op=ALU.mult),
                             reads=[bp2, bcs], writes=[ba2])
                        S.op("pool", lambda e, o=o, a1=a1, a2=a2, sl=sl: e.tensor_tensor(out=o[:, sl], in0=a1[:], in1=a2[:], op=ALU.add),
                             reads=[ba1, ba2], writes=[bo])
                    else:
                        sc = 0.125 if g == 3 else 1.0
                        S.op("act", lambda e, o=o, p1=p1, sl=sl, sc=sc: e.activation(out=o[:, sl], in_=p1[:], func=AF.Copy, scale=sc),
                             reads=[bp1], writes=[bo])
                if dname == "QT":
                    r0 = roff + cc * 128
                    S.dma("sp", T["QT"][r0:r0 + 128, :], o[:], reads=[bo], writes=[T["b_QT"]])
                else:
                    ch, r0 = cc // 2, (cc % 2) * 128
                    S.dma("sp", T[dname][ch][r0:r0 + 128, :], o[:], reads=[bo], writes=[T["b_" + dname][ch]])
                    if cc % 2 == 1:
                        collective(S, "AllGather", T[dname][ch], T[dname + "_all"][ch], reads=[T["b_" + dname][ch]], writes=[T["b_" + dname + "_all"][ch]])
        else:
            for tb in range(16):
                p1, bp1 = banks[2 + tb % 4]
                for k in range(8):
                    S.op("pe", lambda e, p1=p1, w=w, k=k, tb=tb: e.matmul(p1[:], lhsT=xn[:, k, tb * 128:(tb + 1) * 128], rhs=w[:, k, :],
                                                                          start=(k == 0), stop=(k == 7)),
                         reads=[bw, bxn], writes=[bp1])
                if g == 2:
                    v, bv = vst[tb % 2]
                    S.op("act", lambda e, v=v, p1=p1: e.activation(out=v[:, :, 0:128], in_=p1[:].rearrange("p (h c) -> p h c", c=128), func=AF.Copy),
                         reads=[bp1], writes=[bv])
                    ch, r0 = tb // 4, (tb % 4) * 128
                    S.dma("sp", T["Vd"][ch][r0:r0 + 128, :], v[:].rearrange("p h c -> p (h c)"), reads=[bv], writes=[T["b_Vd"][ch]])
                    if tb % 4 == 3:
                        collective(S, "AllGather", T["Vd"][ch], T["Vd_all"][ch], reads=[T["b_Vd"][ch]], writes=[T["b_Vd_all"][ch]])
                else:
                    v, bv = vss[tb % 2]
                    S.op("act", lambda e, v=v, p1=p1: e.activation(out=v[:], in_=p1[:], func=AF.Copy), reads=[bp1], writes=[bv])
                    ch, r0 = tb // 8, (tb % 8) * 128
                    S.dma("sp", T["Vs"][ch][r0:r0 + 128, :], v[:], reads=[bv], writes=[T["b_Vs"][ch]])
                    if tb % 8 == 7:
                        collective(S, "AllGather", T["Vs"][ch], T["Vs_all"][ch], reads=[T["b_Vs"][ch]], writes=[T["b_Vs_all"][ch]])
    C.close()


def tile_iters(J):
    out = []
    for kb in range(16 * J + 16):
        i0 = max(0, kb // 4 - 4 * J)
        z = kb - 16 * J if kb >= 16 * J else None
        out.append((kb, kb % 4, kb // 4, i0, z))
    return out


def phase_attn_core(nc, T):
    C = Ctx(nc); S = C.S
    K = load_consts(C, T)
    identb, bident = K["identb"], K["bident"]
    ones_f, bones = K["ones_f"], K["bones"]
    psall = C.ps([128, 8, 512], F32)
    banks = [(psall[:, i, :], Buf()) for i in range(8)]
    nuf = C.sb([128, 128], F32); bnuf = Buf()
    S.dma("sp", nuf[:], T["nu"], writes=[bnuf])
    NU = C.sb([128, 128], BF16); bNU = Buf()
    S.op("dve", lambda e: e.tensor_copy(out=NU[:], in_=nuf[:]), reads=[bnuf], writes=[bNU])
    onesb = C.sb([128, 128], BF16); bonesb = Buf()
    S.op("pool", lambda e: e.memset(onesb[:], 1.0), writes=[bonesb])
    nmI = C.sb([128, 16, 512], BF16); bnmI = Buf()
    nmS = C.sb([128, 16, 512], BF16); bnmS = Buf()
    for z in range(0, 16, 4):
        S.dma("pool", nmI[:, z:z + 4, :], T["nm_incl"][:, z:z + 4, :], writes=[bnmI])
        S.dma("pool", nmS[:, z:z + 4, :], T["nm_strict"][:, z:z + 4, :], writes=[bnmS])
    lv = C.sb([128, 4, 64], F32); blv = Buf()
    for i, nm in enumerate(("lq1", "lk1", "lq2", "lk2")):
        S.dma("sp", lv[:, i:i + 1, :], T[nm].partition_broadcast(128), writes=[blv])
    lt = C.sb([128, 2, 64], F32); blt = Buf()
    S.op("dve", lambda e: e.tensor_tensor(out=lt[:, 0, :], in0=lv[:, 0, :], in1=lv[:, 1, :], op=ALU.mult), reads=[blv], writes=[blt])
    S.op("dve", lambda e: e.tensor_tensor(out=lt[:, 1, :], in0=lv[:, 2, :], in1=lv[:, 3, :], op=ALU.mult), reads=[blv], writes=[blt])
    ls = C.sb([128, 4], F32); bls = Buf()
    S.op("dve", lambda e: e.reduce_sum(out=ls[:, 0:1], in_=lt[:, 0, :], axis=mybir.AxisListType.X), reads=[blt], writes=[bls])
    S.op("dve", lambda e: e.reduce_sum(out=ls[:, 1:2], in_=lt[:, 1, :], axis=mybir.AxisListType.X), reads=[blt], writes=[bls])
    S.op("act", lambda e: e.activation(out=ls[:, 0:2], in_=ls[:, 0:2], func=AF.Exp), reads=[bls], writes=[bls])
    S.op("dve", lambda e: e.tensor_tensor(out=ls[:, 2:3], in0=ls[:, 1:2], in1=ls[:, 0:1], op=ALU.subtract), reads=[bls], writes=[bls])
    S.op("dve", lambda e: e.tensor_scalar(out=ls[:, 3:4], in0=ls[:, 2:3], scalar1=-0.2, scalar2=None, op0=ALU.add), reads=[bls], writes=[bls])
    negl = ls[:, 3:4]
    sl_ = C.sb([128, 1], F32); bsl = Buf()
    S.dma("sp", sl_[:], T["subln"].rearrange("o v -> v o"), writes=[bsl])
    S.op("dve", lambda e: e.tensor_scalar(out=sl_[:], in0=sl_[:], scalar1=0.8, scalar2=None, op0=ALU.mult), reads=[bsl], writes=[bsl])

    zcb = C.sb([128, 4], BF16); bzcb = Buf()
    S.op("pool", lambda e: e.memset(zcb[:], 0.0), writes=[bzcb])
    for c_ in range(8):
        S.dma("sp", T["MX"][c_][:, 0:3], zcb[:, 0:3], reads=[bzcb], writes=[T["b_MX"][c_]])
    kbuf = [(C.sb([128, 4, NT], BF16), Buf()) for _ in range(2)]
    vbuf = [(C.sb([128, 64 * 129], BF16), Buf()) for _ in range(2)]
    qz = [[(C.sb([128, NT], BF16), Buf()) for _ in range(2)] for _ in range(2)]
    for b_ in range(2):
        S.op("pool", lambda e, b_=b_: e.memset(qz[b_][0][0][64:128, :], 0.0), writes=[qz[b_][0][1]])
        S.op("pool", lambda e, b_=b_: e.memset(qz[b_][1][0][0:64, :], 0.0), writes=[qz[b_][1][1]])
    ostg = [(C.sb([128, 512], BF16), Buf()) for _ in range(2)]
    nload = 0
    nout = 0

    Eb = [(C.sb([128, 2, 512], BF16), Buf()) for _ in range(2)]
    ep = [(C.sb([128, 512], F32), Buf()) for _ in range(5)]
    ktd_all = [a.rearrange("(r x) n -> x r n", r=4) for a in T["KTd_all"]]
    vd_all = [a.rearrange("(r j t) (h c) -> t r j h c", r=4, j=4, h=4) for a in T["Vd_all"]]
    for H in range(4):
        kt, bkt = kbuf[nload % 2]; vv, bvv = vbuf[nload % 2]; qzz = qz[nload % 2]; nload += 1
        for r in range(4):
            S.dma("sp", kt[:, r, :], ktd_all[H // 2][(H % 2) * 128:(H % 2 + 1) * 128, r, :], reads=[T["b_KTd_all"][H // 2]], writes=[bkt])
            for vc in range(4):
                b0 = r * 16 + 4 * vc
                S.dma("sp", vv[:, b0 * 129:(b0 + 4) * 129].rearrange("p (j c) -> p j c", c=129), vd_all[vc][:, r, :, H, :],
                      reads=[T["b_Vd_all"][vc]], writes=[bvv])
        for m in range(2):
            S.dma("sp", qzz[m][0][m * 64:(m + 1) * 64, :], T["QT"][H * 128 + m * 64:H * 128 + (m + 1) * 64, :], reads=[T["b_QT"]], writes=[qzz[m][1]])
        v4 = vv[:].rearrange("p (b c) -> p b c", c=129)
        for J in range(4):
            its = tile_iters(J)
            q0 = J * 512
            for bk_i in (4, 5, 6, 7):
                S.op("dve", lambda e, bk_i=bk_i: e.memset(banks[bk_i][0], 0.0), writes=[banks[bk_i][1]])

            def stage1(it, slot):
                kb, r, j, i0, z = it
                w0 = i0 * 128
                for m in range(2):
                    ps, bps = banks[2 * slot + m]
                    S.op("pe", lambda e, ps=ps, m=m, r=r, j=j, w0=w0: e.matmul(
                        ps[:, w0:512], lhsT=kt[:, r, j * 128:(j + 1) * 128], rhs=qzz[m][0][:, q0 + w0:q0 + 512],
                        start=True, stop=(z is None)), reads=[bkt, qzz[m][1]], writes=[bps])
                    if z is not None:
                        S.op("pe", lambda e, ps=ps, z=z, w0=w0: e.matmul(ps[:, w0:512], lhsT=identb[:], rhs=nmI[:, z, w0:512], start=False, stop=True),
                             reads=[bident, bnmI], writes=[bps])

            def stage2(it, slot):
                kb, r, j, i0, z = it
                w0 = i0 * 128
                E, bE = Eb[slot]
                S.op("act", lambda e, E=E, slot=slot, w0=w0: e.activation(out=E[:, :, w0:512], in_=psall[:, 2 * slot:2 * slot + 2, w0:512], func=AF.Exp, scale=0.125),
                     reads=[banks[2 * slot][1], banks[2 * slot + 1][1]], writes=[bE])

            def stage3(it, slot):
                kb, r, j, i0, z = it
                w0 = i0 * 128
                E, bE = Eb[slot]
                for m in range(2):
                    S.op("pe", lambda e, E=E, m=m, r=r, j=j, w0=w0: e.matmul(banks[4 + m][0][:, w0:512], lhsT=v4[:, r * 16 + j, 0:128], rhs=E[:, m, w0:512],
                                                                             start=False, stop=False, skip_group_check=True),
                         reads=[bE, bvv], writes=[banks[4 + m][1]])
                    S.op("pe", lambda e, E=E, m=m, w0=w0: e.matmul(banks[6 + m][0][:, w0:512], lhsT=onesb[:], rhs=E[:, m, w0:512],
                                                                   start=False, stop=False, skip_group_check=True),
                         reads=[bE, bonesb], writes=[banks[6 + m][1]])

            for n in range(len(its) + 2):
                if n < len(its):
                    stage1(its[n], n % 2)
                if 1 <= n <= len(its):
                    stage2(its[n - 1], (n - 1) % 2)
                if n >= 2:
                    stage3(its[n - 2], (n - 2) % 2)
            (r1, br1), (t1, bt1), (t2, bt2), (od, bod), (sq, bsq) = ep
            S.op("dve", lambda e: e.reciprocal(out=r1[:], in_=banks[6][0]), reads=[banks[6][1]], writes=[br1])
            S.op("dve", lambda e: e.tensor_tensor(out=t1[:], in0=banks[4][0], in1=r1[:], op=ALU.mult), reads=[banks[4][1], br1], writes=[bt1])
            S.op("dve", lambda e: e.reciprocal(out=r1[:], in_=banks[7][0]), reads=[banks[7][1], bt1], writes=[br1])
            S.op("dve", lambda e: e.tensor_tensor(out=t2[:], in0=banks[5][0], in1=r1[:], op=ALU.mult), reads=[banks[5][1], br1], writes=[bt2])
            S.op("dve", lambda e: e.scalar_tensor_tensor(out=od[:], in0=t2[:], scalar=negl, in1=t1[:], op0=ALU.mult, op1=ALU.add),
                 reads=[bt1, bt2, bls], writes=[bod])
            S.op("act", lambda e: e.activation(out=sq[:], in_=od[:], func=AF.Square), reads=[bod], writes=[bsq])
            pss, bpss = banks[0]
            S.op("pe", lambda e: e.matmul(pss, lhsT=ones_f[:], rhs=sq[:], start=True, stop=True), reads=[bones, bsq], writes=[bpss])
            S.op("act", lambda e: e.activation(out=t1[:], in_=pss, func=AF.Ln, scale=1.0 / 128, bias=1e-5), reads=[bpss, bod], writes=[bt1])
            S.op("act", lambda e: e.activation(out=t1[:], in_=t1[:], func=AF.Exp, scale=-0.5), reads=[bt1], writes=[bt1])
            o_, bo_ = ostg[nout % 2]; nout += 1
            S.op("dve", lambda e, o_=o_: e.scalar_tensor_tensor(out=o_[:], in0=od[:], scalar=sl_[:, 0:1], in1=t1[:], op0=ALU.mult, op1=ALU.mult),
                 reads=[bod, bsl, bt1], writes=[bo_])
            S.dma("sp", T["MX"][H][:, 3 + q0:3 + q0 + 512], o_[:], reads=[bo_], writes=[T["b_MX"][H]])
        collective(S, "AllGather", T["MX"][H], T["MXA"][H], reads=[T["b_MX"][H]], writes=[T["b_MXA"][H]])

    eb = [(C.sb([128, 2, 512], F32), Buf()) for _ in range(2)]
    spb = [(C.sb([128, 2, 512], F32), Buf()) for _ in range(2)]
    hib = [(C.sb([128, 2, 512], BF16), Buf()) for _ in range(2)]
    lob = [(C.sb([128, 2, 512], BF16), Buf()) for _ in range(2)]
    Ab = [(C.sb([128, 2, 512], BF16), Buf()) for _ in range(2)]
    fb = [(C.sb([128, 512], F32), Buf()) for _ in range(2)]
    OT = C.sb([128, 512], F32); bOT = Buf()
    kts_all = [a.rearrange("(r x) n -> x r n", r=4) for a in T["KTs_all"]]
    vs_all = [a.rearrange("(r j t) (pr c) -> t r j pr c", r=4, j=8, pr=4) for a in T["Vs_all"]]
    for pr in range(4):
        kt, bkt = kbuf[nload % 2]; vv, bvv = vbuf[nload % 2]; qzz = qz[nload % 2]; nload += 1
        for r in range(4):
            S.dma("sp", kt[:, r, :], kts_all[pr // 2][(pr % 2) * 128:(pr % 2 + 1) * 128, r, :], reads=[T["b_KTs_all"][pr // 2]], writes=[bkt])
            for vc in range(2):
                b0 = r * 16 + 8 * vc
                S.dma("sp", vv[:, b0 * 128:(b0 + 8) * 128].rearrange("p (j c) -> p j c", c=128), vs_all[vc][:, r, :, pr, :],
                      reads=[T["b_Vs_all"][vc]], writes=[bvv])
        for m in range(2):
            S.dma("sp", qzz[m][0][m * 64:(m + 1) * 64, :], T["QT"][512 + pr * 128 + m * 64:512 + pr * 128 + (m + 1) * 64, :], reads=[T["b_QT"]], writes=[qzz[m][1]])
        v4 = vv[:, 0:64 * 128].rearrange("p (b c) -> p b c", c=128)
        for J in range(4):
            its = tile_iters(J)
            q0 = J * 512
            S.op("pool", lambda e: e.memset(OT[:], 0.0), writes=[bOT])

            def stage1(it, slot):
                kb, r, j, i0, z = it
                w0 = i0 * 128
                ee, bee = eb[slot]; sp_, bsp = spb[slot]; hi, bhi = hib[slot]; lo, blo = lob[slot]
                for hh in range(2):
                    p0 = hh * 64
                    ps, bps = banks[2 * slot + hh]
                    S.op("pe", lambda e, ps=ps, r=r, j=j, w0=w0, hh=hh: e.matmul(
                        ps[:, w0:512], lhsT=kt[:, r, j * 128:(j + 1) * 128], rhs=qzz[hh][0][:, q0 + w0:q0 + 512],
                        start=True, stop=(z is None)), reads=[bkt, qzz[hh][1]], writes=[bps])
                    if z is not None:
                        S.op("pe", lambda e, ps=ps, z=z, w0=w0: e.matmul(ps[:, w0:512], lhsT=identb[:], rhs=nmS[:, z, w0:512], start=False, stop=True),
                             reads=[bident, bnmS], writes=[bps])
                zz = psall[:, 2 * slot:2 * slot + 2, w0:512]
                bz = [banks[2 * slot][1], banks[2 * slot + 1][1]]
                S.op("act", lambda e, ee=ee, zz=zz, w0=w0: e.activation(out=ee[:, :, w0:512], in_=zz, func=AF.Exp), reads=bz, writes=[bee])
                S.op("act", lambda e, ee=ee, sp_=sp_, w0=w0: e.activation(out=sp_[:, :, w0:512], in_=ee[:, :, w0:512], func=AF.Ln, bias=1.0),
                     reads=[bee], writes=[bsp])
                S.op("dve", lambda e, hi=hi, sp_=sp_, w0=w0: e.tensor_copy(out=hi[:, :, w0:512], in_=sp_[:, :, w0:512]), reads=[bsp], writes=[bhi])
                S.op("dve", lambda e, lo=lo, hi=hi, sp_=sp_, w0=w0: e.tensor_tensor(out=lo[:, :, w0:512], in0=sp_[:, :, w0:512], in1=hi[:, :, w0:512], op=ALU.subtract),
                     reads=[bsp, bhi], writes=[blo])

            def stage2(it, slot):
                kb, r, j, i0, z = it
                w0 = i0 * 128
                hi, bhi = hib[slot]; lo, blo = lob[slot]; A, bA = Ab[slot]; f, bf_ = fb[slot]
                bz = [banks[2 * slot][1], banks[2 * slot + 1][1]]
                for hh in range(2):
                    ps, bps = banks[2 * slot + hh]
                    S.op("pe", lambda e, ps=ps, hi=hi, hh=hh, w0=w0: e.matmul(ps[:, w0:512], lhsT=NU[:], rhs=hi[:, hh, w0:512], start=False, stop=False, skip_group_check=True),
                         reads=[bNU, bhi], writes=[bps])
                    S.op("pe", lambda e, ps=ps, lo=lo, hh=hh, w0=w0: e.matmul(ps[:, w0:512], lhsT=NU[:], rhs=lo[:, hh, w0:512], start=False, stop=True, skip_group_check=True),
                         reads=[bNU, blo], writes=[bps])
                pC, bpC = banks[6 + slot]
                for hh in range(2):
                    p0 = hh * 64
                    S.op("pe", lambda e, pC=pC, hi=hi, hh=hh, p0=p0, w0=w0: e.matmul(pC[p0:p0 + 64, w0:512], lhsT=onesb[:, 0:64], rhs=hi[:, hh, w0:512], start=True, stop=False),
                         reads=[bhi, bonesb], writes=[bpC])
                    S.op("pe", lambda e, pC=pC, lo=lo, hh=hh, p0=p0, w0=w0: e.matmul(pC[p0:p0 + 64, w0:512], lhsT=onesb[:, 0:64], rhs=lo[:, hh, w0:512], start=False, stop=True),
                         reads=[blo, bonesb], writes=[bpC])
                zz = psall[:, 2 * slot:2 * slot + 2, w0:512]
                S.op("act", lambda e, A=A, zz=zz, w0=w0: e.activation(out=A[:, :, w0:512], in_=zz, func=AF.Exp), reads=bz, writes=[bA])
                S.op("act", lambda e, f=f, pC=pC, w0=w0: e.activation(out=f[:, w0:512], in_=pC[:, w0:512], func=AF.Exp, scale=-1.0), reads=[bpC], writes=[bf_])

            def stage3(it, slot):
                kb, r, j, i0, z = it
                w0 = i0 * 128
                A, bA = Ab[slot]; f, bf_ = fb[slot]
                pP, bpP = banks[4 + slot]
                for hh in range(2):
                    p0 = hh * 64
                    S.op("pe", lambda e, pP=pP, A=A, hh=hh, p0=p0, r=r, j=j, w0=w0: e.matmul(pP[p0:p0 + 64, w0:512], lhsT=v4[:, r * 16 + j, p0:p0 + 64], rhs=A[:, hh, w0:512],
                                                                                            start=True, stop=True),
                         reads=[bA, bvv], writes=[bpP])
                S.op("dve", lambda e, f=f, w0=w0: e.tensor_tensor(out=OT[:, w0:512], in0=OT[:, w0:512], in1=f[:, w0:512], op=ALU.mult), reads=[bOT, bf_], writes=[bOT])
                S.op("dve", lambda e, pP=pP, w0=w0: e.tensor_tensor(out=OT[:, w0:512], in0=pP[:, w0:512], in1=OT[:, w0:512], op=ALU.add), reads=[bOT, bpP], writes=[bOT])

            for n in range(len(its) + 2):
                if n < len(its):
                    stage1(its[n], n % 2)
                if 1 <= n <= len(its):
                    stage2(its[n - 1], (n - 1) % 2)
                if n >= 2:
                    stage3(its[n - 2], (n - 2) % 2)
            o_, bo_ = ostg[nout % 2]; nout += 1
            S.op("act", lambda e, o_=o_: e.activation(out=o_[:], in_=OT[:], func=AF.Copy), reads=[bOT], writes=[bo_])
            S.dma("sp", T["MX"][4 + pr][:, 3 + q0:3 + q0 + 512], o_[:], reads=[bo_], writes=[T["b_MX"][4 + pr]])
        collective(S, "AllGather", T["MX"][4 + pr], T["MXA"][4 + pr], reads=[T["b_MX"][4 + pr]], writes=[T["b_MXA"][4 + pr]])
    C.close()


def phase_attn_out_ffn(nc, T, stage):
    C = Ctx(nc); S = C.S
    K = load_consts(C, T)
    banks = [(C.ps([128, 512], F32), Buf()) for _ in range(8)]
    hT = C.sb([128, 8, NX], F32); bh = Buf()
    xsrc = T["xT1"].rearrange("(k p) n -> p k n", p=128)
    for sl in TOKSL:
        S.dma("sp", hT[:, :, sl], xsrc[:, :, sl], writes=[bh])
    dst = T["dbg"].rearrange("(k p) n -> p k n", p=128)

    def dump():
        for tt in range(4):
            S.dma("sp", dst[:, :, tt * 512:(tt + 1) * 512], hT[:, :, 3 + tt * 512:3 + (tt + 1) * 512], reads=[bh])

    with contextlib.ExitStack() as st2:
        mT = st2.enter_context(nc.sbuf_tensor("mTf", [128, 8, NX], BF16)); bmT = Buf()
        wo = st2.enter_context(nc.sbuf_tensor("wo", [128, 8, D], BF16)); bwo = Buf()
        cache = {}

        def rank_of(e):
            if "c" not in cache:
                cache["c"] = e.partition_id() % 4
            return cache["c"]
        for c in range(8):
            for r in range(4):
                dstv = mT[:, c, 3:3 + NT].rearrange("p (m r t) -> p m r t", m=4, r=4)[:, :, r, :]

                def fn(e, dstv=dstv, c=c, r=r):
                    cid = rank_of(e)
                    src = T["MXA"][c][r * 128:(r + 1) * 128, bass.ds(cid * 512 + 3, 512)]
                    return e.dma_start(out=dstv, in_=src.rearrange("p (m t) -> p m t", m=4))
                S.dma("sp", None, None, reads=[T["b_MXA"][c]], writes=[bmT], fn=fn)

            def fn2(e, c=c):
                cid = rank_of(e)
                return e.dma_start(out=mT[:, c, 0:3], in_=T["MXA"][c][3 * 128:4 * 128, bass.ds(cid * 512, 3)])
            S.dma("sp", None, None, reads=[T["b_MXA"][c]], writes=[bmT], fn=fn2)
            S.dma("pool", wo[:, c, :], T["w_out0"][c * 128:(c + 1) * 128, :], writes=[bwo])
        for tt, sl in enumerate(TOKSL):
            wd_ = sl.stop - sl.start
            for oc in range(8):
                po, bpo = banks[oc % 4]
                for c in range(8):
                    S.op("pe", lambda e, po=po, c=c, oc=oc, sl=sl: e.matmul(po[:, 0:wd_], lhsT=wo[:, c, oc * 128:(oc + 1) * 128], rhs=mT[:, c, sl],
                                                                            start=(c == 0), stop=(c == 7)),
                         reads=[bwo, bmT], writes=[bpo])
                S.op("dve", lambda e, po=po, oc=oc, sl=sl: e.tensor_tensor(out=hT[:, oc, sl], in0=po[:, 0:wd_], in1=hT[:, oc, sl], op=ALU.add),
                     reads=[bpo, bh], writes=[bh])
        if stage == "mix":
            for c in range(8):
                S.op("dve", lambda e, c=c: e.tensor_copy(out=hT[:, c, :], in_=mT[:, c, :]), reads=[bmT, bh], writes=[bh])
        if stage in ("mix", "attn"):
            dump()
            C.S.emit()
            st2.close(); C.st.close()
            return
        C.S.emit()
    C.S = Sched(nc); S = C.S
    ffn_fm(C, T, 0, hT, bh, K["ones_f"], K["bones"], banks, slices=TOKSL)
    if stage == "ffn0":
        dump()
        C.close()
        return
    hdst = T["H1L"].rearrange("(k p) n -> p k n", p=128)
    for sl in TOKSL:
        S.dma("sp", hdst[:, :, sl], hT[:, :, sl], reads=[bh], writes=[T["b_H1L"]])
    C.close()


def load_h1_contig(C, T, x1, bx1, halo):
    S = C.S
    src = T["H1L"].rearrange("(k p) n -> p k n", p=128)
    if halo:
        for sl in TOKSL:
            S.dma("sp", x1[:, :, sl], src[:, :, sl], reads=[T["b_H1L"]], writes=[bx1])
    else:
        for tt in range(4):
            S.dma("sp", x1[:, :, tt * 512:(tt + 1) * 512], src[:, :, 3 + tt * 512:3 + (tt + 1) * 512], reads=[T["b_H1L"]], writes=[bx1])


def phase_ssd_inproj(nc, T):
    C = Ctx(nc); S = C.S
    K = load_consts(C, T)
    identb, bident = K["identb"], K["bident"]
    banks = [(C.ps([128, 512], F32), Buf()) for _ in range(8)]
    x1 = C.sb([128, 8, 3 + NT], F32); bx1 = Buf()
    load_h1_contig(C, T, x1, bx1, True)
    wn = C.sb([128, 8], F32); bwn = Buf()
    S.dma("sp", wn[:], T["ssd_norm"], writes=[bwn])
    xn = C.sb([128, 8, 3 + NT], BF16); bxn = Buf()
    rstd = C.sb([128, 3 + NT], F32); brstd = Buf()
    sqb = [(C.sb([128, 512], F32), Buf()) for _ in range(2)]
    slices = [slice(0, 3)] + [slice(3 + tt * 512, 3 + (tt + 1) * 512) for tt in range(4)]
    rmsnorm_fm(C, x1, bx1, wn, bwn, xn, bxn, K["ones_f"], K["bones"], banks[0:2], sqb=sqb, rstd=rstd, brstd=brstd, slices=slices)

    cw = C.sb([128, 32, 4], F32); bcw = Buf()
    cb = C.sb([128, 32], F32); bcb = Buf()
    S.dma("sp", cw[:], T["conv_w"], writes=[bcw])
    S.dma("sp", cb[:], T["conv_b"], writes=[bcb])
    wb = [(C.sb([128, 8, 512], BF16), Buf()) for _ in range(2)]
    wsrc = T["ssd_w_in"].rearrange("(k p) c -> p k c", p=128)
    ub = [(C.sb([128, 3 + NT], F32), Buf()) for _ in range(2)]
    accb = [(C.sb([128, NT], F32), Buf()) for _ in range(2)]
    xcb = [(C.sb([128, NT], BF16), Buf()) for _ in range(2)]
    ctmp = C.sb([128, NT], F32); bctmp = Buf()
    tst = [(C.sb([128, 8, 128], BF16), Buf()) for _ in range(2)]
    pTs = [(banks[6][0][:].bitcast(BF16), banks[6][1]), (banks[7][0][:].bitcast(BF16), banks[7][1])]
    nst = 0
    for g in range(8):
        w, bw = wb[g % 2]
        for k in range(8):
            S.dma("pool", w[:, k, :], wsrc[:, k, 2048 + g * 512:2048 + (g + 1) * 512], writes=[bw])
        for c4 in range(4):
            cc = g * 4 + c4
            u, bu = ub[cc % 2]; acc, bacc = accb[cc % 2]; xc, bxc = xcb[cc % 2]
            veng = "dve"
            for ti, sl in enumerate(slices):
                wd_ = sl.stop - sl.start
                p1, bp1 = banks[2 + ti % 4]
                for k in range(8):
                    S.op("pe", lambda e, p1=p1, w=w, k=k, c4=c4, sl=sl, wd_=wd_: e.matmul(p1[:, 0:wd_], lhsT=w[:, k, c4 * 128:(c4 + 1) * 128], rhs=xn[:, k, sl],
                                                                                         start=(k == 0), stop=(k == 7)),
                         reads=[bw, bxn], writes=[bp1])
                S.op("act", lambda e, u=u, p1=p1, sl=sl, wd_=wd_: e.activation(out=u[:, sl], in_=p1[:, 0:wd_], func=AF.Copy), reads=[bp1], writes=[bu])
            S.op(veng, lambda e, acc=acc, u=u, cc=cc: e.tensor_scalar(out=acc[:], in0=u[:, 0:NT], scalar1=cw[:, cc, 0:1], scalar2=None, op0=ALU.mult),
                 reads=[bu, bcw], writes=[bacc])
            for tap in range(1, 4):
                if veng == "dve":
                    S.op(veng, lambda e, acc=acc, u=u, cc=cc, tap=tap: e.scalar_tensor_tensor(out=acc[:], in0=u[:, tap:tap + NT], scalar=cw[:, cc, tap:tap + 1],
                                                                                             in1=acc[:], op0=ALU.mult, op1=ALU.add),
                         reads=[bu, bcw, bacc], writes=[bacc])
                else:
                    S.op(veng, lambda e, u=u, cc=cc, tap=tap: e.tensor_scalar(out=ctmp[:], in0=u[:, tap:tap + NT], scalar1=cw[:, cc, tap:tap + 1], scalar2=None, op0=ALU.mult),
                         reads=[bu, bcw], writes=[bctmp])
                    S.op(veng, lambda e, acc=acc: e.tensor_tensor(out=acc[:], in0=acc[:], in1=ctmp[:], op=ALU.add), reads=[bacc, bctmp], writes=[bacc])
            S.op("act", lambda e, xc=xc, acc=acc, cc=cc: e.activation(out=xc[:], in_=acc[:], func=AF.Silu, bias=cb[:, cc:cc + 1]),
                 reads=[bacc, bcb], writes=[bxc])
            if cc >= 16:
                nm = "BT" if cc < 24 else "CT"
                gi = cc - 16 if cc < 24 else cc - 24
                S.dma("sp", T[nm][gi * 128:(gi + 1) * 128, :], xc[:], reads=[bxc], writes=[T["b_" + nm]])
            if cc < 24:
                dname = "XS" if cc < 16 else "BTOK"
                col0 = cc * 128 if cc < 16 else (cc - 16) * 128
                for half in range(2):
                    pT, bpT = pTs[nst % 2]
                    st_, bst = tst[nst % 2]; nst += 1
                    for tb8 in range(8):
                        tb = half * 8 + tb8
                        S.op("pe", lambda e, pT=pT, xc=xc, tb=tb, tb8=tb8: e.transpose(pT[:, tb8 * 128:(tb8 + 1) * 128], xc[:, tb * 128:(tb + 1) * 128], identb[:]),
                             reads=[bxc, bident], writes=[bpT])
                    S.op("dve" if nst % 2 == 0 else "act", (lambda e, st_=st_, pT=pT: e.tensor_copy(out=st_[:], in_=pT.rearrange("p (b c) -> p b c", c=128)))
                         if nst % 2 == 0 else (lambda e, st_=st_, pT=pT: e.activation(out=st_[:], in_=pT.rearrange("p (b c) -> p b c", c=128), func=AF.Copy)),
                         reads=[bpT], writes=[bst])
                    dstv = T[dname][half * 1024:(half + 1) * 1024, col0:col0 + 128].rearrange("(b t) c -> t b c", t=128)
                    S.dma("sp", dstv, st_[:], reads=[bst], writes=[T["b_" + dname]])
    zst = [(C.sb([128, 512], F32), Buf()) for _ in range(2)]
    nz = 0
    for g in range(4):
        w, bw = wb[g % 2]
        for k in range(8):
            S.dma("pool", w[:, k, :], wsrc[:, k, g * 512:(g + 1) * 512], writes=[bw])
        for tb in range(16):
            p1, bp1 = banks[2 + tb % 4]
            for k in range(8):
                S.op("pe", lambda e, p1=p1, w=w, k=k, tb=tb: e.matmul(p1[:], lhsT=xn[:, k, 3 + tb * 128:3 + (tb + 1) * 128], rhs=w[:, k, :],
                                                                      start=(k == 0), stop=(k == 7)),
                     reads=[bw, bxn], writes=[bp1])
            z_, bz = zst[nz % 2]; nz += 1
            S.op("act", lambda e, z_=z_, p1=p1: e.activation(out=z_[:], in_=p1[:], func=AF.Silu), reads=[bp1], writes=[bz])
            S.dma("sp", T["ZS"][tb * 128:(tb + 1) * 128, g * 512:(g + 1) * 512], z_[:], reads=[bz], writes=[T["b_ZS"]])
    wdt = C.sb([128, 8, 32], BF16); bwdt = Buf()
    for k in range(8):
        S.dma("pool", wdt[:, k, :], wsrc[:, k, 6144:6176], writes=[bwdt])
    hv = C.sb([128, 3, 32], F32); bhv = Buf()
    S.dma("sp", hv[:, 0:1, :], T["dt_bias"].partition_broadcast(128), writes=[bhv])
    S.dma("sp", hv[:, 1:2, :], T["a_log"].partition_broadcast(128), writes=[bhv])
    S.op("act", lambda e: e.activation(out=hv[:, 2, :], in_=hv[:, 1, :], func=AF.Exp), reads=[bhv], writes=[bhv])
    dst_ = [(C.sb([128, 64], F32), Buf()) for _ in range(2)]
    for tb in range(16):
        p1, bp1 = banks[2 + tb % 4]
        for k in range(8):
            S.op("pe", lambda e, p1=p1, k=k, tb=tb: e.matmul(p1[:, 0:32], lhsT=xn[:, k, 3 + tb * 128:3 + (tb + 1) * 128], rhs=wdt[:, k, :],
                                                             start=(k == 0), stop=(k == 7)),
                 reads=[bwdt, bxn], writes=[bp1])
        d_, bd = dst_[tb % 2]
        S.op("dve", lambda e, d_=d_, p1=p1: e.tensor_tensor(out=d_[:, 0:32], in0=p1[:, 0:32], in1=hv[:, 0, :], op=ALU.add), reads=[bp1, bhv], writes=[bd])
        S.op("act", lambda e, d_=d_: e.activation(out=d_[:, 0:32], in_=d_[:, 0:32], func=AF.Exp), reads=[bd], writes=[bd])
        S.op("act", lambda e, d_=d_: e.activation(out=d_[:, 0:32], in_=d_[:, 0:32], func=AF.Ln, bias=1.0), reads=[bd], writes=[bd])
        S.op("dve", lambda e, d_=d_: e.scalar_tensor_tensor(out=d_[:, 32:64], in0=d_[:, 0:32], scalar=-1.0, in1=hv[:, 2, :], op0=ALU.mult, op1=ALU.mult),
             reads=[bd, bhv], writes=[bd])
        S.dma("sp", T["DTD"][tb * 128:(tb + 1) * 128, :], d_[:], reads=[bd], writes=[T["b_DTD"]])
    C.close()


def ssd_consts(C, T):
    S = C.S
    k = {}
    for nm in ("tri_incl", "tri_gt", "ntri_incl"):
        k[nm] = C.sb([128, 128], F32); k["b_" + nm] = Buf()
        S.dma("sp", k[nm][:], T[nm], writes=[k["b_" + nm]])
    return k


def phase_ssd_states(nc, T):
    C = Ctx(nc); S = C.S
    K = load_consts(C, T)
    K2 = ssd_consts(C, T)
    banks = [(C.ps([128, 512], F32), Buf()) for _ in range(8)]
    Sloc = C.sb([128, 32, 64], F32); bS = Buf()
    S.op("pool", lambda e: e.memset(Sloc[:], 0.0), writes=[bS])
    ldsum = C.sb([128, 32], F32); bld = Buf()
    S.op("pool", lambda e: e.memset(ldsum[:], 0.0), writes=[bld])
    xsb = [(C.sb([128, 32, 64], BF16), Buf()) for _ in range(2)]
    btb = [(C.sb([128, 1024], BF16), Buf()) for _ in range(2)]
    dtb = [(C.sb([128, 64], F32), Buf()) for _ in range(2)]
    smb = [(C.sb([128, 3, 32], F32), Buf()) for _ in range(2)]
    xwb = [(C.sb([128, 32, 64], BF16), Buf()) for _ in range(2)]
    stb = [(C.sb([128, 2048], F32), Buf()) for _ in range(2)]
    for c in range(16):
        xs, bxs = xsb[c % 2]; bt, bbt = btb[c % 2]; dt_, bdt = dtb[c % 2]; sm, bsm = smb[c % 2]; xw, bxw = xwb[c % 2]; st_, bst = stb[c % 2]
        rows = slice(c * 128, (c + 1) * 128)
        S.dma("sp", xs[:].rearrange("p h c -> p (h c)"), T["XS"][rows, :], reads=[T["b_XS"]], writes=[bxs])
        S.dma("act", bt[:], T["BTOK"][rows, :], reads=[T["b_BTOK"]], writes=[bbt])
        S.dma("sp", dt_[:], T["DTD"][rows, :], reads=[T["b_DTD"]], writes=[bdt])
        pa, bpa = banks[c % 2]
        S.op("pe", lambda e, pa=pa, dt_=dt_: e.matmul(pa[:, 0:32], lhsT=K2["tri_gt"][:], rhs=dt_[:, 32:64], start=True, stop=True),
             reads=[K2["b_tri_gt"], bdt], writes=[bpa])
        S.op("pe", lambda e, pa=pa, dt_=dt_: e.matmul(pa[:, 32:64], lhsT=K["ones_f"][:], rhs=dt_[:, 32:64], start=True, stop=True),
             reads=[K["bones"], bdt], writes=[bpa])
        S.op("act", lambda e, sm=sm, pa=pa: e.activation(out=sm[:, 0:2, :], in_=pa[:, 0:64].rearrange("p (a h) -> p a h", a=2), func=AF.Exp),
             reads=[bpa], writes=[bsm])
        S.op("dve", lambda e, sm=sm, dt_=dt_: e.tensor_tensor(out=sm[:, 2, :], in0=sm[:, 0, :], in1=dt_[:, 0:32], op=ALU.mult), reads=[bsm, bdt], writes=[bsm])
        S.op("dve", lambda e, xw=xw, xs=xs, sm=sm: e.tensor_tensor(out=xw[:], in0=xs[:], in1=sm[:, 2, :].unsqueeze(2).to_broadcast([128, 32, 64]), op=ALU.mult),
             reads=[bxs, bsm], writes=[bxw])
        xwf = xw[:].rearrange("p h c -> p (h c)")
        for g in range(8):
            ps_, bps = banks[2 + g // 2]
            S.op("pe", lambda e, ps_=ps_, bt=bt, g=g, xwf=xwf: e.matmul(ps_[:, (g % 2) * 256:(g % 2 + 1) * 256], lhsT=bt[:, g * 128:(g + 1) * 128],
                                                                        rhs=xwf[:, g * 256:(g + 1) * 256], start=True, stop=True),
                 reads=[bbt, bxw], writes=[bps])
        for bq in range(4):
            ps_, bps = banks[2 + bq]
            S.op("act" if bq % 2 == 0 else "dve",
                 (lambda e, st_=st_, ps_=ps_, bq=bq: e.activation(out=st_[:, bq * 512:(bq + 1) * 512], in_=ps_[:], func=AF.Copy)) if bq % 2 == 0 else
                 (lambda e, st_=st_, ps_=ps_, bq=bq: e.tensor_copy(out=st_[:, bq * 512:(bq + 1) * 512], in_=ps_[:])),
                 reads=[bps], writes=[bst])
        S.dma("sp", T["ST"][c], st_[:], reads=[bst], writes=[T["b_ST"]])
        S.op("pool", lambda e, sm=sm: e.tensor_tensor(out=Sloc[:], in0=Sloc[:], in1=sm[:, 1, :].unsqueeze(2).to_broadcast([128, 32, 64]), op=ALU.mult),
             reads=[bS, bsm], writes=[bS])
        S.op("pool", lambda e, st_=st_: e.tensor_tensor(out=Sloc[:].rearrange("p h c -> p (h c)"), in0=Sloc[:].rearrange("p h c -> p (h c)"), in1=st_[:], op=ALU.add),
             reads=[bS, bst], writes=[bS])
        S.op("dve", lambda e, pa=pa: e.tensor_tensor(out=ldsum[:], in0=pa[:, 32:64], in1=ldsum[:], op=ALU.add), reads=[bpa, bld], writes=[bld])
    S.dma("sp", T["SXa"], Sloc[:].rearrange("p h c -> p (h c)"), reads=[bS], writes=[T["b_SXa"]])
    S.dma("sp", T["SXb"], ldsum[:], reads=[bld], writes=[T["b_SXb"]])
    collective(S, "AllGather", T["SXa"], T["SXa_all"], reads=[T["b_SXa"]], writes=[T["b_SXa_all"]])
    collective(S, "AllGather", T["SXb"], T["SXb_all"], reads=[T["b_SXb"]], writes=[T["b_SXb_all"]])
    C.close()


def phase_ssd_scan(nc, T):
    C = Ctx(nc); S = C.S
    K = load_consts(C, T)
    K2 = ssd_consts(C, T)
    identb, bident = K["identb"], K["bident"]
    identf, bidentf = K["idf"], K["bidf"]
    ones_f, bones = K["ones_f"], K["bones"]
    banks = [(C.ps([128, 512], F32), Buf()) for _ in range(8)]
    prev = C.sb([128, 32, 64], F32); bprev = Buf()
    prevbs = [(C.sb([128, 2048], BF16), Buf()) for _ in range(2)]
    prevb, bprevb = prevbs[0]
    msk = C.sb([128, 20], F32); bmsk = Buf()
    S.dma("sp", msk[:], T["selmask"], writes=[bmsk])
    ld = C.sb([128, 4, 32], F32); bldg = Buf()
    S.dma("sp", ld[:], T["SXb_all"].rearrange("(r p) h -> p r h", p=128), reads=[T["b_SXb_all"]], writes=[bldg])
    coef = C.sb([128, 4, 32], F32); bcoef = Buf()
    for r in range(4):
        S.op("dve", lambda e, r=r: e.tensor_scalar(out=coef[:, r, :], in0=ld[:, 0, :], scalar1=msk[:, 4 + 4 * r:5 + 4 * r], scalar2=None, op0=ALU.mult),
             reads=[bldg, bmsk], writes=[bcoef])
        for r2 in range(1, 4):
            S.op("dve", lambda e, r=r, r2=r2: e.scalar_tensor_tensor(out=coef[:, r, :], in0=ld[:, r2, :], scalar=msk[:, 4 + 4 * r + r2:5 + 4 * r + r2],
                                                                     in1=coef[:, r, :], op0=ALU.mult, op1=ALU.add),
                 reads=[bldg, bmsk, bcoef], writes=[bcoef])
        S.op("act", lambda e, r=r: e.activation(out=coef[:, r, :], in_=coef[:, r, :], func=AF.Exp), reads=[bcoef], writes=[bcoef])
        S.op("dve", lambda e, r=r: e.tensor_scalar(out=coef[:, r, :], in0=coef[:, r, :], scalar1=msk[:, r:r + 1], scalar2=None, op0=ALU.mult),
             reads=[bcoef, bmsk], writes=[bcoef])
    S.op("pool", lambda e: e.memset(prev[:], 0.0), writes=[bprev])
    sg_ = [(C.sb([128, 32, 64], F32), Buf()) for _ in range(2)]
    for r in range(4):
        t_, bt_ = sg_[r % 2]
        S.dma("sp", t_[:].rearrange("p h c -> p (h c)"), T["SXa_all"][r * 128:(r + 1) * 128, :], reads=[T["b_SXa_all"]], writes=[bt_])
        S.op("dve", lambda e, t_=t_, r=r: e.tensor_tensor(out=t_[:], in0=t_[:], in1=coef[:, r, :].unsqueeze(2).to_broadcast([128, 32, 64]), op=ALU.mult),
             reads=[bt_, bcoef], writes=[bt_])
        S.op("dve", lambda e, t_=t_: e.tensor_tensor(out=prev[:], in0=prev[:], in1=t_[:], op=ALU.add), reads=[bt_, bprev], writes=[bprev])
    prevf = prev[:].rearrange("p h c -> p (h c)")
    S.op("act", lambda e: e.activation(out=prevb[:], in_=prevf, func=AF.Copy), reads=[bprev], writes=[bprevb])
    hv = C.sb([128, 32], F32); bhv = Buf()
    S.dma("sp", hv[:].unsqueeze(1), T["ssd_d"].partition_broadcast(128), writes=[bhv])
    gw = C.sb([128, 16], F32); bgw = Buf()
    S.dma("sp", gw[:], T["gnorm"], writes=[bgw])
    negut = C.sb([128, 4, 128], F32); bneg = Buf()
    negutb = C.sb([128, 4, 128], BF16); bnegb = Buf()
    for r in range(4):
        S.dma("sp", negut[:, r, :], T["neg_ut"], writes=[bneg])
    S.op("dve", lambda e: e.tensor_copy(out=negutb[:], in_=negut[:]), reads=[bneg], writes=[bnegb])
    xsb = [(C.sb([128, 32, 64], BF16), Buf()) for _ in range(2)]
    dtb = [(C.sb([128, 64], F32), Buf()) for _ in range(2)]
    btb = [(C.sb([128, 8, 128], BF16), Buf()) for _ in range(2)]
    ctb = [(C.sb([128, 8, 128], BF16), Buf()) for _ in range(2)]
    zsb = [(C.sb([128, 2048], F32), Buf()) for _ in range(2)]
    stb = [(C.sb([128, 2048], F32), Buf()) for _ in range(2)]
    dth = C.sb([128, 2, 32], BF16); bdth = Buf()
    trib = C.sb([128, 128], BF16); ntrib = C.sb([128, 128], BF16); onesb = C.sb([128, 128], BF16); btb_ = Buf()
    S.op("dve", lambda e: e.tensor_copy(out=trib[:], in_=K2["tri_incl"][:]), reads=[K2["b_tri_incl"]], writes=[btb_])
    S.op("dve", lambda e: e.tensor_copy(out=ntrib[:], in_=K2["ntri_incl"][:]), reads=[K2["b_ntri_incl"]], writes=[btb_])
    S.op("dve", lambda e: e.memset(onesb[:], 1.0), writes=[btb_])
    smb = [(C.sb([128, 3, 32], F32), Buf()) for _ in range(2)]
    xdt = C.sb([128, 32, 64], BF16); bxdt = Buf()
    Dmb = [(C.sb([128, 4, 128], F32), Buf()) for _ in range(2)]
    MTb = [(C.sb([128, 4, 128], BF16), Buf()) for _ in range(2)]
    tyb = [(C.sb([128, 4, 64], F32), Buf()) for _ in range(2)]
    yc = C.sb([128, 8, 256], F32); byc = Buf()
    gy = C.sb([128, 8, 256], F32); bgy = Buf()
    ynb = C.sb([128, 2048], BF16); byn = Buf()
    nrm = C.sb([128, 3, 8], F32); bnrm = Buf()
    junk = C.sb([128, 256], F32); bjunk = Buf()
    yTs = [(C.sb([128, 16, 128], BF16), Buf()) for _ in range(2)]
    bt_src = T["BT"].rearrange("(g n) t -> n g t", n=128)
    ct_src = T["CT"].rearrange("(g n) t -> n g t", n=128)
    yt_dst = T["YT"].rearrange("(cc p) t -> p cc t", p=128)
    pT0 = banks[6][0][:].bitcast(BF16); pT1 = banks[7][0][:].bitcast(BF16)
    half_bufs = [[Buf(), Buf()] for _ in range(3)]
    for c in range(16):
        xs, bxs = xsb[c % 2]; dt_, bdt = dtb[c % 2]; bt, bbt = btb[c % 2]; ct, bct = ctb[c % 2]
        zs, bzs = zsb[c % 2]; st_, bst = stb[c % 2]; sm, bsm = smb[c % 2]
        rows = slice(c * 128, (c + 1) * 128)
        S.dma("sp", xs[:].rearrange("p h c -> p (h c)"), T["XS"][rows, :], reads=[T["b_XS"]], writes=[bxs])
        S.dma("sp", dt_[:], T["DTD"][rows, :], reads=[T["b_DTD"]], writes=[bdt])
        S.dma("act", bt[:], bt_src[:, :, rows], reads=[T["b_BT"]], writes=[bbt])
        S.dma("act", ct[:], ct_src[:, :, rows], reads=[T["b_CT"]], writes=[bct])
        S.dma("sp", zs[:], T["ZS"][rows, :], reads=[T["b_ZS"]], writes=[bzs])
        S.dma("sp", st_[:], T["ST"][c], reads=[T["b_ST"]], writes=[bst])
        pa, bpa = banks[0]
        S.op("pe", lambda e, dt_=dt_: e.matmul(pa[:, 0:32], lhsT=K2["tri_incl"][:], rhs=dt_[:, 32:64], start=True, stop=True),
             reads=[K2["b_tri_incl"], bdt], writes=[bpa])
        S.op("pe", lambda e, dt_=dt_: e.matmul(pa[:, 32:64], lhsT=ones_f[:], rhs=dt_[:, 32:64], start=True, stop=True),
             reads=[bones, bdt], writes=[bpa])
        S.op("act", lambda e, sm=sm: e.activation(out=sm[:, 0:2, :], in_=pa[:, 0:64].rearrange("p (a h) -> p a h", a=2), func=AF.Exp),
             reads=[bpa], writes=[bsm])
        S.op("dve", lambda e, dt_=dt_: e.tensor_copy(out=dth[:, 0, :], in_=dt_[:, 32:64]), reads=[bdt], writes=[bdth])
        S.op("dve", lambda e, dt_=dt_: e.tensor_tensor(out=dth[:, 1, :], in0=dt_[:, 32:64], in1=dth[:, 0, :], op=ALU.subtract), reads=[bdt, bdth], writes=[bdth])
        S.op("dve", lambda e, xs=xs, dt_=dt_: e.tensor_tensor(out=xdt[:], in0=xs[:], in1=dt_[:, 0:32].unsqueeze(2).to_broadcast([128, 32, 64]), op=ALU.mult),
             reads=[bxs, bdt], writes=[bxdt])
        prevb, bprevb = prevbs[c % 2]
        nprevb, bnprevb = prevbs[(c + 1) % 2]
        S.op("pool", lambda e, sm=sm: e.tensor_tensor(out=prev[:], in0=prev[:], in1=sm[:, 1, :].unsqueeze(2).to_broadcast([128, 32, 64]), op=ALU.mult),
             reads=[bprev, bsm], writes=[bprev])
        S.op("pool", lambda e, st_=st_: e.tensor_tensor(out=prevf, in0=prevf, in1=st_[:], op=ALU.add), reads=[bprev, bst], writes=[bprev])
        for g in range(8):
            pseg, bpseg = banks[1 + g % 2]
            Dm, bDm = Dmb[g % 2]; MT, bMT = MTb[g % 2]; ty, bty = tyb[g % 2]
            hs = slice(4 * g, 4 * g + 4)
            psv = pseg[:].rearrange("p (h l) -> p h l", h=4)
            for x_ in range(2):
                S.op("pe", lambda e, psv=psv, x_=x_, hs=hs: e.matmul(psv, lhsT=ntrib[:], rhs=dth[:, x_, hs].unsqueeze(2).to_broadcast([128, 4, 128]),
                                                                     start=(x_ == 0), stop=False),
                     reads=[btb_, bdth], writes=[bpseg])
            for r in range(4):
                for x_ in range(2):
                    S.op("pe", lambda e, pseg=pseg, x_=x_, r=r, g=g: e.matmul(pseg[:, r * 128:(r + 1) * 128], lhsT=dth[:, x_, 4 * g + r:4 * g + r + 1].to_broadcast([128, 128]),
                                                                              rhs=trib[:], start=False, stop=False),
                         reads=[btb_, bdth], writes=[bpseg])
            S.op("pe", lambda e, pseg=pseg: e.matmul(pseg[:], lhsT=identb[:], rhs=negutb[:].rearrange("p h l -> p (h l)"), start=False, stop=True),
                 reads=[bident, bnegb], writes=[bpseg])
            S.op("act", lambda e, Dm=Dm, pseg=pseg: e.activation(out=Dm[:].rearrange("p h l -> p (h l)"), in_=pseg[:], func=AF.Exp), reads=[bpseg], writes=[bDm])
            pg = banks[3][0]; bpg = half_bufs[0][g % 2]
            gsl = slice((g % 2) * 128, (g % 2 + 1) * 128)
            S.op("pe", lambda e, bt=bt, ct=ct, g=g, gsl=gsl: e.matmul(pg[:, gsl], lhsT=bt[:, g, :], rhs=ct[:, g, :], start=True, stop=True),
                 reads=[bbt, bct], writes=[bpg])
            S.op("dve", lambda e, MT=MT, Dm=Dm, gsl=gsl: e.tensor_tensor(out=MT[:], in0=Dm[:], in1=pg[:, gsl].unsqueeze(1).to_broadcast([128, 4, 128]), op=ALU.mult),
                 reads=[bDm, bpg], writes=[bMT])
            pyd = banks[4][0]; bpyd = half_bufs[1][g % 2]
            pyo = banks[5][0]; bpyo = half_bufs[2][g % 2]
            ysl = slice((g % 2) * 256, (g % 2 + 1) * 256)
            for r in range(4):
                S.op("pe", lambda e, MT=MT, r=r, g=g, ysl=ysl: e.matmul(pyd[:, ysl.start + r * 64:ysl.start + (r + 1) * 64], lhsT=MT[:, r, :], rhs=xdt[:, 4 * g + r, :],
                                                                        start=True, stop=True),
                     reads=[bMT, bxdt], writes=[bpyd])
            S.op("pe", lambda e, ct=ct, g=g, ysl=ysl: e.matmul(pyo[:, ysl], lhsT=ct[:, g, :], rhs=prevb[:, g * 256:(g + 1) * 256], start=True, stop=True),
                 reads=[bct, bprevb], writes=[bpyo])
            S.op("dve", lambda e, ty=ty, sm=sm, hs=hs, ysl=ysl: e.tensor_tensor(out=ty[:], in0=pyo[:, ysl].rearrange("p (r c) -> p r c", r=4),
                                                                                in1=sm[:, 0, hs].unsqueeze(2).to_broadcast([128, 4, 64]), op=ALU.mult),
                 reads=[bpyo, bsm], writes=[bty])
            S.op("dve", lambda e, ty=ty, g=g, ysl=ysl: e.tensor_tensor(out=yc[:, g, :], in0=pyd[:, ysl], in1=ty[:].rearrange("p r c -> p (r c)"), op=ALU.add),
                 reads=[bpyd, bty], writes=[byc])
        S.op("act", lambda e, nprevb=nprevb: e.activation(out=nprevb[:], in_=prevf, func=AF.Copy), reads=[bprev], writes=[bnprevb])
        ycv = yc[:].rearrange("p g (r c) -> p (g r) c", r=4)
        S.op("pool", lambda e, xs=xs: e.tensor_tensor(out=gy[:].rearrange("p g (r c) -> p (g r) c", r=4), in0=xs[:], in1=hv[:].unsqueeze(2).to_broadcast([128, 32, 64]), op=ALU.mult),
             reads=[bxs, bhv], writes=[bgy])
        S.op("pool", lambda e: e.tensor_tensor(out=yc[:], in0=yc[:], in1=gy[:], op=ALU.add), reads=[byc, bgy], writes=[byc])
        S.op("dve", lambda e, zs=zs: e.tensor_tensor(out=gy[:], in0=yc[:], in1=zs[:].rearrange("p (g c) -> p g c", g=8), op=ALU.mult), reads=[byc, bzs, bgy], writes=[bgy])
        for g in range(8):
            S.op("act", lambda e, g=g: e.activation(out=junk[:], in_=gy[:, g, :], func=AF.Square, accum_out=nrm[:, 0, g:g + 1]), reads=[bgy], writes=[bjunk, bnrm])
        S.op("act", lambda e: e.activation(out=nrm[:, 1, :], in_=nrm[:, 0, :], func=AF.Ln, scale=1.0 / 256, bias=1e-5), reads=[bnrm], writes=[bnrm])
        S.op("act", lambda e: e.activation(out=nrm[:, 2, :], in_=nrm[:, 1, :], func=AF.Exp, scale=-0.5), reads=[bnrm], writes=[bnrm])
        S.op("dve", lambda e: e.tensor_tensor(out=ynb[:].rearrange("p (g c) -> p g c", g=8), in0=gy[:], in1=nrm[:, 2, :].unsqueeze(2).to_broadcast([128, 8, 256]), op=ALU.mult),
             reads=[bgy, bnrm], writes=[byn])
        yT, byT = yTs[c % 2]
        for cc in range(16):
            pT = pT0 if cc < 8 else pT1
            S.op("pe", lambda e, pT=pT, cc=cc: e.transpose(pT[:, (cc % 8) * 128:(cc % 8 + 1) * 128], ynb[:, cc * 128:(cc + 1) * 128], identb[:]),
                 reads=[byn, bident], writes=[banks[6][1] if cc < 8 else banks[7][1]])
        S.op("dve", lambda e, yT=yT: e.tensor_tensor(out=yT[:, 0:8, :], in0=pT0.rearrange("p (b c) -> p b c", c=128), in1=gw[:, 0:8].unsqueeze(2).to_broadcast([128, 8, 128]), op=ALU.mult),
             reads=[banks[6][1], bgw], writes=[byT])
        S.op("dve", lambda e, yT=yT: e.tensor_tensor(out=yT[:, 8:16, :], in0=pT1.rearrange("p (b c) -> p b c", c=128), in1=gw[:, 8:16].unsqueeze(2).to_broadcast([128, 8, 128]), op=ALU.mult),
             reads=[banks[7][1], bgw], writes=[byT])
        S.dma("sp", yt_dst[:, :, rows], yT[:], reads=[byT], writes=[T["b_YT"]])
    C.close()


def phase_ssd_out_ffn(nc, T, stage):
    C = Ctx(nc); S = C.S
    K = load_consts(C, T)
    banks = [(C.ps([128, 512], F32), Buf()) for _ in range(8)]
    hT = C.sb([128, 8, NT], F32); bh = Buf()
    load_h1_contig(C, T, hT, bh, False)
    dst = T["dbg"].rearrange("(k p) n -> p k n", p=128)
    with contextlib.ExitStack() as st2:
        yT = st2.enter_context(nc.sbuf_tensor("yTf", [128, 16, NT], BF16)); byT = Buf()
        wo = st2.enter_context(nc.sbuf_tensor("wo1", [128, 16, D], BF16)); bwo = Buf()
        ysrc = T["YT"].rearrange("(cc p) t -> p cc t", p=128)
        for cc in range(16):
            S.dma("act", yT[:, cc, :], ysrc[:, cc, :], reads=[T["b_YT"]], writes=[byT])
            S.dma("pool", wo[:, cc, :], T["ssd_w_out"][cc * 128:(cc + 1) * 128, :], writes=[bwo])
        for tt in range(4):
            sl = slice(tt * 512, (tt + 1) * 512)
            for oc in range(8):
                po, bpo = banks[oc % 4]
                for cc in range(16):
                    S.op("pe", lambda e, po=po, cc=cc, oc=oc, sl=sl: e.matmul(po[:], lhsT=wo[:, cc, oc * 128:(oc + 1) * 128], rhs=yT[:, cc, sl],
                                                                              start=(cc == 0), stop=(cc == 15)),
                         reads=[bwo, byT], writes=[bpo])
                S.op("dve", lambda e, po=po, oc=oc, sl=sl: e.tensor_tensor(out=hT[:, oc, sl], in0=po[:], in1=hT[:, oc, sl], op=ALU.add),
                     reads=[bpo, bh], writes=[bh])
        if stage == "ssd":
            for tt in range(4):
                S.dma("sp", dst[:, :, tt * 512:(tt + 1) * 512], hT[:, :, tt * 512:(tt + 1) * 512], reads=[bh])
            C.S.emit()
            st2.close(); C.st.close()
            return
        C.S.emit()
    C.S = Sched(nc); S = C.S
    Cf = Ctx(nc); Cf.S = C.S
    ffn_fm(Cf, T, 1, hT, bh, K["ones_f"], K["bones"], banks)
    C.S.emit()
    Cf.st.close()
    C.S = Sched(nc); S = C.S
    if stage != "ffn1":
        fw = C.sb([128, 8], F32); bfw = Buf()
        S.dma("sp", fw[:], T["final_norm"], writes=[bfw])
        rstd = C.sb([128, NT], F32); brstd = Buf()
        sq = [(C.sb([128, 512], F32), Buf()) for _ in range(2)]
        for tt in range(4):
            sl = slice(tt * 512, (tt + 1) * 512)
            ps, bps = banks[tt % 2]
            for k in range(8):
                q_, bq_ = sq[k % 2]
                S.op("act", lambda e, q_=q_, k=k, sl=sl: e.activation(out=q_[:], in_=hT[:, k, sl], func=AF.Square), reads=[bh], writes=[bq_])
                S.op("pe", lambda e, ps=ps, q_=q_, k=k: e.matmul(ps[:], lhsT=K["ones_f"][:], rhs=q_[:], start=(k == 0), stop=(k == 7)),
                     reads=[bq_, K["bones"]], writes=[bps])
            S.op("act", lambda e, ps=ps, sl=sl: e.activation(out=rstd[:, sl], in_=ps[:], func=AF.Ln, scale=1.0 / D, bias=1e-6), reads=[bps], writes=[brstd])
            S.op("act", lambda e, sl=sl: e.activation(out=rstd[:, sl], in_=rstd[:, sl], func=AF.Exp, scale=-0.5), reads=[brstd], writes=[brstd])
            for k in range(8):
                S.op("dve", lambda e, k=k, sl=sl: e.scalar_tensor_tensor(out=hT[:, k, sl], in0=hT[:, k, sl], scalar=fw[:, k:k + 1], in1=rstd[:, sl],
                                                                         op0=ALU.mult, op1=ALU.mult),
                     reads=[bh, bfw, brstd], writes=[bh])
    for tt in range(4):
        S.dma("sp", dst[:, :, tt * 512:(tt + 1) * 512], hT[:, :, tt * 512:(tt + 1) * 512], reads=[bh])
    C.close()


def build_program(stage):
    Buf.ALL = []
    nc = bass.Bass("TRN2", target_bir_lowering=False)
    Sched.STATE = SemState(nc)
    T = {}

    def inp(name, shape, dt=F32):
        T[name] = nc.dram_tensor(name, list(shape), dt, kind="ExternalInput").ap()

    def scr(name, shape, dt):
        T[name] = nc.dram_tensor(name, list(shape), dt).ap()
        T["b_" + name] = Buf(name)

    inp("xT0", [D, NT]); inp("cs_tab", [128, 2, NT]); inp("attn_norm", [128, 8])
    inp("w_in0", [D, 3072]); inp("w_sw0", [D, 1024]); inp("w_out0", [D, D])
    inp("nm_incl", [128, 16, 512]); inp("nm_strict", [128, 16, 512])
    inp("ident", [128, 128]); inp("nu", [128, 128])
    for nm in ("lq1", "lk1", "lq2", "lk2"):
        inp(nm, [1, 64])
    inp("subln", [1, 128])
    inp("ffn_norm", [2, 128, 8]); inp("ffn_wg", [2, D, D_FF]); inp("ffn_wu", [2, D, D_FF]); inp("ffn_wd", [2, D_FF, D])
    scr("QT", [1536, NT], BF16)
    def scr_list(name, n, shape, dt):
        T[name] = [nc.dram_tensor("%s_%d" % (name, i), list(shape), dt).ap() for i in range(n)]
        T["b_" + name] = [Buf(name) for _ in range(n)]

    scr_list("KTd", 4, [256, NT], BF16); scr_list("KTd_all", 4, [1024, NT], BF16)
    scr_list("KTs", 2, [256, NT], BF16); scr_list("KTs_all", 2, [1024, NT], BF16)
    scr_list("Vd", 4, [512, 516], BF16); scr_list("Vd_all", 4, [2048, 516], BF16)
    scr_list("Vs", 2, [1024, 512], BF16); scr_list("Vs_all", 2, [4096, 512], BF16)
    scr_list("MX", 8, [128, NX], BF16); scr_list("MXA", 8, [512, NX], BF16)
    scr("H1L", [D, NX], F32)
    inp("xT1", [D, NX])
    inp("ssd_norm", [128, 8]); inp("ssd_w_in", [D, 6176]); inp("conv_w", [128, 32, 4]); inp("conv_b", [128, 32])
    inp("dt_bias", [1, 32]); inp("a_log", [1, 32]); inp("ssd_d", [1, 32]); inp("gnorm", [128, 16]); inp("ssd_w_out", [2048, D])
    inp("final_norm", [128, 8]); inp("selmask", [128, 20])
    inp("tri_incl", [128, 128]); inp("tri_gt", [128, 128]); inp("ntri_incl", [128, 128]); inp("neg_ut", [128, 128])
    scr("XS", [NT, 2048], BF16); scr("BTOK", [NT, 1024], BF16); scr("BT", [1024, NT], BF16); scr("CT", [1024, NT], BF16)
    scr("ZS", [NT, 2048], F32); scr("DTD", [NT, 64], F32)
    T["ST"] = [nc.dram_tensor("ST_%d" % i, [128, 2048], F32).ap() for i in range(16)]; T["b_ST"] = Buf("ST")
    scr("SXa", [128, 2048], F32); scr("SXa_all", [512, 2048], F32); scr("SXb", [128, 32], F32); scr("SXb_all", [512, 32], F32)
    scr("YT", [2048, NT], BF16)
    T["dbg"] = nc.dram_tensor("dbg", [D, NT], F32, kind="ExternalOutput").ap()

    nph = int(os.environ.get("KPH", "99"))
    phase_attn_inproj(nc, T)
    if nph >= 2:
        phase_attn_core(nc, T)
    if nph >= 3:
        phase_attn_out_ffn(nc, T, stage)
    if stage not in ("mix", "attn", "ffn0"):
        if nph >= 4:
            phase_ssd_inproj(nc, T)
        if nph >= 5:
            phase_ssd_states(nc, T)
        if nph >= 6:
            phase_ssd_scan(nc, T)
        if nph >= 7:
            phase_ssd_out_ffn(nc, T, stage)
    Sched.STATE.close()
    return nc


def rope_tables(q):
    half = 32
    inv_freq = (10000.0 ** (-np.arange(half, dtype=np.float32) / half)).astype(np.float32)
    n = np.arange(NT)
    pos = ((4 * (n // 128) + q) * 128 + (n % 128)).astype(np.float32)
    ang = pos[None, :] * inv_freq[:, None]
    cos = np.cos(ang).astype(np.float32)
    sin = np.sin(ang).astype(np.float32)
    tab = np.zeros((128, 2, NT), np.float32)
    for p in range(128):
        d = p % 64
        tab[p, 0] = cos[d % 32]
        tab[p, 1] = -sin[d % 32] if d < 32 else sin[d % 32]
    return tab


def neg_masks(q):
    jj = np.arange(128)[:, None]
    tt = np.arange(128)[None, :]
    out = []
    for strict in (False, True):
        keep_diag = (jj < tt) if strict else (jj <= tt)
        m = np.zeros((128, 16, 512), np.float32)
        for kbz in range(16):
            for i in range(4):
                z = kbz - 4 * i
                blk = m[:, kbz, i * 128:(i + 1) * 128]
                if z < 0:
                    continue
                if z > 3 or z > q:
                    blk[:] = NEG
                elif z == q:
                    blk[:] = np.where(keep_diag, 0.0, NEG)
        out.append(m)
    return out


def make_in_maps(inputs):
    f = lambda a: np.ascontiguousarray(np.asarray(a, dtype=np.float32))
    x = f(inputs["x"])
    w_in = f(inputs["attn_w_in"][0])
    qk = w_in[:, :1024].reshape(D, 16, 2, 32)
    w_sw = np.ascontiguousarray(qk[:, :, ::-1, :].reshape(D, 1024))
    ar = np.arange(128)
    common = {
        "w_in0": w_in, "w_sw0": w_sw, "w_out0": f(inputs["attn_w_out"][0]),
        "attn_norm": f(inputs["attn_norm"][0].reshape(8, 128).T),
        "ident": np.eye(128, dtype=np.float32),
        "nu": -(np.arange(128)[:, None] >= np.arange(128)[None, :]).astype(np.float32),
        "lq1": f(inputs["diff_lq1"]), "lk1": f(inputs["diff_lk1"]), "lq2": f(inputs["diff_lq2"]), "lk2": f(inputs["diff_lk2"]),
        "subln": f(inputs["diff_subln"]),
        "ffn_norm": f(np.stack([inputs["ffn_norm"][l].reshape(8, 128).T for l in range(2)])),
        "ffn_wg": f(inputs["ffn_w_gate"]), "ffn_wu": f(inputs["ffn_w_up"]), "ffn_wd": f(inputs["ffn_w_down"]),
        "ssd_norm": f(inputs["ssd_norm"][0].reshape(8, 128).T), "ssd_w_in": f(inputs["ssd_w_in"][0]),
        "conv_w": f(inputs["ssd_conv_w"][0].reshape(4, 32, 128).transpose(2, 1, 0)),
        "conv_b": f(inputs["ssd_conv_b"][0].reshape(32, 128).T),
        "dt_bias": f(inputs["ssd_dt_bias"]), "a_log": f(inputs["ssd_a_log"]), "ssd_d": f(inputs["ssd_d"]),
        "gnorm": f(inputs["ssd_gnorm"][0].reshape(16, 128).T), "ssd_w_out": f(inputs["ssd_w_out"][0]),
        "final_norm": f(inputs["final_norm"].reshape(8, 128).T),
        "tri_incl": (ar[:, None] <= ar[None, :]).astype(np.float32),
        "tri_gt": (ar[:, None] > ar[None, :]).astype(np.float32),
        "ntri_incl": -(ar[:, None] <= ar[None, :]).astype(np.float32),
        "neg_ut": np.where(ar[None, :] < ar[:, None], NEG, 0.0).astype(np.float32),
    }
    maps = []
    for c in range(8):
        b, q = c // 4, c % 4
        n = np.arange(NT)
        pos = (4 * (n // 128) + q) * 128 + (n % 128)
        m = dict(common)
        m["xT0"] = np.ascontiguousarray(x[b, pos, :].T)
        x1 = np.zeros((D, NX), np.float32)
        x1[:, 3:] = x[b, q * NT:(q + 1) * NT, :].T
        if q > 0:
            x1[:, 0:3] = x[b, q * NT - 3:q * NT, :].T
        m["xT1"] = x1
        m["cs_tab"] = rope_tables(q)
        mi, ms = neg_masks(q)
        m["nm_incl"], m["nm_strict"] = mi, ms
        sel = np.zeros((128, 20), np.float32)
        for r in range(4):
            sel[:, r] = 1.0 if r < q else 0.0
            for r2 in range(4):
                sel[:, 4 + 4 * r + r2] = 1.0 if (r < r2 < q) else 0.0
        m["selmask"] = sel
        maps.append(m)
    return maps


def kernel(**inputs):
    stage = os.environ.get("KSTAGE", "final")
    nc = build_program(stage)
    maps = make_in_maps(inputs)
    res = run_bass_kernel_spmd(nc, maps, core_ids=list(range(8)))
    out = np.zeros((2, SEQ, D), np.float32)
    for c in range(8):
        b, q = c // 4, c % 4
        o = res.results[c]["dbg"]
        out[b, q * NT:(q + 1) * NT, :] = o.T
    return out
```
